# Optimizing a Trainium2 kernel written in Bass

```python
import jax, jax.numpy as jnp
from jax import lax
import numpy as np

D_MODEL = 1024
BATCH = 16
SEQ = 2048
DEPTH = 1

HEAD_DIM = 64
N_HEADS_SB = 8
N_HEADS_DSA = 8
N_IDX_HEADS = 8
IDX_DIM = 64
TOPK_MAX = 256
D_FF = 2816
PLE_DIM = 256
ROPE_THETA = 500000.0
ROPE_DIM = HEAD_DIM // 4
Q_BLOCK = 128
EPS = 1e-6

W_SB = N_HEADS_SB * HEAD_DIM
W_DSA = N_HEADS_DSA * HEAD_DIM
SPLIT_SIZES = (W_SB, W_SB, W_SB,
               W_DSA, HEAD_DIM, HEAD_DIM,
               N_IDX_HEADS * IDX_DIM, IDX_DIM, N_IDX_HEADS,
               D_MODEL, D_MODEL)
D_IN = 3 * W_SB + W_DSA + 2 * HEAD_DIM + N_IDX_HEADS * IDX_DIM + IDX_DIM + N_IDX_HEADS + 2 * D_MODEL

kernel_name = "hybrid_stickbreak_dsa_macaron_block"


def rmsnorm(x, g):
    xf = x.astype(jnp.float32)
    y = xf * lax.rsqrt(jnp.mean(xf * xf, axis=-1, keepdims=True) + EPS)
    return (y * g.astype(jnp.float32)).astype(x.dtype)


def partial_rotary(x, positions):
    half = ROPE_DIM // 2
    inv_freq = ROPE_THETA ** (-jnp.arange(0, ROPE_DIM, 2, dtype=jnp.float32) / ROPE_DIM)
    ang = positions.astype(jnp.float32)[..., None] * inv_freq
    extra = x.ndim - 3
    ang = ang.reshape(ang.shape[:2] + (1,) * extra + (half,))
    cos, sin = jnp.cos(ang), jnp.sin(ang)
    xf = x.astype(jnp.float32)
    x1, x2, rest = xf[..., :half], xf[..., half:ROPE_DIM], xf[..., ROPE_DIM:]
    out = jnp.concatenate([x1 * cos - x2 * sin, x2 * cos + x1 * sin, rest], axis=-1)
    return out.astype(x.dtype)


def swiglu(x, w1, w2):
    a, b = jnp.split(x @ w1, 2, axis=-1)
    return (jax.nn.silu(a) * b) @ w2


def to_blocks(a):
    B, S = a.shape[:2]
    return jnp.moveaxis(a.reshape((B, S // Q_BLOCK, Q_BLOCK) + a.shape[2:]), 1, 0)


def from_blocks(a):
    a = jnp.moveaxis(a, 0, 1)
    return a.reshape((a.shape[0], a.shape[1] * a.shape[2]) + a.shape[3:])


def stick_breaking_attention(q, k, v):
    S, d = q.shape[1], q.shape[3]
    nb = S // Q_BLOCK
    key_idx = jnp.arange(S)

    def block(args):
        blk, qb = args
        t = blk * Q_BLOCK + jnp.arange(Q_BLOCK)
        causal = (key_idx[None, :] < t[:, None])[None, None]
        z = jnp.einsum('bqhd,bkhd->bhqk', qb, k).astype(jnp.float32) * (d ** -0.5)
        log_keep = jnp.where(causal, jax.nn.log_sigmoid(-z), 0.0)
        log_prefix = lax.cumsum(log_keep, axis=3, reverse=True) - log_keep
        a = jnp.where(causal, jnp.exp(jax.nn.log_sigmoid(z) + log_prefix), 0.0)
        return jnp.einsum('bhqk,bkhd->bqhd', a.astype(v.dtype), v)

    out = lax.map(block, (jnp.arange(nb), to_blocks(q)))
    return from_blocks(out)


def dsa_attention(q, k, v, q_idx, k_idx, w_idx):
    S, d = q.shape[1], q.shape[3]
    nb = S // Q_BLOCK
    n_sel = min(TOPK_MAX, S // 4)
    key_idx = jnp.arange(S)
    gather = jax.vmap(lambda src, ids: src[ids])

    def block(args):
        blk, qb, qib, wib = args
        t = blk * Q_BLOCK + jnp.arange(Q_BLOCK)
        causal = key_idx[None, :] <= t[:, None]
        dots = jnp.einsum('bqhd,bkd->bqhk', qib, k_idx).astype(jnp.float32) * (IDX_DIM ** -0.5)
        score = jnp.einsum('bqh,bqhk->bqk',
                           wib.astype(jnp.float32) * (N_IDX_HEADS ** -0.5), jax.nn.relu(dots))
        score = jnp.where(causal[None], score, -jnp.inf)
        _, sel = lax.top_k(score, n_sel)
        valid = sel <= t[None, :, None]
        k_sel = gather(k, sel)
        v_sel = gather(v, sel)
        logits = jnp.einsum('bqhd,bqnd->bqhn', qb, k_sel).astype(jnp.float32) * (d ** -0.5)
        logits = jnp.where(valid[:, :, None, :], logits, -jnp.inf)
        probs = jax.nn.softmax(logits, axis=-1)
        return jnp.einsum('bqhn,bqnd->bqhd', probs.astype(v.dtype), v_sel)

    out = lax.map(block, (jnp.arange(nb), to_blocks(q), to_blocks(q_idx), to_blocks(w_idx)))
    return from_blocks(out)


def split_columns(c):
    parts, off = [], 0
    for size in SPLIT_SIZES:
        parts.append(c[..., off:off + size])
        off += size
    return parts


def setup_inputs(seed: int = 0) -> dict:
    key = jax.random.key(seed)
    ks = jax.random.split(key, 20)
    f32 = jnp.float32

    def w(k, shape, fan_in):
        return jax.random.normal(k, shape, f32) * (fan_in ** -0.5)

    def gain(k, shape):
        return 1.0 + 0.01 * jax.random.normal(k, shape, f32)

    return {
        "x": jax.random.normal(ks[0], (BATCH, SEQ, D_MODEL), f32),
        "p": jax.random.normal(ks[1], (DEPTH, BATCH, SEQ, PLE_DIM), f32),
        "positions": jnp.broadcast_to(jnp.arange(SEQ, dtype=jnp.int32), (BATCH, SEQ)),
        "ffn1_norm": gain(ks[2], (DEPTH, D_MODEL)),
        "ffn1_w1": w(ks[3], (DEPTH, D_MODEL, 2 * D_FF), D_MODEL),
        "ffn1_w2": w(ks[4], (DEPTH, D_FF, D_MODEL), D_FF),
        "mix_norm": gain(ks[5], (DEPTH, D_MODEL)),
        "w_in": w(ks[6], (DEPTH, D_MODEL, D_IN), D_MODEL),
        "w_out_sb": w(ks[7], (DEPTH, W_SB, D_MODEL), W_SB),
        "w_out_dsa": w(ks[8], (DEPTH, W_DSA, D_MODEL), W_DSA),
        "w_out": w(ks[9], (DEPTH, D_MODEL, D_MODEL), D_MODEL),
        "ffn2_norm": gain(ks[10], (DEPTH, D_MODEL)),
        "ffn2_w1": w(ks[11], (DEPTH, D_MODEL, 2 * D_FF), D_MODEL),
        "ffn2_w2": w(ks[12], (DEPTH, D_FF, D_MODEL), D_FF),
        "ple_norm": gain(ks[13], (DEPTH, D_MODEL)),
        "ple_w_gate": w(ks[14], (DEPTH, D_MODEL, D_MODEL), D_MODEL),
        "ple_w_proj": w(ks[15], (DEPTH, PLE_DIM, D_MODEL), PLE_DIM),
        "final_norm": gain(ks[16], (D_MODEL,)),
    }


def reference(x, p, positions, ffn1_norm, ffn1_w1, ffn1_w2, mix_norm, w_in, w_out_sb,
              w_out_dsa, w_out, ffn2_norm, ffn2_w1, ffn2_w2, ple_norm, ple_w_gate,
              ple_w_proj, final_norm):
    B, S, _ = x.shape
    h = x
    for i in range(DEPTH):
        h = h + 0.5 * swiglu(rmsnorm(h, ffn1_norm[i]), ffn1_w1[i], ffn1_w2[i])

        u = rmsnorm(h, mix_norm[i])
        (q_sb, k_sb, v_sb, q_d, k_d, v_d, q_i, k_i, w_i, g_sb, g_dsa) = split_columns(u @ w_in[i])

        y_sb = stick_breaking_attention(q_sb.reshape(B, S, N_HEADS_SB, HEAD_DIM),
                                        k_sb.reshape(B, S, N_HEADS_SB, HEAD_DIM),
                                        v_sb.reshape(B, S, N_HEADS_SB, HEAD_DIM))
        y_sb = y_sb.reshape(B, S, W_SB) @ w_out_sb[i]

        q_d = partial_rotary(q_d.reshape(B, S, N_HEADS_DSA, HEAD_DIM), positions)
        k_d = partial_rotary(k_d, positions)
        q_i = partial_rotary(q_i.reshape(B, S, N_IDX_HEADS, IDX_DIM), positions)
        k_i = partial_rotary(k_i, positions)
        y_dsa = dsa_attention(q_d, k_d, v_d, q_i, k_i, w_i)
        y_dsa = y_dsa.reshape(B, S, W_DSA) @ w_out_dsa[i]

        merged = jax.nn.sigmoid(g_sb) * y_sb + jax.nn.sigmoid(g_dsa) * y_dsa
        h = h + merged @ w_out[i]

        h = h + 0.5 * swiglu(rmsnorm(h, ffn2_norm[i]), ffn2_w1[i], ffn2_w2[i])

        ple_gate = jax.nn.sigmoid(rmsnorm(h, ple_norm[i]) @ ple_w_gate[i])
        h = h + ple_gate * (p[i] @ ple_w_proj[i])
    return rmsnorm(h, final_norm)
```

```python
import contextlib
import numpy as np
import concourse.bass as bass
import concourse.mybir as mybir
from concourse.bass_utils import run_bass_kernel_spmd

F32 = mybir.dt.float32
BF16 = mybir.dt.bfloat16
I32 = mybir.dt.int32
AF = mybir.ActivationFunctionType
ALU = mybir.AluOpType
AX = mybir.AxisListType

D = 1024
KC = 8
SEQ = 2048
NSEQ = 2
NTOK = NSEQ * SEQ
NT = NTOK // 128
TB = 512
NB = NTOK // TB
FF = 2816
FC = FF // 128
DIN = 4808
EPS = 1e-6
TOPK = 256
NBIS = 14
OFF_QSB, OFF_KSB, OFF_VSB, OFF_QD, OFF_KD, OFF_VD, OFF_QI, OFF_KI, OFF_WI, OFF_GSB, OFF_GD = (
    0, 512, 1024, 1536, 2048, 2112, 2176, 2688, 2752, 2760, 3784)
TWO_PI = 6.283185307179586
CW1 = 6.28125
CW2 = TWO_PI - CW1


class _Key:
    __slots__ = ("w", "rs")

    def __init__(self):
        self.w = None
        self.rs = []


class _Rec:
    def __init__(self):
        self.call = None

    def __getattr__(self, name):
        def f(*a, **k):
            self.call = (name, a, k)
            return self
        return f


def _freeze(fn):
    rec = _Rec()
    fn(rec)
    name, a, k = rec.call
    return lambda e: getattr(e, name)(*a, **k)


class Prog:
    ENG = ("pe", "act", "dve", "pool", "sp")

    def __init__(self, nc, same_engine_sync=True, ndma_sems=8):
        self.nc = nc
        self.es = contextlib.ExitStack()
        self.streams = {e: [] for e in self.ENG}
        self.cnt = {e: 0 for e in self.ENG}
        self.sems = {}
        for e in self.ENG:
            self.sems["p_" + e] = self.es.enter_context(nc.semaphore("prog_" + e))
        self.known = {e: {} for e in self.ENG}
        self.same = same_engine_sync
        self.ndma = ndma_sems
        self.dq = {}
        self.keys = {}

    def key(self, name):
        k = self.keys.get(name)
        if k is None:
            k = _Key()
            self.keys[name] = k
        return k

    def _deps(self, reads, writes):
        ev = []
        for r in reads:
            k = self.key(r)
            if k.w is not None:
                ev.append(k.w)
        for w in writes:
            k = self.key(w)
            if k.w is not None:
                ev.append(k.w)
            ev.extend(k.rs)
        return ev

    def _waits(self, eng, evs):
        need = {}
        kn = self.known[eng]
        for (sid, val) in evs:
            if sid == "p_" + eng and (eng == "pe" or not self.same):
                continue
            if kn.get(sid, 0) >= val:
                continue
            if need.get(sid, 0) < val:
                need[sid] = val
        for sid, val in need.items():
            kn[sid] = val
        return list(need.items())

    def _commit(self, reads, writes, event):
        for r in reads:
            self.key(r).rs.append(event)
        for w in writes:
            k = self.key(w)
            k.w = event
            k.rs = []

    def op(self, eng, fn, reads=(), writes=()):
        waits = self._waits(eng, self._deps(reads, writes))
        self.cnt[eng] += 1
        event = ("p_" + eng, self.cnt[eng])
        self.streams[eng].append((waits, _freeze(fn), ("p_" + eng, 1)))
        self._commit(reads, writes, event)
        return event

    def dma(self, q, out, in_, reads=(), writes=()):
        d = self.dq.get(q)
        if d is None:
            ids = [f"d_{q}_{j}" for j in range(self.ndma)]
            for s in ids:
                self.sems[s] = self.es.enter_context(self.nc.semaphore(s))
            d = dict(i=0, ids=ids, vals=[0] * self.ndma, last=[None] * self.ndma)
            self.dq[q] = d
        j = d["i"] % self.ndma
        d["i"] += 1
        evs = self._deps(reads, writes)
        if d["last"][j] is not None:
            evs.append(d["last"][j])
        waits = self._waits(q, evs)
        d["vals"][j] += 16
        event = (d["ids"][j], d["vals"][j])
        d["last"][j] = event
        fn = lambda e, out=out, in_=in_: e.dma_start(out=out, in_=in_)
        self.streams[q].append((waits, fn, (d["ids"][j], 16)))
        self._commit(reads, writes, event)
        return event

    def _all_events(self):
        ev = []
        for q, d in self.dq.items():
            for e in d["last"]:
                if e is not None:
                    ev.append(e)
        for e in self.ENG:
            if self.cnt[e] > 0:
                ev.append(("p_" + e, self.cnt[e]))
        return ev

    def run_threads(self, gens):
        live = list(gens)
        while live:
            nxt = []
            for g in live:
                try:
                    next(g)
                    nxt.append(g)
                except StopIteration:
                    pass
            live = nxt

    def barrier(self):
        ev = self._all_events()
        for e in self.ENG:
            w = self._waits(e, ev)
            if w:
                self.streams[e].append((w, None, None))
        self.keys = {}

    def emit(self):
        nc = self.nc
        fw = self._waits("sp", self._all_events())
        self.streams["sp"].append((fw, None, None))
        with nc.Block() as block:
            def run(engname, handle):
                for waits, fn, inc in self.streams[engname]:
                    for sid, val in waits:
                        handle.wait_ge(self.sems[sid], val)
                    if fn is not None:
                        fn(handle).then_inc(self.sems[inc[0]], inc[1])

            @block.tensor
            def _(e):
                run("pe", e)

            @block.scalar
            def _(e):
                run("act", e)

            @block.vector
            def _(e):
                run("dve", e)

            @block.gpsimd
            def _(e):
                run("pool", e)

            @block.sync
            def _(e):
                run("sp", e)
        self.es.close()


def build(nc, dbg=False, phases=("p1", "p2", "p3", "p4a", "p4b", "p4c")):
    P = Prog(nc)

    def din(name, shape, dt=F32):
        return nc.dram_tensor(name, shape, dt, kind="ExternalInput").ap()

    def dscr(name, shape, dt):
        return nc.dram_tensor(name, shape, dt, kind=("ExternalOutput" if dbg else "Internal")).ap()

    x_d = din("x", [NTOK, D])
    p_d = din("p", [NTOK, 256])
    pos_d = din("pos", [1, NTOK], I32)
    w1a_d = din("w1a", [D, 2 * FF])
    w2a_d = din("w2a", [FF, D])
    win_d = din("win", [D, DIN])
    wosb_d = din("wosb", [512, D])
    wod_d = din("wod", [512, D])
    wo_d = din("wo", [D, D])
    w1b_d = din("w1b", [D, 2 * FF])
    w2b_d = din("w2b", [FF, D])
    wpg_d = din("wpg", [D, D])
    wpp_d = din("wpp", [256, D])
    gcols_d = din("gcols", [128, 4 * KC])
    gfin_d = din("gfin", [1, D])
    cb_d = din("cb", [128, 384 + 2048])
    cf_d = din("cf", [128, 130])
    out_d = nc.dram_tensor("out", [NTOK, D], F32, kind="ExternalOutput").ap()

    h_s = dscr("h_s", [NTOK, D], F32)
    uT_s = dscr("uT_s", [KC, 128, NTOK], BF16)
    qk_s = dscr("qk_s", [18, 128, NTOK], BF16)
    v_s = dscr("v_s", [NTOK, 576], BF16)
    wi_s = dscr("wi_s", [NTOK, 8], F32)
    ysbT_s = dscr("ysbT_s", [8, 64, NTOK], BF16)
    ydT_s = dscr("ydT_s", [4, 128, NTOK], BF16)

    def tsl(t):
        return slice(t * 128, (t + 1) * 128)

    ccount = [0]

    def load_consts(es, need_cb=True):
        ccount[0] += 1
        sb = lambda n, s, d=F32: es.enter_context(nc.sbuf_tensor("%s_%d" % (n, ccount[0]), s, d))
        c = {}
        c["gcols"] = sb("c_gcols", [128, 4 * KC])
        P.dma("sp", c["gcols"][:], gcols_d, writes=["c_gcols"])
        c["ident"] = sb("c_ident", [128, 128], BF16)
        P.dma("pool", c["ident"][:], cb_d[:, 0:128], writes=["c_ident"])
        return c

    def make_gB(es, c, which, name):
        gB = es.enter_context(nc.sbuf_tensor(name, [128, KC, 128], F32))
        src = c["gcols"][:, which * KC:(which + 1) * KC]
        P.op("dve", lambda e: e.tensor_copy(gB[:], src.unsqueeze(2).to_broadcast([128, KC, 128])),
             reads=["c_gcols"], writes=[name])
        return gB

    def rstd_from_ss(ss, rstd, n, rkeys, wkey):
        P.op("dve", lambda e: e.tensor_scalar(ss[:, 0:n], ss[:, 0:n], 1.0 / D, EPS, op0=ALU.mult, op1=ALU.add),
             reads=rkeys, writes=rkeys)
        P.op("act", lambda e: e.activation(out=ss[:, 0:n], in_=ss[:, 0:n], func=AF.Sqrt), reads=rkeys, writes=rkeys)
        P.op("dve", lambda e: e.reciprocal(rstd[:, 0:n], ss[:, 0:n]), reads=rkeys, writes=[wkey])

    def ffn_phase(tag, src_d, dst_d, w1_d, w2_d, gsel, post_gsel):
        with contextlib.ExitStack() as es:
            sb = lambda n, s, d=F32: es.enter_context(nc.sbuf_tensor(tag + n, s, d))
            ps = lambda n, s, d=F32: es.enter_context(nc.psum_tensor(tag + n, s, d))
            c = load_consts(es)
            w1 = sb("w1", [128, KC, 2 * FF], BF16)
            w2 = sb("w2", [128, FC, D], BF16)
            for k in range(KC):
                P.dma("pool", w1[:, k, :], w1_d[tsl(k), :], writes=["w1"])
            for k in range(FC):
                P.dma("pool", w2[:, k, :], w2_d[tsl(k), :], writes=["w2"])
            gB = make_gB(es, c, gsel, tag + "gB")
            gB2 = make_gB(es, c, post_gsel, tag + "gB2") if post_gsel is not None else None
            NXS = 6 if post_gsel is not None else 8
            xs = sb("xs", [128, NXS, D])
            issued = set()

            def issue_load(t):
                if t in issued or t >= NT:
                    return
                issued.add(t)
                P.dma("sp", xs[:, t % NXS, :], src_d[tsl(t), :], reads=["hd%d" % t], writes=["xs%d" % (t % NXS)])
            xn = sb("xn", [128, 2, D], BF16)
            xnT = sb("xnT", [128, KC, TB], BF16)
            gT = sb("gT", [128, FC, TB], BF16)
            stmp = sb("stmp", [128, 2, TB])
            ss = sb("ss", [128, 2, 4])
            rstd = sb("rstd", [128, 2, 4])
            ss2 = sb("ss2", [128, 2, 4])
            rstd2 = sb("rstd2", [128, 2, 4])
            ust = sb("ust", [128, 2, KC, 128], BF16) if post_gsel is not None else None
            pT = [ps("pT%d" % i, [128, KC, 128], BF16) for i in range(2)]
            pA = [ps("pA%d" % i, [128, TB]) for i in range(2)]
            pB = [ps("pB%d" % i, [128, TB]) for i in range(2)]
            pO = [ps("pO%d" % i, [128, 512]) for i in range(2)]
            ident = c["ident"]
            xnc = [0]
            ptc = [0]

            def norm_T(tile_ap, xs_key, rstd_col, rstd_key, gBt, gB_key, out_ap, out_key, dve_out=True):
                s = xnc[0] % 2
                xnc[0] += 1
                q = ptc[0] % 2
                ptc[0] += 1
                P.op("dve", lambda e: e.tensor_scalar(xn[:, s, :], tile_ap, rstd_col, None, op0=ALU.mult),
                     reads=[xs_key, rstd_key], writes=["xn%d" % s])
                for k in range(KC):
                    P.op("pe", lambda e, k=k: e.transpose(pT[q][:, k, :], xn[:, s, tsl(k)], ident[:]),
                         reads=["xn%d" % s, "c_ident"], writes=["pT%d" % q])
                P.op("dve", lambda e: e.tensor_tensor(out_ap, pT[q][:], gBt[:], op=ALU.mult),
                     reads=[gB_key], writes=["pT%d" % q, out_key])

            for b in range(NB):
                sp_ = b % 2
                tiles = [4 * b + i for i in range(4)]
                for i, t in enumerate(tiles):
                    sl = t % NXS
                    issue_load(t)
                    s = xnc[0] % 2
                    P.op("act", lambda e, sl=sl, s=s, i=i: e.activation(out=xn[:, s, :], in_=xs[:, sl, :], func=AF.Square,
                                                                       accum_out=ss[:, sp_, i:i + 1]),
                         reads=["xs%d" % sl], writes=["xn%d" % s, "ss%d" % sp_])
                rstd_from_ss(ss[:, sp_, :], rstd[:, sp_, :], 4, ["ss%d" % sp_], "rstd%d" % sp_)
                for i, t in enumerate(tiles):
                    sl = t % NXS
                    norm_T(xs[:, sl, :], "xs%d" % sl, rstd[:, sp_, i:i + 1], "rstd%d" % sp_, gB, tag + "gB",
                           xnT[:, :, tsl(i)], "xnT")
                for j in range(FC):
                    q = j % 2
                    for k in range(KC):
                        P.op("pe", lambda e, k=k, j=j, q=q: e.matmul(pA[q][:], w1[:, k, tsl(j)], xnT[:, k, :],
                                                                    start=(k == 0), stop=(k == KC - 1)),
                             reads=["w1", "xnT"], writes=["pA%d" % q])
                    for k in range(KC):
                        P.op("pe", lambda e, k=k, j=j, q=q: e.matmul(pB[q][:], w1[:, k, FF + j * 128:FF + (j + 1) * 128],
                                                                    xnT[:, k, :], start=(k == 0), stop=(k == KC - 1)),
                             reads=["w1", "xnT"], writes=["pB%d" % q])
                    P.op("act", lambda e, q=q: e.activation(out=stmp[:, q, :], in_=pA[q][:], func=AF.Silu),
                         reads=[], writes=["pA%d" % q, "stmp%d" % q])
                    P.op("dve", lambda e, q=q, j=j: e.tensor_tensor(gT[:, j, :], stmp[:, q, :], pB[q][:], op=ALU.mult),
                         reads=["stmp%d" % q], writes=["pB%d" % q, "gT"])
                for t2_ in range(4 * b + 4, 4 * b + 4 + (NXS - 4)):
                    issue_load(t2_)
                oc = 0
                for i, t in enumerate(tiles):
                    sl = t % NXS
                    for c2 in range(2):
                        q = oc % 2
                        oc += 1
                        for j in range(FC):
                            P.op("pe", lambda e, j=j, i=i, c2=c2, q=q: e.matmul(
                                pO[q][:], gT[:, j, tsl(i)], w2[:, j, c2 * 512:(c2 + 1) * 512],
                                start=(j == 0), stop=(j == FC - 1)),
                                reads=["w2", "gT"], writes=["pO%d" % q])
                        P.op("dve", lambda e, q=q, sl=sl, c2=c2: e.scalar_tensor_tensor(
                            xs[:, sl, c2 * 512:(c2 + 1) * 512], pO[q][:], 0.5, xs[:, sl, c2 * 512:(c2 + 1) * 512],
                            op0=ALU.mult, op1=ALU.add),
                            reads=[], writes=["pO%d" % q, "xs%d" % sl])
                    P.dma("sp", dst_d[tsl(t), :], xs[:, sl, :], reads=["xs%d" % sl], writes=["hd%d" % t])
                    if post_gsel is not None:
                        s = xnc[0] % 2
                        P.op("act", lambda e, sl=sl, s=s, i=i: e.activation(out=xn[:, s, :], in_=xs[:, sl, :], func=AF.Square,
                                                                           accum_out=ss2[:, sp_, i:i + 1]),
                             reads=["xs%d" % sl], writes=["xn%d" % s, "ss2%d" % sp_])
                if post_gsel is not None:
                    rstd_from_ss(ss2[:, sp_, :], rstd2[:, sp_, :], 4, ["ss2%d" % sp_], "rstd2%d" % sp_)
                    for i, t in enumerate(tiles):
                        sl = t % NXS
                        u = t % 2
                        norm_T(xs[:, sl, :], "xs%d" % sl, rstd2[:, sp_, i:i + 1], "rstd2%d" % sp_, gB2, tag + "gB2",
                               ust[:, u, :, :], "ust%d" % u)
                        P.dma("sp", uT_s[:, :, tsl(t)].rearrange("c p t -> p c t"), ust[:, u, :, :],
                              reads=["ust%d" % u], writes=["uTd%d" % t])
            P.barrier()

    def p2_phase():
        with contextlib.ExitStack() as es:
            sb = lambda n, s, d=F32: es.enter_context(nc.sbuf_tensor("p2" + n, s, d))
            ps = lambda n, s, d=F32: es.enter_context(nc.psum_tensor("p2" + n, s, d))
            win = sb("win", [128, KC, 2760], BF16)
            for k in range(KC):
                P.dma("pool", win[:, k, :], win_d[tsl(k), 0:2760], writes=["win"])
            wp = sb("wp", [128, KC, 1280], BF16)
            wk2 = sb("wk2", [128, KC, 256], BF16)
            cf = sb("cf", [128, 130])
            P.dma("sp", cf[:], cf_d, writes=["cf"])
            posi = sb("posi", [128, NTOK], I32)
            P.dma("sp", posi[:], pos_d.broadcast_to([128, NTOK]), writes=["posi"])
            ang = sb("ang", [128, NTOK])
            kk = sb("kk", [128, NTOK])
            kki = sb("kki", [128, NTOK], I32)
            Ct = sb("Ct", [128, NTOK])
            St = sb("St", [128, NTOK])
            invf = cf[:, 128:129]
            sgn = cf[:, 129:130]
            P.op("dve", lambda e: e.tensor_copy(ang[:], posi[:]), reads=["posi"], writes=["ang"])
            P.op("dve", lambda e: e.tensor_scalar(ang[:], ang[:], invf, None, op0=ALU.mult), reads=["ang", "cf"], writes=["ang"])

            def reduce_to(dst, shift, key):
                P.op("dve", lambda e: e.tensor_scalar(kk[:], ang[:], shift, 1.0 / TWO_PI, op0=ALU.add, op1=ALU.mult),
                     reads=["ang"], writes=["kk"])
                P.op("dve", lambda e: e.tensor_copy(kki[:], kk[:]), reads=["kk"], writes=["kki"])
                P.op("dve", lambda e: e.tensor_copy(kk[:], kki[:]), reads=["kki"], writes=["kk"])
                P.op("dve", lambda e: e.scalar_tensor_tensor(dst[:], kk[:], -CW1, ang[:], op0=ALU.mult, op1=ALU.add),
                     reads=["kk", "ang"], writes=[key])
                P.op("dve", lambda e: e.scalar_tensor_tensor(dst[:], kk[:], -CW2, dst[:], op0=ALU.mult, op1=ALU.add),
                     reads=["kk"], writes=[key])
                P.op("dve", lambda e: e.tensor_scalar(dst[:], dst[:], shift, 3.1415925, op0=ALU.add, op1=ALU.min),
                     reads=[], writes=[key])
                P.op("dve", lambda e: e.tensor_scalar(dst[:], dst[:], -3.1415925, None, op0=ALU.max), reads=[], writes=[key])
                P.op("act", lambda e: e.activation(out=dst[:], in_=dst[:], func=AF.Sin), reads=[], writes=[key])

            reduce_to(St, 0.0, "St")
            P.op("dve", lambda e: e.tensor_scalar(St[:], St[:], sgn, None, op0=ALU.mult), reads=["cf"], writes=["St"])
            reduce_to(Ct, float(np.pi / 2), "Ct")
            P.op("pool", lambda e: e.tensor_scalar(win[:, :, 0:512], win[:, :, 0:512], 0.125, None, op0=ALU.mult),
                 reads=[], writes=["win"])
            P.op("pool", lambda e: e.tensor_scalar(win[:, :, OFF_QD:OFF_QD + 512], win[:, :, OFF_QD:OFF_QD + 512], 0.125, None,
                                                   op0=ALU.mult), reads=[], writes=["win"])
            P.op("pool", lambda e: e.memset(wp[:], 0.0), writes=["wp"])
            for hh in range(2):
                P.op("pool", lambda e, hh=hh: e.tensor_copy(wk2[:, :, hh * 64:(hh + 1) * 64], win[:, :, OFF_KD:OFF_KD + 64]),
                     reads=["win"], writes=["wk2"])
                P.op("pool", lambda e, hh=hh: e.tensor_copy(wk2[:, :, 128 + hh * 64:128 + (hh + 1) * 64],
                                                            win[:, :, OFF_KI:OFF_KI + 64]), reads=["win"], writes=["wk2"])
            pbase = [OFF_QD + 128 * j for j in range(4)] + [OFF_QI + 128 * j for j in range(4)]
            for pc in range(10):
                for hh in range(2):
                    if pc < 8:
                        b0 = pbase[pc] + 64 * hh
                    else:
                        b0 = OFF_KD if pc == 8 else OFF_KI
                    o = pc * 128 + 64 * hh
                    P.op("pool", lambda e, o=o, b0=b0: e.tensor_copy(wp[:, :, o:o + 8], win[:, :, b0 + 8:b0 + 16]),
                         reads=["win"], writes=["wp"])
                    P.op("pool", lambda e, o=o, b0=b0: e.tensor_copy(wp[:, :, o + 8:o + 16], win[:, :, b0:b0 + 8]),
                         reads=["win"], writes=["wp"])
            uT = sb("uT", [128, 2, KC, TB], BF16)
            fst = sb("fst", [128, 4, TB], BF16)
            t1 = sb("t1", [128, 2, TB])
            t2 = sb("t2", [128, 2, TB])
            vst = sb("vst", [128, 2, 576], BF16)
            wist = sb("wist", [128, 2, 8])
            pA = [ps("pA%d" % i, [128, TB]) for i in range(3)]
            pB = [ps("pB%d" % i, [128, TB]) for i in range(2)]
            pV = [ps("pV%d" % i, [128, 512]) for i in range(2)]
            pW = ps("pW", [128, 128])
            chunks = []
            for j in range(4):
                chunks.append((j, win, OFF_QSB + 128 * j, None))
            for j in range(4):
                chunks.append((4 + j, win, OFF_KSB + 128 * j, None))
            for j in range(4):
                chunks.append((8 + j, win, OFF_QD + 128 * j, j))
            for j in range(4):
                chunks.append((12 + j, win, OFF_QI + 128 * j, 4 + j))
            chunks.append((16, wk2, 0, 8))
            chunks.append((17, wk2, 128, 9))
            ca = cbn = fs = rc = vc = 0
            def load_uT(b):
                if b < NB:
                    P.dma("sp", uT[:, b % 2, :, :], uT_s[:, :, b * TB:(b + 1) * TB].rearrange("c p t -> p c t"),
                          reads=["uTd%d" % t for t in range(4 * b, 4 * b + 4)], writes=["uT%d" % (b % 2)])
            load_uT(0)
            for b in range(NB):
                u = b % 2
                load_uT(b + 1)
                tok = slice(b * TB, (b + 1) * TB)
                for (ci, wt, off, pidx) in chunks:
                    qa = ca % 3
                    ca += 1
                    for k in range(KC):
                        P.op("pe", lambda e, k=k, wt=wt, off=off, qa=qa: e.matmul(pA[qa][:], wt[:, k, off:off + 128], uT[:, u, k, :],
                                                                                 start=(k == 0), stop=(k == KC - 1)),
                             reads=["win", "wk2", "uT%d" % u], writes=["pA%d" % qa])
                    f = fs % 4
                    fs += 1
                    if pidx is None:
                        P.op("act", lambda e, qa=qa, f=f: e.copy(fst[:, f, :], pA[qa][:]), reads=[], writes=["pA%d" % qa, "fst%d" % f])
                    else:
                        qb = cbn % 2
                        cbn += 1
                        for k in range(KC):
                            P.op("pe", lambda e, k=k, pidx=pidx, qb=qb: e.matmul(pB[qb][:], wp[:, k, pidx * 128:(pidx + 1) * 128],
                                                                                uT[:, u, k, :], start=(k == 0), stop=(k == KC - 1)),
                                 reads=["wp", "uT%d" % u], writes=["pB%d" % qb])
                        r = rc % 2
                        rc += 1
                        P.op("dve", lambda e, qa=qa, r=r: e.tensor_tensor(t1[:, r, :], pA[qa][:], Ct[:, tok], op=ALU.mult),
                             reads=["Ct"], writes=["pA%d" % qa, "t1%d" % r])
                        P.op("dve", lambda e, qb=qb, r=r: e.tensor_tensor(t2[:, r, :], pB[qb][:], St[:, tok], op=ALU.mult),
                             reads=["St"], writes=["pB%d" % qb, "t2%d" % r])
                        P.op("pool", lambda e, r=r, f=f: e.tensor_tensor(fst[:, f, :], t1[:, r, :], t2[:, r, :], op=ALU.add),
                             reads=["t1%d" % r, "t2%d" % r], writes=["fst%d" % f])
                    P.dma("sp", qk_s[ci, :, tok], fst[:, f, :], reads=["fst%d" % f], writes=["qkd%d_%d" % (ci, b)])
                for i in range(4):
                    t = 4 * b + i
                    q = vc % 2
                    vc += 1
                    for k in range(KC):
                        P.op("pe", lambda e, k=k, i=i, q=q: e.matmul(pV[q][:], uT[:, u, k, tsl(i)], win[:, k, OFF_VSB:OFF_VSB + 512],
                                                                    start=(k == 0), stop=(k == KC - 1)),
                             reads=["win", "uT%d" % u], writes=["pV%d" % q])
                    for k in range(KC):
                        P.op("pe", lambda e, k=k, i=i: e.matmul(pW[:, 0:64], uT[:, u, k, tsl(i)], win[:, k, OFF_VD:OFF_VD + 64],
                                                               start=(k == 0), stop=(k == KC - 1), skip_group_check=True),
                             reads=["win", "uT%d" % u], writes=["pW"])
                    for k in range(KC):
                        P.op("pe", lambda e, k=k, i=i: e.matmul(pW[:, 64:72], uT[:, u, k, tsl(i)], win[:, k, OFF_WI:OFF_WI + 8],
                                                               start=False, stop=(k == KC - 1), skip_group_check=True),
                             reads=["win", "uT%d" % u], writes=["pW"])
                    P.op("act", lambda e, q=q: e.copy(vst[:, q, 0:512], pV[q][:]), reads=[], writes=["pV%d" % q, "vst%d" % q])
                    P.op("dve", lambda e, q=q: e.tensor_copy(vst[:, q, 512:576], pW[:, 0:64]), reads=[], writes=["pW", "vst%d" % q])
                    P.op("dve", lambda e, q=q: e.tensor_scalar(wist[:, q, :], pW[:, 64:72], float(8 ** -0.5 * 0.125), None, op0=ALU.mult),
                         reads=[], writes=["pW", "wist%d" % q])
                    P.dma("sp", v_s[tsl(t), :], vst[:, q, :], reads=["vst%d" % q], writes=["vd%d" % t])
                    P.dma("sp", wi_s[tsl(t), :], wist[:, q, :], reads=["wist%d" % q], writes=["wid%d" % t])
            P.barrier()

    def p3_phase():
        with contextlib.ExitStack() as es:
            sb = lambda n, s, d=F32: es.enter_context(nc.sbuf_tensor("p3" + n, s, d))
            cb = sb("cb", [128, 384 + 2048], BF16)
            P.dma("pool", cb[:], cb_d, writes=["cb"])
            cf = sb("cf", [128, 130])
            P.dma("sp", cf[:], cf_d, writes=["cf"])
            ident = cb[:, 0:128]
            nUincl = cb[:, 128:256]
            nLstr = cb[:, 256:384]
            sbmask = cb[:, 384:384 + 2048].rearrange("p (r t) -> p r t", r=4)
            dsaneg = cf[:, 0:128]
            qz = [sb("qz%d" % i, [128, 4, SEQ], BF16) for i in range(2)]
            P.op("pool", lambda e: e.memset(qz[0][64:128, :, :], 0.0), writes=["qz0z"])
            P.op("pool", lambda e: e.memset(qz[1][0:64, :, :], 0.0), writes=["qz1z"])
            ksb = sb("ksb", [128, 4, SEQ], BF16)
            nksb = sb("nksb", [128, 4, SEQ], BF16)
            qd = sb("qd", [128, 4, SEQ], BF16)
            qi = sb("qi", [128, 4, SEQ], BF16)
            kd2 = sb("kd2", [128, SEQ], BF16)
            ki2 = sb("ki2", [128, SEQ], BF16)
            v = sb("v", [128, 16, 578], BF16)
            wi = sb("wi", [128, 16, 8])
            P.op("pool", lambda e: e.memset(v[:, :, 576:578], 0.0), writes=["vone"])
            P.op("pool", lambda e: e.memset(v[:, :, 576:577], 1.0), writes=["vone"])
            for sq in range(NSEQ):
                tok = slice(sq * SEQ, (sq + 1) * SEQ)
                rk = lambda ci: ["qkd%d_%d" % (ci, b) for b in range(4 * sq, 4 * sq + 4)]
                for j in range(4):
                    P.dma("sp", qz[0][0:64, j, :], qk_s[j, 0:64, tok], reads=rk(j), writes=["qsb"])
                    P.dma("sp", qz[1][64:128, j, :], qk_s[j, 64:128, tok], reads=rk(j), writes=["qsb"])
                for (tile_, c0, key) in ((ksb, 4, "ksb"), (qd, 8, "qd"), (qi, 12, "qi")):
                    for j in range(4):
                        P.dma("sp", tile_[:, j, :], qk_s[c0 + j, :, tok], reads=rk(c0 + j), writes=[key])
                P.dma("sp", kd2[:], qk_s[16, :, tok], reads=rk(16), writes=["kd2"])
                P.dma("sp", ki2[:], qk_s[17, :, tok], reads=rk(17), writes=["ki2"])
                P.dma("act", v[:, :, 0:576], v_s[tok, :].rearrange("(n p) c -> p n c", p=128), writes=["v"])
                P.dma("act", wi[:], wi_s[tok, :].rearrange("(n p) c -> p n c", p=128), writes=["wi"])
                P.op("pool", lambda e: e.tensor_scalar(nksb[:], ksb[:], -1.0, None, op0=ALU.mult), reads=["ksb"], writes=["nksb"])

                with contextlib.ExitStack() as es2:
                  if "nosb" not in phases:
                    sb2 = lambda n, s, d=F32: es2.enter_context(nc.sbuf_tensor("sb%d" % sq + n, s, d))
                    ps2 = lambda n, s, d=F32: es2.enter_context(nc.psum_tensor("sb%d" % sq + n, s, d))
                    NS = 4
                    E = sb2("E", [128, NS, 512])
                    SP = sb2("SP", [128, NS, 2, 512], BF16)
                    A = sb2("A", [128, NS, 2, 512], BF16)
                    yacc = sb2("yacc", [64, NS, 512])
                    yst = sb2("yst", [64, NS, 512], BF16)
                    pZ = [ps2("pZ%d" % i, [128, 512]) for i in range(2)]
                    pC = [ps2("pC%d" % i, [128, 512]) for i in range(NS)]
                    pY = [ps2("pY%d" % i, [64, 512]) for i in range(2)]
                    mask128 = sbmask[:, 0, 0:128]

                    def sb_stream(s_, h, qc):
                        j, half = h // 2, h % 2
                        po = slice(64 * half, 64 * half + 64)
                        zb = s_ % 2
                        S = "s%d" % s_
                        kmax = 4 * qc + 3
                        nstep = kmax + 1
                        P.op("pool", lambda e: e.memset(yacc[:, s_, :], 0.0), writes=["yacc" + S])
                        if s_ >= 2:
                            yield
                        for step in range(nstep):
                            kb = kmax - step
                            r = kb - 4 * qc
                            c0 = 128 * max(0, r)
                            cols = slice(c0, 512)
                            dcols = slice(c0, c0 + 128)
                            qcols = slice(qc * 512 + c0, (qc + 1) * 512)
                            kcols = tsl(kb)
                            par = step % 2
                            spk = "SP%s%d" % (S, par)
                            ak = "A%s%d" % (S, par)
                            P.op("pe", lambda e: e.matmul(pZ[zb][:, cols], ksb[:, j, kcols], qz[half][:, j, qcols], start=True, stop=True,
                                                          skip_group_check=True),
                                 reads=["ksb", "qsb", "qz0z", "qz1z"], writes=["pZ%d" % zb])
                            yield
                            P.op("act", lambda e: e.activation(out=E[:, s_, cols], in_=pZ[zb][:, cols], func=AF.Exp),
                                 reads=[], writes=["pZ%d" % zb, "E" + S])
                            P.op("act", lambda e: e.activation(out=SP[:, s_, par, cols], in_=E[:, s_, cols], func=AF.Ln, bias=1.0),
                                 reads=["E" + S], writes=[spk])
                            if r >= 0:
                                P.op("pool", lambda e: e.tensor_tensor(SP[:, s_, par, dcols], SP[:, s_, par, dcols], mask128, op=ALU.mult),
                                     reads=["cb", spk], writes=[spk])
                            yield
                            P.op("pe", lambda e: e.matmul(pC[s_][:, cols], ksb[:, j, kcols], qz[half][:, j, qcols], start=(step == 0), stop=False,
                                                          skip_group_check=True),
                                 reads=["ksb", "qsb"], writes=["pC" + S])
                            P.op("pe", lambda e: e.matmul(pC[s_][:, cols], nUincl, SP[:, s_, par, cols], start=False, stop=True,
                                                          skip_group_check=True),
                                 reads=["cb", spk], writes=["pC" + S])
                            yield
                            P.op("act", lambda e: e.activation(out=A[:, s_, par, cols], in_=pC[s_][:, cols], func=AF.Exp),
                                 reads=[], writes=["pC" + S, ak])
                            if r >= 0:
                                P.op("pool", lambda e: e.tensor_tensor(A[:, s_, par, dcols], A[:, s_, par, dcols], mask128, op=ALU.mult),
                                     reads=["cb", ak], writes=[ak])
                            yield
                            P.op("pe", lambda e: e.matmul(pY[zb][:, cols], v[:, kb, h * 64:(h + 1) * 64], A[:, s_, par, cols],
                                                          start=True, stop=True, skip_group_check=True),
                                 reads=["v", ak], writes=["pY%d" % zb])
                            if step < nstep - 1:
                                P.op("pe", lambda e: e.matmul(pC[s_][:, cols], nksb[:, j, kcols], qz[half][:, j, qcols], start=False, stop=False,
                                                              skip_group_check=True),
                                     reads=["nksb", "qsb"], writes=["pC" + S])
                                P.op("pe", lambda e: e.matmul(pC[s_][:, cols], nLstr, SP[:, s_, par, cols], start=False, stop=False,
                                                              skip_group_check=True),
                                     reads=["cb", spk], writes=["pC" + S])
                            P.op("dve", lambda e: e.tensor_tensor(yacc[:, s_, cols], yacc[:, s_, cols], pY[zb][:, cols], op=ALU.add),
                                 reads=["yacc" + S], writes=["pY%d" % zb, "yacc" + S])
                            yield
                        P.op("dve", lambda e: e.tensor_copy(yst[:, s_, :], yacc[:, s_, :]), reads=["yacc" + S], writes=["yst" + S])
                        P.dma("sp", ysbT_s[h, :, sq * SEQ + qc * 512: sq * SEQ + (qc + 1) * 512], yst[:, s_, :],
                              reads=["yst" + S], writes=["ysbd%d_%d" % (h, sq * 4 + qc)])

                    for qc in range(4):
                        for g in range(2):
                            P.run_threads([sb_stream(s_, 4 * g + s_, qc) for s_ in range(NS)])
                    P.barrier()

                with contextlib.ExitStack() as es2:
                  if "nodsa" not in phases:
                    sb2 = lambda n, s, d=F32: es2.enter_context(nc.sbuf_tensor("ds%d" % sq + n, s, d))
                    ps2 = lambda n, s, d=F32: es2.enter_context(nc.psum_tensor("ds%d" % sq + n, s, d))
                    Sc = sb2("Sc", [128, 2, SEQ])
                    R = sb2("R", [128, 2, 512])
                    Mb = sb2("Mb", [128, SEQ], BF16)
                    MT = sb2("MT", [128, 2, 16, 128], BF16)
                    junk = sb2("junk", [128, SEQ], BF16)
                    Pe = sb2("Pe", [128, 2, 512], BF16)
                    PT = sb2("PT", [128, 2, 512], BF16)
                    yd = sb2("yd", [128, 512], BF16)
                    ydst = sb2("ydst", [128, 2, 4, 128], BF16)
                    sm = sb2("sm", [128, 8])
                    rec = sb2("rec", [128, 8, 1])
                    pD = [ps2("pD%d" % i, [128, 512]) for i in range(2)]
                    pM = ps2("pM", [128, 8, 128], BF16)
                    pL = [ps2("pL%d" % i, [128, 512]) for i in range(2)]
                    pYd = ps2("pYd", [128, 2, 512])
                    pM2 = ps2("pM2", [128, 4, 128], BF16)
                    cnts = dict(d=0, l=0)
                    lo, hi, mid, cnt, dlt = (sm[:, i:i + 1] for i in range(5))

                    def stage1(i):
                        nk = (i + 1) * 128
                        nch = (nk + 511) // 512
                        tcols = tsl(i)
                        z = i % 2
                        sck = "Sc%d" % z
                        for hh in range(8):
                            j, s_ = hh // 2, hh % 2
                            po = slice(64 * s_, 64 * s_ + 64)
                            for c in range(nch):
                                n = min(512, nk - c * 512)
                                q = cnts["d"] % 2
                                cnts["d"] += 1
                                cc = slice(c * 512, c * 512 + n)
                                P.op("pe", lambda e: e.matmul(pD[q][:, 0:n], qi[po, j, tcols], ki2[po, cc], start=True, stop=True),
                                     reads=["qi", "ki2"], writes=["pD%d" % q])
                                P.op("act", lambda e: e.activation(out=R[:, q, 0:n], in_=pD[q][:, 0:n], func=AF.Relu),
                                     reads=[], writes=["pD%d" % q, "R%d" % q])
                                if hh == 0:
                                    P.op("dve", lambda e: e.tensor_scalar(Sc[:, z, cc], R[:, q, 0:n], wi[:, i, 0:1], None, op0=ALU.mult),
                                         reads=["R%d" % q, "wi"], writes=[sck])
                                else:
                                    P.op("dve", lambda e: e.scalar_tensor_tensor(Sc[:, z, cc], R[:, q, 0:n], wi[:, i, hh:hh + 1], Sc[:, z, cc],
                                                                               op0=ALU.mult, op1=ALU.add),
                                         reads=["R%d" % q, "wi", sck], writes=[sck])
                                yield

                    def stage2(i):
                        nk = (i + 1) * 128
                        z = i % 2
                        sck = "Sc%d" % z
                        if i >= 2:
                            P.op("dve", lambda e: e.tensor_reduce(lo, Sc[:, z, 0:nk], axis=AX.X, op=ALU.min), reads=[sck], writes=["sm"])
                        P.op("dve", lambda e: e.tensor_tensor(Sc[:, z, nk - 128:nk], Sc[:, z, nk - 128:nk], dsaneg, op=ALU.add),
                             reads=["cf", sck], writes=[sck])
                        yield
                        if i >= 2:
                            P.op("dve", lambda e: e.tensor_reduce(hi, Sc[:, z, 0:nk], axis=AX.X, op=ALU.max), reads=[sck], writes=["sm"])
                            P.op("dve", lambda e: e.tensor_tensor(hi, hi, lo, op=ALU.subtract), reads=["sm"], writes=["sm"])
                            for it in range(NBIS):
                                ck = float(0.5 ** (it + 1))
                                P.op("dve", lambda e: e.scalar_tensor_tensor(mid, hi, ck, lo, op0=ALU.mult, op1=ALU.add),
                                     reads=["sm"], writes=["sm"])
                                P.op("dve", lambda e: e.tensor_scalar(junk[:, 0:nk], Sc[:, z, 0:nk], mid, 0.0, op0=ALU.is_gt, op1=ALU.add,
                                                                      accum_out=cnt), reads=[sck, "sm"], writes=["junk", "sm"])
                                P.op("dve", lambda e: e.tensor_scalar(dlt, cnt, float(TOPK) - 0.5, ck, op0=ALU.is_gt, op1=ALU.mult),
                                     reads=["sm"], writes=["sm"])
                                P.op("dve", lambda e: e.scalar_tensor_tensor(lo, dlt, hi, lo, op0=ALU.mult, op1=ALU.add),
                                     reads=["sm"], writes=["sm"])
                                yield
                            P.op("dve", lambda e: e.tensor_scalar(Mb[:, 0:nk], Sc[:, z, 0:nk], lo, None, op0=ALU.is_gt),
                                 reads=[sck, "sm"], writes=["Mb"])
                        else:
                            P.op("dve", lambda e: e.tensor_scalar(Mb[:, 0:nk], Sc[:, z, 0:nk], -1e29, None, op0=ALU.is_gt),
                                 reads=[sck], writes=["Mb"])
                        yield
                        for g0 in range(0, i + 1, 8):
                            g1 = min(i + 1, g0 + 8)
                            for kb in range(g0, g1):
                                P.op("pe", lambda e: e.transpose(pM[:, kb - g0, :], Mb[:, tsl(kb)], ident),
                                     reads=["Mb", "cb"], writes=["pM"])
                            P.op("act", lambda e: e.copy(MT[:, z, g0:g1, :], pM[:, 0:g1 - g0, :]),
                                 reads=[], writes=["pM", "MT%d" % z])
                            yield

                    def stage3(i):
                        tcols = tsl(i)
                        z = i % 2
                        for kb in range(i + 1):
                            for s_ in range(2):
                                q = cnts["l"] % 2
                                cnts["l"] += 1
                                po = slice(64 * s_, 64 * s_ + 64)
                                for j in range(4):
                                    P.op("pe", lambda e: e.matmul(pL[q][:, j * 128:(j + 1) * 128], kd2[po, tsl(kb)], qd[po, j, tcols],
                                                                  start=True, stop=True, skip_group_check=True),
                                         reads=["kd2", "qd"], writes=["pL%d" % q])
                                P.op("act", lambda e: e.activation(out=Pe[:, q, :], in_=pL[q][:], func=AF.Exp),
                                     reads=[], writes=["pL%d" % q, "Pe%d" % q])
                                P.op("pool", lambda e: e.tensor_tensor(
                                    PT[:, q, :].rearrange("p (h t) -> p h t", h=4), Pe[:, q, :].rearrange("p (h t) -> p h t", h=4),
                                    MT[:, z, kb:kb + 1, :].to_broadcast([128, 4, 128]), op=ALU.mult),
                                    reads=["Pe%d" % q, "MT%d" % z], writes=["PT%d" % q])
                                for j in range(4):
                                    P.op("pe", lambda e: e.matmul(pYd[:, s_, j * 66:(j + 1) * 66], PT[:, q, j * 128:(j + 1) * 128], v[:, kb, 512:578],
                                                                  start=(kb == 0 and j == 0), stop=(kb == i), skip_group_check=True),
                                         reads=["PT%d" % q, "v", "vone"], writes=["pYd%d" % s_])
                                yield
                        for bank in range(2):
                            yv = pYd[:, bank, 0:264].rearrange("p (h c) -> p h c", h=4)
                            ydv = yd[:].rearrange("p (j s c) -> p j s c", j=4, s=2)[:, :, bank, :]
                            P.op("dve", lambda e: e.reciprocal(rec[:, bank * 4:(bank + 1) * 4, :], yv[:, :, 64:65]),
                                 reads=[], writes=["pYd%d" % bank, "rec%d" % bank])
                            P.op("dve", lambda e: e.tensor_tensor(ydv, yv[:, :, 0:64],
                                                                  rec[:, bank * 4:(bank + 1) * 4, :].to_broadcast([128, 4, 64]), op=ALU.mult),
                                 reads=["rec%d" % bank], writes=["pYd%d" % bank, "yd"])
                        yield
                        u = i % 2
                        for cchunk in range(4):
                            P.op("pe", lambda e: e.transpose(pM2[:, cchunk, :], yd[:, tsl(cchunk)], ident),
                                 reads=["yd", "cb"], writes=["pM2"])
                        P.op("act", lambda e: e.copy(ydst[:, u, :, :], pM2[:]), reads=[], writes=["pM2", "ydst%d" % u])
                        P.dma("sp", ydT_s[:, :, sq * SEQ + i * 128: sq * SEQ + (i + 1) * 128].rearrange("c p t -> p c t"),
                              ydst[:, u, :, :], reads=["ydst%d" % u], writes=["ydd%d" % (sq * 16 + i)])

                    order = list(range(15, -1, -1))
                    for tau in range(len(order) + 2):
                        th = []
                        if tau < len(order):
                            th.append(stage1(order[tau]))
                        if 0 <= tau - 1 < len(order):
                            th.append(stage2(order[tau - 1]))
                        if 0 <= tau - 2 < len(order):
                            th.append(stage3(order[tau - 2]))
                        P.run_threads(th)
                    P.barrier()
            P.barrier()

    def p4a_phase():
        with contextlib.ExitStack() as es:
            sb = lambda n, s, d=F32: es.enter_context(nc.sbuf_tensor("p4a" + n, s, d))
            ps = lambda n, s, d=F32: es.enter_context(nc.psum_tensor("p4a" + n, s, d))
            wg = sb("wg", [128, KC, 2048], BF16)
            for k in range(KC):
                P.dma("pool", wg[:, k, :], win_d[tsl(k), OFF_GSB:OFF_GSB + 2048], writes=["wg"])
            wosb = sb("wosb", [64, 8, D], BF16)
            P.dma("pool", wosb[:], wosb_d.rearrange("(h p) n -> p h n", p=64), writes=["wosb"])
            wod = sb("wod", [128, 4, D], BF16)
            P.dma("pool", wod[:], wod_d.rearrange("(c p) n -> p c n", p=128), writes=["wod"])
            wo = sb("wo", [128, KC, D], BF16)
            P.dma("pool", wo[:], wo_d.rearrange("(c p) n -> p c n", p=128), writes=["wo"])
            uT = sb("uT", [128, 2, KC, TB], BF16)
            ysbT = sb("ysbT", [64, 2, 8, TB], BF16)
            ydT = sb("ydT", [128, 2, 4, TB], BF16)
            mT = sb("mT", [128, KC, TB], BF16)
            s1 = sb("s1", [128, 2, TB])
            s2 = sb("s2", [128, 2, TB])
            t1 = sb("t1", [128, 2, TB])
            t2 = sb("t2", [128, 2, TB])
            hs = sb("hs", [128, 4, D])
            pG = [ps("pG%d" % i, [128, TB]) for i in range(2)]
            pY = [ps("pY%d" % i, [128, TB]) for i in range(2)]
            pO = [ps("pO%d" % i, [128, 512]) for i in range(2)]
            oc = hc = 0
            def load_blk(b):
                if b >= NB:
                    return
                u = b % 2
                tok = slice(b * TB, (b + 1) * TB)
                P.dma("sp", uT[:, u, :, :], uT_s[:, :, tok].rearrange("c p t -> p c t"), writes=["uT%d" % u])
                P.dma("sp", ysbT[:, u, :, :], ysbT_s[:, :, tok].rearrange("h p t -> p h t"), writes=["ysbT%d" % u])
                P.dma("sp", ydT[:, u, :, :], ydT_s[:, :, tok].rearrange("c p t -> p c t"), writes=["ydT%d" % u])
            load_blk(0)
            for b in range(NB):
                u = b % 2
                tok = slice(b * TB, (b + 1) * TB)
                load_blk(b + 1)
                for i in range(4):
                    P.dma("sp", hs[:, i, :], h_s[tsl(4 * b + i), :], reads=["hd%d" % (4 * b + i)], writes=["hs%d" % i])
                for c in range(KC):
                    r = c % 2
                    for (g, pg, sdst, skey) in ((0, pG[0], s1, "s1"), (1, pG[1], s2, "s2")):
                        for k in range(KC):
                            P.op("pe", lambda e, k=k, g=g, pg=pg, c=c: e.matmul(
                                pg[:], wg[:, k, g * 1024 + c * 128: g * 1024 + (c + 1) * 128], uT[:, u, k, :],
                                start=(k == 0), stop=(k == KC - 1)),
                                reads=["wg", "uT%d" % u], writes=["pG%d" % g])
                        P.op("act", lambda e, pg=pg, sdst=sdst, r=r: e.activation(out=sdst[:, r, :], in_=pg[:], func=AF.Tanh, scale=0.5),
                             reads=[], writes=["pG%d" % g, "%s%d" % (skey, r)])
                    for hh in range(8):
                        P.op("pe", lambda e, hh=hh, c=c: e.matmul(pY[0][:], wosb[:, hh, tsl(c)], ysbT[:, u, hh, :],
                                                                   start=(hh == 0), stop=(hh == 7)),
                             reads=["wosb", "ysbT%d" % u], writes=["pY0"])
                    for k in range(4):
                        P.op("pe", lambda e, k=k, c=c: e.matmul(pY[1][:], wod[:, k, tsl(c)], ydT[:, u, k, :],
                                                                 start=(k == 0), stop=(k == 3)),
                             reads=["wod", "ydT%d" % u], writes=["pY1"])
                    P.op("dve", lambda e, r=r: e.scalar_tensor_tensor(t1[:, r, :], s1[:, r, :], 1.0, pY[0][:], op0=ALU.add, op1=ALU.mult),
                         reads=["s1%d" % r], writes=["pY0", "t1%d" % r])
                    P.op("dve", lambda e, r=r: e.scalar_tensor_tensor(t2[:, r, :], s2[:, r, :], 1.0, pY[1][:], op0=ALU.add, op1=ALU.mult),
                         reads=["s2%d" % r], writes=["pY1", "t2%d" % r])
                    P.op("pool", lambda e, r=r, c=c: e.tensor_tensor(mT[:, c, :], t1[:, r, :], t2[:, r, :], op=ALU.add),
                         reads=["t1%d" % r, "t2%d" % r], writes=["mT"])
                for i in range(4):
                    t = 4 * b + i
                    sl = i
                    for c2 in range(2):
                        q = oc % 2
                        oc += 1
                        for k in range(KC):
                            P.op("pe", lambda e, k=k, i=i, c2=c2, q=q: e.matmul(pO[q][:], mT[:, k, tsl(i)], wo[:, k, c2 * 512:(c2 + 1) * 512],
                                                                               start=(k == 0), stop=(k == KC - 1)),
                                 reads=["mT", "wo"], writes=["pO%d" % q])
                        P.op("dve", lambda e, q=q, sl=sl, c2=c2: e.scalar_tensor_tensor(
                            hs[:, sl, c2 * 512:(c2 + 1) * 512], pO[q][:], 0.5, hs[:, sl, c2 * 512:(c2 + 1) * 512],
                            op0=ALU.mult, op1=ALU.add), reads=[], writes=["pO%d" % q, "hs%d" % sl])
                    P.dma("sp", h_s[tsl(t), :], hs[:, sl, :], reads=["hs%d" % sl], writes=["hd%d" % t])
            P.barrier()

    def p4c_phase():
        with contextlib.ExitStack() as es:
            sb = lambda n, s, d=F32: es.enter_context(nc.sbuf_tensor("p4c" + n, s, d))
            ps = lambda n, s, d=F32: es.enter_context(nc.psum_tensor("p4c" + n, s, d))
            c = load_consts(es)
            ident = c["ident"]
            wpg = sb("wpg", [128, KC, D], BF16)
            P.dma("pool", wpg[:], wpg_d.rearrange("(c p) n -> p c n", p=128), writes=["wpg"])
            wpp = sb("wpp", [128, 2, D], BF16)
            P.dma("pool", wpp[:], wpp_d.rearrange("(c p) n -> p c n", p=128), writes=["wpp"])
            gB = make_gB(es, c, 3, "p4cgB")
            gfin = sb("gfin", [128, D])
            P.dma("sp", gfin[:], gfin_d.broadcast_to([128, D]), writes=["gfin"])
            hs = sb("hs", [128, 3, D])
            pt = sb("pt", [128, 2, 256], BF16)
            xn = sb("xn", [128, 2, D], BF16)
            hnT = sb("hnT", [128, 2, KC, 128], BF16)
            pTs = sb("pTs", [128, 2, 2, 128], BF16)
            gate = sb("gate", [128, 2, D])
            ot = sb("ot", [128, 2, D])
            ss = sb("ss", [128, 8])
            rstd = sb("rstd", [128, 8])
            pT = ps("pT", [128, KC, 128], BF16)
            pP = ps("pP", [128, 2, 128], BF16)
            pGt = [ps("pGt%d" % i, [128, 512]) for i in range(2)]
            pPp = [ps("pPp%d" % i, [128, 512]) for i in range(2)]
            def load_t(t):
                if t < NT:
                    P.dma("sp", hs[:, t % 3, :], h_s[tsl(t), :], reads=["hd%d" % t], writes=["hs%d" % (t % 3)])
                    P.dma("pool", pt[:, t % 2, :], p_d[tsl(t), :], writes=["pt%d" % (t % 2)])
            load_t(0)
            for t in range(NT):
                sl = t % 3
                u = t % 2
                a = (t % 4)
                load_t(t + 1)
                P.op("act", lambda e, sl=sl, u=u, a=a: e.activation(out=xn[:, u, :], in_=hs[:, sl, :], func=AF.Square,
                                                                   accum_out=ss[:, a:a + 1]),
                     reads=["hs%d" % sl], writes=["xn%d" % u, "ss%d" % a])
                rstd_from_ss(ss[:, a:a + 1], rstd[:, a:a + 1], 1, ["ss%d" % a], "rstd%d" % a)
                P.op("dve", lambda e, sl=sl, u=u, a=a: e.tensor_scalar(xn[:, u, :], hs[:, sl, :], rstd[:, a:a + 1], None, op0=ALU.mult),
                     reads=["hs%d" % sl, "rstd%d" % a], writes=["xn%d" % u])
                for k in range(KC):
                    P.op("pe", lambda e, k=k, u=u: e.transpose(pT[:, k, :], xn[:, u, tsl(k)], ident[:]),
                         reads=["xn%d" % u, "c_ident"], writes=["pT"])
                P.op("dve", lambda e, u=u: e.tensor_tensor(hnT[:, u, :, :], pT[:], gB[:], op=ALU.mult),
                     reads=["p4cgB"], writes=["pT", "hnT%d" % u])
                for k in range(2):
                    P.op("pe", lambda e, k=k, u=u: e.transpose(pP[:, k, :], pt[:, u, tsl(k)], ident[:]),
                         reads=["pt%d" % u, "c_ident"], writes=["pP"])
                P.op("act", lambda e, u=u: e.copy(pTs[:, u, :, :], pP[:]), reads=[], writes=["pP", "pTs%d" % u])
                for c2 in range(2):
                    for k in range(KC):
                        P.op("pe", lambda e, k=k, u=u, c2=c2: e.matmul(pGt[c2][:], hnT[:, u, k, :], wpg[:, k, c2 * 512:(c2 + 1) * 512],
                                                                      start=(k == 0), stop=(k == KC - 1)),
                             reads=["hnT%d" % u, "wpg"], writes=["pGt%d" % c2])
                    for k in range(2):
                        P.op("pe", lambda e, k=k, u=u, c2=c2: e.matmul(pPp[c2][:], pTs[:, u, k, :], wpp[:, k, c2 * 512:(c2 + 1) * 512],
                                                                      start=(k == 0), stop=(k == 1)),
                             reads=["pTs%d" % u, "wpp"], writes=["pPp%d" % c2])
                    cs = slice(c2 * 512, (c2 + 1) * 512)
                    P.op("act", lambda e, u=u, c2=c2, cs=cs: e.activation(out=gate[:, u, cs], in_=pGt[c2][:], func=AF.Tanh, scale=0.5),
                         reads=[], writes=["pGt%d" % c2, "gate%d" % u])
                    P.op("dve", lambda e, u=u, c2=c2, cs=cs: e.scalar_tensor_tensor(gate[:, u, cs], gate[:, u, cs], 1.0, pPp[c2][:],
                                                                                 op0=ALU.add, op1=ALU.mult),
                         reads=[], writes=["pPp%d" % c2, "gate%d" % u])
                P.op("dve", lambda e, u=u, sl=sl: e.scalar_tensor_tensor(hs[:, sl, :], gate[:, u, :], 0.5, hs[:, sl, :],
                                                                         op0=ALU.mult, op1=ALU.add),
                     reads=["gate%d" % u], writes=["hs%d" % sl])
                a2 = 4 + a
                P.op("act", lambda e, sl=sl, u=u, a2=a2: e.activation(out=xn[:, u, :], in_=hs[:, sl, :], func=AF.Square,
                                                                     accum_out=ss[:, a2:a2 + 1]),
                     reads=["hs%d" % sl], writes=["xn%d" % u, "ss%d" % a2])
                rstd_from_ss(ss[:, a2:a2 + 1], rstd[:, a2:a2 + 1], 1, ["ss%d" % a2], "rstd%d" % a2)
                P.op("dve", lambda e, sl=sl, u=u, a2=a2: e.scalar_tensor_tensor(ot[:, u, :], hs[:, sl, :], rstd[:, a2:a2 + 1], gfin[:],
                                                                               op0=ALU.mult, op1=ALU.mult),
                     reads=["hs%d" % sl, "rstd%d" % a2, "gfin"], writes=["ot%d" % u])
                P.dma("sp", out_d[tsl(t), :], ot[:, u, :], reads=["ot%d" % u], writes=["outd%d" % t])
            P.barrier()

    if "p1" in phases:
        ffn_phase("f1", x_d, h_s, w1a_d, w2a_d, 0, 1)
    if "p2" in phases:
        p2_phase()
    if "p3" in phases:
        p3_phase()
    if "p4a" in phases:
        p4a_phase()
    if "p4b" in phases:
        ffn_phase("f2", h_s, h_s, w1b_d, w2b_d, 2, None)
    if "p4c" in phases:
        p4c_phase()
    P.emit()
    return nc


def host_consts():
    j = np.arange(128)
    ident = np.eye(128, dtype=np.float32)
    nUincl = -(j[:, None] >= j[None, :]).astype(np.float32)
    nLstr = -(j[:, None] < j[None, :]).astype(np.float32)
    t = np.arange(512)
    sbmask = np.stack([(j[:, None] + 128 * r < t[None, :]).astype(np.float32) for r in range(4)], 1)
    cb = np.concatenate([ident, nUincl, nLstr, sbmask.reshape(128, 2048)], 1).astype(np.float32)
    dsaneg = np.where(j[None, :] > j[:, None], -1e30, 0.0).astype(np.float32)
    inv_freq = (500000.0 ** (-np.arange(0, 16, 2, dtype=np.float32) / 16)).astype(np.float32)
    pm = j % 64
    invf = np.where(pm < 16, inv_freq[pm % 8], 0.0).astype(np.float32)
    sgn = np.where(pm < 8, -1.0, np.where(pm < 16, 1.0, 0.0)).astype(np.float32)
    cf = np.concatenate([dsaneg, invf[:, None], sgn[:, None]], 1).astype(np.float32)
    return cb, cf


def make_in_maps(inputs, ncores=8):
    f = lambda a: np.ascontiguousarray(np.asarray(a), dtype=np.float32)
    x = f(inputs["x"])
    p = f(inputs["p"])[0]
    pos = np.ascontiguousarray(np.asarray(inputs["positions"]), dtype=np.int32)
    cb, cf = host_consts()
    gl = lambda g: f(g).reshape(KC, 128).T
    gcols = np.ascontiguousarray(np.concatenate([gl(inputs["ffn1_norm"][0]), gl(inputs["mix_norm"][0]),
                                                 gl(inputs["ffn2_norm"][0]), gl(inputs["ple_norm"][0])], 1))
    shared = {
        "w1a": f(inputs["ffn1_w1"][0]), "w2a": f(inputs["ffn1_w2"][0]), "win": f(inputs["w_in"][0]),
        "wosb": f(inputs["w_out_sb"][0]), "wod": f(inputs["w_out_dsa"][0]), "wo": f(inputs["w_out"][0]),
        "w1b": f(inputs["ffn2_w1"][0]), "w2b": f(inputs["ffn2_w2"][0]), "wpg": f(inputs["ple_w_gate"][0]),
        "wpp": f(inputs["ple_w_proj"][0]), "gcols": gcols, "gfin": f(inputs["final_norm"]).reshape(1, D),
        "cb": cb, "cf": cf,
    }
    maps = []
    for c in range(ncores):
        m = dict(shared)
        m["x"] = np.ascontiguousarray(x[NSEQ * c:NSEQ * (c + 1)].reshape(NTOK, D))
        m["p"] = np.ascontiguousarray(p[NSEQ * c:NSEQ * (c + 1)].reshape(NTOK, 256))
        m["pos"] = np.ascontiguousarray(pos[NSEQ * c:NSEQ * (c + 1)].reshape(1, NTOK))
        maps.append(m)
    return maps


def kernel(**inputs):
    nc = bass.Bass("TRN2", target_bir_lowering=False)
    build(nc)
    maps = make_in_maps(inputs, 8)
    res = run_bass_kernel_spmd(nc, maps, core_ids=list(range(8)))
    out = np.stack([r["out"].reshape(NSEQ, SEQ, D) for r in res.results], 0).reshape(16, SEQ, D)
    return out.astype(np.float32)
```

```python
import contextlib
import numpy as np
import concourse.bass as bass
import concourse.mybir as mybir
from concourse.bass_utils import run_bass_kernel_spmd

F32 = mybir.dt.float32
BF16 = mybir.dt.bfloat16
I32 = mybir.dt.int32
AF = mybir.ActivationFunctionType
ALU = mybir.AluOpType
AX = mybir.AxisListType

D = 1024
KC = 8
SEQ = 2048
NSEQ = 2
NTOK = NSEQ * SEQ
NT = NTOK // 128
TB = 512
NB = NTOK // TB
FF = 2816
FC = FF // 128
DIN = 4808
EPS = 1e-6
TOPK = 256
NBIS = 14
OFF_QSB, OFF_KSB, OFF_VSB, OFF_QD, OFF_KD, OFF_VD, OFF_QI, OFF_KI, OFF_WI, OFF_GSB, OFF_GD = (
    0, 512, 1024, 1536, 2048, 2112, 2176, 2688, 2752, 2760, 3784)
TWO_PI = 6.283185307179586
CW1 = 6.28125
CW2 = TWO_PI - CW1


class _Key:
    __slots__ = ("w", "rs")

    def __init__(self):
        self.w = None
        self.rs = []


class _Rec:
    def __init__(self):
        self.call = None

    def __getattr__(self, name):
        def f(*a, **k):
            self.call = (name, a, k)
            return self
        return f


def _freeze(fn):
    rec = _Rec()
    fn(rec)
    name, a, k = rec.call
    return lambda e: getattr(e, name)(*a, **k)


class Prog:
    ENG = ("pe", "act", "dve", "pool", "sp")

    def __init__(self, nc, same_engine_sync=True, ndma_sems=8):
        self.nc = nc
        self.es = contextlib.ExitStack()
        self.streams = {e: [] for e in self.ENG}
        self.cnt = {e: 0 for e in self.ENG}
        self.sems = {}
        for e in self.ENG:
            self.sems["p_" + e] = self.es.enter_context(nc.semaphore("prog_" + e))
        self.known = {e: {} for e in self.ENG}
        self.same = same_engine_sync
        self.ndma = ndma_sems
        self.dq = {}
        self.keys = {}

    def key(self, name):
        k = self.keys.get(name)
        if k is None:
            k = _Key()
            self.keys[name] = k
        return k

    def _deps(self, reads, writes):
        ev = []
        for r in reads:
            k = self.key(r)
            if k.w is not None:
                ev.append(k.w)
        for w in writes:
            k = self.key(w)
            if k.w is not None:
                ev.append(k.w)
            ev.extend(k.rs)
        return ev

    def _waits(self, eng, evs):
        need = {}
        kn = self.known[eng]
        for (sid, val) in evs:
            if sid == "p_" + eng and (eng == "pe" or not self.same):
                continue
            if kn.get(sid, 0) >= val:
                continue
            if need.get(sid, 0) < val:
                need[sid] = val
        for sid, val in need.items():
            kn[sid] = val
        return list(need.items())

    def _commit(self, reads, writes, event):
        for r in reads:
            self.key(r).rs.append(event)
        for w in writes:
            k = self.key(w)
            k.w = event
            k.rs = []

    def op(self, eng, fn, reads=(), writes=()):
        waits = self._waits(eng, self._deps(reads, writes))
        self.cnt[eng] += 1
        event = ("p_" + eng, self.cnt[eng])
        self.streams[eng].append((waits, _freeze(fn), ("p_" + eng, 1)))
        self._commit(reads, writes, event)
        return event

    def dma(self, q, out, in_, reads=(), writes=()):
        d = self.dq.get(q)
        if d is None:
            ids = [f"d_{q}_{j}" for j in range(self.ndma)]
            for s in ids:
                self.sems[s] = self.es.enter_context(self.nc.semaphore(s))
            d = dict(i=0, ids=ids, vals=[0] * self.ndma, last=[None] * self.ndma)
            self.dq[q] = d
        j = d["i"] % self.ndma
        d["i"] += 1
        evs = self._deps(reads, writes)
        if d["last"][j] is not None:
            evs.append(d["last"][j])
        waits = self._waits(q, evs)
        d["vals"][j] += 16
        event = (d["ids"][j], d["vals"][j])
        d["last"][j] = event
        fn = lambda e, out=out, in_=in_: e.dma_start(out=out, in_=in_)
        self.streams[q].append((waits, fn, (d["ids"][j], 16)))
        self._commit(reads, writes, event)
        return event

    def _all_events(self):
        ev = []
        for q, d in self.dq.items():
            for e in d["last"]:
                if e is not None:
                    ev.append(e)
        for e in self.ENG:
            if self.cnt[e] > 0:
                ev.append(("p_" + e, self.cnt[e]))
        return ev

    def run_threads(self, gens):
        live = list(gens)
        while live:
            nxt = []
            for g in live:
                try:
                    next(g)
                    nxt.append(g)
                except StopIteration:
                    pass
            live = nxt

    def barrier(self):
        ev = self._all_events()
        for e in self.ENG:
            w = self._waits(e, ev)
            if w:
                self.streams[e].append((w, None, None))
        self.keys = {}

    def emit(self):
        nc = self.nc
        fw = self._waits("sp", self._all_events())
        self.streams["sp"].append((fw, None, None))
        with nc.Block() as block:
            def run(engname, handle):
                for waits, fn, inc in self.streams[engname]:
                    for sid, val in waits:
                        handle.wait_ge(self.sems[sid], val)
                    if fn is not None:
                        fn(handle).then_inc(self.sems[inc[0]], inc[1])

            @block.tensor
            def _(e):
                run("pe", e)

            @block.scalar
            def _(e):
                run("act", e)

            @block.vector
            def _(e):
                run("dve", e)

            @block.gpsimd
            def _(e):
                run("pool", e)

            @block.sync
            def _(e):
                run("sp", e)
        self.es.close()


def build(nc, dbg=False, phases=("p1", "p2", "p3", "p4a", "p4b", "p4c")):
    P = Prog(nc)

    def din(name, shape, dt=F32):
        return nc.dram_tensor(name, shape, dt, kind="ExternalInput").ap()

    def dscr(name, shape, dt):
        return nc.dram_tensor(name, shape, dt, kind=("ExternalOutput" if dbg else "Internal")).ap()

    x_d = din("x", [NTOK, D])
    p_d = din("p", [NTOK, 256])
    pos_d = din("pos", [1, NTOK], I32)
    w1a_d = din("w1a", [D, 2 * FF])
    w2a_d = din("w2a", [FF, D])
    win_d = din("win", [D, DIN])
    wosb_d = din("wosb", [512, D])
    wod_d = din("wod", [512, D])
    wo_d = din("wo", [D, D])
    w1b_d = din("w1b", [D, 2 * FF])
    w2b_d = din("w2b", [FF, D])
    wpg_d = din("wpg", [D, D])
    wpp_d = din("wpp", [256, D])
    gcols_d = din("gcols", [128, 4 * KC])
    gfin_d = din("gfin", [1, D])
    cb_d = din("cb", [128, 384 + 2048])
    cf_d = din("cf", [128, 130])
    out_d = nc.dram_tensor("out", [NTOK, D], F32, kind="ExternalOutput").ap()

    h_s = dscr("h_s", [NTOK, D], F32)
    uT_s = dscr("uT_s", [KC, 128, NTOK], BF16)
    qk_s = dscr("qk_s", [18, 128, NTOK], BF16)
    v_s = dscr("v_s", [NTOK, 576], BF16)
    wi_s = dscr("wi_s", [NTOK, 8], F32)
    ysbT_s = dscr("ysbT_s", [8, 64, NTOK], BF16)
    ydT_s = dscr("ydT_s", [4, 128, NTOK], BF16)

    def tsl(t):
        return slice(t * 128, (t + 1) * 128)

    ccount = [0]

    def load_consts(es, need_cb=True):
        ccount[0] += 1
        sb = lambda n, s, d=F32: es.enter_context(nc.sbuf_tensor("%s_%d" % (n, ccount[0]), s, d))
        c = {}
        c["gcols"] = sb("c_gcols", [128, 4 * KC])
        P.dma("sp", c["gcols"][:], gcols_d, writes=["c_gcols"])
        c["ident"] = sb("c_ident", [128, 128], BF16)
        P.dma("pool", c["ident"][:], cb_d[:, 0:128], writes=["c_ident"])
        return c

    def make_gB(es, c, which, name):
        gB = es.enter_context(nc.sbuf_tensor(name, [128, KC, 128], F32))
        src = c["gcols"][:, which * KC:(which + 1) * KC]
        P.op("dve", lambda e: e.tensor_copy(gB[:], src.unsqueeze(2).to_broadcast([128, KC, 128])),
             reads=["c_gcols"], writes=[name])
        return gB

    def rstd_from_ss(ss, rstd, n, rkeys, wkey):
        P.op("dve", lambda e: e.tensor_scalar(ss[:, 0:n], ss[:, 0:n], 1.0 / D, EPS, op0=ALU.mult, op1=ALU.add),
             reads=rkeys, writes=rkeys)
        P.op("act", lambda e: e.activation(out=ss[:, 0:n], in_=ss[:, 0:n], func=AF.Sqrt), reads=rkeys, writes=rkeys)
        P.op("dve", lambda e: e.reciprocal(rstd[:, 0:n], ss[:, 0:n]), reads=rkeys, writes=[wkey])

    def ffn_phase(tag, src_d, dst_d, w1_d, w2_d, gsel, post_gsel):
        with contextlib.ExitStack() as es:
            sb = lambda n, s, d=F32: es.enter_context(nc.sbuf_tensor(tag + n, s, d))
            ps = lambda n, s, d=F32: es.enter_context(nc.psum_tensor(tag + n, s, d))
            c = load_consts(es)
            w1 = sb("w1", [128, KC, 2 * FF], BF16)
            w2 = sb("w2", [128, FC, D], BF16)
            for k in range(KC):
                P.dma("pool", w1[:, k, :], w1_d[tsl(k), :], writes=["w1"])
            for k in range(FC):
                P.dma("pool", w2[:, k, :], w2_d[tsl(k), :], writes=["w2"])
            gB = make_gB(es, c, gsel, tag + "gB")
            gB2 = make_gB(es, c, post_gsel, tag + "gB2") if post_gsel is not None else None
            NXS = 6 if post_gsel is not None else 8
            xs = sb("xs", [128, NXS, D])
            issued = set()

            def issue_load(t):
                if t in issued or t >= NT:
                    return
                issued.add(t)
                P.dma("sp", xs[:, t % NXS, :], src_d[tsl(t), :], reads=["hd%d" % t], writes=["xs%d" % (t % NXS)])
            xn = sb("xn", [128, 2, D], BF16)
            xnT = sb("xnT", [128, KC, TB], BF16)
            gT = sb("gT", [128, FC, TB], BF16)
            stmp = sb("stmp", [128, 2, TB])
            ss = sb("ss", [128, 2, 4])
            rstd = sb("rstd", [128, 2, 4])
            ss2 = sb("ss2", [128, 2, 4])
            rstd2 = sb("rstd2", [128, 2, 4])
            ust = sb("ust", [128, 2, KC, 128], BF16) if post_gsel is not None else None
            pT = [ps("pT%d" % i, [128, KC, 128], BF16) for i in range(2)]
            pA = [ps("pA%d" % i, [128, TB]) for i in range(2)]
            pB = [ps("pB%d" % i, [128, TB]) for i in range(2)]
            pO = [ps("pO%d" % i, [128, 512]) for i in range(2)]
            ident = c["ident"]
            xnc = [0]
            ptc = [0]

            def norm_T(tile_ap, xs_key, rstd_col, rstd_key, gBt, gB_key, out_ap, out_key, dve_out=True):
                s = xnc[0] % 2
                xnc[0] += 1
                q = ptc[0] % 2
                ptc[0] += 1
                P.op("dve", lambda e: e.tensor_scalar(xn[:, s, :], tile_ap, rstd_col, None, op0=ALU.mult),
                     reads=[xs_key, rstd_key], writes=["xn%d" % s])
                for k in range(KC):
                    P.op("pe", lambda e, k=k: e.transpose(pT[q][:, k, :], xn[:, s, tsl(k)], ident[:]),
                         reads=["xn%d" % s, "c_ident"], writes=["pT%d" % q])
                P.op("dve", lambda e: e.tensor_tensor(out_ap, pT[q][:], gBt[:], op=ALU.mult),
                     reads=[gB_key], writes=["pT%d" % q, out_key])

            for b in range(NB):
                sp_ = b % 2
                tiles = [4 * b + i for i in range(4)]
                for i, t in enumerate(tiles):
                    sl = t % NXS
                    issue_load(t)
                    s = xnc[0] % 2
                    P.op("act", lambda e, sl=sl, s=s, i=i: e.activation(out=xn[:, s, :], in_=xs[:, sl, :], func=AF.Square,
                                                                       accum_out=ss[:, sp_, i:i + 1]),
                         reads=["xs%d" % sl], writes=["xn%d" % s, "ss%d" % sp_])
                rstd_from_ss(ss[:, sp_, :], rstd[:, sp_, :], 4, ["ss%d" % sp_], "rstd%d" % sp_)
                for i, t in enumerate(tiles):
                    sl = t % NXS
                    norm_T(xs[:, sl, :], "xs%d" % sl, rstd[:, sp_, i:i + 1], "rstd%d" % sp_, gB, tag + "gB",
                           xnT[:, :, tsl(i)], "xnT")
                for j in range(FC):
                    q = j % 2
                    for k in range(KC):
                        P.op("pe", lambda e, k=k, j=j, q=q: e.matmul(pA[q][:], w1[:, k, tsl(j)], xnT[:, k, :],
                                                                    start=(k == 0), stop=(k == KC - 1)),
                             reads=["w1", "xnT"], writes=["pA%d" % q])
                    for k in range(KC):
                        P.op("pe", lambda e, k=k, j=j, q=q: e.matmul(pB[q][:], w1[:, k, FF + j * 128:FF + (j + 1) * 128],
                                                                    xnT[:, k, :], start=(k == 0), stop=(k == KC - 1)),
                             reads=["w1", "xnT"], writes=["pB%d" % q])
                    P.op("act", lambda e, q=q: e.activation(out=stmp[:, q, :], in_=pA[q][:], func=AF.Silu),
                         reads=[], writes=["pA%d" % q, "stmp%d" % q])
                    P.op("dve", lambda e, q=q, j=j: e.tensor_tensor(gT[:, j, :], stmp[:, q, :], pB[q][:], op=ALU.mult),
                         reads=["stmp%d" % q], writes=["pB%d" % q, "gT"])
                for t2_ in range(4 * b + 4, 4 * b + 4 + (NXS - 4)):
                    issue_load(t2_)
                oc = 0
                for i, t in enumerate(tiles):
                    sl = t % NXS
                    for c2 in range(2):
                        q = oc % 2
                        oc += 1
                        for j in range(FC):
                            P.op("pe", lambda e, j=j, i=i, c2=c2, q=q: e.matmul(
                                pO[q][:], gT[:, j, tsl(i)], w2[:, j, c2 * 512:(c2 + 1) * 512],
                                start=(j == 0), stop=(j == FC - 1)),
                                reads=["w2", "gT"], writes=["pO%d" % q])
                        P.op("dve", lambda e, q=q, sl=sl, c2=c2: e.scalar_tensor_tensor(
                            xs[:, sl, c2 * 512:(c2 + 1) * 512], pO[q][:], 0.5, xs[:, sl, c2 * 512:(c2 + 1) * 512],
                            op0=ALU.mult, op1=ALU.add),
                            reads=[], writes=["pO%d" % q, "xs%d" % sl])
                    P.dma("sp", dst_d[tsl(t), :], xs[:, sl, :], reads=["xs%d" % sl], writes=["hd%d" % t])
                    if post_gsel is not None:
                        s = xnc[0] % 2
                        P.op("act", lambda e, sl=sl, s=s, i=i: e.activation(out=xn[:, s, :], in_=xs[:, sl, :], func=AF.Square,
                                                                           accum_out=ss2[:, sp_, i:i + 1]),
                             reads=["xs%d" % sl], writes=["xn%d" % s, "ss2%d" % sp_])
                if post_gsel is not None:
                    rstd_from_ss(ss2[:, sp_, :], rstd2[:, sp_, :], 4, ["ss2%d" % sp_], "rstd2%d" % sp_)
                    for i, t in enumerate(tiles):
                        sl = t % NXS
                        u = t % 2
                        norm_T(xs[:, sl, :], "xs%d" % sl, rstd2[:, sp_, i:i + 1], "rstd2%d" % sp_, gB2, tag + "gB2",
                               ust[:, u, :, :], "ust%d" % u)
                        P.dma("sp", uT_s[:, :, tsl(t)].rearrange("c p t -> p c t"), ust[:, u, :, :],
                              reads=["ust%d" % u], writes=["uTd%d" % t])
            P.barrier()

    def p2_phase():
        with contextlib.ExitStack() as es:
            sb = lambda n, s, d=F32: es.enter_context(nc.sbuf_tensor("p2" + n, s, d))
            ps = lambda n, s, d=F32: es.enter_context(nc.psum_tensor("p2" + n, s, d))
            win = sb("win", [128, KC, 2760], BF16)
            for k in range(KC):
                P.dma("pool", win[:, k, :], win_d[tsl(k), 0:2760], writes=["win"])
            wp = sb("wp", [128, KC, 1280], BF16)
            wk2 = sb("wk2", [128, KC, 256], BF16)
            cf = sb("cf", [128, 130])
            P.dma("sp", cf[:], cf_d, writes=["cf"])
            posi = sb("posi", [128, NTOK], I32)
            P.dma("sp", posi[:], pos_d.broadcast_to([128, NTOK]), writes=["posi"])
            ang = sb("ang", [128, NTOK])
            kk = sb("kk", [128, NTOK])
            kki = sb("kki", [128, NTOK], I32)
            Ct = sb("Ct", [128, NTOK])
            St = sb("St", [128, NTOK])
            invf = cf[:, 128:129]
            sgn = cf[:, 129:130]
            P.op("dve", lambda e: e.tensor_copy(ang[:], posi[:]), reads=["posi"], writes=["ang"])
            P.op("dve", lambda e: e.tensor_scalar(ang[:], ang[:], invf, None, op0=ALU.mult), reads=["ang", "cf"], writes=["ang"])

            def reduce_to(dst, shift, key):
                P.op("dve", lambda e: e.tensor_scalar(kk[:], ang[:], shift, 1.0 / TWO_PI, op0=ALU.add, op1=ALU.mult),
                     reads=["ang"], writes=["kk"])
                P.op("dve", lambda e: e.tensor_copy(kki[:], kk[:]), reads=["kk"], writes=["kki"])
                P.op("dve", lambda e: e.tensor_copy(kk[:], kki[:]), reads=["kki"], writes=["kk"])
                P.op("dve", lambda e: e.scalar_tensor_tensor(dst[:], kk[:], -CW1, ang[:], op0=ALU.mult, op1=ALU.add),
                     reads=["kk", "ang"], writes=[key])
                P.op("dve", lambda e: e.scalar_tensor_tensor(dst[:], kk[:], -CW2, dst[:], op0=ALU.mult, op1=ALU.add),
                     reads=["kk"], writes=[key])
                P.op("dve", lambda e: e.tensor_scalar(dst[:], dst[:], shift, 3.1415925, op0=ALU.add, op1=ALU.min),
                     reads=[], writes=[key])
                P.op("dve", lambda e: e.tensor_scalar(dst[:], dst[:], -3.1415925, None, op0=ALU.max), reads=[], writes=[key])
                P.op("act", lambda e: e.activation(out=dst[:], in_=dst[:], func=AF.Sin), reads=[], writes=[key])

            reduce_to(St, 0.0, "St")
            P.op("dve", lambda e: e.tensor_scalar(St[:], St[:], sgn, None, op0=ALU.mult), reads=["cf"], writes=["St"])
            reduce_to(Ct, float(np.pi / 2), "Ct")
            P.op("pool", lambda e: e.tensor_scalar(win[:, :, 0:512], win[:, :, 0:512], 0.125, None, op0=ALU.mult),
                 reads=[], writes=["win"])
            P.op("pool", lambda e: e.tensor_scalar(win[:, :, OFF_QD:OFF_QD + 512], win[:, :, OFF_QD:OFF_QD + 512], 0.125, None,
                                                   op0=ALU.mult), reads=[], writes=["win"])
            P.op("pool", lambda e: e.memset(wp[:], 0.0), writes=["wp"])
            for hh in range(2):
                P.op("pool", lambda e, hh=hh: e.tensor_copy(wk2[:, :, hh * 64:(hh + 1) * 64], win[:, :, OFF_KD:OFF_KD + 64]),
                     reads=["win"], writes=["wk2"])
                P.op("pool", lambda e, hh=hh: e.tensor_copy(wk2[:, :, 128 + hh * 64:128 + (hh + 1) * 64],
                                                            win[:, :, OFF_KI:OFF_KI + 64]), reads=["win"], writes=["wk2"])
            pbase = [OFF_QD + 128 * j for j in range(4)] + [OFF_QI + 128 * j for j in range(4)]
            for pc in range(10):
                for hh in range(2):
                    if pc < 8:
                        b0 = pbase[pc] + 64 * hh
                    else:
                        b0 = OFF_KD if pc == 8 else OFF_KI
                    o = pc * 128 + 64 * hh
                    P.op("pool", lambda e, o=o, b0=b0: e.tensor_copy(wp[:, :, o:o + 8], win[:, :, b0 + 8:b0 + 16]),
                         reads=["win"], writes=["wp"])
                    P.op("pool", lambda e, o=o, b0=b0: e.tensor_copy(wp[:, :, o + 8:o + 16], win[:, :, b0:b0 + 8]),
                         reads=["win"], writes=["wp"])
            uT = sb("uT", [128, 2, KC, TB], BF16)
            fst = sb("fst", [128, 4, TB], BF16)
            t1 = sb("t1", [128, 2, TB])
            t2 = sb("t2", [128, 2, TB])
            vst = sb("vst", [128, 2, 576], BF16)
            wist = sb("wist", [128, 2, 8])
            pA = [ps("pA%d" % i, [128, TB]) for i in range(3)]
            pB = [ps("pB%d" % i, [128, TB]) for i in range(2)]
            pV = [ps("pV%d" % i, [128, 512]) for i in range(2)]
            pW = ps("pW", [128, 128])
            chunks = []
            for j in range(4):
                chunks.append((j, win, OFF_QSB + 128 * j, None))
            for j in range(4):
                chunks.append((4 + j, win, OFF_KSB + 128 * j, None))
            for j in range(4):
                chunks.append((8 + j, win, OFF_QD + 128 * j, j))
            for j in range(4):
                chunks.append((12 + j, win, OFF_QI + 128 * j, 4 + j))
            chunks.append((16, wk2, 0, 8))
            chunks.append((17, wk2, 128, 9))
            ca = cbn = fs = rc = vc = 0
            def load_uT(b):
                if b < NB:
                    P.dma("sp", uT[:, b % 2, :, :], uT_s[:, :, b * TB:(b + 1) * TB].rearrange("c p t -> p c t"),
                          reads=["uTd%d" % t for t in range(4 * b, 4 * b + 4)], writes=["uT%d" % (b % 2)])
            load_uT(0)
            for b in range(NB):
                u = b % 2
                load_uT(b + 1)
                tok = slice(b * TB, (b + 1) * TB)
                for (ci, wt, off, pidx) in chunks:
                    qa = ca % 3
                    ca += 1
                    for k in range(KC):
                        P.op("pe", lambda e, k=k, wt=wt, off=off, qa=qa: e.matmul(pA[qa][:], wt[:, k, off:off + 128], uT[:, u, k, :],
                                                                                 start=(k == 0), stop=(k == KC - 1)),
                             reads=["win", "wk2", "uT%d" % u], writes=["pA%d" % qa])
                    f = fs % 4
                    fs += 1
                    if pidx is None:
                        P.op("act", lambda e, qa=qa, f=f: e.copy(fst[:, f, :], pA[qa][:]), reads=[], writes=["pA%d" % qa, "fst%d" % f])
                    else:
                        qb = cbn % 2
                        cbn += 1
                        for k in range(KC):
                            P.op("pe", lambda e, k=k, pidx=pidx, qb=qb: e.matmul(pB[qb][:], wp[:, k, pidx * 128:(pidx + 1) * 128],
                                                                                uT[:, u, k, :], start=(k == 0), stop=(k == KC - 1)),
                                 reads=["wp", "uT%d" % u], writes=["pB%d" % qb])
                        r = rc % 2
                        rc += 1
                        P.op("dve", lambda e, qa=qa, r=r: e.tensor_tensor(t1[:, r, :], pA[qa][:], Ct[:, tok], op=ALU.mult),
                             reads=["Ct"], writes=["pA%d" % qa, "t1%d" % r])
                        P.op("dve", lambda e, qb=qb, r=r: e.tensor_tensor(t2[:, r, :], pB[qb][:], St[:, tok], op=ALU.mult),
                             reads=["St"], writes=["pB%d" % qb, "t2%d" % r])
                        P.op("pool", lambda e, r=r, f=f: e.tensor_tensor(fst[:, f, :], t1[:, r, :], t2[:, r, :], op=ALU.add),
                             reads=["t1%d" % r, "t2%d" % r], writes=["fst%d" % f])
                    P.dma("sp", qk_s[ci, :, tok], fst[:, f, :], reads=["fst%d" % f], writes=["qkd%d_%d" % (ci, b)])
                for i in range(4):
                    t = 4 * b + i
                    q = vc % 2
                    vc += 1
                    for k in range(KC):
                        P.op("pe", lambda e, k=k, i=i, q=q: e.matmul(pV[q][:], uT[:, u, k, tsl(i)], win[:, k, OFF_VSB:OFF_VSB + 512],
                                                                    start=(k == 0), stop=(k == KC - 1)),
                             reads=["win", "uT%d" % u], writes=["pV%d" % q])
                    for k in range(KC):
                        P.op("pe", lambda e, k=k, i=i: e.matmul(pW[:, 0:64], uT[:, u, k, tsl(i)], win[:, k, OFF_VD:OFF_VD + 64],
                                                               start=(k == 0), stop=(k == KC - 1), skip_group_check=True),
                             reads=["win", "uT%d" % u], writes=["pW"])
                    for k in range(KC):
                        P.op("pe", lambda e, k=k, i=i: e.matmul(pW[:, 64:72], uT[:, u, k, tsl(i)], win[:, k, OFF_WI:OFF_WI + 8],
                                                               start=False, stop=(k == KC - 1), skip_group_check=True),
                             reads=["win", "uT%d" % u], writes=["pW"])
                    P.op("act", lambda e, q=q: e.copy(vst[:, q, 0:512], pV[q][:]), reads=[], writes=["pV%d" % q, "vst%d" % q])
                    P.op("dve", lambda e, q=q: e.tensor_copy(vst[:, q, 512:576], pW[:, 0:64]), reads=[], writes=["pW", "vst%d" % q])
                    P.op("dve", lambda e, q=q: e.tensor_scalar(wist[:, q, :], pW[:, 64:72], float(8 ** -0.5 * 0.125), None, op0=ALU.mult),
                         reads=[], writes=["pW", "wist%d" % q])
                    P.dma("sp", v_s[tsl(t), :], vst[:, q, :], reads=["vst%d" % q], writes=["vd%d" % t])
                    P.dma("sp", wi_s[tsl(t), :], wist[:, q, :], reads=["wist%d" % q], writes=["wid%d" % t])
            P.barrier()

    def p3_phase():
        with contextlib.ExitStack() as es:
            sb = lambda n, s, d=F32: es.enter_context(nc.sbuf_tensor("p3" + n, s, d))
            cb = sb("cb", [128, 384 + 2048], BF16)
            P.dma("pool", cb[:], cb_d, writes=["cb"])
            cf = sb("cf", [128, 130])
            P.dma("sp", cf[:], cf_d, writes=["cf"])
            ident = cb[:, 0:128]
            nUincl = cb[:, 128:256]
            nLstr = cb[:, 256:384]
            sbmask = cb[:, 384:384 + 2048].rearrange("p (r t) -> p r t", r=4)
            dsaneg = cf[:, 0:128]
            qz = [sb("qz%d" % i, [128, 4, SEQ], BF16) for i in range(2)]
            P.op("pool", lambda e: e.memset(qz[0][64:128, :, :], 0.0), writes=["qz0z"])
            P.op("pool", lambda e: e.memset(qz[1][0:64, :, :], 0.0), writes=["qz1z"])
            ksb = sb("ksb", [128, 4, SEQ], BF16)
            nksb = sb("nksb", [128, 4, SEQ], BF16)
            qd = sb("qd", [128, 4, SEQ], BF16)
            qi = sb("qi", [128, 4, SEQ], BF16)
            kd2 = sb("kd2", [128, SEQ], BF16)
            ki2 = sb("ki2", [128, SEQ], BF16)
            v = sb("v", [128, 16, 578], BF16)
            wi = sb("wi", [128, 16, 8])
            P.op("pool", lambda e: e.memset(v[:, :, 576:578], 0.0), writes=["vone"])
            P.op("pool", lambda e: e.memset(v[:, :, 576:577], 1.0), writes=["vone"])
            for sq in range(NSEQ):
                tok = slice(sq * SEQ, (sq + 1) * SEQ)
                rk = lambda ci: ["qkd%d_%d" % (ci, b) for b in range(4 * sq, 4 * sq + 4)]
                for j in range(4):
                    P.dma("sp", qz[0][0:64, j, :], qk_s[j, 0:64, tok], reads=rk(j), writes=["qsb"])
                    P.dma("sp", qz[1][64:128, j, :], qk_s[j, 64:128, tok], reads=rk(j), writes=["qsb"])
                for (tile_, c0, key) in ((ksb, 4, "ksb"), (qd, 8, "qd"), (qi, 12, "qi")):
                    for j in range(4):
                        P.dma("sp", tile_[:, j, :], qk_s[c0 + j, :, tok], reads=rk(c0 + j), writes=[key])
                P.dma("sp", kd2[:], qk_s[16, :, tok], reads=rk(16), writes=["kd2"])
                P.dma("sp", ki2[:], qk_s[17, :, tok], reads=rk(17), writes=["ki2"])
                P.dma("act", v[:, :, 0:576], v_s[tok, :].rearrange("(n p) c -> p n c", p=128), writes=["v"])
                P.dma("act", wi[:], wi_s[tok, :].rearrange("(n p) c -> p n c", p=128), writes=["wi"])
                P.op("pool", lambda e: e.tensor_scalar(nksb[:], ksb[:], -1.0, None, op0=ALU.mult), reads=["ksb"], writes=["nksb"])

                with contextlib.ExitStack() as es2:
                  if "nosb" not in phases:
                    sb2 = lambda n, s, d=F32: es2.enter_context(nc.sbuf_tensor("sb%d" % sq + n, s, d))
                    ps2 = lambda n, s, d=F32: es2.enter_context(nc.psum_tensor("sb%d" % sq + n, s, d))
                    NS = 4
                    E = sb2("E", [128, NS, 512])
                    SP = sb2("SP", [128, NS, 2, 512], BF16)
                    A = sb2("A", [128, NS, 2, 512], BF16)
                    yacc = sb2("yacc", [64, NS, 512])
                    yst = sb2("yst", [64, NS, 512], BF16)
                    pZ = [ps2("pZ%d" % i, [128, 512]) for i in range(2)]
                    pC = [ps2("pC%d" % i, [128, 512]) for i in range(NS)]
                    pY = [ps2("pY%d" % i, [64, 512]) for i in range(2)]
                    mask128 = sbmask[:, 0, 0:128]

                    def sb_stream(s_, h, qc):
                        j, half = h // 2, h % 2
                        po = slice(64 * half, 64 * half + 64)
                        zb = s_ % 2
                        S = "s%d" % s_
                        kmax = 4 * qc + 3
                        nstep = kmax + 1
                        P.op("pool", lambda e: e.memset(yacc[:, s_, :], 0.0), writes=["yacc" + S])
                        if s_ >= 2:
                            yield
                        for step in range(nstep):
                            kb = kmax - step
                            r = kb - 4 * qc
                            c0 = 128 * max(0, r)
                            cols = slice(c0, 512)
                            dcols = slice(c0, c0 + 128)
                            qcols = slice(qc * 512 + c0, (qc + 1) * 512)
                            kcols = tsl(kb)
                            par = step % 2
                            spk = "SP%s%d" % (S, par)
                            ak = "A%s%d" % (S, par)
                            P.op("pe", lambda e: e.matmul(pZ[zb][:, cols], ksb[:, j, kcols], qz[half][:, j, qcols], start=True, stop=True,
                                                          skip_group_check=True),
                                 reads=["ksb", "qsb", "qz0z", "qz1z"], writes=["pZ%d" % zb])
                            yield
                            P.op("act", lambda e: e.activation(out=E[:, s_, cols], in_=pZ[zb][:, cols], func=AF.Exp),
                                 reads=[], writes=["pZ%d" % zb, "E" + S])
                            P.op("act", lambda e: e.activation(out=SP[:, s_, par, cols], in_=E[:, s_, cols], func=AF.Ln, bias=1.0),
                                 reads=["E" + S], writes=[spk])
                            if r >= 0:
                                P.op("pool", lambda e: e.tensor_tensor(SP[:, s_, par, dcols], SP[:, s_, par, dcols], mask128, op=ALU.mult),
                                     reads=["cb", spk], writes=[spk])
                            yield
                            P.op("pe", lambda e: e.matmul(pC[s_][:, cols], ksb[:, j, kcols], qz[half][:, j, qcols], start=(step == 0), stop=False,
                                                          skip_group_check=True),
                                 reads=["ksb", "qsb"], writes=["pC" + S])
                            P.op("pe", lambda e: e.matmul(pC[s_][:, cols], nUincl, SP[:, s_, par, cols], start=False, stop=True,
                                                          skip_group_check=True),
                                 reads=["cb", spk], writes=["pC" + S])
                            yield
                            P.op("act", lambda e: e.activation(out=A[:, s_, par, cols], in_=pC[s_][:, cols], func=AF.Exp),
                                 reads=[], writes=["pC" + S, ak])
                            if r >= 0:
                                P.op("pool", lambda e: e.tensor_tensor(A[:, s_, par, dcols], A[:, s_, par, dcols], mask128, op=ALU.mult),
                                     reads=["cb", ak], writes=[ak])
                            yield
                            P.op("pe", lambda e: e.matmul(pY[zb][:, cols], v[:, kb, h * 64:(h + 1) * 64], A[:, s_, par, cols],
                                                          start=True, stop=True, skip_group_check=True),
                                 reads=["v", ak], writes=["pY%d" % zb])
                            if step < nstep - 1:
                                P.op("pe", lambda e: e.matmul(pC[s_][:, cols], nksb[:, j, kcols], qz[half][:, j, qcols], start=False, stop=False,
                                                              skip_group_check=True),
                                     reads=["nksb", "qsb"], writes=["pC" + S])
                                P.op("pe", lambda e: e.matmul(pC[s_][:, cols], nLstr, SP[:, s_, par, cols], start=False, stop=False,
                                                              skip_group_check=True),
                                     reads=["cb", spk], writes=["pC" + S])
                            P.op("dve", lambda e: e.tensor_tensor(yacc[:, s_, cols], yacc[:, s_, cols], pY[zb][:, cols], op=ALU.add),
                                 reads=["yacc" + S], writes=["pY%d" % zb, "yacc" + S])
                            yield
                        P.op("dve", lambda e: e.tensor_copy(yst[:, s_, :], yacc[:, s_, :]), reads=["yacc" + S], writes=["yst" + S])
                        P.dma("sp", ysbT_s[h, :, sq * SEQ + qc * 512: sq * SEQ + (qc + 1) * 512], yst[:, s_, :],
                              reads=["yst" + S], writes=["ysbd%d_%d" % (h, sq * 4 + qc)])

                    for qc in range(4):
                        for g in range(2):
                            P.run_threads([sb_stream(s_, 4 * g + s_, qc) for s_ in range(NS)])
                    P.barrier()

                with contextlib.ExitStack() as es2:
                  if "nodsa" not in phases:
                    sb2 = lambda n, s, d=F32: es2.enter_context(nc.sbuf_tensor("ds%d" % sq + n, s, d))
                    ps2 = lambda n, s, d=F32: es2.enter_context(nc.psum_tensor("ds%d" % sq + n, s, d))
                    Sc = sb2("Sc", [128, 4, SEQ])
                    R = sb2("R", [128, 2, 512])
                    Mb = sb2("Mb", [128, 2, SEQ], BF16)
                    MT = sb2("MT", [128, 4, 16, 128], BF16)
                    junk = sb2("junk", [128, 2, SEQ], BF16)
                    Pe = sb2("Pe", [128, 2, 512], BF16)
                    PT = sb2("PT", [128, 2, 512], BF16)
                    yd = sb2("yd", [128, 512], BF16)
                    ydst = sb2("ydst", [128, 2, 4, 128], BF16)
                    sm = sb2("sm", [128, 16])
                    rec = sb2("rec", [128, 8, 1])
                    pD = [ps2("pD%d" % i, [128, 512]) for i in range(2)]
                    pM = ps2("pM", [128, 8, 128], BF16)
                    pL = [ps2("pL%d" % i, [128, 512]) for i in range(2)]
                    pYd = ps2("pYd", [128, 2, 512])
                    pM2 = ps2("pM2", [128, 8, 128], BF16)
                    cnts = dict(d=0, l=0)

                    def stage1(i):
                        nk = (i + 1) * 128
                        nch = (nk + 511) // 512
                        tcols = tsl(i)
                        z = i % 4
                        sck = "Sc%d" % z
                        for hh in range(8):
                            j, s_ = hh // 2, hh % 2
                            po = slice(64 * s_, 64 * s_ + 64)
                            for c in range(nch):
                                n = min(512, nk - c * 512)
                                q = cnts["d"] % 2
                                cnts["d"] += 1
                                cc = slice(c * 512, c * 512 + n)
                                P.op("pe", lambda e: e.matmul(pD[q][:, 0:n], qi[po, j, tcols], ki2[po, cc], start=True, stop=True),
                                     reads=["qi", "ki2"], writes=["pD%d" % q])
                                P.op("act", lambda e: e.activation(out=R[:, q, 0:n], in_=pD[q][:, 0:n], func=AF.Relu),
                                     reads=[], writes=["pD%d" % q, "R%d" % q])
                                if hh == 0:
                                    P.op("dve", lambda e: e.tensor_scalar(Sc[:, z, cc], R[:, q, 0:n], wi[:, i, 0:1], None, op0=ALU.mult),
                                         reads=["R%d" % q, "wi"], writes=[sck])
                                else:
                                    P.op("dve", lambda e: e.scalar_tensor_tensor(Sc[:, z, cc], R[:, q, 0:n], wi[:, i, hh:hh + 1], Sc[:, z, cc],
                                                                               op0=ALU.mult, op1=ALU.add),
                                         reads=["R%d" % q, "wi", sck], writes=[sck])
                                yield

                    def stage2(i):
                        nk = (i + 1) * 128
                        z = i % 4
                        w = i % 2
                        smk = "sm%d" % w
                        mbk = "Mb%d" % w
                        lo, hi, mid, cnt, dlt = (sm[:, 8 * w + c:8 * w + c + 1] for c in range(5))
                        sck = "Sc%d" % z
                        if i >= 2:
                            P.op("dve", lambda e: e.tensor_reduce(lo, Sc[:, z, 0:nk], axis=AX.X, op=ALU.min), reads=[sck], writes=[smk])
                        P.op("dve", lambda e: e.tensor_tensor(Sc[:, z, nk - 128:nk], Sc[:, z, nk - 128:nk], dsaneg, op=ALU.add),
                             reads=["cf", sck], writes=[sck])
                        yield
                        if i >= 2:
                            P.op("dve", lambda e: e.tensor_reduce(hi, Sc[:, z, 0:nk], axis=AX.X, op=ALU.max), reads=[sck], writes=[smk])
                            P.op("dve", lambda e: e.tensor_tensor(hi, hi, lo, op=ALU.subtract), reads=[smk], writes=[smk])
                            for it in range(NBIS):
                                ck = float(0.5 ** (it + 1))
                                P.op("dve", lambda e: e.scalar_tensor_tensor(mid, hi, ck, lo, op0=ALU.mult, op1=ALU.add),
                                     reads=[smk], writes=[smk])
                                yield
                                P.op("dve", lambda e: e.tensor_scalar(junk[:, w, 0:nk], Sc[:, z, 0:nk], mid, 0.0, op0=ALU.is_gt, op1=ALU.add,
                                                                      accum_out=cnt), reads=[sck, smk], writes=["junk%d" % w, smk])
                                yield
                                P.op("dve", lambda e: e.tensor_scalar(dlt, cnt, float(TOPK) - 0.5, ck, op0=ALU.is_gt, op1=ALU.mult),
                                     reads=[smk], writes=[smk])
                                P.op("dve", lambda e: e.scalar_tensor_tensor(lo, dlt, hi, lo, op0=ALU.mult, op1=ALU.add),
                                     reads=[smk], writes=[smk])
                                yield
                            P.op("dve", lambda e: e.tensor_scalar(Mb[:, w, 0:nk], Sc[:, z, 0:nk], lo, None, op0=ALU.is_gt),
                                 reads=[sck, smk], writes=[mbk])
                        else:
                            P.op("dve", lambda e: e.tensor_scalar(Mb[:, w, 0:nk], Sc[:, z, 0:nk], -1e29, None, op0=ALU.is_gt),
                                 reads=[sck], writes=[mbk])
                        yield
                        for g0 in range(0, i + 1, 8):
                            g1 = min(i + 1, g0 + 8)
                            for kb in range(g0, g1):
                                P.op("pe", lambda e: e.transpose(pM[:, kb - g0, :], Mb[:, w, tsl(kb)], ident),
                                     reads=[mbk, "cb"], writes=["pM"])
                            P.op("act", lambda e: e.copy(MT[:, z, g0:g1, :], pM[:, 0:g1 - g0, :]),
                                 reads=[], writes=["pM", "MT%d" % z])
                            yield

                    def stage3(i):
                        tcols = tsl(i)
                        z = i % 4
                        def emit_pv(kb, s_, q):
                            for j in range(4):
                                P.op("pe", lambda e: e.matmul(pYd[:, s_, j * 66:(j + 1) * 66], PT[:, q, j * 128:(j + 1) * 128], v[:, kb, 512:578],
                                                              start=(kb == 0 and j == 0), stop=(kb == i), skip_group_check=True),
                                     reads=["PT%d" % q, "v", "vone"], writes=["pYd%d" % s_])
                        prev = None
                        for kb in range(i + 1):
                            for s_ in range(2):
                                q = cnts["l"] % 2
                                cnts["l"] += 1
                                po = slice(64 * s_, 64 * s_ + 64)
                                for j in range(4):
                                    P.op("pe", lambda e: e.matmul(pL[q][:, j * 128:(j + 1) * 128], kd2[po, tsl(kb)], qd[po, j, tcols],
                                                                  start=True, stop=True, skip_group_check=True),
                                         reads=["kd2", "qd"], writes=["pL%d" % q])
                                P.op("act", lambda e: e.activation(out=Pe[:, q, :], in_=pL[q][:], func=AF.Exp),
                                     reads=[], writes=["pL%d" % q, "Pe%d" % q])
                                P.op("pool", lambda e: e.tensor_tensor(
                                    PT[:, q, :].rearrange("p (h t) -> p h t", h=4), Pe[:, q, :].rearrange("p (h t) -> p h t", h=4),
                                    MT[:, z, kb:kb + 1, :].to_broadcast([128, 4, 128]), op=ALU.mult),
                                    reads=["Pe%d" % q, "MT%d" % z], writes=["PT%d" % q])
                                if prev is not None:
                                    emit_pv(*prev)
                                prev = (kb, s_, q)
                                yield
                        emit_pv(*prev)
                        for bank in range(2):
                            yv = pYd[:, bank, 0:264].rearrange("p (h c) -> p h c", h=4)
                            ydv = yd[:].rearrange("p (j s c) -> p j s c", j=4, s=2)[:, :, bank, :]
                            P.op("dve", lambda e: e.reciprocal(rec[:, bank * 4:(bank + 1) * 4, :], yv[:, :, 64:65]),
                                 reads=[], writes=["pYd%d" % bank, "rec%d" % bank])
                            P.op("dve", lambda e: e.tensor_tensor(ydv, yv[:, :, 0:64],
                                                                  rec[:, bank * 4:(bank + 1) * 4, :].to_broadcast([128, 4, 64]), op=ALU.mult),
                                 reads=["rec%d" % bank], writes=["pYd%d" % bank, "yd"])
                        yield
                        u = i % 2
                        for cchunk in range(4):
                            P.op("pe", lambda e: e.transpose(pM2[:, cchunk, :], yd[:, tsl(cchunk)], ident),
                                 reads=["yd", "cb"], writes=["pM2"])
                        P.op("act", lambda e: e.copy(ydst[:, u, :, :], pM2[:, 0:4, :]), reads=[], writes=["pM2", "ydst%d" % u])
                        P.dma("sp", ydT_s[:, :, sq * SEQ + i * 128: sq * SEQ + (i + 1) * 128].rearrange("c p t -> p c t"),
                              ydst[:, u, :, :], reads=["ydst%d" % u], writes=["ydd%d" % (sq * 16 + i)])

                    def seq(*gens):
                        for g in gens:
                            yield from g
                    pairs = [(2 * m + 1, 2 * m) for m in range(7, -1, -1)]
                    for tau in range(len(pairs) + 2):
                        th = []
                        if tau < len(pairs):
                            th.append(seq(stage1(pairs[tau][0]), stage1(pairs[tau][1])))
                        if 0 <= tau - 1 < len(pairs):
                            th.append(stage2(pairs[tau - 1][0]))
                            th.append(stage2(pairs[tau - 1][1]))
                        if 0 <= tau - 2 < len(pairs):
                            th.append(seq(stage3(pairs[tau - 2][0]), stage3(pairs[tau - 2][1])))
                        P.run_threads(th)
                    P.barrier()
            P.barrier()

    def p4a_phase():
        with contextlib.ExitStack() as es:
            sb = lambda n, s, d=F32: es.enter_context(nc.sbuf_tensor("p4a" + n, s, d))
            ps = lambda n, s, d=F32: es.enter_context(nc.psum_tensor("p4a" + n, s, d))
            wg = sb("wg", [128, KC, 2048], BF16)
            for k in range(KC):
                P.dma("pool", wg[:, k, :], win_d[tsl(k), OFF_GSB:OFF_GSB + 2048], writes=["wg"])
            wosb = sb("wosb", [64, 8, D], BF16)
            P.dma("pool", wosb[:], wosb_d.rearrange("(h p) n -> p h n", p=64), writes=["wosb"])
            wod = sb("wod", [128, 4, D], BF16)
            P.dma("pool", wod[:], wod_d.rearrange("(c p) n -> p c n", p=128), writes=["wod"])
            wo = sb("wo", [128, KC, D], BF16)
            P.dma("pool", wo[:], wo_d.rearrange("(c p) n -> p c n", p=128), writes=["wo"])
            uT = sb("uT", [128, 2, KC, TB], BF16)
            ysbT = sb("ysbT", [64, 2, 8, TB], BF16)
            ydT = sb("ydT", [128, 2, 4, TB], BF16)
            mT = sb("mT", [128, KC, TB], BF16)
            s1 = sb("s1", [128, 2, TB])
            s2 = sb("s2", [128, 2, TB])
            t1 = sb("t1", [128, 2, TB])
            t2 = sb("t2", [128, 2, TB])
            hs = sb("hs", [128, 4, D])
            pG = [ps("pG%d" % i, [128, TB]) for i in range(2)]
            pY = [ps("pY%d" % i, [128, TB]) for i in range(2)]
            pO = [ps("pO%d" % i, [128, 512]) for i in range(2)]
            oc = hc = 0
            def load_blk(b):
                if b >= NB:
                    return
                u = b % 2
                tok = slice(b * TB, (b + 1) * TB)
                P.dma("sp", uT[:, u, :, :], uT_s[:, :, tok].rearrange("c p t -> p c t"), writes=["uT%d" % u])
                P.dma("sp", ysbT[:, u, :, :], ysbT_s[:, :, tok].rearrange("h p t -> p h t"), writes=["ysbT%d" % u])
                P.dma("sp", ydT[:, u, :, :], ydT_s[:, :, tok].rearrange("c p t -> p c t"), writes=["ydT%d" % u])
            load_blk(0)
            for b in range(NB):
                u = b % 2
                tok = slice(b * TB, (b + 1) * TB)
                load_blk(b + 1)
                for i in range(4):
                    P.dma("sp", hs[:, i, :], h_s[tsl(4 * b + i), :], reads=["hd%d" % (4 * b + i)], writes=["hs%d" % i])
                for c in range(KC):
                    r = c % 2
                    for (g, pg, sdst, skey) in ((0, pG[0], s1, "s1"), (1, pG[1], s2, "s2")):
                        for k in range(KC):
                            P.op("pe", lambda e, k=k, g=g, pg=pg, c=c: e.matmul(
                                pg[:], wg[:, k, g * 1024 + c * 128: g * 1024 + (c + 1) * 128], uT[:, u, k, :],
                                start=(k == 0), stop=(k == KC - 1)),
                                reads=["wg", "uT%d" % u], writes=["pG%d" % g])
                        P.op("act", lambda e, pg=pg, sdst=sdst, r=r: e.activation(out=sdst[:, r, :], in_=pg[:], func=AF.Tanh, scale=0.5),
                             reads=[], writes=["pG%d" % g, "%s%d" % (skey, r)])
                    for hh in range(8):
                        P.op("pe", lambda e, hh=hh, c=c: e.matmul(pY[0][:], wosb[:, hh, tsl(c)], ysbT[:, u, hh, :],
                                                                   start=(hh == 0), stop=(hh == 7)),
                             reads=["wosb", "ysbT%d" % u], writes=["pY0"])
                    for k in range(4):
                        P.op("pe", lambda e, k=k, c=c: e.matmul(pY[1][:], wod[:, k, tsl(c)], ydT[:, u, k, :],
                                                                 start=(k == 0), stop=(k == 3)),
                             reads=["wod", "ydT%d" % u], writes=["pY1"])
                    P.op("dve", lambda e, r=r: e.scalar_tensor_tensor(t1[:, r, :], s1[:, r, :], 1.0, pY[0][:], op0=ALU.add, op1=ALU.mult),
                         reads=["s1%d" % r], writes=["pY0", "t1%d" % r])
                    P.op("dve", lambda e, r=r: e.scalar_tensor_tensor(t2[:, r, :], s2[:, r, :], 1.0, pY[1][:], op0=ALU.add, op1=ALU.mult),
                         reads=["s2%d" % r], writes=["pY1", "t2%d" % r])
                    P.op("pool", lambda e, r=r, c=c: e.tensor_tensor(mT[:, c, :], t1[:, r, :], t2[:, r, :], op=ALU.add),
                         reads=["t1%d" % r, "t2%d" % r], writes=["mT"])
                for i in range(4):
                    t = 4 * b + i
                    sl = i
                    for c2 in range(2):
                        q = oc % 2
                        oc += 1
                        for k in range(KC):
                            P.op("pe", lambda e, k=k, i=i, c2=c2, q=q: e.matmul(pO[q][:], mT[:, k, tsl(i)], wo[:, k, c2 * 512:(c2 + 1) * 512],
                                                                               start=(k == 0), stop=(k == KC - 1)),
                                 reads=["mT", "wo"], writes=["pO%d" % q])
                        P.op("dve", lambda e, q=q, sl=sl, c2=c2: e.scalar_tensor_tensor(
                            hs[:, sl, c2 * 512:(c2 + 1) * 512], pO[q][:], 0.5, hs[:, sl, c2 * 512:(c2 + 1) * 512],
                            op0=ALU.mult, op1=ALU.add), reads=[], writes=["pO%d" % q, "hs%d" % sl])
                    P.dma("sp", h_s[tsl(t), :], hs[:, sl, :], reads=["hs%d" % sl], writes=["hd%d" % t])
            P.barrier()

    def p4c_phase():
        with contextlib.ExitStack() as es:
            sb = lambda n, s, d=F32: es.enter_context(nc.sbuf_tensor("p4c" + n, s, d))
            ps = lambda n, s, d=F32: es.enter_context(nc.psum_tensor("p4c" + n, s, d))
            c = load_consts(es)
            ident = c["ident"]
            wpg = sb("wpg", [128, KC, D], BF16)
            P.dma("pool", wpg[:], wpg_d.rearrange("(c p) n -> p c n", p=128), writes=["wpg"])
            wpp = sb("wpp", [128, 2, D], BF16)
            P.dma("pool", wpp[:], wpp_d.rearrange("(c p) n -> p c n", p=128), writes=["wpp"])
            gB = make_gB(es, c, 3, "p4cgB")
            gfin = sb("gfin", [128, D])
            P.dma("sp", gfin[:], gfin_d.broadcast_to([128, D]), writes=["gfin"])
            hs = sb("hs", [128, 3, D])
            pt = sb("pt", [128, 2, 256], BF16)
            xn = sb("xn", [128, 2, D], BF16)
            hnT = sb("hnT", [128, 2, KC, 128], BF16)
            pTs = sb("pTs", [128, 2, 2, 128], BF16)
            gate = sb("gate", [128, 2, D])
            ot = sb("ot", [128, 2, D])
            ss = sb("ss", [128, 8])
            rstd = sb("rstd", [128, 8])
            pT = ps("pT", [128, KC, 128], BF16)
            pP = ps("pP", [128, 2, 128], BF16)
            pGt = [ps("pGt%d" % i, [128, 512]) for i in range(2)]
            pPp = [ps("pPp%d" % i, [128, 512]) for i in range(2)]
            def load_t(t):
                if t < NT:
                    P.dma("sp", hs[:, t % 3, :], h_s[tsl(t), :], reads=["hd%d" % t], writes=["hs%d" % (t % 3)])
                    P.dma("pool", pt[:, t % 2, :], p_d[tsl(t), :], writes=["pt%d" % (t % 2)])
            load_t(0)
            for t in range(NT):
                sl = t % 3
                u = t % 2
                a = (t % 4)
                load_t(t + 1)
                P.op("act", lambda e, sl=sl, u=u, a=a: e.activation(out=xn[:, u, :], in_=hs[:, sl, :], func=AF.Square,
                                                                   accum_out=ss[:, a:a + 1]),
                     reads=["hs%d" % sl], writes=["xn%d" % u, "ss%d" % a])
                rstd_from_ss(ss[:, a:a + 1], rstd[:, a:a + 1], 1, ["ss%d" % a], "rstd%d" % a)
                P.op("dve", lambda e, sl=sl, u=u, a=a: e.tensor_scalar(xn[:, u, :], hs[:, sl, :], rstd[:, a:a + 1], None, op0=ALU.mult),
                     reads=["hs%d" % sl, "rstd%d" % a], writes=["xn%d" % u])
                for k in range(KC):
                    P.op("pe", lambda e, k=k, u=u: e.transpose(pT[:, k, :], xn[:, u, tsl(k)], ident[:]),
                         reads=["xn%d" % u, "c_ident"], writes=["pT"])
                P.op("dve", lambda e, u=u: e.tensor_tensor(hnT[:, u, :, :], pT[:], gB[:], op=ALU.mult),
                     reads=["p4cgB"], writes=["pT", "hnT%d" % u])
                for k in range(2):
                    P.op("pe", lambda e, k=k, u=u: e.transpose(pP[:, k, :], pt[:, u, tsl(k)], ident[:]),
                         reads=["pt%d" % u, "c_ident"], writes=["pP"])
                P.op("act", lambda e, u=u: e.copy(pTs[:, u, :, :], pP[:]), reads=[], writes=["pP", "pTs%d" % u])
                for c2 in range(2):
                    for k in range(KC):
                        P.op("pe", lambda e, k=k, u=u, c2=c2: e.matmul(pGt[c2][:], hnT[:, u, k, :], wpg[:, k, c2 * 512:(c2 + 1) * 512],
                                                                      start=(k == 0), stop=(k == KC - 1)),
                             reads=["hnT%d" % u, "wpg"], writes=["pGt%d" % c2])
                    for k in range(2):
                        P.op("pe", lambda e, k=k, u=u, c2=c2: e.matmul(pPp[c2][:], pTs[:, u, k, :], wpp[:, k, c2 * 512:(c2 + 1) * 512],
                                                                      start=(k == 0), stop=(k == 1)),
                             reads=["pTs%d" % u, "wpp"], writes=["pPp%d" % c2])
                    cs = slice(c2 * 512, (c2 + 1) * 512)
                    P.op("act", lambda e, u=u, c2=c2, cs=cs: e.activation(out=gate[:, u, cs], in_=pGt[c2][:], func=AF.Tanh, scale=0.5),
                         reads=[], writes=["pGt%d" % c2, "gate%d" % u])
                    P.op("dve", lambda e, u=u, c2=c2, cs=cs: e.scalar_tensor_tensor(gate[:, u, cs], gate[:, u, cs], 1.0, pPp[c2][:],
                                                                                 op0=ALU.add, op1=ALU.mult),
                         reads=[], writes=["pPp%d" % c2, "gate%d" % u])
                P.op("dve", lambda e, u=u, sl=sl: e.scalar_tensor_tensor(hs[:, sl, :], gate[:, u, :], 0.5, hs[:, sl, :],
                                                                         op0=ALU.mult, op1=ALU.add),
                     reads=["gate%d" % u], writes=["hs%d" % sl])
                a2 = 4 + a
                P.op("act", lambda e, sl=sl, u=u, a2=a2: e.activation(out=xn[:, u, :], in_=hs[:, sl, :], func=AF.Square,
                                                                     accum_out=ss[:, a2:a2 + 1]),
                     reads=["hs%d" % sl], writes=["xn%d" % u, "ss%d" % a2])
                rstd_from_ss(ss[:, a2:a2 + 1], rstd[:, a2:a2 + 1], 1, ["ss%d" % a2], "rstd%d" % a2)
                P.op("dve", lambda e, sl=sl, u=u, a2=a2: e.scalar_tensor_tensor(ot[:, u, :], hs[:, sl, :], rstd[:, a2:a2 + 1], gfin[:],
                                                                               op0=ALU.mult, op1=ALU.mult),
                     reads=["hs%d" % sl, "rstd%d" % a2, "gfin"], writes=["ot%d" % u])
                P.dma("sp", out_d[tsl(t), :], ot[:, u, :], reads=["ot%d" % u], writes=["outd%d" % t])
            P.barrier()

    if "p1" in phases:
        ffn_phase("f1", x_d, h_s, w1a_d, w2a_d, 0, 1)
    if "p2" in phases:
        p2_phase()
    if "p3" in phases:
        p3_phase()
    if "p4a" in phases:
        p4a_phase()
    if "p4b" in phases:
        ffn_phase("f2", h_s, h_s, w1b_d, w2b_d, 2, None)
    if "p4c" in phases:
        p4c_phase()
    P.emit()
    return nc


def host_consts():
    j = np.arange(128)
    ident = np.eye(128, dtype=np.float32)
    nUincl = -(j[:, None] >= j[None, :]).astype(np.float32)
    nLstr = -(j[:, None] < j[None, :]).astype(np.float32)
    t = np.arange(512)
    sbmask = np.stack([(j[:, None] + 128 * r < t[None, :]).astype(np.float32) for r in range(4)], 1)
    cb = np.concatenate([ident, nUincl, nLstr, sbmask.reshape(128, 2048)], 1).astype(np.float32)
    dsaneg = np.where(j[None, :] > j[:, None], -1e30, 0.0).astype(np.float32)
    inv_freq = (500000.0 ** (-np.arange(0, 16, 2, dtype=np.float32) / 16)).astype(np.float32)
    pm = j % 64
    invf = np.where(pm < 16, inv_freq[pm % 8], 0.0).astype(np.float32)
    sgn = np.where(pm < 8, -1.0, np.where(pm < 16, 1.0, 0.0)).astype(np.float32)
    cf = np.concatenate([dsaneg, invf[:, None], sgn[:, None]], 1).astype(np.float32)
    return cb, cf


def make_in_maps(inputs, ncores=8):
    f = lambda a: np.ascontiguousarray(np.asarray(a), dtype=np.float32)
    x = f(inputs["x"])
    p = f(inputs["p"])[0]
    pos = np.ascontiguousarray(np.asarray(inputs["positions"]), dtype=np.int32)
    cb, cf = host_consts()
    gl = lambda g: f(g).reshape(KC, 128).T
    gcols = np.ascontiguousarray(np.concatenate([gl(inputs["ffn1_norm"][0]), gl(inputs["mix_norm"][0]),
                                                 gl(inputs["ffn2_norm"][0]), gl(inputs["ple_norm"][0])], 1))
    shared = {
        "w1a": f(inputs["ffn1_w1"][0]), "w2a": f(inputs["ffn1_w2"][0]), "win": f(inputs["w_in"][0]),
        "wosb": f(inputs["w_out_sb"][0]), "wod": f(inputs["w_out_dsa"][0]), "wo": f(inputs["w_out"][0]),
        "w1b": f(inputs["ffn2_w1"][0]), "w2b": f(inputs["ffn2_w2"][0]), "wpg": f(inputs["ple_w_gate"][0]),
        "wpp": f(inputs["ple_w_proj"][0]), "gcols": gcols, "gfin": f(inputs["final_norm"]).reshape(1, D),
        "cb": cb, "cf": cf,
    }
    maps = []
    for c in range(ncores):
        m = dict(shared)
        m["x"] = np.ascontiguousarray(x[NSEQ * c:NSEQ * (c + 1)].reshape(NTOK, D))
        m["p"] = np.ascontiguousarray(p[NSEQ * c:NSEQ * (c + 1)].reshape(NTOK, 256))
        m["pos"] = np.ascontiguousarray(pos[NSEQ * c:NSEQ * (c + 1)].reshape(1, NTOK))
        maps.append(m)
    return maps


def kernel(**inputs):
    nc = bass.Bass("TRN2", target_bir_lowering=False)
    build(nc)
    maps = make_in_maps(inputs, 8)
    res = run_bass_kernel_spmd(nc, maps, core_ids=list(range(8)))
    out = np.stack([r["out"].reshape(NSEQ, SEQ, D) for r in res.results], 0).reshape(16, SEQ, D)
    return out.astype(np.float32)
```

```python
import contextlib
import numpy as np
import concourse.bass as bass
import concourse.mybir as mybir
from concourse.bass_utils import run_bass_kernel_spmd

F32 = mybir.dt.float32
BF16 = mybir.dt.bfloat16
I32 = mybir.dt.int32
AF = mybir.ActivationFunctionType
ALU = mybir.AluOpType
AX = mybir.AxisListType

D = 1024
KC = 8
SEQ = 2048
NSEQ = 2
NTOK = NSEQ * SEQ
NT = NTOK // 128
TB = 512
NB = NTOK // TB
FF = 2816
FC = FF // 128
DIN = 4808
EPS = 1e-6
TOPK = 256
NBIS = 14
ACT_CHAINS = (1,)
OFF_QSB, OFF_KSB, OFF_VSB, OFF_QD, OFF_KD, OFF_VD, OFF_QI, OFF_KI, OFF_WI, OFF_GSB, OFF_GD = (
    0, 512, 1024, 1536, 2048, 2112, 2176, 2688, 2752, 2760, 3784)
TWO_PI = 6.283185307179586
CW1 = 6.28125
CW2 = TWO_PI - CW1


class _Key:
    __slots__ = ("w", "rs")

    def __init__(self):
        self.w = None
        self.rs = []


class _Rec:
    def __init__(self):
        self.call = None

    def __getattr__(self, name):
        def f(*a, **k):
            self.call = (name, a, k)
            return self
        return f


def _freeze(fn):
    rec = _Rec()
    fn(rec)
    name, a, k = rec.call
    return lambda e: getattr(e, name)(*a, **k)


class Prog:
    ENG = ("pe", "act", "dve", "pool", "sp")

    def __init__(self, nc, same_engine_sync=True, ndma_sems=8):
        self.nc = nc
        self.es = contextlib.ExitStack()
        self.streams = {e: [] for e in self.ENG}
        self.cnt = {e: 0 for e in self.ENG}
        self.sems = {}
        for e in self.ENG:
            self.sems["p_" + e] = self.es.enter_context(nc.semaphore("prog_" + e))
        self.known = {e: {} for e in self.ENG}
        self.same = same_engine_sync
        self.ndma = ndma_sems
        self.dq = {}
        self.keys = {}

    def key(self, name):
        k = self.keys.get(name)
        if k is None:
            k = _Key()
            self.keys[name] = k
        return k

    def _deps(self, reads, writes):
        ev = []
        for r in reads:
            k = self.key(r)
            if k.w is not None:
                ev.append(k.w)
        for w in writes:
            k = self.key(w)
            if k.w is not None:
                ev.append(k.w)
            ev.extend(k.rs)
        return ev

    def _waits(self, eng, evs):
        need = {}
        kn = self.known[eng]
        for (sid, val) in evs:
            if sid == "p_" + eng and (eng == "pe" or not self.same):
                continue
            if kn.get(sid, 0) >= val:
                continue
            if need.get(sid, 0) < val:
                need[sid] = val
        for sid, val in need.items():
            kn[sid] = val
        return list(need.items())

    def _commit(self, reads, writes, event):
        for r in reads:
            self.key(r).rs.append(event)
        for w in writes:
            k = self.key(w)
            k.w = event
            k.rs = []

    def op(self, eng, fn, reads=(), writes=()):
        waits = self._waits(eng, self._deps(reads, writes))
        self.cnt[eng] += 1
        event = ("p_" + eng, self.cnt[eng])
        self.streams[eng].append((waits, _freeze(fn), ("p_" + eng, 1)))
        self._commit(reads, writes, event)
        return event

    def dma(self, q, out, in_, reads=(), writes=()):
        d = self.dq.get(q)
        if d is None:
            ids = [f"d_{q}_{j}" for j in range(self.ndma)]
            for s in ids:
                self.sems[s] = self.es.enter_context(self.nc.semaphore(s))
            d = dict(i=0, ids=ids, vals=[0] * self.ndma, last=[None] * self.ndma)
            self.dq[q] = d
        j = d["i"] % self.ndma
        d["i"] += 1
        evs = self._deps(reads, writes)
        if d["last"][j] is not None:
            evs.append(d["last"][j])
        waits = self._waits(q, evs)
        d["vals"][j] += 16
        event = (d["ids"][j], d["vals"][j])
        d["last"][j] = event
        fn = lambda e, out=out, in_=in_: e.dma_start(out=out, in_=in_)
        self.streams[q].append((waits, fn, (d["ids"][j], 16)))
        self._commit(reads, writes, event)
        return event

    def _all_events(self):
        ev = []
        for q, d in self.dq.items():
            for e in d["last"]:
                if e is not None:
                    ev.append(e)
        for e in self.ENG:
            if self.cnt[e] > 0:
                ev.append(("p_" + e, self.cnt[e]))
        return ev

    def run_threads(self, gens):
        live = list(gens)
        while live:
            nxt = []
            for g in live:
                try:
                    next(g)
                    nxt.append(g)
                except StopIteration:
                    pass
            live = nxt

    def barrier(self):
        ev = self._all_events()
        for e in self.ENG:
            w = self._waits(e, ev)
            if w:
                self.streams[e].append((w, None, None))
        self.keys = {}

    def emit(self):
        nc = self.nc
        fw = self._waits("sp", self._all_events())
        self.streams["sp"].append((fw, None, None))
        with nc.Block() as block:
            def run(engname, handle):
                for waits, fn, inc in self.streams[engname]:
                    for sid, val in waits:
                        handle.wait_ge(self.sems[sid], val)
                    if fn is not None:
                        fn(handle).then_inc(self.sems[inc[0]], inc[1])

            @block.tensor
            def _(e):
                run("pe", e)

            @block.scalar
            def _(e):
                run("act", e)

            @block.vector
            def _(e):
                run("dve", e)

            @block.gpsimd
            def _(e):
                run("pool", e)

            @block.sync
            def _(e):
                run("sp", e)
        self.es.close()


def build(nc, dbg=False, phases=("p1", "p2", "p3", "p4a", "p4b", "p4c")):
    P = Prog(nc)

    def din(name, shape, dt=F32):
        return nc.dram_tensor(name, shape, dt, kind="ExternalInput").ap()

    def dscr(name, shape, dt):
        return nc.dram_tensor(name, shape, dt, kind=("ExternalOutput" if dbg else "Internal")).ap()

    x_d = din("x", [NTOK, D])
    p_d = din("p", [NTOK, 256])
    pos_d = din("pos", [1, NTOK], I32)
    w1a_d = din("w1a", [D, 2 * FF])
    w2a_d = din("w2a", [FF, D])
    win_d = din("win", [D, DIN])
    wosb_d = din("wosb", [512, D])
    wod_d = din("wod", [512, D])
    wo_d = din("wo", [D, D])
    w1b_d = din("w1b", [D, 2 * FF])
    w2b_d = din("w2b", [FF, D])
    wpg_d = din("wpg", [D, D])
    wpp_d = din("wpp", [256, D])
    gcols_d = din("gcols", [128, 4 * KC])
    gfin_d = din("gfin", [1, D])
    cb_d = din("cb", [128, 384 + 2048])
    cf_d = din("cf", [128, 130])
    out_d = nc.dram_tensor("out", [NTOK, D], F32, kind="ExternalOutput").ap()

    h_s = dscr("h_s", [NTOK, D], F32)
    uT_s = dscr("uT_s", [KC, 128, NTOK], BF16)
    qk_s = dscr("qk_s", [18, 128, NTOK], BF16)
    v_s = dscr("v_s", [NTOK, 576], BF16)
    wi_s = dscr("wi_s", [NTOK, 8], F32)
    ysbT_s = dscr("ysbT_s", [4, 128, NTOK], BF16)
    ydT_s = dscr("ydT_s", [4, 128, NTOK], BF16)

    def tsl(t):
        return slice(t * 128, (t + 1) * 128)

    ccount = [0]

    def load_consts(es, need_cb=True):
        ccount[0] += 1
        sb = lambda n, s, d=F32: es.enter_context(nc.sbuf_tensor("%s_%d" % (n, ccount[0]), s, d))
        c = {}
        c["gcols"] = sb("c_gcols", [128, 4 * KC])
        P.dma("sp", c["gcols"][:], gcols_d, writes=["c_gcols"])
        c["ident"] = sb("c_ident", [128, 128], BF16)
        P.dma("pool", c["ident"][:], cb_d[:, 0:128], writes=["c_ident"])
        return c

    def make_gB(es, c, which, name):
        gB = es.enter_context(nc.sbuf_tensor(name, [128, KC, 128], F32))
        src = c["gcols"][:, which * KC:(which + 1) * KC]
        P.op("dve", lambda e: e.tensor_copy(gB[:], src.unsqueeze(2).to_broadcast([128, KC, 128])),
             reads=["c_gcols"], writes=[name])
        return gB

    def rstd_from_ss(ss, rstd, n, rkeys, wkey):
        P.op("dve", lambda e: e.tensor_scalar(ss[:, 0:n], ss[:, 0:n], 1.0 / D, EPS, op0=ALU.mult, op1=ALU.add),
             reads=rkeys, writes=rkeys)
        P.op("act", lambda e: e.activation(out=ss[:, 0:n], in_=ss[:, 0:n], func=AF.Sqrt), reads=rkeys, writes=rkeys)
        P.op("dve", lambda e: e.reciprocal(rstd[:, 0:n], ss[:, 0:n]), reads=rkeys, writes=[wkey])

    def ffn_phase(tag, src_d, dst_d, w1_d, w2_d, gsel, post_gsel):
        with contextlib.ExitStack() as es:
            sb = lambda n, s, d=F32: es.enter_context(nc.sbuf_tensor(tag + n, s, d))
            ps = lambda n, s, d=F32: es.enter_context(nc.psum_tensor(tag + n, s, d))
            c = load_consts(es)
            w1 = sb("w1", [128, KC, 2 * FF], BF16)
            w2 = sb("w2", [128, FC, D], BF16)
            for k in range(KC):
                P.dma("pool", w1[:, k, :], w1_d[tsl(k), :], writes=["w1"])
            for k in range(FC):
                P.dma("pool", w2[:, k, :], w2_d[tsl(k), :], writes=["w2"])
            gB = make_gB(es, c, gsel, tag + "gB")
            gB2 = make_gB(es, c, post_gsel, tag + "gB2") if post_gsel is not None else None
            NXS = 6 if post_gsel is not None else 8
            xs = sb("xs", [128, NXS, D])
            issued = set()

            def issue_load(t):
                if t in issued or t >= NT:
                    return
                issued.add(t)
                P.dma("sp", xs[:, t % NXS, :], src_d[tsl(t), :], reads=["hd%d" % t], writes=["xs%d" % (t % NXS)])
            xn = sb("xn", [128, 2, D], BF16)
            xnT = sb("xnT", [128, KC, TB], BF16)
            gT = sb("gT", [128, FC, TB], BF16)
            stmp = sb("stmp", [128, 2, TB])
            ss = sb("ss", [128, 2, 4])
            rstd = sb("rstd", [128, 2, 4])
            ss2 = sb("ss2", [128, 2, 4])
            rstd2 = sb("rstd2", [128, 2, 4])
            ust = sb("ust", [128, 2, KC, 128], BF16) if post_gsel is not None else None
            pT = [ps("pT%d" % i, [128, KC, 128], BF16) for i in range(2)]
            pA = [ps("pA%d" % i, [128, TB]) for i in range(2)]
            pB = [ps("pB%d" % i, [128, TB]) for i in range(2)]
            pO = [ps("pO%d" % i, [128, 512]) for i in range(2)]
            ident = c["ident"]
            xnc = [0]
            ptc = [0]

            def norm_T(tile_ap, xs_key, rstd_col, rstd_key, gBt, gB_key, out_ap, out_key, dve_out=True):
                s = xnc[0] % 2
                xnc[0] += 1
                q = ptc[0] % 2
                ptc[0] += 1
                P.op("dve", lambda e: e.tensor_scalar(xn[:, s, :], tile_ap, rstd_col, None, op0=ALU.mult),
                     reads=[xs_key, rstd_key], writes=["xn%d" % s])
                for k in range(KC):
                    P.op("pe", lambda e, k=k: e.transpose(pT[q][:, k, :], xn[:, s, tsl(k)], ident[:]),
                         reads=["xn%d" % s, "c_ident"], writes=["pT%d" % q])
                P.op("dve", lambda e: e.tensor_tensor(out_ap, pT[q][:], gBt[:], op=ALU.mult),
                     reads=[gB_key], writes=["pT%d" % q, out_key])

            for b in range(NB):
                sp_ = b % 2
                tiles = [4 * b + i for i in range(4)]
                for i, t in enumerate(tiles):
                    sl = t % NXS
                    issue_load(t)
                    s = xnc[0] % 2
                    P.op("act", lambda e, sl=sl, s=s, i=i: e.activation(out=xn[:, s, :], in_=xs[:, sl, :], func=AF.Square,
                                                                       accum_out=ss[:, sp_, i:i + 1]),
                         reads=["xs%d" % sl], writes=["xn%d" % s, "ss%d" % sp_])
                rstd_from_ss(ss[:, sp_, :], rstd[:, sp_, :], 4, ["ss%d" % sp_], "rstd%d" % sp_)
                for i, t in enumerate(tiles):
                    sl = t % NXS
                    norm_T(xs[:, sl, :], "xs%d" % sl, rstd[:, sp_, i:i + 1], "rstd%d" % sp_, gB, tag + "gB",
                           xnT[:, :, tsl(i)], "xnT")
                for j in range(FC):
                    q = j % 2
                    for k in range(KC):
                        P.op("pe", lambda e, k=k, j=j, q=q: e.matmul(pA[q][:], w1[:, k, tsl(j)], xnT[:, k, :],
                                                                    start=(k == 0), stop=(k == KC - 1)),
                             reads=["w1", "xnT"], writes=["pA%d" % q])
                    for k in range(KC):
                        P.op("pe", lambda e, k=k, j=j, q=q: e.matmul(pB[q][:], w1[:, k, FF + j * 128:FF + (j + 1) * 128],
                                                                    xnT[:, k, :], start=(k == 0), stop=(k == KC - 1)),
                             reads=["w1", "xnT"], writes=["pB%d" % q])
                    P.op("act", lambda e, q=q: e.activation(out=stmp[:, q, :], in_=pA[q][:], func=AF.Silu),
                         reads=[], writes=["pA%d" % q, "stmp%d" % q])
                    P.op("dve", lambda e, q=q, j=j: e.tensor_tensor(gT[:, j, :], stmp[:, q, :], pB[q][:], op=ALU.mult),
                         reads=["stmp%d" % q], writes=["pB%d" % q, "gT"])
                for t2_ in range(4 * b + 4, 4 * b + 4 + (NXS - 4)):
                    issue_load(t2_)
                oc = 0
                for i, t in enumerate(tiles):
                    sl = t % NXS
                    for c2 in range(2):
                        q = oc % 2
                        oc += 1
                        for j in range(FC):
                            P.op("pe", lambda e, j=j, i=i, c2=c2, q=q: e.matmul(
                                pO[q][:], gT[:, j, tsl(i)], w2[:, j, c2 * 512:(c2 + 1) * 512],
                                start=(j == 0), stop=(j == FC - 1)),
                                reads=["w2", "gT"], writes=["pO%d" % q])
                        P.op("dve", lambda e, q=q, sl=sl, c2=c2: e.scalar_tensor_tensor(
                            xs[:, sl, c2 * 512:(c2 + 1) * 512], pO[q][:], 0.5, xs[:, sl, c2 * 512:(c2 + 1) * 512],
                            op0=ALU.mult, op1=ALU.add),
                            reads=[], writes=["pO%d" % q, "xs%d" % sl])
                    P.dma("sp", dst_d[tsl(t), :], xs[:, sl, :], reads=["xs%d" % sl], writes=["hd%d" % t])
                    if post_gsel is not None:
                        s = xnc[0] % 2
                        P.op("act", lambda e, sl=sl, s=s, i=i: e.activation(out=xn[:, s, :], in_=xs[:, sl, :], func=AF.Square,
                                                                           accum_out=ss2[:, sp_, i:i + 1]),
                             reads=["xs%d" % sl], writes=["xn%d" % s, "ss2%d" % sp_])
                if post_gsel is not None:
                    rstd_from_ss(ss2[:, sp_, :], rstd2[:, sp_, :], 4, ["ss2%d" % sp_], "rstd2%d" % sp_)
                    for i, t in enumerate(tiles):
                        sl = t % NXS
                        u = t % 2
                        norm_T(xs[:, sl, :], "xs%d" % sl, rstd2[:, sp_, i:i + 1], "rstd2%d" % sp_, gB2, tag + "gB2",
                               ust[:, u, :, :], "ust%d" % u)
                        P.dma("sp", uT_s[:, :, tsl(t)].rearrange("c p t -> p c t"), ust[:, u, :, :],
                              reads=["ust%d" % u], writes=["uTd%d" % t])
            P.barrier()

    def p2_phase():
        with contextlib.ExitStack() as es:
            sb = lambda n, s, d=F32: es.enter_context(nc.sbuf_tensor("p2" + n, s, d))
            ps = lambda n, s, d=F32: es.enter_context(nc.psum_tensor("p2" + n, s, d))
            win = sb("win", [128, KC, 2760], BF16)
            for k in range(KC):
                P.dma("pool", win[:, k, :], win_d[tsl(k), 0:2760], writes=["win"])
            wp = sb("wp", [128, KC, 1280], BF16)
            wk2 = sb("wk2", [128, KC, 256], BF16)
            cf = sb("cf", [128, 130])
            P.dma("sp", cf[:], cf_d, writes=["cf"])
            posi = sb("posi", [128, NTOK], I32)
            P.dma("sp", posi[:], pos_d.broadcast_to([128, NTOK]), writes=["posi"])
            ang = sb("ang", [128, NTOK])
            kk = sb("kk", [128, NTOK])
            kki = sb("kki", [128, NTOK], I32)
            Ct = sb("Ct", [128, NTOK])
            St = sb("St", [128, NTOK])
            invf = cf[:, 128:129]
            sgn = cf[:, 129:130]
            P.op("dve", lambda e: e.tensor_copy(ang[:], posi[:]), reads=["posi"], writes=["ang"])
            P.op("dve", lambda e: e.tensor_scalar(ang[:], ang[:], invf, None, op0=ALU.mult), reads=["ang", "cf"], writes=["ang"])

            def reduce_to(dst, shift, key):
                P.op("dve", lambda e: e.tensor_scalar(kk[:], ang[:], shift, 1.0 / TWO_PI, op0=ALU.add, op1=ALU.mult),
                     reads=["ang"], writes=["kk"])
                P.op("dve", lambda e: e.tensor_copy(kki[:], kk[:]), reads=["kk"], writes=["kki"])
                P.op("dve", lambda e: e.tensor_copy(kk[:], kki[:]), reads=["kki"], writes=["kk"])
                P.op("dve", lambda e: e.scalar_tensor_tensor(dst[:], kk[:], -CW1, ang[:], op0=ALU.mult, op1=ALU.add),
                     reads=["kk", "ang"], writes=[key])
                P.op("dve", lambda e: e.scalar_tensor_tensor(dst[:], kk[:], -CW2, dst[:], op0=ALU.mult, op1=ALU.add),
                     reads=["kk"], writes=[key])
                P.op("dve", lambda e: e.tensor_scalar(dst[:], dst[:], shift, 3.1415925, op0=ALU.add, op1=ALU.min),
                     reads=[], writes=[key])
                P.op("dve", lambda e: e.tensor_scalar(dst[:], dst[:], -3.1415925, None, op0=ALU.max), reads=[], writes=[key])
                P.op("act", lambda e: e.activation(out=dst[:], in_=dst[:], func=AF.Sin), reads=[], writes=[key])

            reduce_to(St, 0.0, "St")
            P.op("dve", lambda e: e.tensor_scalar(St[:], St[:], sgn, None, op0=ALU.mult), reads=["cf"], writes=["St"])
            reduce_to(Ct, float(np.pi / 2), "Ct")
            P.op("pool", lambda e: e.tensor_scalar(win[:, :, 0:512], win[:, :, 0:512], 0.125, None, op0=ALU.mult),
                 reads=[], writes=["win"])
            P.op("pool", lambda e: e.tensor_scalar(win[:, :, OFF_QD:OFF_QD + 512], win[:, :, OFF_QD:OFF_QD + 512], 0.125, None,
                                                   op0=ALU.mult), reads=[], writes=["win"])
            P.op("pool", lambda e: e.memset(wp[:], 0.0), writes=["wp"])
            for hh in range(2):
                P.op("pool", lambda e, hh=hh: e.tensor_copy(wk2[:, :, hh * 64:(hh + 1) * 64], win[:, :, OFF_KD:OFF_KD + 64]),
                     reads=["win"], writes=["wk2"])
                P.op("pool", lambda e, hh=hh: e.tensor_copy(wk2[:, :, 128 + hh * 64:128 + (hh + 1) * 64],
                                                            win[:, :, OFF_KI:OFF_KI + 64]), reads=["win"], writes=["wk2"])
            pbase = [OFF_QD + 128 * j for j in range(4)] + [OFF_QI + 128 * j for j in range(4)]
            for pc in range(10):
                for hh in range(2):
                    if pc < 8:
                        b0 = pbase[pc] + 64 * hh
                    else:
                        b0 = OFF_KD if pc == 8 else OFF_KI
                    o = pc * 128 + 64 * hh
                    P.op("pool", lambda e, o=o, b0=b0: e.tensor_copy(wp[:, :, o:o + 8], win[:, :, b0 + 8:b0 + 16]),
                         reads=["win"], writes=["wp"])
                    P.op("pool", lambda e, o=o, b0=b0: e.tensor_copy(wp[:, :, o + 8:o + 16], win[:, :, b0:b0 + 8]),
                         reads=["win"], writes=["wp"])
            uT = sb("uT", [128, 2, KC, TB], BF16)
            fst = sb("fst", [128, 4, TB], BF16)
            t1 = sb("t1", [128, 2, TB])
            t2 = sb("t2", [128, 2, TB])
            vst = sb("vst", [128, 2, 576], BF16)
            wist = sb("wist", [128, 2, 8])
            pA = [ps("pA%d" % i, [128, TB]) for i in range(3)]
            pB = [ps("pB%d" % i, [128, TB]) for i in range(2)]
            pV = [ps("pV%d" % i, [128, 512]) for i in range(2)]
            pW = ps("pW", [128, 128])
            chunks = []
            for j in range(4):
                chunks.append((j, win, OFF_QSB + 128 * j, None))
            for j in range(4):
                chunks.append((4 + j, win, OFF_KSB + 128 * j, None))
            for j in range(4):
                chunks.append((8 + j, win, OFF_QD + 128 * j, j))
            for j in range(4):
                chunks.append((12 + j, win, OFF_QI + 128 * j, 4 + j))
            chunks.append((16, wk2, 0, 8))
            chunks.append((17, wk2, 128, 9))
            ca = cbn = fs = rc = vc = 0
            def load_uT(b):
                if b < NB:
                    P.dma("sp", uT[:, b % 2, :, :], uT_s[:, :, b * TB:(b + 1) * TB].rearrange("c p t -> p c t"),
                          reads=["uTd%d" % t for t in range(4 * b, 4 * b + 4)], writes=["uT%d" % (b % 2)])
            load_uT(0)
            for b in range(NB):
                u = b % 2
                load_uT(b + 1)
                tok = slice(b * TB, (b + 1) * TB)
                for (ci, wt, off, pidx) in chunks:
                    qa = ca % 3
                    ca += 1
                    for k in range(KC):
                        P.op("pe", lambda e, k=k, wt=wt, off=off, qa=qa: e.matmul(pA[qa][:], wt[:, k, off:off + 128], uT[:, u, k, :],
                                                                                 start=(k == 0), stop=(k == KC - 1)),
                             reads=["win", "wk2", "uT%d" % u], writes=["pA%d" % qa])
                    f = fs % 4
                    fs += 1
                    if pidx is None:
                        P.op("act", lambda e, qa=qa, f=f: e.copy(fst[:, f, :], pA[qa][:]), reads=[], writes=["pA%d" % qa, "fst%d" % f])
                    else:
                        qb = cbn % 2
                        cbn += 1
                        for k in range(KC):
                            P.op("pe", lambda e, k=k, pidx=pidx, qb=qb: e.matmul(pB[qb][:], wp[:, k, pidx * 128:(pidx + 1) * 128],
                                                                                uT[:, u, k, :], start=(k == 0), stop=(k == KC - 1)),
                                 reads=["wp", "uT%d" % u], writes=["pB%d" % qb])
                        r = rc % 2
                        rc += 1
                        P.op("dve", lambda e, qa=qa, r=r: e.tensor_tensor(t1[:, r, :], pA[qa][:], Ct[:, tok], op=ALU.mult),
                             reads=["Ct"], writes=["pA%d" % qa, "t1%d" % r])
                        P.op("dve", lambda e, qb=qb, r=r: e.tensor_tensor(t2[:, r, :], pB[qb][:], St[:, tok], op=ALU.mult),
                             reads=["St"], writes=["pB%d" % qb, "t2%d" % r])
                        P.op("pool", lambda e, r=r, f=f: e.tensor_tensor(fst[:, f, :], t1[:, r, :], t2[:, r, :], op=ALU.add),
                             reads=["t1%d" % r, "t2%d" % r], writes=["fst%d" % f])
                    P.dma("sp", qk_s[ci, :, tok], fst[:, f, :], reads=["fst%d" % f], writes=["qkd%d_%d" % (ci, b)])
                for i in range(4):
                    t = 4 * b + i
                    q = vc % 2
                    vc += 1
                    for k in range(KC):
                        P.op("pe", lambda e, k=k, i=i, q=q: e.matmul(pV[q][:], uT[:, u, k, tsl(i)], win[:, k, OFF_VSB:OFF_VSB + 512],
                                                                    start=(k == 0), stop=(k == KC - 1)),
                             reads=["win", "uT%d" % u], writes=["pV%d" % q])
                    for k in range(KC):
                        P.op("pe", lambda e, k=k, i=i: e.matmul(pW[:, 0:64], uT[:, u, k, tsl(i)], win[:, k, OFF_VD:OFF_VD + 64],
                                                               start=(k == 0), stop=(k == KC - 1), skip_group_check=True),
                             reads=["win", "uT%d" % u], writes=["pW"])
                    for k in range(KC):
                        P.op("pe", lambda e, k=k, i=i: e.matmul(pW[:, 64:72], uT[:, u, k, tsl(i)], win[:, k, OFF_WI:OFF_WI + 8],
                                                               start=False, stop=(k == KC - 1), skip_group_check=True),
                             reads=["win", "uT%d" % u], writes=["pW"])
                    P.op("act", lambda e, q=q: e.copy(vst[:, q, 0:512], pV[q][:]), reads=[], writes=["pV%d" % q, "vst%d" % q])
                    P.op("dve", lambda e, q=q: e.tensor_copy(vst[:, q, 512:576], pW[:, 0:64]), reads=[], writes=["pW", "vst%d" % q])
                    P.op("dve", lambda e, q=q: e.tensor_scalar(wist[:, q, :], pW[:, 64:72], float(8 ** -0.5 * 0.125), None, op0=ALU.mult),
                         reads=[], writes=["pW", "wist%d" % q])
                    P.dma("sp", v_s[tsl(t), :], vst[:, q, :], reads=["vst%d" % q], writes=["vd%d" % t])
                    P.dma("sp", wi_s[tsl(t), :], wist[:, q, :], reads=["wist%d" % q], writes=["wid%d" % t])
            P.barrier()

    def p3_phase():
        with contextlib.ExitStack() as es:
            sb = lambda n, s, d=F32: es.enter_context(nc.sbuf_tensor("p3" + n, s, d))
            cb = sb("cb", [128, 384 + 2048], BF16)
            P.dma("pool", cb[:], cb_d, writes=["cb"])
            cf = sb("cf", [128, 130])
            P.dma("sp", cf[:], cf_d, writes=["cf"])
            ident = cb[:, 0:128]
            nUincl = cb[:, 128:256]
            nLstr = cb[:, 256:384]
            sbmask = cb[:, 384:384 + 2048].rearrange("p (r t) -> p r t", r=4)
            dsaneg = cf[:, 0:128]
            qz = [sb("qz%d" % i, [128, 4, SEQ], BF16) for i in range(2)]
            P.op("pool", lambda e: e.memset(qz[0][64:128, :, :], 0.0), writes=["qz0z"])
            P.op("pool", lambda e: e.memset(qz[1][0:64, :, :], 0.0), writes=["qz1z"])
            ksb = sb("ksb", [128, 4, SEQ], BF16)
            nksb = sb("nksb", [128, 4, SEQ], BF16)
            qd = sb("qd", [128, 4, SEQ], BF16)
            qi = sb("qi", [128, 4, SEQ], BF16)
            kd2 = sb("kd2", [128, SEQ], BF16)
            ki2 = sb("ki2", [128, SEQ], BF16)
            v = sb("v", [128, 16, 578], BF16)
            wi = sb("wi", [128, 16, 8])
            P.op("pool", lambda e: e.memset(v[:, :, 576:578], 0.0), writes=["vone"])
            P.op("pool", lambda e: e.memset(v[:, :, 576:577], 1.0), writes=["vone"])
            def load_sb(sq):
                tok = slice(sq * SEQ, (sq + 1) * SEQ)
                for j in range(4):
                    P.dma("sp", qz[0][0:64, j, :], qk_s[j, 0:64, tok], writes=["qsb"])
                    P.dma("sp", qz[1][64:128, j, :], qk_s[j, 64:128, tok], writes=["qsb"])
                for j in range(4):
                    P.dma("sp", ksb[:, j, :], qk_s[4 + j, :, tok], writes=["ksb"])
                P.op("pool", lambda e: e.tensor_scalar(nksb[:], ksb[:], -1.0, None, op0=ALU.mult), reads=["ksb"], writes=["nksb"])

            def load_rest(sq):
                tok = slice(sq * SEQ, (sq + 1) * SEQ)
                P.dma("act", v[:, :, 0:576], v_s[tok, :].rearrange("(n p) c -> p n c", p=128), writes=["v"])
                P.dma("act", wi[:], wi_s[tok, :].rearrange("(n p) c -> p n c", p=128), writes=["wi"])
                for (tile_, c0, key) in ((qd, 8, "qd"), (qi, 12, "qi")):
                    for j in range(4):
                        P.dma("sp", tile_[:, j, :], qk_s[c0 + j, :, tok], writes=[key])
                P.dma("sp", kd2[:], qk_s[16, :, tok], writes=["kd2"])
                P.dma("sp", ki2[:], qk_s[17, :, tok], writes=["ki2"])

            load_sb(0)
            for sq in range(NSEQ):
                load_rest(sq)

                with contextlib.ExitStack() as es2:
                  if "nosb" not in phases:
                    sb2 = lambda n, s, d=F32: es2.enter_context(nc.sbuf_tensor("sb%d" % sq + n, s, d))
                    ps2 = lambda n, s, d=F32: es2.enter_context(nc.psum_tensor("sb%d" % sq + n, s, d))
                    NS = 4
                    E = sb2("E", [128, NS, 512])
                    SP = sb2("SP", [128, NS, 2, 512], BF16)
                    A = sb2("A", [128, NS, 2, 512], BF16)
                    yacc = sb2("yacc", [64, NS, 512])
                    yst = sb2("yst", [64, NS, 512], BF16)
                    pZ = [ps2("pZ%d" % i, [128, 512]) for i in range(2)]
                    pC = [ps2("pC%d" % i, [128, 512]) for i in range(NS)]
                    pY = [ps2("pY%d" % i, [64, 512]) for i in range(2)]
                    mask128 = sbmask[:, 0, 0:128]

                    def sb_stream(s_, h, qc):
                        j, half = h // 2, h % 2
                        po = slice(64 * half, 64 * half + 64)
                        zb = s_ % 2
                        S = "s%d" % s_
                        kmax = 4 * qc + 3
                        nstep = kmax + 1
                        P.op("pool", lambda e: e.memset(yacc[:, s_, :], 0.0), writes=["yacc" + S])
                        if s_ >= 2:
                            yield
                        for step in range(nstep):
                            kb = kmax - step
                            r = kb - 4 * qc
                            c0 = 128 * max(0, r)
                            cols = slice(c0, 512)
                            dcols = slice(c0, c0 + 128)
                            qcols = slice(qc * 512 + c0, (qc + 1) * 512)
                            kcols = tsl(kb)
                            par = step % 2
                            spk = "SP%s%d" % (S, par)
                            ak = "A%s%d" % (S, par)
                            P.op("pe", lambda e: e.matmul(pZ[zb][:, cols], ksb[:, j, kcols], qz[half][:, j, qcols], start=True, stop=True,
                                                          skip_group_check=True),
                                 reads=["ksb", "qsb", "qz0z", "qz1z"], writes=["pZ%d" % zb])
                            yield
                            P.op("act", lambda e: e.activation(out=E[:, s_, cols], in_=pZ[zb][:, cols], func=AF.Exp),
                                 reads=[], writes=["pZ%d" % zb, "E" + S])
                            P.op("act", lambda e: e.activation(out=SP[:, s_, par, cols], in_=E[:, s_, cols], func=AF.Ln, bias=1.0),
                                 reads=["E" + S], writes=[spk])
                            if r >= 0:
                                P.op("pool", lambda e: e.tensor_tensor(SP[:, s_, par, dcols], SP[:, s_, par, dcols], mask128, op=ALU.mult),
                                     reads=["cb", spk], writes=[spk])
                            yield
                            P.op("pe", lambda e: e.matmul(pC[s_][:, cols], ksb[:, j, kcols], qz[half][:, j, qcols], start=(step == 0), stop=False,
                                                          skip_group_check=True),
                                 reads=["ksb", "qsb"], writes=["pC" + S])
                            P.op("pe", lambda e: e.matmul(pC[s_][:, cols], nUincl, SP[:, s_, par, cols], start=False, stop=True,
                                                          skip_group_check=True),
                                 reads=["cb", spk], writes=["pC" + S])
                            yield
                            P.op("act", lambda e: e.activation(out=A[:, s_, par, cols], in_=pC[s_][:, cols], func=AF.Exp),
                                 reads=[], writes=["pC" + S, ak])
                            if r >= 0:
                                P.op("pool", lambda e: e.tensor_tensor(A[:, s_, par, dcols], A[:, s_, par, dcols], mask128, op=ALU.mult),
                                     reads=["cb", ak], writes=[ak])
                            yield
                            P.op("pe", lambda e: e.matmul(pY[zb][:, cols], v[:, kb, h * 64:(h + 1) * 64], A[:, s_, par, cols],
                                                          start=True, stop=True, skip_group_check=True),
                                 reads=["v", ak], writes=["pY%d" % zb])
                            if step < nstep - 1:
                                P.op("pe", lambda e: e.matmul(pC[s_][:, cols], nksb[:, j, kcols], qz[half][:, j, qcols], start=False, stop=False,
                                                              skip_group_check=True),
                                     reads=["nksb", "qsb"], writes=["pC" + S])
                                P.op("pe", lambda e: e.matmul(pC[s_][:, cols], nLstr, SP[:, s_, par, cols], start=False, stop=False,
                                                              skip_group_check=True),
                                     reads=["cb", spk], writes=["pC" + S])
                            P.op("dve", lambda e: e.tensor_tensor(yacc[:, s_, cols], yacc[:, s_, cols], pY[zb][:, cols], op=ALU.add),
                                 reads=["yacc" + S], writes=["pY%d" % zb, "yacc" + S])
                            yield
                        P.op("dve", lambda e: e.tensor_copy(yst[:, s_, :], yacc[:, s_, :]), reads=["yacc" + S], writes=["yst" + S])
                        P.dma("sp", ysbT_s[h // 2, 64 * (h % 2):64 * (h % 2) + 64, sq * SEQ + qc * 512: sq * SEQ + (qc + 1) * 512], yst[:, s_, :],
                              reads=["yst" + S], writes=["ysbd%d_%d" % (h, sq * 4 + qc)])

                    for qc in range(4):
                        for g in range(2):
                            P.run_threads([sb_stream(s_, 4 * g + s_, qc) for s_ in range(NS)])
                    P.barrier()
                    if sq + 1 < NSEQ:
                        load_sb(sq + 1)

                with contextlib.ExitStack() as es2:
                  if "nodsa" not in phases:
                    sb2 = lambda n, s, d=F32: es2.enter_context(nc.sbuf_tensor("ds%d" % sq + n, s, d))
                    ps2 = lambda n, s, d=F32: es2.enter_context(nc.psum_tensor("ds%d" % sq + n, s, d))
                    Sc = sb2("Sc", [128, 4, SEQ])
                    R = sb2("R", [128, 2, 512])
                    Mb = sb2("Mb", [128, 2, SEQ], BF16)
                    MT = sb2("MT", [128, 4, 16, 128], BF16)
                    junk = sb2("junk", [128, 2, SEQ], BF16)
                    Pe = sb2("Pe", [128, 2, 512], BF16)
                    PT = sb2("PT", [128, 3, 512], BF16)
                    yd = sb2("yd", [128, 512], BF16)
                    ydst = sb2("ydst", [128, 2, 4, 128], BF16)
                    sm = sb2("sm", [128, 16])
                    rec = sb2("rec", [128, 8, 1])
                    pD = [ps2("pD%d" % i, [128, 512]) for i in range(2)]
                    pM = ps2("pM", [128, 8, 128], BF16)
                    pL = [ps2("pL%d" % i, [128, 512]) for i in range(2)]
                    pYd = ps2("pYd", [128, 2, 512])
                    pM2 = ps2("pM2", [128, 8, 128], BF16)
                    cnts = dict(d=0, l=0)

                    def stage1(i):
                        nk = (i + 1) * 128
                        nch = (nk + 511) // 512
                        tcols = tsl(i)
                        z = i % 4
                        sck = "Sc%d" % z
                        for hh in range(8):
                            j, s_ = hh // 2, hh % 2
                            po = slice(64 * s_, 64 * s_ + 64)
                            for c in range(nch):
                                n = min(512, nk - c * 512)
                                q = cnts["d"] % 2
                                cnts["d"] += 1
                                cc = slice(c * 512, c * 512 + n)
                                P.op("pe", lambda e: e.matmul(pD[q][:, 0:n], qi[po, j, tcols], ki2[po, cc], start=True, stop=True),
                                     reads=["qi", "ki2"], writes=["pD%d" % q])
                                P.op("act", lambda e: e.activation(out=R[:, q, 0:n], in_=pD[q][:, 0:n], func=AF.Relu),
                                     reads=[], writes=["pD%d" % q, "R%d" % q])
                                if hh == 0:
                                    P.op("dve", lambda e: e.tensor_scalar(Sc[:, z, cc], R[:, q, 0:n], wi[:, i, 0:1], None, op0=ALU.mult),
                                         reads=["R%d" % q, "wi"], writes=[sck])
                                else:
                                    P.op("dve", lambda e: e.scalar_tensor_tensor(Sc[:, z, cc], R[:, q, 0:n], wi[:, i, hh:hh + 1], Sc[:, z, cc],
                                                                               op0=ALU.mult, op1=ALU.add),
                                         reads=["R%d" % q, "wi", sck], writes=[sck])
                                yield

                    def stage2(i):
                        nk = (i + 1) * 128
                        z = i % 4
                        w = i % 2
                        smk = "sm%d" % w
                        mbk = "Mb%d" % w
                        lo, hi, mid, cnt, dlt = (sm[:, 8 * w + c:8 * w + c + 1] for c in range(5))
                        sck = "Sc%d" % z
                        if i >= 2:
                            P.op("dve", lambda e: e.tensor_reduce(lo, Sc[:, z, 0:nk], axis=AX.X, op=ALU.min), reads=[sck], writes=[smk])
                        P.op("dve", lambda e: e.tensor_tensor(Sc[:, z, nk - 128:nk], Sc[:, z, nk - 128:nk], dsaneg, op=ALU.add),
                             reads=["cf", sck], writes=[sck])
                        yield
                        if i >= 2:
                            P.op("dve", lambda e: e.tensor_reduce(hi, Sc[:, z, 0:nk], axis=AX.X, op=ALU.max), reads=[sck], writes=[smk])
                            P.op("dve", lambda e: e.tensor_tensor(hi, hi, lo, op=ALU.subtract), reads=[smk], writes=[smk])
                            on_act = w in ACT_CHAINS
                            sg = -1.0 if on_act else 1.0
                            if on_act:
                                P.op("dve", lambda e: e.tensor_scalar(lo, lo, -1.0, None, op0=ALU.mult), reads=[smk], writes=[smk])
                            yield
                            thr = float(2 * TOPK - 1 - nk) if on_act else (float(TOPK) - 0.5)
                            for it in range(NBIS):
                                ck = float(0.5 ** (it + 1))
                                P.op("dve", lambda e: e.scalar_tensor_tensor(mid, hi, sg * ck, lo, op0=ALU.mult, op1=ALU.add),
                                     reads=[smk], writes=[smk + "m"])
                                yield
                                if on_act:
                                    P.op("act", lambda e: e.activation(out=junk[:, w, 0:nk], in_=Sc[:, z, 0:nk], func=AF.Sign, bias=mid,
                                                                       accum_out=cnt),
                                         reads=[sck, smk + "m"], writes=["junk%d" % w, smk + "c"])
                                else:
                                    P.op("dve", lambda e: e.tensor_scalar(junk[:, w, 0:nk], Sc[:, z, 0:nk], mid, 0.0, op0=ALU.is_gt, op1=ALU.add,
                                                                          accum_out=cnt), reads=[sck, smk + "m"], writes=["junk%d" % w, smk + "c"])
                                yield
                                P.op("dve", lambda e: e.tensor_scalar(dlt, cnt, thr, sg * ck, op0=ALU.is_gt, op1=ALU.mult),
                                     reads=[smk + "c"], writes=[smk + "d"])
                                P.op("dve", lambda e: e.scalar_tensor_tensor(lo, dlt, hi, lo, op0=ALU.mult, op1=ALU.add),
                                     reads=[smk + "d", smk], writes=[smk])
                                yield
                            if on_act:
                                P.op("dve", lambda e: e.tensor_scalar(lo, lo, -1.0, None, op0=ALU.mult), reads=[smk], writes=[smk])
                            P.op("dve", lambda e: e.tensor_scalar(Mb[:, w, 0:nk], Sc[:, z, 0:nk], lo, None, op0=ALU.is_gt),
                                 reads=[sck, smk], writes=[mbk])
                        else:
                            P.op("dve", lambda e: e.tensor_scalar(Mb[:, w, 0:nk], Sc[:, z, 0:nk], -1e29, None, op0=ALU.is_gt),
                                 reads=[sck], writes=[mbk])
                        yield
                        for g0 in range(0, i + 1, 8):
                            g1 = min(i + 1, g0 + 8)
                            for kb in range(g0, g1):
                                P.op("pe", lambda e: e.transpose(pM[:, kb - g0, :], Mb[:, w, tsl(kb)], ident),
                                     reads=[mbk, "cb"], writes=["pM"])
                            P.op("act", lambda e: e.copy(MT[:, z, g0:g1, :], pM[:, 0:g1 - g0, :]),
                                 reads=[], writes=["pM", "MT%d" % z])
                            yield

                    def stage3(i):
                        tcols = tsl(i)
                        z = i % 4
                        def emit_pv(kb, s_, q3):
                            for j in range(4):
                                P.op("pe", lambda e: e.matmul(pYd[:, s_, j * 66:(j + 1) * 66], PT[:, q3, j * 128:(j + 1) * 128], v[:, kb, 512:578],
                                                              start=(kb == 0 and j == 0), stop=(kb == i), skip_group_check=True),
                                     reads=["PT%d" % q3, "v", "vone"], writes=["pYd%d" % s_])
                        pend = []
                        for kb in range(i + 1):
                            for s_ in range(2):
                                q = cnts["l"] % 2
                                q3 = cnts["l"] % 3
                                cnts["l"] += 1
                                po = slice(64 * s_, 64 * s_ + 64)
                                for j in range(4):
                                    P.op("pe", lambda e: e.matmul(pL[q][:, j * 128:(j + 1) * 128], kd2[po, tsl(kb)], qd[po, j, tcols],
                                                                  start=True, stop=True, skip_group_check=True),
                                         reads=["kd2", "qd"], writes=["pL%d" % q])
                                P.op("act", lambda e: e.activation(out=Pe[:, q, :], in_=pL[q][:], func=AF.Exp),
                                     reads=[], writes=["pL%d" % q, "Pe%d" % q])
                                P.op("pool", lambda e: e.tensor_tensor(
                                    PT[:, q3, :].rearrange("p (h t) -> p h t", h=4), Pe[:, q, :].rearrange("p (h t) -> p h t", h=4),
                                    MT[:, z, kb:kb + 1, :].to_broadcast([128, 4, 128]), op=ALU.mult),
                                    reads=["Pe%d" % q, "MT%d" % z], writes=["PT%d" % q3])
                                pend.append((kb, s_, q3))
                                if len(pend) > 2:
                                    emit_pv(*pend.pop(0))
                                yield
                        while pend:
                            emit_pv(*pend.pop(0))
                        for bank in range(2):
                            yv = pYd[:, bank, 0:264].rearrange("p (h c) -> p h c", h=4)
                            ydv = yd[:].rearrange("p (j s c) -> p j s c", j=4, s=2)[:, :, bank, :]
                            P.op("dve", lambda e: e.reciprocal(rec[:, bank * 4:(bank + 1) * 4, :], yv[:, :, 64:65]),
                                 reads=[], writes=["pYd%d" % bank, "rec%d" % bank])
                            P.op("dve", lambda e: e.tensor_tensor(ydv, yv[:, :, 0:64],
                                                                  rec[:, bank * 4:(bank + 1) * 4, :].to_broadcast([128, 4, 64]), op=ALU.mult),
                                 reads=["rec%d" % bank], writes=["pYd%d" % bank, "yd"])
                        yield
                        u = i % 2
                        for cchunk in range(4):
                            P.op("pe", lambda e: e.transpose(pM2[:, cchunk, :], yd[:, tsl(cchunk)], ident),
                                 reads=["yd", "cb"], writes=["pM2"])
                        P.op("act", lambda e: e.copy(ydst[:, u, :, :], pM2[:, 0:4, :]), reads=[], writes=["pM2", "ydst%d" % u])
                        P.dma("sp", ydT_s[:, :, sq * SEQ + i * 128: sq * SEQ + (i + 1) * 128].rearrange("c p t -> p c t"),
                              ydst[:, u, :, :], reads=["ydst%d" % u], writes=["ydd%d" % (sq * 16 + i)])

                    def seq(*gens):
                        for g in gens:
                            yield from g
                    pairs = [(2 * m + 1, 2 * m) for m in range(7, -1, -1)]
                    for tau in range(len(pairs) + 2):
                        th = []
                        if tau < len(pairs):
                            th.append(seq(stage1(pairs[tau][0]), stage1(pairs[tau][1])))
                        if 0 <= tau - 1 < len(pairs):
                            th.append(stage2(pairs[tau - 1][0]))
                            th.append(stage2(pairs[tau - 1][1]))
                        if 0 <= tau - 2 < len(pairs):
                            th.append(seq(stage3(pairs[tau - 2][0]), stage3(pairs[tau - 2][1])))
                        P.run_threads(th)
                    P.barrier()
            P.barrier()

    def p4a_phase():
        with contextlib.ExitStack() as es:
            sb = lambda n, s, d=F32: es.enter_context(nc.sbuf_tensor("p4a" + n, s, d))
            ps = lambda n, s, d=F32: es.enter_context(nc.psum_tensor("p4a" + n, s, d))
            wg = sb("wg", [128, KC, 2048], BF16)
            for k in range(KC):
                P.dma("pool", wg[:, k, :], win_d[tsl(k), OFF_GSB:OFF_GSB + 2048], writes=["wg"])
            wosb = sb("wosb", [128, 4, D], BF16)
            P.dma("pool", wosb[:], wosb_d.rearrange("(c p) n -> p c n", p=128), writes=["wosb"])
            wod = sb("wod", [128, 4, D], BF16)
            P.dma("pool", wod[:], wod_d.rearrange("(c p) n -> p c n", p=128), writes=["wod"])
            wo = sb("wo", [128, KC, D], BF16)
            P.dma("pool", wo[:], wo_d.rearrange("(c p) n -> p c n", p=128), writes=["wo"])
            uT = sb("uT", [128, 2, KC, TB], BF16)
            ysbT = sb("ysbT", [128, 2, 4, TB], BF16)
            ydT = sb("ydT", [128, 2, 4, TB], BF16)
            mT = sb("mT", [128, KC, TB], BF16)
            s1 = sb("s1", [128, 2, TB])
            s2 = sb("s2", [128, 2, TB])
            t1 = sb("t1", [128, 2, TB])
            t2 = sb("t2", [128, 2, TB])
            hs = sb("hs", [128, 4, D])
            pG = [ps("pG%d" % i, [128, TB]) for i in range(2)]
            pY = [ps("pY%d" % i, [128, TB]) for i in range(2)]
            pO = [ps("pO%d" % i, [128, 512]) for i in range(2)]
            oc = hc = 0
            def load_blk(b):
                if b >= NB:
                    return
                u = b % 2
                tok = slice(b * TB, (b + 1) * TB)
                P.dma("sp", uT[:, u, :, :], uT_s[:, :, tok].rearrange("c p t -> p c t"), writes=["uT%d" % u])
                P.dma("sp", ysbT[:, u, :, :], ysbT_s[:, :, tok].rearrange("c p t -> p c t"), writes=["ysbT%d" % u])
                P.dma("sp", ydT[:, u, :, :], ydT_s[:, :, tok].rearrange("c p t -> p c t"), writes=["ydT%d" % u])
            load_blk(0)
            for b in range(NB):
                u = b % 2
                tok = slice(b * TB, (b + 1) * TB)
                load_blk(b + 1)
                for i in range(4):
                    P.dma("sp", hs[:, i, :], h_s[tsl(4 * b + i), :], reads=["hd%d" % (4 * b + i)], writes=["hs%d" % i])
                for c in range(KC):
                    r = c % 2
                    for (g, pg, sdst, skey) in ((0, pG[0], s1, "s1"), (1, pG[1], s2, "s2")):
                        for k in range(KC):
                            P.op("pe", lambda e, k=k, g=g, pg=pg, c=c: e.matmul(
                                pg[:], wg[:, k, g * 1024 + c * 128: g * 1024 + (c + 1) * 128], uT[:, u, k, :],
                                start=(k == 0), stop=(k == KC - 1)),
                                reads=["wg", "uT%d" % u], writes=["pG%d" % g])
                        P.op("act", lambda e, pg=pg, sdst=sdst, r=r: e.activation(out=sdst[:, r, :], in_=pg[:], func=AF.Tanh, scale=0.5),
                             reads=[], writes=["pG%d" % g, "%s%d" % (skey, r)])
                    for hh in range(4):
                        P.op("pe", lambda e, hh=hh, c=c: e.matmul(pY[0][:], wosb[:, hh, tsl(c)], ysbT[:, u, hh, :],
                                                                   start=(hh == 0), stop=(hh == 3)),
                             reads=["wosb", "ysbT%d" % u], writes=["pY0"])
                    for k in range(4):
                        P.op("pe", lambda e, k=k, c=c: e.matmul(pY[1][:], wod[:, k, tsl(c)], ydT[:, u, k, :],
                                                                 start=(k == 0), stop=(k == 3)),
                             reads=["wod", "ydT%d" % u], writes=["pY1"])
                    P.op("dve", lambda e, r=r: e.scalar_tensor_tensor(t1[:, r, :], s1[:, r, :], 1.0, pY[0][:], op0=ALU.add, op1=ALU.mult),
                         reads=["s1%d" % r], writes=["pY0", "t1%d" % r])
                    P.op("dve", lambda e, r=r: e.scalar_tensor_tensor(t2[:, r, :], s2[:, r, :], 1.0, pY[1][:], op0=ALU.add, op1=ALU.mult),
                         reads=["s2%d" % r], writes=["pY1", "t2%d" % r])
                    P.op("pool", lambda e, r=r, c=c: e.tensor_tensor(mT[:, c, :], t1[:, r, :], t2[:, r, :], op=ALU.add),
                         reads=["t1%d" % r, "t2%d" % r], writes=["mT"])
                for i in range(4):
                    t = 4 * b + i
                    sl = i
                    for c2 in range(2):
                        q = oc % 2
                        oc += 1
                        for k in range(KC):
                            P.op("pe", lambda e, k=k, i=i, c2=c2, q=q: e.matmul(pO[q][:], mT[:, k, tsl(i)], wo[:, k, c2 * 512:(c2 + 1) * 512],
                                                                               start=(k == 0), stop=(k == KC - 1)),
                                 reads=["mT", "wo"], writes=["pO%d" % q])
                        P.op("dve", lambda e, q=q, sl=sl, c2=c2: e.scalar_tensor_tensor(
                            hs[:, sl, c2 * 512:(c2 + 1) * 512], pO[q][:], 0.5, hs[:, sl, c2 * 512:(c2 + 1) * 512],
                            op0=ALU.mult, op1=ALU.add), reads=[], writes=["pO%d" % q, "hs%d" % sl])
                    P.dma("sp", h_s[tsl(t), :], hs[:, sl, :], reads=["hs%d" % sl], writes=["hd%d" % t])
            P.barrier()

    def p4c_phase():
        with contextlib.ExitStack() as es:
            sb = lambda n, s, d=F32: es.enter_context(nc.sbuf_tensor("p4c" + n, s, d))
            ps = lambda n, s, d=F32: es.enter_context(nc.psum_tensor("p4c" + n, s, d))
            c = load_consts(es)
            ident = c["ident"]
            wpg = sb("wpg", [128, KC, D], BF16)
            P.dma("pool", wpg[:], wpg_d.rearrange("(c p) n -> p c n", p=128), writes=["wpg"])
            wpp = sb("wpp", [128, 2, D], BF16)
            P.dma("pool", wpp[:], wpp_d.rearrange("(c p) n -> p c n", p=128), writes=["wpp"])
            gB = make_gB(es, c, 3, "p4cgB")
            gfin = sb("gfin", [128, D])
            P.dma("sp", gfin[:], gfin_d.broadcast_to([128, D]), writes=["gfin"])
            hs = sb("hs", [128, 8, D])
            pt = sb("pt", [128, 8, 256], BF16)
            xn = sb("xn", [128, 2, D], BF16)
            hnT = sb("hnT", [128, 2, KC, 128], BF16)
            pTs = sb("pTs", [128, 2, 2, 128], BF16)
            gate = sb("gate", [128, 2, D])
            ot = sb("ot", [128, 2, D])
            ss = sb("ss", [128, 2, 4])
            rstd = sb("rstd", [128, 2, 4])
            ss2 = sb("ss2", [128, 2, 4])
            rstd2 = sb("rstd2", [128, 2, 4])
            pT = [ps("pT%d" % i, [128, KC, 128], BF16) for i in range(2)]
            pP = [ps("pP%d" % i, [128, KC, 128], BF16) for i in range(2)]
            pGt = [ps("pGt%d" % i, [128, 512]) for i in range(2)]
            pPp = [ps("pPp%d" % i, [128, 512]) for i in range(2)]
            NG = NT // 4

            def load_g(g):
                if g < NG:
                    for i in range(4):
                        t = 4 * g + i
                        sl = (g % 2) * 4 + i
                        P.dma("sp", hs[:, sl, :], h_s[tsl(t), :], reads=["hd%d" % t], writes=["hs%d" % sl])
                        P.dma("pool", pt[:, sl, :], p_d[tsl(t), :], writes=["pt%d" % sl])

            def tile_thread(g, i):
                sl = (g % 2) * 4 + i
                u = i % 2
                gp = g % 2
                P.op("dve", lambda e: e.tensor_scalar(xn[:, u, :], hs[:, sl, :], rstd[:, gp, i:i + 1], None, op0=ALU.mult),
                     reads=["hs%d" % sl, "rstd%d" % gp], writes=["xn%d" % u])
                yield
                for k in range(KC):
                    P.op("pe", lambda e: e.transpose(pT[u][:, k, :], xn[:, u, tsl(k)], ident[:]),
                         reads=["xn%d" % u, "c_ident"], writes=["pT%d" % u])
                for k in range(2):
                    P.op("pe", lambda e: e.transpose(pP[u][:, k, :], pt[:, sl, tsl(k)], ident[:]),
                         reads=["pt%d" % sl, "c_ident"], writes=["pP%d" % u])
                yield
                P.op("dve", lambda e: e.tensor_tensor(hnT[:, u, :, :], pT[u][:], gB[:], op=ALU.mult),
                     reads=["p4cgB"], writes=["pT%d" % u, "hnT%d" % u])
                P.op("act", lambda e: e.copy(pTs[:, u, :, :], pP[u][:, 0:2, :]), reads=[], writes=["pP%d" % u, "pTs%d" % u])
                yield
                for c2 in range(2):
                    cs = slice(c2 * 512, (c2 + 1) * 512)
                    for k in range(KC):
                        P.op("pe", lambda e: e.matmul(pGt[u][:], hnT[:, u, k, :], wpg[:, k, cs], start=(k == 0), stop=(k == KC - 1)),
                             reads=["hnT%d" % u, "wpg"], writes=["pGt%d" % u])
                    for k in range(2):
                        P.op("pe", lambda e: e.matmul(pPp[u][:], pTs[:, u, k, :], wpp[:, k, cs], start=(k == 0), stop=(k == 1)),
                             reads=["pTs%d" % u, "wpp"], writes=["pPp%d" % u])
                    yield
                    P.op("act", lambda e: e.activation(out=gate[:, u, cs], in_=pGt[u][:], func=AF.Tanh, scale=0.5),
                         reads=[], writes=["pGt%d" % u, "gate%d" % u])
                    yield
                    P.op("dve", lambda e: e.scalar_tensor_tensor(gate[:, u, cs], gate[:, u, cs], 1.0, pPp[u][:], op0=ALU.add, op1=ALU.mult),
                         reads=["gate%d" % u], writes=["pPp%d" % u, "gate%d" % u])
                    yield
                P.op("dve", lambda e: e.scalar_tensor_tensor(hs[:, sl, :], gate[:, u, :], 0.5, hs[:, sl, :], op0=ALU.mult, op1=ALU.add),
                     reads=["gate%d" % u, "hs%d" % sl], writes=["hs%d" % sl])
                yield

            def seq(*gens):
                for g_ in gens:
                    yield from g_

            load_g(0)
            for g in range(NG):
                gp = g % 2
                load_g(g + 1)
                for i in range(4):
                    sl = gp * 4 + i
                    P.op("act", lambda e: e.activation(out=xn[:, i % 2, :], in_=hs[:, sl, :], func=AF.Square, accum_out=ss[:, gp, i:i + 1]),
                         reads=["hs%d" % sl], writes=["xn%d" % (i % 2), "ss%d" % gp])
                rstd_from_ss(ss[:, gp, :], rstd[:, gp, :], 4, ["ss%d" % gp], "rstd%d" % gp)
                P.run_threads([seq(tile_thread(g, 0), tile_thread(g, 2)), seq(tile_thread(g, 1), tile_thread(g, 3))])
                for i in range(4):
                    sl = gp * 4 + i
                    P.op("act", lambda e: e.activation(out=xn[:, i % 2, :], in_=hs[:, sl, :], func=AF.Square, accum_out=ss2[:, gp, i:i + 1]),
                         reads=["hs%d" % sl], writes=["xn%d" % (i % 2), "ss2%d" % gp])
                rstd_from_ss(ss2[:, gp, :], rstd2[:, gp, :], 4, ["ss2%d" % gp], "rstd2%d" % gp)
                for i in range(4):
                    sl = gp * 4 + i
                    t = 4 * g + i
                    u = i % 2
                    P.op("dve", lambda e: e.scalar_tensor_tensor(ot[:, u, :], hs[:, sl, :], rstd2[:, gp, i:i + 1], gfin[:], op0=ALU.mult, op1=ALU.mult),
                         reads=["hs%d" % sl, "rstd2%d" % gp, "gfin"], writes=["ot%d" % u])
                    P.dma("sp", out_d[tsl(t), :], ot[:, u, :], reads=["ot%d" % u], writes=["outd%d" % t])
            P.barrier()

    if "p1" in phases:
        ffn_phase("f1", x_d, h_s, w1a_d, w2a_d, 0, 1)
    if "p2" in phases:
        p2_phase()
    if "p3" in phases:
        p3_phase()
    if "p4a" in phases:
        p4a_phase()
    if "p4b" in phases:
        ffn_phase("f2", h_s, h_s, w1b_d, w2b_d, 2, None)
    if "p4c" in phases:
        p4c_phase()
    P.emit()
    return nc


def host_consts():
    j = np.arange(128)
    ident = np.eye(128, dtype=np.float32)
    nUincl = -(j[:, None] >= j[None, :]).astype(np.float32)
    nLstr = -(j[:, None] < j[None, :]).astype(np.float32)
    t = np.arange(512)
    sbmask = np.stack([(j[:, None] + 128 * r < t[None, :]).astype(np.float32) for r in range(4)], 1)
    cb = np.concatenate([ident, nUincl, nLstr, sbmask.reshape(128, 2048)], 1).astype(np.float32)
    dsaneg = np.where(j[None, :] > j[:, None], -1e30, 0.0).astype(np.float32)
    inv_freq = (500000.0 ** (-np.arange(0, 16, 2, dtype=np.float32) / 16)).astype(np.float32)
    pm = j % 64
    invf = np.where(pm < 16, inv_freq[pm % 8], 0.0).astype(np.float32)
    sgn = np.where(pm < 8, -1.0, np.where(pm < 16, 1.0, 0.0)).astype(np.float32)
    cf = np.concatenate([dsaneg, invf[:, None], sgn[:, None]], 1).astype(np.float32)
    return cb, cf


def make_in_maps(inputs, ncores=8):
    f = lambda a: np.ascontiguousarray(np.asarray(a), dtype=np.float32)
    x = f(inputs["x"])
    p = f(inputs["p"])[0]
    pos = np.ascontiguousarray(np.asarray(inputs["positions"]), dtype=np.int32)
    cb, cf = host_consts()
    gl = lambda g: f(g).reshape(KC, 128).T
    gcols = np.ascontiguousarray(np.concatenate([gl(inputs["ffn1_norm"][0]), gl(inputs["mix_norm"][0]),
                                                 gl(inputs["ffn2_norm"][0]), gl(inputs["ple_norm"][0])], 1))
    shared = {
        "w1a": f(inputs["ffn1_w1"][0]), "w2a": f(inputs["ffn1_w2"][0]), "win": f(inputs["w_in"][0]),
        "wosb": f(inputs["w_out_sb"][0]), "wod": f(inputs["w_out_dsa"][0]), "wo": f(inputs["w_out"][0]),
        "w1b": f(inputs["ffn2_w1"][0]), "w2b": f(inputs["ffn2_w2"][0]), "wpg": f(inputs["ple_w_gate"][0]),
        "wpp": f(inputs["ple_w_proj"][0]), "gcols": gcols, "gfin": f(inputs["final_norm"]).reshape(1, D),
        "cb": cb, "cf": cf,
    }
    maps = []
    for c in range(ncores):
        m = dict(shared)
        m["x"] = np.ascontiguousarray(x[NSEQ * c:NSEQ * (c + 1)].reshape(NTOK, D))
        m["p"] = np.ascontiguousarray(p[NSEQ * c:NSEQ * (c + 1)].reshape(NTOK, 256))
        m["pos"] = np.ascontiguousarray(pos[NSEQ * c:NSEQ * (c + 1)].reshape(1, NTOK))
        maps.append(m)
    return maps


def kernel(**inputs):
    nc = bass.Bass("TRN2", target_bir_lowering=False)
    build(nc)
    maps = make_in_maps(inputs, 8)
    res = run_bass_kernel_spmd(nc, maps, core_ids=list(range(8)))
    out = np.stack([r["out"].reshape(NSEQ, SEQ, D) for r in res.results], 0).reshape(16, SEQ, D)
    return out.astype(np.float32)
```

```python
import contextlib
import numpy as np
import concourse.bass as bass
import concourse.mybir as mybir
from concourse.bass_utils import run_bass_kernel_spmd

F32 = mybir.dt.float32
BF16 = mybir.dt.bfloat16
I32 = mybir.dt.int32
AF = mybir.ActivationFunctionType
ALU = mybir.AluOpType
AX = mybir.AxisListType

D = 1024
KC = 8
SEQ = 2048
NSEQ = 2
NTOK = NSEQ * SEQ
NT = NTOK // 128
TB = 512
NB = NTOK // TB
FF = 2816
FC = FF // 128
DIN = 4808
EPS = 1e-6
TOPK = 256
NBIS = 14
MASK_BIG = 30000.0
ACT_CHAINS = (1,)
OFF_QSB, OFF_KSB, OFF_VSB, OFF_QD, OFF_KD, OFF_VD, OFF_QI, OFF_KI, OFF_WI, OFF_GSB, OFF_GD = (
    0, 512, 1024, 1536, 2048, 2112, 2176, 2688, 2752, 2760, 3784)
TWO_PI = 6.283185307179586
CW1 = 6.28125
CW2 = TWO_PI - CW1


class _Key:
    __slots__ = ("w", "rs")

    def __init__(self):
        self.w = None
        self.rs = []


class _Rec:
    def __init__(self):
        self.call = None

    def __getattr__(self, name):
        def f(*a, **k):
            self.call = (name, a, k)
            return self
        return f


def _freeze(fn):
    rec = _Rec()
    fn(rec)
    name, a, k = rec.call
    return lambda e: getattr(e, name)(*a, **k)


class Prog:
    ENG = ("pe", "act", "dve", "pool", "sp")

    def __init__(self, nc, same_engine_sync=True, ndma_sems=8):
        self.nc = nc
        self.es = contextlib.ExitStack()
        self.streams = {e: [] for e in self.ENG}
        self.cnt = {e: 0 for e in self.ENG}
        self.sems = {}
        for e in self.ENG:
            self.sems["p_" + e] = self.es.enter_context(nc.semaphore("prog_" + e))
        self.known = {e: {} for e in self.ENG}
        self.same = same_engine_sync
        self.ndma = ndma_sems
        self.dq = {}
        self.keys = {}

    def key(self, name):
        k = self.keys.get(name)
        if k is None:
            k = _Key()
            self.keys[name] = k
        return k

    def _deps(self, reads, writes):
        ev = []
        for r in reads:
            k = self.key(r)
            if k.w is not None:
                ev.append(k.w)
        for w in writes:
            k = self.key(w)
            if k.w is not None:
                ev.append(k.w)
            ev.extend(k.rs)
        return ev

    def _waits(self, eng, evs):
        need = {}
        kn = self.known[eng]
        for (sid, val) in evs:
            if sid == "p_" + eng and (eng == "pe" or not self.same):
                continue
            if kn.get(sid, 0) >= val:
                continue
            if need.get(sid, 0) < val:
                need[sid] = val
        for sid, val in need.items():
            kn[sid] = val
        return list(need.items())

    def _commit(self, reads, writes, event):
        for r in reads:
            self.key(r).rs.append(event)
        for w in writes:
            k = self.key(w)
            k.w = event
            k.rs = []

    def op(self, eng, fn, reads=(), writes=()):
        waits = self._waits(eng, self._deps(reads, writes))
        self.cnt[eng] += 1
        event = ("p_" + eng, self.cnt[eng])
        self.streams[eng].append((waits, _freeze(fn), ("p_" + eng, 1)))
        self._commit(reads, writes, event)
        return event

    def dma(self, q, out, in_, reads=(), writes=()):
        d = self.dq.get(q)
        if d is None:
            ids = [f"d_{q}_{j}" for j in range(self.ndma)]
            for s in ids:
                self.sems[s] = self.es.enter_context(self.nc.semaphore(s))
            d = dict(i=0, ids=ids, vals=[0] * self.ndma, last=[None] * self.ndma)
            self.dq[q] = d
        j = d["i"] % self.ndma
        d["i"] += 1
        evs = self._deps(reads, writes)
        if d["last"][j] is not None:
            evs.append(d["last"][j])
        waits = self._waits(q, evs)
        d["vals"][j] += 16
        event = (d["ids"][j], d["vals"][j])
        d["last"][j] = event
        fn = lambda e, out=out, in_=in_: e.dma_start(out=out, in_=in_)
        self.streams[q].append((waits, fn, (d["ids"][j], 16)))
        self._commit(reads, writes, event)
        return event

    def _all_events(self):
        ev = []
        for q, d in self.dq.items():
            for e in d["last"]:
                if e is not None:
                    ev.append(e)
        for e in self.ENG:
            if self.cnt[e] > 0:
                ev.append(("p_" + e, self.cnt[e]))
        return ev

    def run_threads(self, gens):
        live = list(gens)
        while live:
            nxt = []
            for g in live:
                try:
                    next(g)
                    nxt.append(g)
                except StopIteration:
                    pass
            live = nxt

    def barrier(self):
        ev = self._all_events()
        for e in self.ENG:
            w = self._waits(e, ev)
            if w:
                self.streams[e].append((w, None, None))
        self.keys = {}

    def emit(self):
        nc = self.nc
        fw = self._waits("sp", self._all_events())
        self.streams["sp"].append((fw, None, None))
        with nc.Block() as block:
            def run(engname, handle):
                for waits, fn, inc in self.streams[engname]:
                    for sid, val in waits:
                        handle.wait_ge(self.sems[sid], val)
                    if fn is not None:
                        fn(handle).then_inc(self.sems[inc[0]], inc[1])

            @block.tensor
            def _(e):
                run("pe", e)

            @block.scalar
            def _(e):
                run("act", e)

            @block.vector
            def _(e):
                run("dve", e)

            @block.gpsimd
            def _(e):
                run("pool", e)

            @block.sync
            def _(e):
                run("sp", e)
        self.es.close()


def build(nc, dbg=False, phases=("p1", "p2", "p3", "p4a", "p4b", "p4c")):
    P = Prog(nc)

    def din(name, shape, dt=F32):
        return nc.dram_tensor(name, shape, dt, kind="ExternalInput").ap()

    def dscr(name, shape, dt):
        return nc.dram_tensor(name, shape, dt, kind=("ExternalOutput" if dbg else "Internal")).ap()

    x_d = din("x", [NTOK, D])
    p_d = din("p", [NTOK, 256])
    pos_d = din("pos", [1, NTOK], I32)
    w1a_d = din("w1a", [D, 2 * FF])
    w2a_d = din("w2a", [FF, D])
    win_d = din("win", [D, DIN])
    wosb_d = din("wosb", [512, D])
    wod_d = din("wod", [512, D])
    wo_d = din("wo", [D, D])
    w1b_d = din("w1b", [D, 2 * FF])
    w2b_d = din("w2b", [FF, D])
    wpg_d = din("wpg", [D, D])
    wpp_d = din("wpp", [256, D])
    gcols_d = din("gcols", [128, 4 * KC])
    gfin_d = din("gfin", [1, D])
    cb_d = din("cb", [128, 384 + 2048])
    cf_d = din("cf", [128, 130])
    out_d = nc.dram_tensor("out", [NTOK, D], F32, kind="ExternalOutput").ap()

    h_s = dscr("h_s", [NTOK, D], F32)
    uT_s = dscr("uT_s", [KC, 128, NTOK], BF16)
    qk_s = dscr("qk_s", [18, 128, NTOK], BF16)
    v_s = dscr("v_s", [NTOK, 576], BF16)
    wi_s = dscr("wi_s", [NTOK, 8], F32)
    ysbT_s = dscr("ysbT_s", [4, 128, NTOK], BF16)
    ydT_s = dscr("ydT_s", [4, 128, NTOK], BF16)

    def tsl(t):
        return slice(t * 128, (t + 1) * 128)

    ccount = [0]

    def load_consts(es, need_cb=True):
        ccount[0] += 1
        sb = lambda n, s, d=F32: es.enter_context(nc.sbuf_tensor("%s_%d" % (n, ccount[0]), s, d))
        c = {}
        c["gcols"] = sb("c_gcols", [128, 4 * KC])
        P.dma("sp", c["gcols"][:], gcols_d, writes=["c_gcols"])
        c["ident"] = sb("c_ident", [128, 128], BF16)
        P.dma("pool", c["ident"][:], cb_d[:, 0:128], writes=["c_ident"])
        return c

    def make_gB(es, c, which, name):
        gB = es.enter_context(nc.sbuf_tensor(name, [128, KC, 128], F32))
        src = c["gcols"][:, which * KC:(which + 1) * KC]
        P.op("dve", lambda e: e.tensor_copy(gB[:], src.unsqueeze(2).to_broadcast([128, KC, 128])),
             reads=["c_gcols"], writes=[name])
        return gB

    def rstd_from_ss(ss, rstd, n, rkeys, wkey):
        P.op("dve", lambda e: e.tensor_scalar(ss[:, 0:n], ss[:, 0:n], 1.0 / D, EPS, op0=ALU.mult, op1=ALU.add),
             reads=rkeys, writes=rkeys)
        P.op("act", lambda e: e.activation(out=ss[:, 0:n], in_=ss[:, 0:n], func=AF.Sqrt), reads=rkeys, writes=rkeys)
        P.op("dve", lambda e: e.reciprocal(rstd[:, 0:n], ss[:, 0:n]), reads=rkeys, writes=[wkey])

    def ffn_phase(tag, src_d, dst_d, w1_d, w2_d, gsel, post_gsel):
        with contextlib.ExitStack() as es:
            sb = lambda n, s, d=F32: es.enter_context(nc.sbuf_tensor(tag + n, s, d))
            ps = lambda n, s, d=F32: es.enter_context(nc.psum_tensor(tag + n, s, d))
            c = load_consts(es)
            w1 = sb("w1", [128, KC, 2 * FF], BF16)
            w2 = sb("w2", [128, FC, D], BF16)
            for k in range(KC):
                P.dma("pool", w1[:, k, :], w1_d[tsl(k), :], writes=["w1"])
            for k in range(FC):
                P.dma("pool", w2[:, k, :], w2_d[tsl(k), :], writes=["w2"])
            gB = make_gB(es, c, gsel, tag + "gB")
            gB2 = make_gB(es, c, post_gsel, tag + "gB2") if post_gsel is not None else None
            NXS = 6 if post_gsel is not None else 8
            xs = sb("xs", [128, NXS, D])
            issued = set()

            def issue_load(t):
                if t in issued or t >= NT:
                    return
                issued.add(t)
                P.dma("sp", xs[:, t % NXS, :], src_d[tsl(t), :], reads=["hd%d" % t], writes=["xs%d" % (t % NXS)])
            xn = sb("xn", [128, 2, D], BF16)
            xnT = sb("xnT", [128, KC, TB], BF16)
            gT = sb("gT", [128, FC, TB], BF16)
            stmp = sb("stmp", [128, 2, TB])
            ss = sb("ss", [128, 2, 4])
            rstd = sb("rstd", [128, 2, 4])
            ss2 = sb("ss2", [128, 2, 4])
            rstd2 = sb("rstd2", [128, 2, 4])
            ust = sb("ust", [128, 2, KC, 128], BF16) if post_gsel is not None else None
            pT = [ps("pT%d" % i, [128, KC, 128], BF16) for i in range(2)]
            pA = [ps("pA%d" % i, [128, TB]) for i in range(2)]
            pB = [ps("pB%d" % i, [128, TB]) for i in range(2)]
            pO = [ps("pO%d" % i, [128, 512]) for i in range(2)]
            ident = c["ident"]
            xnc = [0]
            ptc = [0]

            def norm_T(tile_ap, xs_key, rstd_col, rstd_key, gBt, gB_key, out_ap, out_key, dve_out=True):
                s = xnc[0] % 2
                xnc[0] += 1
                q = ptc[0] % 2
                ptc[0] += 1
                P.op("dve", lambda e: e.tensor_scalar(xn[:, s, :], tile_ap, rstd_col, None, op0=ALU.mult),
                     reads=[xs_key, rstd_key], writes=["xn%d" % s])
                for k in range(KC):
                    P.op("pe", lambda e, k=k: e.transpose(pT[q][:, k, :], xn[:, s, tsl(k)], ident[:]),
                         reads=["xn%d" % s, "c_ident"], writes=["pT%d" % q])
                P.op("dve", lambda e: e.tensor_tensor(out_ap, pT[q][:], gBt[:], op=ALU.mult),
                     reads=[gB_key], writes=["pT%d" % q, out_key])

            for b in range(NB):
                sp_ = b % 2
                tiles = [4 * b + i for i in range(4)]
                for i, t in enumerate(tiles):
                    sl = t % NXS
                    issue_load(t)
                    s = xnc[0] % 2
                    P.op("act", lambda e, sl=sl, s=s, i=i: e.activation(out=xn[:, s, :], in_=xs[:, sl, :], func=AF.Square,
                                                                       accum_out=ss[:, sp_, i:i + 1]),
                         reads=["xs%d" % sl], writes=["xn%d" % s, "ss%d" % sp_])
                rstd_from_ss(ss[:, sp_, :], rstd[:, sp_, :], 4, ["ss%d" % sp_], "rstd%d" % sp_)
                for i, t in enumerate(tiles):
                    sl = t % NXS
                    norm_T(xs[:, sl, :], "xs%d" % sl, rstd[:, sp_, i:i + 1], "rstd%d" % sp_, gB, tag + "gB",
                           xnT[:, :, tsl(i)], "xnT")
                for j in range(FC):
                    q = j % 2
                    for k in range(KC):
                        P.op("pe", lambda e, k=k, j=j, q=q: e.matmul(pA[q][:], w1[:, k, tsl(j)], xnT[:, k, :],
                                                                    start=(k == 0), stop=(k == KC - 1)),
                             reads=["w1", "xnT"], writes=["pA%d" % q])
                    for k in range(KC):
                        P.op("pe", lambda e, k=k, j=j, q=q: e.matmul(pB[q][:], w1[:, k, FF + j * 128:FF + (j + 1) * 128],
                                                                    xnT[:, k, :], start=(k == 0), stop=(k == KC - 1)),
                             reads=["w1", "xnT"], writes=["pB%d" % q])
                    P.op("act", lambda e, q=q: e.activation(out=stmp[:, q, :], in_=pA[q][:], func=AF.Silu),
                         reads=[], writes=["pA%d" % q, "stmp%d" % q])
                    P.op("dve", lambda e, q=q, j=j: e.tensor_tensor(gT[:, j, :], stmp[:, q, :], pB[q][:], op=ALU.mult),
                         reads=["stmp%d" % q], writes=["pB%d" % q, "gT"])
                for t2_ in range(4 * b + 4, 4 * b + 4 + (NXS - 4)):
                    issue_load(t2_)
                oc = 0
                for i, t in enumerate(tiles):
                    sl = t % NXS
                    for c2 in range(2):
                        q = oc % 2
                        oc += 1
                        for j in range(FC):
                            P.op("pe", lambda e, j=j, i=i, c2=c2, q=q: e.matmul(
                                pO[q][:], gT[:, j, tsl(i)], w2[:, j, c2 * 512:(c2 + 1) * 512],
                                start=(j == 0), stop=(j == FC - 1)),
                                reads=["w2", "gT"], writes=["pO%d" % q])
                        P.op("dve", lambda e, q=q, sl=sl, c2=c2: e.scalar_tensor_tensor(
                            xs[:, sl, c2 * 512:(c2 + 1) * 512], pO[q][:], 0.5, xs[:, sl, c2 * 512:(c2 + 1) * 512],
                            op0=ALU.mult, op1=ALU.add),
                            reads=[], writes=["pO%d" % q, "xs%d" % sl])
                    P.dma("sp", dst_d[tsl(t), :], xs[:, sl, :], reads=["xs%d" % sl], writes=["hd%d" % t])
                    if post_gsel is not None:
                        s = xnc[0] % 2
                        P.op("act", lambda e, sl=sl, s=s, i=i: e.activation(out=xn[:, s, :], in_=xs[:, sl, :], func=AF.Square,
                                                                           accum_out=ss2[:, sp_, i:i + 1]),
                             reads=["xs%d" % sl], writes=["xn%d" % s, "ss2%d" % sp_])
                if post_gsel is not None:
                    rstd_from_ss(ss2[:, sp_, :], rstd2[:, sp_, :], 4, ["ss2%d" % sp_], "rstd2%d" % sp_)
                    for i, t in enumerate(tiles):
                        sl = t % NXS
                        u = t % 2
                        norm_T(xs[:, sl, :], "xs%d" % sl, rstd2[:, sp_, i:i + 1], "rstd2%d" % sp_, gB2, tag + "gB2",
                               ust[:, u, :, :], "ust%d" % u)
                        P.dma("sp", uT_s[:, :, tsl(t)].rearrange("c p t -> p c t"), ust[:, u, :, :],
                              reads=["ust%d" % u], writes=["uTd%d" % t])
            P.barrier()

    def p2_phase():
        with contextlib.ExitStack() as es:
            sb = lambda n, s, d=F32: es.enter_context(nc.sbuf_tensor("p2" + n, s, d))
            ps = lambda n, s, d=F32: es.enter_context(nc.psum_tensor("p2" + n, s, d))
            win = sb("win", [128, KC, 2760], BF16)
            for k in range(KC):
                P.dma("pool", win[:, k, :], win_d[tsl(k), 0:2760], writes=["win"])
            wp = sb("wp", [128, KC, 1280], BF16)
            wk2 = sb("wk2", [128, KC, 256], BF16)
            cf = sb("cf", [128, 130])
            P.dma("sp", cf[:], cf_d, writes=["cf"])
            posi = sb("posi", [128, NTOK], I32)
            P.dma("sp", posi[:], pos_d.broadcast_to([128, NTOK]), writes=["posi"])
            ang = sb("ang", [128, NTOK])
            kk = sb("kk", [128, NTOK])
            kki = sb("kki", [128, NTOK], I32)
            Ct = sb("Ct", [128, NTOK])
            St = sb("St", [128, NTOK])
            invf = cf[:, 128:129]
            sgn = cf[:, 129:130]
            P.op("dve", lambda e: e.tensor_copy(ang[:], posi[:]), reads=["posi"], writes=["ang"])
            P.op("dve", lambda e: e.tensor_scalar(ang[:], ang[:], invf, None, op0=ALU.mult), reads=["ang", "cf"], writes=["ang"])

            def reduce_to(dst, shift, key):
                P.op("dve", lambda e: e.tensor_scalar(kk[:], ang[:], shift, 1.0 / TWO_PI, op0=ALU.add, op1=ALU.mult),
                     reads=["ang"], writes=["kk"])
                P.op("dve", lambda e: e.tensor_copy(kki[:], kk[:]), reads=["kk"], writes=["kki"])
                P.op("dve", lambda e: e.tensor_copy(kk[:], kki[:]), reads=["kki"], writes=["kk"])
                P.op("dve", lambda e: e.scalar_tensor_tensor(dst[:], kk[:], -CW1, ang[:], op0=ALU.mult, op1=ALU.add),
                     reads=["kk", "ang"], writes=[key])
                P.op("dve", lambda e: e.scalar_tensor_tensor(dst[:], kk[:], -CW2, dst[:], op0=ALU.mult, op1=ALU.add),
                     reads=["kk"], writes=[key])
                P.op("dve", lambda e: e.tensor_scalar(dst[:], dst[:], shift, 3.1415925, op0=ALU.add, op1=ALU.min),
                     reads=[], writes=[key])
                P.op("dve", lambda e: e.tensor_scalar(dst[:], dst[:], -3.1415925, None, op0=ALU.max), reads=[], writes=[key])
                P.op("act", lambda e: e.activation(out=dst[:], in_=dst[:], func=AF.Sin), reads=[], writes=[key])

            reduce_to(St, 0.0, "St")
            P.op("dve", lambda e: e.tensor_scalar(St[:], St[:], sgn, None, op0=ALU.mult), reads=["cf"], writes=["St"])
            reduce_to(Ct, float(np.pi / 2), "Ct")
            P.op("pool", lambda e: e.tensor_scalar(win[:, :, 0:512], win[:, :, 0:512], 0.125, None, op0=ALU.mult),
                 reads=[], writes=["win"])
            P.op("pool", lambda e: e.tensor_scalar(win[:, :, OFF_QD:OFF_QD + 512], win[:, :, OFF_QD:OFF_QD + 512], 0.125, None,
                                                   op0=ALU.mult), reads=[], writes=["win"])
            P.op("pool", lambda e: e.memset(wp[:], 0.0), writes=["wp"])
            for hh in range(2):
                P.op("pool", lambda e, hh=hh: e.tensor_copy(wk2[:, :, hh * 64:(hh + 1) * 64], win[:, :, OFF_KD:OFF_KD + 64]),
                     reads=["win"], writes=["wk2"])
                P.op("pool", lambda e, hh=hh: e.tensor_copy(wk2[:, :, 128 + hh * 64:128 + (hh + 1) * 64],
                                                            win[:, :, OFF_KI:OFF_KI + 64]), reads=["win"], writes=["wk2"])
            pbase = [OFF_QD + 128 * j for j in range(4)] + [OFF_QI + 128 * j for j in range(4)]
            for pc in range(10):
                for hh in range(2):
                    if pc < 8:
                        b0 = pbase[pc] + 64 * hh
                    else:
                        b0 = OFF_KD if pc == 8 else OFF_KI
                    o = pc * 128 + 64 * hh
                    P.op("pool", lambda e, o=o, b0=b0: e.tensor_copy(wp[:, :, o:o + 8], win[:, :, b0 + 8:b0 + 16]),
                         reads=["win"], writes=["wp"])
                    P.op("pool", lambda e, o=o, b0=b0: e.tensor_copy(wp[:, :, o + 8:o + 16], win[:, :, b0:b0 + 8]),
                         reads=["win"], writes=["wp"])
            uT = sb("uT", [128, 2, KC, TB], BF16)
            fst = sb("fst", [128, 4, TB], BF16)
            t1 = sb("t1", [128, 2, TB])
            t2 = sb("t2", [128, 2, TB])
            vst = sb("vst", [128, 2, 576], BF16)
            wist = sb("wist", [128, 2, 8])
            pA = [ps("pA%d" % i, [128, TB]) for i in range(3)]
            pB = [ps("pB%d" % i, [128, TB]) for i in range(2)]
            pV = [ps("pV%d" % i, [128, 512]) for i in range(2)]
            pW = ps("pW", [128, 128])
            chunks = []
            for j in range(4):
                chunks.append((j, win, OFF_QSB + 128 * j, None))
            for j in range(4):
                chunks.append((4 + j, win, OFF_KSB + 128 * j, None))
            for j in range(4):
                chunks.append((8 + j, win, OFF_QD + 128 * j, j))
            for j in range(4):
                chunks.append((12 + j, win, OFF_QI + 128 * j, 4 + j))
            chunks.append((16, wk2, 0, 8))
            chunks.append((17, wk2, 128, 9))
            ca = cbn = fs = rc = vc = 0
            def load_uT(b):
                if b < NB:
                    P.dma("sp", uT[:, b % 2, :, :], uT_s[:, :, b * TB:(b + 1) * TB].rearrange("c p t -> p c t"),
                          reads=["uTd%d" % t for t in range(4 * b, 4 * b + 4)], writes=["uT%d" % (b % 2)])
            load_uT(0)
            for b in range(NB):
                u = b % 2
                load_uT(b + 1)
                tok = slice(b * TB, (b + 1) * TB)
                for (ci, wt, off, pidx) in chunks:
                    qa = ca % 3
                    ca += 1
                    for k in range(KC):
                        P.op("pe", lambda e, k=k, wt=wt, off=off, qa=qa: e.matmul(pA[qa][:], wt[:, k, off:off + 128], uT[:, u, k, :],
                                                                                 start=(k == 0), stop=(k == KC - 1)),
                             reads=["win", "wk2", "uT%d" % u], writes=["pA%d" % qa])
                    f = fs % 4
                    fs += 1
                    if pidx is None:
                        P.op("act", lambda e, qa=qa, f=f: e.copy(fst[:, f, :], pA[qa][:]), reads=[], writes=["pA%d" % qa, "fst%d" % f])
                    else:
                        qb = cbn % 2
                        cbn += 1
                        for k in range(KC):
                            P.op("pe", lambda e, k=k, pidx=pidx, qb=qb: e.matmul(pB[qb][:], wp[:, k, pidx * 128:(pidx + 1) * 128],
                                                                                uT[:, u, k, :], start=(k == 0), stop=(k == KC - 1)),
                                 reads=["wp", "uT%d" % u], writes=["pB%d" % qb])
                        r = rc % 2
                        rc += 1
                        P.op("dve", lambda e, qa=qa, r=r: e.tensor_tensor(t1[:, r, :], pA[qa][:], Ct[:, tok], op=ALU.mult),
                             reads=["Ct"], writes=["pA%d" % qa, "t1%d" % r])
                        P.op("dve", lambda e, qb=qb, r=r: e.tensor_tensor(t2[:, r, :], pB[qb][:], St[:, tok], op=ALU.mult),
                             reads=["St"], writes=["pB%d" % qb, "t2%d" % r])
                        P.op("pool", lambda e, r=r, f=f: e.tensor_tensor(fst[:, f, :], t1[:, r, :], t2[:, r, :], op=ALU.add),
                             reads=["t1%d" % r, "t2%d" % r], writes=["fst%d" % f])
                    P.dma("sp", qk_s[ci, :, tok], fst[:, f, :], reads=["fst%d" % f], writes=["qkd%d_%d" % (ci, b)])
                for i in range(4):
                    t = 4 * b + i
                    q = vc % 2
                    vc += 1
                    for k in range(KC):
                        P.op("pe", lambda e, k=k, i=i, q=q: e.matmul(pV[q][:], uT[:, u, k, tsl(i)], win[:, k, OFF_VSB:OFF_VSB + 512],
                                                                    start=(k == 0), stop=(k == KC - 1)),
                             reads=["win", "uT%d" % u], writes=["pV%d" % q])
                    for k in range(KC):
                        P.op("pe", lambda e, k=k, i=i: e.matmul(pW[:, 0:64], uT[:, u, k, tsl(i)], win[:, k, OFF_VD:OFF_VD + 64],
                                                               start=(k == 0), stop=(k == KC - 1), skip_group_check=True),
                             reads=["win", "uT%d" % u], writes=["pW"])
                    for k in range(KC):
                        P.op("pe", lambda e, k=k, i=i: e.matmul(pW[:, 64:72], uT[:, u, k, tsl(i)], win[:, k, OFF_WI:OFF_WI + 8],
                                                               start=False, stop=(k == KC - 1), skip_group_check=True),
                             reads=["win", "uT%d" % u], writes=["pW"])
                    P.op("act", lambda e, q=q: e.copy(vst[:, q, 0:512], pV[q][:]), reads=[], writes=["pV%d" % q, "vst%d" % q])
                    P.op("dve", lambda e, q=q: e.tensor_copy(vst[:, q, 512:576], pW[:, 0:64]), reads=[], writes=["pW", "vst%d" % q])
                    P.op("dve", lambda e, q=q: e.tensor_scalar(wist[:, q, :], pW[:, 64:72], float(8 ** -0.5 * 0.125), None, op0=ALU.mult),
                         reads=[], writes=["pW", "wist%d" % q])
                    P.dma("sp", v_s[tsl(t), :], vst[:, q, :], reads=["vst%d" % q], writes=["vd%d" % t])
                    P.dma("sp", wi_s[tsl(t), :], wist[:, q, :], reads=["wist%d" % q], writes=["wid%d" % t])
            P.barrier()

    def p3_phase():
        with contextlib.ExitStack() as es:
            sb = lambda n, s, d=F32: es.enter_context(nc.sbuf_tensor("p3" + n, s, d))
            cb = sb("cb", [128, 384 + 2048], BF16)
            P.dma("pool", cb[:], cb_d, writes=["cb"])
            cf = sb("cf", [128, 130])
            P.dma("sp", cf[:], cf_d, writes=["cf"])
            ident = cb[:, 0:128]
            nUincl = cb[:, 128:256]
            nLstr = cb[:, 256:384]
            sbmask = cb[:, 384:384 + 2048].rearrange("p (r t) -> p r t", r=4)
            dsaneg = cf[:, 0:128]
            qz = [sb("qz%d" % i, [128, 4, SEQ], BF16) for i in range(2)]
            P.op("pool", lambda e: e.memset(qz[0][64:128, :, :], 0.0), writes=["qz0z"])
            P.op("pool", lambda e: e.memset(qz[1][0:64, :, :], 0.0), writes=["qz1z"])
            ksb = sb("ksb", [128, 4, SEQ], BF16)
            nksb = sb("nksb", [128, 4, SEQ], BF16)
            qd = sb("qd", [128, 4, SEQ], BF16)
            qi = sb("qi", [128, 4, SEQ], BF16)
            kdz = sb("kdz", [128, 2, SEQ], BF16)
            kiz = sb("kiz", [128, 2, SEQ], BF16)
            for kz_ in (kdz, kiz):
                P.op("pool", lambda e: e.memset(kz_[64:128, 0, :], 0.0), writes=["kzz"])
                P.op("pool", lambda e: e.memset(kz_[0:64, 1, :], 0.0), writes=["kzz"])
            v = sb("v", [128, 16, 578], BF16)
            wi = sb("wi", [128, 16, 8])
            P.op("pool", lambda e: e.memset(v[:, :, 576:578], 0.0), writes=["vone"])
            P.op("pool", lambda e: e.memset(v[:, :, 576:577], 1.0), writes=["vone"])
            def load_sb(sq):
                tok = slice(sq * SEQ, (sq + 1) * SEQ)
                for j in range(4):
                    P.dma("sp", qz[0][0:64, j, :], qk_s[j, 0:64, tok], writes=["qsb"])
                    P.dma("sp", qz[1][64:128, j, :], qk_s[j, 64:128, tok], writes=["qsb"])
                for j in range(4):
                    P.dma("sp", ksb[:, j, :], qk_s[4 + j, :, tok], writes=["ksb"])
                P.op("pool", lambda e: e.tensor_scalar(nksb[:], ksb[:], -1.0, None, op0=ALU.mult), reads=["ksb"], writes=["nksb"])

            def load_rest(sq):
                tok = slice(sq * SEQ, (sq + 1) * SEQ)
                P.dma("act", v[:, :, 0:576], v_s[tok, :].rearrange("(n p) c -> p n c", p=128), writes=["v"])
                P.dma("act", wi[:], wi_s[tok, :].rearrange("(n p) c -> p n c", p=128), writes=["wi"])
                for (tile_, c0, key) in ((qd, 8, "qd"), (qi, 12, "qi")):
                    for j in range(4):
                        P.dma("sp", tile_[:, j, :], qk_s[c0 + j, :, tok], writes=[key])
                for (kz_, ci, key) in ((kdz, 16, "kd2"), (kiz, 17, "ki2")):
                    P.dma("sp", kz_[0:64, 0, :], qk_s[ci, 0:64, tok], writes=[key])
                    P.dma("sp", kz_[64:128, 1, :], qk_s[ci, 64:128, tok], writes=[key])

            load_sb(0)
            for sq in range(NSEQ):
                load_rest(sq)

                with contextlib.ExitStack() as es2:
                  if "nosb" not in phases:
                    sb2 = lambda n, s, d=F32: es2.enter_context(nc.sbuf_tensor("sb%d" % sq + n, s, d))
                    ps2 = lambda n, s, d=F32: es2.enter_context(nc.psum_tensor("sb%d" % sq + n, s, d))
                    NS = 4
                    E = sb2("E", [128, NS, 512])
                    SP = sb2("SP", [128, NS, 2, 512], BF16)
                    A = sb2("A", [128, NS, 2, 512], BF16)
                    yacc = sb2("yacc", [64, NS, 512])
                    yst = sb2("yst", [64, NS, 512], BF16)
                    pZ = [ps2("pZ%d" % i, [128, 512]) for i in range(2)]
                    pC = [ps2("pC%d" % i, [128, 512]) for i in range(NS)]
                    pY = [ps2("pY%d" % i, [64, 512]) for i in range(2)]
                    mask128 = sbmask[:, 0, 0:128]

                    def sb_stream(s_, h, qc):
                        j, half = h // 2, h % 2
                        po = slice(64 * half, 64 * half + 64)
                        zb = s_ % 2
                        S = "s%d" % s_
                        kmax = 4 * qc + 3
                        nstep = kmax + 1
                        P.op("pool", lambda e: e.memset(yacc[:, s_, :], 0.0), writes=["yacc" + S])
                        if s_ >= 2:
                            yield
                        for step in range(nstep):
                            kb = kmax - step
                            r = kb - 4 * qc
                            c0 = 128 * max(0, r)
                            cols = slice(c0, 512)
                            dcols = slice(c0, c0 + 128)
                            qcols = slice(qc * 512 + c0, (qc + 1) * 512)
                            kcols = tsl(kb)
                            par = step % 2
                            spk = "SP%s%d" % (S, par)
                            ak = "A%s%d" % (S, par)
                            P.op("pe", lambda e: e.matmul(pZ[zb][:, cols], ksb[:, j, kcols], qz[half][:, j, qcols], start=True, stop=True,
                                                          skip_group_check=True),
                                 reads=["ksb", "qsb", "qz0z", "qz1z"], writes=["pZ%d" % zb])
                            yield
                            P.op("act", lambda e: e.activation(out=E[:, s_, cols], in_=pZ[zb][:, cols], func=AF.Exp),
                                 reads=[], writes=["pZ%d" % zb, "E" + S])
                            P.op("act", lambda e: e.activation(out=SP[:, s_, par, cols], in_=E[:, s_, cols], func=AF.Ln, bias=1.0),
                                 reads=["E" + S], writes=[spk])
                            if r >= 0:
                                P.op("pool", lambda e: e.tensor_tensor(SP[:, s_, par, dcols], SP[:, s_, par, dcols], mask128, op=ALU.mult),
                                     reads=["cb", spk], writes=[spk])
                            yield
                            P.op("pe", lambda e: e.matmul(pC[s_][:, cols], ksb[:, j, kcols], qz[half][:, j, qcols], start=(step == 0), stop=False,
                                                          skip_group_check=True),
                                 reads=["ksb", "qsb"], writes=["pC" + S])
                            P.op("pe", lambda e: e.matmul(pC[s_][:, cols], nUincl, SP[:, s_, par, cols], start=False, stop=True,
                                                          skip_group_check=True),
                                 reads=["cb", spk], writes=["pC" + S])
                            yield
                            P.op("act", lambda e: e.activation(out=A[:, s_, par, cols], in_=pC[s_][:, cols], func=AF.Exp),
                                 reads=[], writes=["pC" + S, ak])
                            if r >= 0:
                                P.op("pool", lambda e: e.tensor_tensor(A[:, s_, par, dcols], A[:, s_, par, dcols], mask128, op=ALU.mult),
                                     reads=["cb", ak], writes=[ak])
                            yield
                            P.op("pe", lambda e: e.matmul(pY[zb][:, cols], v[:, kb, h * 64:(h + 1) * 64], A[:, s_, par, cols],
                                                          start=True, stop=True, skip_group_check=True),
                                 reads=["v", ak], writes=["pY%d" % zb])
                            if step < nstep - 1:
                                P.op("pe", lambda e: e.matmul(pC[s_][:, cols], nksb[:, j, kcols], qz[half][:, j, qcols], start=False, stop=False,
                                                              skip_group_check=True),
                                     reads=["nksb", "qsb"], writes=["pC" + S])
                                P.op("pe", lambda e: e.matmul(pC[s_][:, cols], nLstr, SP[:, s_, par, cols], start=False, stop=False,
                                                              skip_group_check=True),
                                     reads=["cb", spk], writes=["pC" + S])
                            P.op("dve", lambda e: e.tensor_tensor(yacc[:, s_, cols], yacc[:, s_, cols], pY[zb][:, cols], op=ALU.add),
                                 reads=["yacc" + S], writes=["pY%d" % zb, "yacc" + S])
                            yield
                        P.op("dve", lambda e: e.tensor_copy(yst[:, s_, :], yacc[:, s_, :]), reads=["yacc" + S], writes=["yst" + S])
                        P.dma("sp", ysbT_s[h // 2, 64 * (h % 2):64 * (h % 2) + 64, sq * SEQ + qc * 512: sq * SEQ + (qc + 1) * 512], yst[:, s_, :],
                              reads=["yst" + S], writes=["ysbd%d_%d" % (h, sq * 4 + qc)])

                    for qc in range(4):
                        for g in range(2):
                            P.run_threads([sb_stream(s_, 4 * g + s_, qc) for s_ in range(NS)])
                    P.barrier()
                    if sq + 1 < NSEQ:
                        load_sb(sq + 1)

                with contextlib.ExitStack() as es2:
                  if "nodsa" not in phases:
                    sb2 = lambda n, s, d=F32: es2.enter_context(nc.sbuf_tensor("ds%d" % sq + n, s, d))
                    ps2 = lambda n, s, d=F32: es2.enter_context(nc.psum_tensor("ds%d" % sq + n, s, d))
                    Sc = sb2("Sc", [128, 4, SEQ])
                    R = sb2("R", [128, 2, 512])
                    Mb = sb2("Mb", [128, 2, SEQ], BF16)
                    MT = sb2("MT", [128, 4, 16, 128], BF16)
                    PT = sb2("PT", [128, 3, 512], BF16)
                    yd = sb2("yd", [128, 512], BF16)
                    ydst = sb2("ydst", [128, 2, 4, 128], BF16)
                    sm = sb2("sm", [128, 16])
                    rec = sb2("rec", [128, 8, 1])
                    pD = [ps2("pD%d" % i, [128, 512]) for i in range(2)]
                    pM = ps2("pM", [128, 8, 128], BF16)
                    pL = [ps2("pL%d" % i, [128, 512]) for i in range(2)]
                    pYd = ps2("pYd", [128, 2, 512])
                    pM2 = ps2("pM2", [128, 8, 128], BF16)
                    cnts = dict(d=0, l=0)

                    def stage1(i):
                        nk = (i + 1) * 128
                        nch = (nk + 511) // 512
                        tcols = tsl(i)
                        z = i % 4
                        sck = "Sc%d" % z
                        for hh in range(8):
                            j, s_ = hh // 2, hh % 2
                            po = slice(64 * s_, 64 * s_ + 64)
                            for c in range(nch):
                                n = min(512, nk - c * 512)
                                q = cnts["d"] % 2
                                cnts["d"] += 1
                                cc = slice(c * 512, c * 512 + n)
                                P.op("pe", lambda e: e.matmul(pD[q][:, 0:n], qi[:, j, tcols], kiz[:, s_, cc], start=True, stop=True),
                                     reads=["qi", "ki2", "kzz"], writes=["pD%d" % q])
                                P.op("act", lambda e: e.activation(out=R[:, q, 0:n], in_=pD[q][:, 0:n], func=AF.Relu),
                                     reads=[], writes=["pD%d" % q, "R%d" % q])
                                if hh == 0:
                                    P.op("dve", lambda e: e.tensor_scalar(Sc[:, z, cc], R[:, q, 0:n], wi[:, i, 0:1], None, op0=ALU.mult),
                                         reads=["R%d" % q, "wi"], writes=[sck])
                                else:
                                    P.op("dve", lambda e: e.scalar_tensor_tensor(Sc[:, z, cc], R[:, q, 0:n], wi[:, i, hh:hh + 1], Sc[:, z, cc],
                                                                               op0=ALU.mult, op1=ALU.add),
                                         reads=["R%d" % q, "wi", sck], writes=[sck])
                                yield

                    def stage2(i):
                        nk = (i + 1) * 128
                        z = i % 4
                        w = i % 2
                        smk = "sm%d" % w
                        mbk = "Mb%d" % w
                        lo, hi, mid, cnt, dlt = (sm[:, 8 * w + c:8 * w + c + 1] for c in range(5))
                        sck = "Sc%d" % z
                        if i >= 2:
                            P.op("dve", lambda e: e.tensor_reduce(lo, Sc[:, z, 0:nk], axis=AX.X, op=ALU.min), reads=[sck], writes=[smk])
                        P.op("dve", lambda e: e.tensor_tensor(Sc[:, z, nk - 128:nk], Sc[:, z, nk - 128:nk], dsaneg, op=ALU.add),
                             reads=["cf", sck], writes=[sck])
                        yield
                        if i >= 2:
                            P.op("dve", lambda e: e.tensor_reduce(hi, Sc[:, z, 0:nk], axis=AX.X, op=ALU.max), reads=[sck], writes=[smk])
                            P.op("dve", lambda e: e.tensor_tensor(hi, hi, lo, op=ALU.subtract), reads=[smk], writes=[smk])
                            on_act = w in ACT_CHAINS
                            sg = -1.0 if on_act else 1.0
                            if on_act:
                                P.op("dve", lambda e: e.tensor_scalar(lo, lo, -1.0, None, op0=ALU.mult), reads=[smk], writes=[smk])
                            yield
                            thr = float(2 * TOPK - 1 - nk) if on_act else (float(TOPK) - 0.5)
                            for it in range(NBIS):
                                ck = float(0.5 ** (it + 1))
                                P.op("dve", lambda e: e.scalar_tensor_tensor(mid, hi, sg * ck, lo, op0=ALU.mult, op1=ALU.add),
                                     reads=[smk], writes=[smk + "m"])
                                yield
                                if on_act:
                                    P.op("act", lambda e: e.activation(out=Mb[:, w, 0:nk], in_=Sc[:, z, 0:nk], func=AF.Sign, bias=mid,
                                                                       accum_out=cnt),
                                         reads=[sck, smk + "m"], writes=[mbk, smk + "c"])
                                else:
                                    P.op("dve", lambda e: e.tensor_scalar(Mb[:, w, 0:nk], Sc[:, z, 0:nk], mid, 0.0, op0=ALU.is_gt, op1=ALU.add,
                                                                          accum_out=cnt), reads=[sck, smk + "m"], writes=[mbk, smk + "c"])
                                yield
                                P.op("dve", lambda e: e.tensor_scalar(dlt, cnt, thr, sg * ck, op0=ALU.is_gt, op1=ALU.mult),
                                     reads=[smk + "c"], writes=[smk + "d"])
                                P.op("dve", lambda e: e.scalar_tensor_tensor(lo, dlt, hi, lo, op0=ALU.mult, op1=ALU.add),
                                     reads=[smk + "d", smk], writes=[smk])
                                yield
                            if on_act:
                                P.op("dve", lambda e: e.tensor_scalar(lo, lo, -1.0, None, op0=ALU.mult), reads=[smk], writes=[smk])
                            P.op("dve", lambda e: e.tensor_scalar(Mb[:, w, 0:nk], Sc[:, z, 0:nk], lo, None, op0=ALU.is_gt),
                                 reads=[sck, smk], writes=[mbk])
                        else:
                            P.op("dve", lambda e: e.tensor_scalar(Mb[:, w, 0:nk], Sc[:, z, 0:nk], -1e29, None, op0=ALU.is_gt),
                                 reads=[sck], writes=[mbk])
                        yield
                        for g0 in range(0, i + 1, 8):
                            g1 = min(i + 1, g0 + 8)
                            for kb in range(g0, g1):
                                P.op("pe", lambda e: e.transpose(pM[:, kb - g0, :], Mb[:, w, tsl(kb)], ident),
                                     reads=[mbk, "cb"], writes=["pM"])
                            P.op("act", lambda e: e.activation(out=MT[:, z, g0:g1, :], in_=pM[:, 0:g1 - g0, :], func=AF.Identity,
                                                               scale=MASK_BIG, bias=-MASK_BIG),
                                 reads=[], writes=["pM", "MT%d" % z])
                            yield

                    def stage3(i):
                        tcols = tsl(i)
                        z = i % 4
                        def emit_pv(kb, s_, q3):
                            for j in range(4):
                                P.op("pe", lambda e: e.matmul(pYd[:, s_, j * 66:(j + 1) * 66], PT[:, q3, j * 128:(j + 1) * 128], v[:, kb, 512:578],
                                                              start=(kb == 0 and j == 0), stop=(kb == i), skip_group_check=True),
                                     reads=["PT%d" % q3, "v", "vone"], writes=["pYd%d" % s_])
                        pend = []
                        for kb in range(i + 1):
                            for s_ in range(2):
                                q = cnts["l"] % 2
                                q3 = cnts["l"] % 3
                                cnts["l"] += 1
                                po = slice(64 * s_, 64 * s_ + 64)
                                for j in range(4):
                                    P.op("pe", lambda e: e.matmul(pL[q][:, j * 128:(j + 1) * 128], kdz[:, s_, tsl(kb)], qd[:, j, tcols],
                                                                  start=True, stop=False, skip_group_check=True),
                                         reads=["kd2", "kzz", "qd"], writes=["pL%d" % q])
                                    P.op("pe", lambda e: e.matmul(pL[q][:, j * 128:(j + 1) * 128], ident, MT[:, z, kb, :],
                                                                  start=False, stop=True, skip_group_check=True),
                                         reads=["cb", "MT%d" % z], writes=["pL%d" % q])
                                P.op("act", lambda e: e.activation(out=PT[:, q3, :], in_=pL[q][:], func=AF.Exp),
                                     reads=[], writes=["pL%d" % q, "PT%d" % q3])
                                pend.append((kb, s_, q3))
                                if len(pend) > 2:
                                    emit_pv(*pend.pop(0))
                                yield
                        while pend:
                            emit_pv(*pend.pop(0))
                        for bank in range(2):
                            yv = pYd[:, bank, 0:264].rearrange("p (h c) -> p h c", h=4)
                            ydv = yd[:].rearrange("p (j s c) -> p j s c", j=4, s=2)[:, :, bank, :]
                            P.op("dve", lambda e: e.reciprocal(rec[:, bank * 4:(bank + 1) * 4, :], yv[:, :, 64:65]),
                                 reads=[], writes=["pYd%d" % bank, "rec%d" % bank])
                            P.op("dve", lambda e: e.tensor_tensor(ydv, yv[:, :, 0:64],
                                                                  rec[:, bank * 4:(bank + 1) * 4, :].to_broadcast([128, 4, 64]), op=ALU.mult),
                                 reads=["rec%d" % bank], writes=["pYd%d" % bank, "yd"])
                        yield
                        u = i % 2
                        for cchunk in range(4):
                            P.op("pe", lambda e: e.transpose(pM2[:, cchunk, :], yd[:, tsl(cchunk)], ident),
                                 reads=["yd", "cb"], writes=["pM2"])
                        P.op("act", lambda e: e.copy(ydst[:, u, :, :], pM2[:, 0:4, :]), reads=[], writes=["pM2", "ydst%d" % u])
                        P.dma("sp", ydT_s[:, :, sq * SEQ + i * 128: sq * SEQ + (i + 1) * 128].rearrange("c p t -> p c t"),
                              ydst[:, u, :, :], reads=["ydst%d" % u], writes=["ydd%d" % (sq * 16 + i)])

                    def seq(*gens):
                        for g in gens:
                            yield from g
                    pairs = [(2 * m + 1, 2 * m) for m in range(7, -1, -1)]
                    for tau in range(len(pairs) + 2):
                        th = []
                        if tau < len(pairs):
                            th.append(seq(stage1(pairs[tau][0]), stage1(pairs[tau][1])))
                        if 0 <= tau - 1 < len(pairs):
                            th.append(stage2(pairs[tau - 1][0]))
                            th.append(stage2(pairs[tau - 1][1]))
                        if 0 <= tau - 2 < len(pairs):
                            th.append(seq(stage3(pairs[tau - 2][0]), stage3(pairs[tau - 2][1])))
                        P.run_threads(th)
                    P.barrier()
            P.barrier()

    def p4a_phase():
        with contextlib.ExitStack() as es:
            sb = lambda n, s, d=F32: es.enter_context(nc.sbuf_tensor("p4a" + n, s, d))
            ps = lambda n, s, d=F32: es.enter_context(nc.psum_tensor("p4a" + n, s, d))
            wg = sb("wg", [128, KC, 2048], BF16)
            for k in range(KC):
                P.dma("pool", wg[:, k, :], win_d[tsl(k), OFF_GSB:OFF_GSB + 2048], writes=["wg"])
            wosb = sb("wosb", [128, 4, D], BF16)
            P.dma("pool", wosb[:], wosb_d.rearrange("(c p) n -> p c n", p=128), writes=["wosb"])
            wod = sb("wod", [128, 4, D], BF16)
            P.dma("pool", wod[:], wod_d.rearrange("(c p) n -> p c n", p=128), writes=["wod"])
            wo = sb("wo", [128, KC, D], BF16)
            P.dma("pool", wo[:], wo_d.rearrange("(c p) n -> p c n", p=128), writes=["wo"])
            uT = sb("uT", [128, 2, KC, TB], BF16)
            ysbT = sb("ysbT", [128, 2, 4, TB], BF16)
            ydT = sb("ydT", [128, 2, 4, TB], BF16)
            mT = sb("mT", [128, KC, TB], BF16)
            s1 = sb("s1", [128, 2, TB])
            s2 = sb("s2", [128, 2, TB])
            t1 = sb("t1", [128, 2, TB])
            t2 = sb("t2", [128, 2, TB])
            hs = sb("hs", [128, 4, D])
            pG = [ps("pG%d" % i, [128, TB]) for i in range(2)]
            pY = [ps("pY%d" % i, [128, TB]) for i in range(2)]
            pO = [ps("pO%d" % i, [128, 512]) for i in range(2)]
            oc = hc = 0
            def load_blk(b):
                if b >= NB:
                    return
                u = b % 2
                tok = slice(b * TB, (b + 1) * TB)
                P.dma("sp", uT[:, u, :, :], uT_s[:, :, tok].rearrange("c p t -> p c t"), writes=["uT%d" % u])
                P.dma("sp", ysbT[:, u, :, :], ysbT_s[:, :, tok].rearrange("c p t -> p c t"), writes=["ysbT%d" % u])
                P.dma("sp", ydT[:, u, :, :], ydT_s[:, :, tok].rearrange("c p t -> p c t"), writes=["ydT%d" % u])
            load_blk(0)
            for b in range(NB):
                u = b % 2
                tok = slice(b * TB, (b + 1) * TB)
                load_blk(b + 1)
                for i in range(4):
                    P.dma("sp", hs[:, i, :], h_s[tsl(4 * b + i), :], reads=["hd%d" % (4 * b + i)], writes=["hs%d" % i])
                for c in range(KC):
                    r = c % 2
                    for (g, pg, sdst, skey) in ((0, pG[0], s1, "s1"), (1, pG[1], s2, "s2")):
                        for k in range(KC):
                            P.op("pe", lambda e, k=k, g=g, pg=pg, c=c: e.matmul(
                                pg[:], wg[:, k, g * 1024 + c * 128: g * 1024 + (c + 1) * 128], uT[:, u, k, :],
                                start=(k == 0), stop=(k == KC - 1)),
                                reads=["wg", "uT%d" % u], writes=["pG%d" % g])
                        P.op("act", lambda e, pg=pg, sdst=sdst, r=r: e.activation(out=sdst[:, r, :], in_=pg[:], func=AF.Tanh, scale=0.5),
                             reads=[], writes=["pG%d" % g, "%s%d" % (skey, r)])
                    for hh in range(4):
                        P.op("pe", lambda e, hh=hh, c=c: e.matmul(pY[0][:], wosb[:, hh, tsl(c)], ysbT[:, u, hh, :],
                                                                   start=(hh == 0), stop=(hh == 3)),
                             reads=["wosb", "ysbT%d" % u], writes=["pY0"])
                    for k in range(4):
                        P.op("pe", lambda e, k=k, c=c: e.matmul(pY[1][:], wod[:, k, tsl(c)], ydT[:, u, k, :],
                                                                 start=(k == 0), stop=(k == 3)),
                             reads=["wod", "ydT%d" % u], writes=["pY1"])
                    P.op("dve", lambda e, r=r: e.scalar_tensor_tensor(t1[:, r, :], s1[:, r, :], 1.0, pY[0][:], op0=ALU.add, op1=ALU.mult),
                         reads=["s1%d" % r], writes=["pY0", "t1%d" % r])
                    P.op("dve", lambda e, r=r: e.scalar_tensor_tensor(t2[:, r, :], s2[:, r, :], 1.0, pY[1][:], op0=ALU.add, op1=ALU.mult),
                         reads=["s2%d" % r], writes=["pY1", "t2%d" % r])
                    P.op("pool", lambda e, r=r, c=c: e.tensor_tensor(mT[:, c, :], t1[:, r, :], t2[:, r, :], op=ALU.add),
                         reads=["t1%d" % r, "t2%d" % r], writes=["mT"])
                for i in range(4):
                    t = 4 * b + i
                    sl = i
                    for c2 in range(2):
                        q = oc % 2
                        oc += 1
                        for k in range(KC):
                            P.op("pe", lambda e, k=k, i=i, c2=c2, q=q: e.matmul(pO[q][:], mT[:, k, tsl(i)], wo[:, k, c2 * 512:(c2 + 1) * 512],
                                                                               start=(k == 0), stop=(k == KC - 1)),
                                 reads=["mT", "wo"], writes=["pO%d" % q])
                        P.op("dve", lambda e, q=q, sl=sl, c2=c2: e.scalar_tensor_tensor(
                            hs[:, sl, c2 * 512:(c2 + 1) * 512], pO[q][:], 0.5, hs[:, sl, c2 * 512:(c2 + 1) * 512],
                            op0=ALU.mult, op1=ALU.add), reads=[], writes=["pO%d" % q, "hs%d" % sl])
                    P.dma("sp", h_s[tsl(t), :], hs[:, sl, :], reads=["hs%d" % sl], writes=["hd%d" % t])
            P.barrier()

    def p4c_phase():
        with contextlib.ExitStack() as es:
            sb = lambda n, s, d=F32: es.enter_context(nc.sbuf_tensor("p4c" + n, s, d))
            ps = lambda n, s, d=F32: es.enter_context(nc.psum_tensor("p4c" + n, s, d))
            c = load_consts(es)
            ident = c["ident"]
            wpg = sb("wpg", [128, KC, D], BF16)
            P.dma("pool", wpg[:], wpg_d.rearrange("(c p) n -> p c n", p=128), writes=["wpg"])
            wpp = sb("wpp", [128, 2, D], BF16)
            P.dma("pool", wpp[:], wpp_d.rearrange("(c p) n -> p c n", p=128), writes=["wpp"])
            gB = make_gB(es, c, 3, "p4cgB")
            gfin = sb("gfin", [128, D])
            P.dma("sp", gfin[:], gfin_d.broadcast_to([128, D]), writes=["gfin"])
            hs = sb("hs", [128, 8, D])
            pt = sb("pt", [128, 8, 256], BF16)
            xn = sb("xn", [128, 2, D], BF16)
            hnT = sb("hnT", [128, 2, KC, 128], BF16)
            pTs = sb("pTs", [128, 2, 2, 128], BF16)
            gate = sb("gate", [128, 2, D])
            ot = sb("ot", [128, 2, D])
            ss = sb("ss", [128, 2, 4])
            rstd = sb("rstd", [128, 2, 4])
            ss2 = sb("ss2", [128, 2, 4])
            rstd2 = sb("rstd2", [128, 2, 4])
            pT = [ps("pT%d" % i, [128, KC, 128], BF16) for i in range(2)]
            pP = [ps("pP%d" % i, [128, KC, 128], BF16) for i in range(2)]
            pGt = [ps("pGt%d" % i, [128, 512]) for i in range(2)]
            pPp = [ps("pPp%d" % i, [128, 512]) for i in range(2)]
            NG = NT // 4

            def load_g(g):
                if g < NG:
                    for i in range(4):
                        t = 4 * g + i
                        sl = (g % 2) * 4 + i
                        P.dma("sp", hs[:, sl, :], h_s[tsl(t), :], reads=["hd%d" % t], writes=["hs%d" % sl])
                        P.dma("pool", pt[:, sl, :], p_d[tsl(t), :], writes=["pt%d" % sl])

            def tile_thread(g, i):
                sl = (g % 2) * 4 + i
                u = i % 2
                gp = g % 2
                P.op("dve", lambda e: e.tensor_scalar(xn[:, u, :], hs[:, sl, :], rstd[:, gp, i:i + 1], None, op0=ALU.mult),
                     reads=["hs%d" % sl, "rstd%d" % gp], writes=["xn%d" % u])
                yield
                for k in range(KC):
                    P.op("pe", lambda e: e.transpose(pT[u][:, k, :], xn[:, u, tsl(k)], ident[:]),
                         reads=["xn%d" % u, "c_ident"], writes=["pT%d" % u])
                for k in range(2):
                    P.op("pe", lambda e: e.transpose(pP[u][:, k, :], pt[:, sl, tsl(k)], ident[:]),
                         reads=["pt%d" % sl, "c_ident"], writes=["pP%d" % u])
                yield
                P.op("dve", lambda e: e.tensor_tensor(hnT[:, u, :, :], pT[u][:], gB[:], op=ALU.mult),
                     reads=["p4cgB"], writes=["pT%d" % u, "hnT%d" % u])
                P.op("act", lambda e: e.copy(pTs[:, u, :, :], pP[u][:, 0:2, :]), reads=[], writes=["pP%d" % u, "pTs%d" % u])
                yield
                for c2 in range(2):
                    cs = slice(c2 * 512, (c2 + 1) * 512)
                    for k in range(KC):
                        P.op("pe", lambda e: e.matmul(pGt[u][:], hnT[:, u, k, :], wpg[:, k, cs], start=(k == 0), stop=(k == KC - 1)),
                             reads=["hnT%d" % u, "wpg"], writes=["pGt%d" % u])
                    for k in range(2):
                        P.op("pe", lambda e: e.matmul(pPp[u][:], pTs[:, u, k, :], wpp[:, k, cs], start=(k == 0), stop=(k == 1)),
                             reads=["pTs%d" % u, "wpp"], writes=["pPp%d" % u])
                    yield
                    P.op("act", lambda e: e.activation(out=gate[:, u, cs], in_=pGt[u][:], func=AF.Tanh, scale=0.5),
                         reads=[], writes=["pGt%d" % u, "gate%d" % u])
                    yield
                    P.op("dve", lambda e: e.scalar_tensor_tensor(gate[:, u, cs], gate[:, u, cs], 1.0, pPp[u][:], op0=ALU.add, op1=ALU.mult),
                         reads=["gate%d" % u], writes=["pPp%d" % u, "gate%d" % u])
                    yield
                P.op("dve", lambda e: e.scalar_tensor_tensor(hs[:, sl, :], gate[:, u, :], 0.5, hs[:, sl, :], op0=ALU.mult, op1=ALU.add),
                     reads=["gate%d" % u, "hs%d" % sl], writes=["hs%d" % sl])
                yield

            def seq(*gens):
                for g_ in gens:
                    yield from g_

            load_g(0)
            for g in range(NG):
                gp = g % 2
                load_g(g + 1)
                for i in range(4):
                    sl = gp * 4 + i
                    P.op("act", lambda e: e.activation(out=xn[:, i % 2, :], in_=hs[:, sl, :], func=AF.Square, accum_out=ss[:, gp, i:i + 1]),
                         reads=["hs%d" % sl], writes=["xn%d" % (i % 2), "ss%d" % gp])
                rstd_from_ss(ss[:, gp, :], rstd[:, gp, :], 4, ["ss%d" % gp], "rstd%d" % gp)
                P.run_threads([seq(tile_thread(g, 0), tile_thread(g, 2)), seq(tile_thread(g, 1), tile_thread(g, 3))])
                for i in range(4):
                    sl = gp * 4 + i
                    P.op("act", lambda e: e.activation(out=xn[:, i % 2, :], in_=hs[:, sl, :], func=AF.Square, accum_out=ss2[:, gp, i:i + 1]),
                         reads=["hs%d" % sl], writes=["xn%d" % (i % 2), "ss2%d" % gp])
                rstd_from_ss(ss2[:, gp, :], rstd2[:, gp, :], 4, ["ss2%d" % gp], "rstd2%d" % gp)
                for i in range(4):
                    sl = gp * 4 + i
                    t = 4 * g + i
                    u = i % 2
                    P.op("dve", lambda e: e.scalar_tensor_tensor(ot[:, u, :], hs[:, sl, :], rstd2[:, gp, i:i + 1], gfin[:], op0=ALU.mult, op1=ALU.mult),
                         reads=["hs%d" % sl, "rstd2%d" % gp, "gfin"], writes=["ot%d" % u])
                    P.dma("sp", out_d[tsl(t), :], ot[:, u, :], reads=["ot%d" % u], writes=["outd%d" % t])
            P.barrier()

    if "p1" in phases:
        ffn_phase("f1", x_d, h_s, w1a_d, w2a_d, 0, 1)
    if "p2" in phases:
        p2_phase()
    if "p3" in phases:
        p3_phase()
    if "p4a" in phases:
        p4a_phase()
    if "p4b" in phases:
        ffn_phase("f2", h_s, h_s, w1b_d, w2b_d, 2, None)
    if "p4c" in phases:
        p4c_phase()
    P.emit()
    return nc


def host_consts():
    j = np.arange(128)
    ident = np.eye(128, dtype=np.float32)
    nUincl = -(j[:, None] >= j[None, :]).astype(np.float32)
    nLstr = -(j[:, None] < j[None, :]).astype(np.float32)
    t = np.arange(512)
    sbmask = np.stack([(j[:, None] + 128 * r < t[None, :]).astype(np.float32) for r in range(4)], 1)
    cb = np.concatenate([ident, nUincl, nLstr, sbmask.reshape(128, 2048)], 1).astype(np.float32)
    dsaneg = np.where(j[None, :] > j[:, None], -1e30, 0.0).astype(np.float32)
    inv_freq = (500000.0 ** (-np.arange(0, 16, 2, dtype=np.float32) / 16)).astype(np.float32)
    pm = j % 64
    invf = np.where(pm < 16, inv_freq[pm % 8], 0.0).astype(np.float32)
    sgn = np.where(pm < 8, -1.0, np.where(pm < 16, 1.0, 0.0)).astype(np.float32)
    cf = np.concatenate([dsaneg, invf[:, None], sgn[:, None]], 1).astype(np.float32)
    return cb, cf


def make_in_maps(inputs, ncores=8):
    f = lambda a: np.ascontiguousarray(np.asarray(a), dtype=np.float32)
    x = f(inputs["x"])
    p = f(inputs["p"])[0]
    pos = np.ascontiguousarray(np.asarray(inputs["positions"]), dtype=np.int32)
    cb, cf = host_consts()
    gl = lambda g: f(g).reshape(KC, 128).T
    gcols = np.ascontiguousarray(np.concatenate([gl(inputs["ffn1_norm"][0]), gl(inputs["mix_norm"][0]),
                                                 gl(inputs["ffn2_norm"][0]), gl(inputs["ple_norm"][0])], 1))
    shared = {
        "w1a": f(inputs["ffn1_w1"][0]), "w2a": f(inputs["ffn1_w2"][0]), "win": f(inputs["w_in"][0]),
        "wosb": f(inputs["w_out_sb"][0]), "wod": f(inputs["w_out_dsa"][0]), "wo": f(inputs["w_out"][0]),
        "w1b": f(inputs["ffn2_w1"][0]), "w2b": f(inputs["ffn2_w2"][0]), "wpg": f(inputs["ple_w_gate"][0]),
        "wpp": f(inputs["ple_w_proj"][0]), "gcols": gcols, "gfin": f(inputs["final_norm"]).reshape(1, D),
        "cb": cb, "cf": cf,
    }
    maps = []
    for c in range(ncores):
        m = dict(shared)
        m["x"] = np.ascontiguousarray(x[NSEQ * c:NSEQ * (c + 1)].reshape(NTOK, D))
        m["p"] = np.ascontiguousarray(p[NSEQ * c:NSEQ * (c + 1)].reshape(NTOK, 256))
        m["pos"] = np.ascontiguousarray(pos[NSEQ * c:NSEQ * (c + 1)].reshape(1, NTOK))
        maps.append(m)
    return maps


def kernel(**inputs):
    nc = bass.Bass("TRN2", target_bir_lowering=False)
    build(nc)
    maps = make_in_maps(inputs, 8)
    res = run_bass_kernel_spmd(nc, maps, core_ids=list(range(8)))
    out = np.stack([r["out"].reshape(NSEQ, SEQ, D) for r in res.results], 0).reshape(16, SEQ, D)
    return out.astype(np.float32)
```

```python
import contextlib
import numpy as np
import concourse.bass as bass
import concourse.mybir as mybir
from concourse.bass_utils import run_bass_kernel_spmd

F32 = mybir.dt.float32
BF16 = mybir.dt.bfloat16
I32 = mybir.dt.int32
AF = mybir.ActivationFunctionType
ALU = mybir.AluOpType
AX = mybir.AxisListType

D = 1024
KC = 8
SEQ = 2048
NSEQ = 2
NTOK = NSEQ * SEQ
NT = NTOK // 128
TB = 512
NB = NTOK // TB
FF = 2816
FC = FF // 128
DIN = 4808
EPS = 1e-6
TOPK = 256
NBIS = 14
MASK_BIG = 30000.0
ACT_CHAINS = (1,)
OFF_QSB, OFF_KSB, OFF_VSB, OFF_QD, OFF_KD, OFF_VD, OFF_QI, OFF_KI, OFF_WI, OFF_GSB, OFF_GD = (
    0, 512, 1024, 1536, 2048, 2112, 2176, 2688, 2752, 2760, 3784)
TWO_PI = 6.283185307179586
CW1 = 6.28125
CW2 = TWO_PI - CW1


class _Key:
    __slots__ = ("w", "rs")

    def __init__(self):
        self.w = None
        self.rs = []


class _Rec:
    def __init__(self):
        self.call = None

    def __getattr__(self, name):
        def f(*a, **k):
            self.call = (name, a, k)
            return self
        return f


def _freeze(fn):
    rec = _Rec()
    fn(rec)
    name, a, k = rec.call
    return lambda e: getattr(e, name)(*a, **k)


class Prog:
    ENG = ("pe", "act", "dve", "pool", "sp")

    def __init__(self, nc, same_engine_sync=True, ndma_sems=8):
        self.nc = nc
        self.es = contextlib.ExitStack()
        self.streams = {e: [] for e in self.ENG}
        self.cnt = {e: 0 for e in self.ENG}
        self.sems = {}
        for e in self.ENG:
            self.sems["p_" + e] = self.es.enter_context(nc.semaphore("prog_" + e))
        self.known = {e: {} for e in self.ENG}
        self.same = same_engine_sync
        self.ndma = ndma_sems
        self.dq = {}
        self.keys = {}

    def key(self, name):
        k = self.keys.get(name)
        if k is None:
            k = _Key()
            self.keys[name] = k
        return k

    def _deps(self, reads, writes):
        ev = []
        for r in reads:
            k = self.key(r)
            if k.w is not None:
                ev.append(k.w)
        for w in writes:
            k = self.key(w)
            if k.w is not None:
                ev.append(k.w)
            ev.extend(k.rs)
        return ev

    def _waits(self, eng, evs):
        need = {}
        kn = self.known[eng]
        for (sid, val) in evs:
            if sid == "p_" + eng and (eng == "pe" or not self.same):
                continue
            if kn.get(sid, 0) >= val:
                continue
            if need.get(sid, 0) < val:
                need[sid] = val
        for sid, val in need.items():
            kn[sid] = val
        return list(need.items())

    def _commit(self, reads, writes, event):
        for r in reads:
            self.key(r).rs.append(event)
        for w in writes:
            k = self.key(w)
            k.w = event
            k.rs = []

    def op(self, eng, fn, reads=(), writes=()):
        waits = self._waits(eng, self._deps(reads, writes))
        self.cnt[eng] += 1
        event = ("p_" + eng, self.cnt[eng])
        self.streams[eng].append((waits, _freeze(fn), ("p_" + eng, 1)))
        self._commit(reads, writes, event)
        return event

    def dma(self, q, out, in_, reads=(), writes=()):
        d = self.dq.get(q)
        if d is None:
            ids = [f"d_{q}_{j}" for j in range(self.ndma)]
            for s in ids:
                self.sems[s] = self.es.enter_context(self.nc.semaphore(s))
            d = dict(i=0, ids=ids, vals=[0] * self.ndma, last=[None] * self.ndma)
            self.dq[q] = d
        j = d["i"] % self.ndma
        d["i"] += 1
        evs = self._deps(reads, writes)
        if d["last"][j] is not None:
            evs.append(d["last"][j])
        waits = self._waits(q, evs)
        d["vals"][j] += 16
        event = (d["ids"][j], d["vals"][j])
        d["last"][j] = event
        fn = lambda e, out=out, in_=in_: e.dma_start(out=out, in_=in_)
        self.streams[q].append((waits, fn, (d["ids"][j], 16)))
        self._commit(reads, writes, event)
        return event

    def _all_events(self):
        ev = []
        for q, d in self.dq.items():
            for e in d["last"]:
                if e is not None:
                    ev.append(e)
        for e in self.ENG:
            if self.cnt[e] > 0:
                ev.append(("p_" + e, self.cnt[e]))
        return ev

    def run_threads(self, gens):
        live = list(gens)
        while live:
            nxt = []
            for g in live:
                try:
                    next(g)
                    nxt.append(g)
                except StopIteration:
                    pass
            live = nxt

    def barrier(self):
        ev = self._all_events()
        for e in self.ENG:
            w = self._waits(e, ev)
            if w:
                self.streams[e].append((w, None, None))
        self.keys = {}

    def emit(self):
        nc = self.nc
        fw = self._waits("sp", self._all_events())
        self.streams["sp"].append((fw, None, None))
        with nc.Block() as block:
            def run(engname, handle):
                for waits, fn, inc in self.streams[engname]:
                    for sid, val in waits:
                        handle.wait_ge(self.sems[sid], val)
                    if fn is not None:
                        fn(handle).then_inc(self.sems[inc[0]], inc[1])

            @block.tensor
            def _(e):
                run("pe", e)

            @block.scalar
            def _(e):
                run("act", e)

            @block.vector
            def _(e):
                run("dve", e)

            @block.gpsimd
            def _(e):
                run("pool", e)

            @block.sync
            def _(e):
                run("sp", e)
        self.es.close()


def build(nc, dbg=False, phases=("p1", "p2", "p3", "p4a", "p4b", "p4c")):
    P = Prog(nc)

    def din(name, shape, dt=F32):
        return nc.dram_tensor(name, shape, dt, kind="ExternalInput").ap()

    def dscr(name, shape, dt):
        return nc.dram_tensor(name, shape, dt, kind=("ExternalOutput" if dbg else "Internal")).ap()

    x_d = din("x", [NTOK, D])
    p_d = din("p", [NTOK, 256])
    pos_d = din("pos", [1, NTOK], I32)
    w1a_d = din("w1a", [D, 2 * FF])
    w2a_d = din("w2a", [FF, D])
    win_d = din("win", [D, DIN])
    wosb_d = din("wosb", [512, D])
    wod_d = din("wod", [512, D])
    wo_d = din("wo", [D, D])
    w1b_d = din("w1b", [D, 2 * FF])
    w2b_d = din("w2b", [FF, D])
    wpg_d = din("wpg", [D, D])
    wpp_d = din("wpp", [256, D])
    gcols_d = din("gcols", [128, 4 * KC])
    gfin_d = din("gfin", [1, D])
    cb_d = din("cb", [128, 384 + 2048])
    cf_d = din("cf", [128, 130])
    out_d = nc.dram_tensor("out", [NTOK, D], F32, kind="ExternalOutput").ap()

    h_s = dscr("h_s", [NTOK, D], F32)
    uT_s = dscr("uT_s", [KC, 128, NTOK], BF16)
    qk_s = dscr("qk_s", [18, 128, NTOK], BF16)
    v_s = dscr("v_s", [NTOK, 576], BF16)
    wi_s = dscr("wi_s", [NTOK, 8], F32)
    ysbT_s = dscr("ysbT_s", [4, 128, NTOK], BF16)
    ydT_s = dscr("ydT_s", [4, 128, NTOK], BF16)

    def tsl(t):
        return slice(t * 128, (t + 1) * 128)

    ccount = [0]

    def load_consts(es, need_cb=True):
        ccount[0] += 1
        sb = lambda n, s, d=F32: es.enter_context(nc.sbuf_tensor("%s_%d" % (n, ccount[0]), s, d))
        c = {}
        c["gcols"] = sb("c_gcols", [128, 4 * KC])
        P.dma("sp", c["gcols"][:], gcols_d, writes=["c_gcols"])
        c["ident"] = sb("c_ident", [128, 128], BF16)
        P.dma("pool", c["ident"][:], cb_d[:, 0:128], writes=["c_ident"])
        return c

    def make_gB(es, c, which, name):
        gB = es.enter_context(nc.sbuf_tensor(name, [128, KC, 128], F32))
        src = c["gcols"][:, which * KC:(which + 1) * KC]
        P.op("dve", lambda e: e.tensor_copy(gB[:], src.unsqueeze(2).to_broadcast([128, KC, 128])),
             reads=["c_gcols"], writes=[name])
        return gB

    def rstd_from_ss(ss, rstd, n, rkeys, wkey):
        P.op("dve", lambda e: e.tensor_scalar(ss[:, 0:n], ss[:, 0:n], 1.0 / D, EPS, op0=ALU.mult, op1=ALU.add),
             reads=rkeys, writes=rkeys)
        P.op("act", lambda e: e.activation(out=ss[:, 0:n], in_=ss[:, 0:n], func=AF.Sqrt), reads=rkeys, writes=rkeys)
        P.op("dve", lambda e: e.reciprocal(rstd[:, 0:n], ss[:, 0:n]), reads=rkeys, writes=[wkey])

    def ffn_phase(tag, src_d, dst_d, w1_d, w2_d, gsel, post_gsel):
        with contextlib.ExitStack() as es:
            sb = lambda n, s, d=F32: es.enter_context(nc.sbuf_tensor(tag + n, s, d))
            ps = lambda n, s, d=F32: es.enter_context(nc.psum_tensor(tag + n, s, d))
            c = load_consts(es)
            w1 = sb("w1", [128, KC, 2 * FF], BF16)
            w2 = sb("w2", [128, FC, D], BF16)
            for k in range(KC):
                P.dma("pool", w1[:, k, :], w1_d[tsl(k), :], writes=["w1"])
            for k in range(FC):
                P.dma("pool", w2[:, k, :], w2_d[tsl(k), :], writes=["w2"])
            gB = make_gB(es, c, gsel, tag + "gB")
            gB2 = make_gB(es, c, post_gsel, tag + "gB2") if post_gsel is not None else None
            NXS = 6 if post_gsel is not None else 8
            xs = sb("xs", [128, NXS, D])
            issued = set()

            def issue_load(t):
                if t in issued or t >= NT:
                    return
                issued.add(t)
                P.dma("sp", xs[:, t % NXS, :], src_d[tsl(t), :], reads=["hd%d" % t], writes=["xs%d" % (t % NXS)])
            xn = sb("xn", [128, 2, D], BF16)
            xnT = sb("xnT", [128, KC, TB], BF16)
            gT = sb("gT", [128, FC, TB], BF16)
            stmp = sb("stmp", [128, 2, TB])
            ss = sb("ss", [128, 2, 4])
            rstd = sb("rstd", [128, 2, 4])
            ss2 = sb("ss2", [128, 2, 4])
            rstd2 = sb("rstd2", [128, 2, 4])
            ust = sb("ust", [128, 2, KC, 128], BF16) if post_gsel is not None else None
            pT = [ps("pT%d" % i, [128, KC, 128], BF16) for i in range(2)]
            pA = [ps("pA%d" % i, [128, TB]) for i in range(2)]
            pB = [ps("pB%d" % i, [128, TB]) for i in range(2)]
            pO = [ps("pO%d" % i, [128, 512]) for i in range(2)]
            ident = c["ident"]
            xnc = [0]
            ptc = [0]

            def norm_T(tile_ap, xs_key, rstd_col, rstd_key, gBt, gB_key, out_ap, out_key, dve_out=True):
                s = xnc[0] % 2
                xnc[0] += 1
                q = ptc[0] % 2
                ptc[0] += 1
                P.op("dve", lambda e: e.tensor_scalar(xn[:, s, :], tile_ap, rstd_col, None, op0=ALU.mult),
                     reads=[xs_key, rstd_key], writes=["xn%d" % s])
                for k in range(KC):
                    P.op("pe", lambda e, k=k: e.transpose(pT[q][:, k, :], xn[:, s, tsl(k)], ident[:]),
                         reads=["xn%d" % s, "c_ident"], writes=["pT%d" % q])
                P.op("dve", lambda e: e.tensor_tensor(out_ap, pT[q][:], gBt[:], op=ALU.mult),
                     reads=[gB_key], writes=["pT%d" % q, out_key])

            for b in range(NB):
                sp_ = b % 2
                tiles = [4 * b + i for i in range(4)]
                for i, t in enumerate(tiles):
                    sl = t % NXS
                    issue_load(t)
                    s = xnc[0] % 2
                    P.op("act", lambda e, sl=sl, s=s, i=i: e.activation(out=xn[:, s, :], in_=xs[:, sl, :], func=AF.Square,
                                                                       accum_out=ss[:, sp_, i:i + 1]),
                         reads=["xs%d" % sl], writes=["xn%d" % s, "ss%d" % sp_])
                rstd_from_ss(ss[:, sp_, :], rstd[:, sp_, :], 4, ["ss%d" % sp_], "rstd%d" % sp_)
                for i, t in enumerate(tiles):
                    sl = t % NXS
                    norm_T(xs[:, sl, :], "xs%d" % sl, rstd[:, sp_, i:i + 1], "rstd%d" % sp_, gB, tag + "gB",
                           xnT[:, :, tsl(i)], "xnT")
                for j in range(FC):
                    q = j % 2
                    for k in range(KC):
                        P.op("pe", lambda e, k=k, j=j, q=q: e.matmul(pA[q][:], w1[:, k, tsl(j)], xnT[:, k, :],
                                                                    start=(k == 0), stop=(k == KC - 1)),
                             reads=["w1", "xnT"], writes=["pA%d" % q])
                    for k in range(KC):
                        P.op("pe", lambda e, k=k, j=j, q=q: e.matmul(pB[q][:], w1[:, k, FF + j * 128:FF + (j + 1) * 128],
                                                                    xnT[:, k, :], start=(k == 0), stop=(k == KC - 1)),
                             reads=["w1", "xnT"], writes=["pB%d" % q])
                    P.op("act", lambda e, q=q: e.activation(out=stmp[:, q, :], in_=pA[q][:], func=AF.Silu),
                         reads=[], writes=["pA%d" % q, "stmp%d" % q])
                    P.op("dve", lambda e, q=q, j=j: e.tensor_tensor(gT[:, j, :], stmp[:, q, :], pB[q][:], op=ALU.mult),
                         reads=["stmp%d" % q], writes=["pB%d" % q, "gT"])
                for t2_ in range(4 * b + 4, 4 * b + 4 + (NXS - 4)):
                    issue_load(t2_)
                oc = 0
                for i, t in enumerate(tiles):
                    sl = t % NXS
                    for c2 in range(2):
                        q = oc % 2
                        oc += 1
                        for j in range(FC):
                            P.op("pe", lambda e, j=j, i=i, c2=c2, q=q: e.matmul(
                                pO[q][:], gT[:, j, tsl(i)], w2[:, j, c2 * 512:(c2 + 1) * 512],
                                start=(j == 0), stop=(j == FC - 1)),
                                reads=["w2", "gT"], writes=["pO%d" % q])
                        P.op("dve", lambda e, q=q, sl=sl, c2=c2: e.scalar_tensor_tensor(
                            xs[:, sl, c2 * 512:(c2 + 1) * 512], pO[q][:], 0.5, xs[:, sl, c2 * 512:(c2 + 1) * 512],
                            op0=ALU.mult, op1=ALU.add),
                            reads=[], writes=["pO%d" % q, "xs%d" % sl])
                    P.dma("sp", dst_d[tsl(t), :], xs[:, sl, :], reads=["xs%d" % sl], writes=["hd%d" % t])
                    if post_gsel is not None:
                        s = xnc[0] % 2
                        P.op("act", lambda e, sl=sl, s=s, i=i: e.activation(out=xn[:, s, :], in_=xs[:, sl, :], func=AF.Square,
                                                                           accum_out=ss2[:, sp_, i:i + 1]),
                             reads=["xs%d" % sl], writes=["xn%d" % s, "ss2%d" % sp_])
                if post_gsel is not None:
                    rstd_from_ss(ss2[:, sp_, :], rstd2[:, sp_, :], 4, ["ss2%d" % sp_], "rstd2%d" % sp_)
                    for i, t in enumerate(tiles):
                        sl = t % NXS
                        u = t % 2
                        norm_T(xs[:, sl, :], "xs%d" % sl, rstd2[:, sp_, i:i + 1], "rstd2%d" % sp_, gB2, tag + "gB2",
                               ust[:, u, :, :], "ust%d" % u)
                        P.dma("sp", uT_s[:, :, tsl(t)].rearrange("c p t -> p c t"), ust[:, u, :, :],
                              reads=["ust%d" % u], writes=["uTd%d" % t])
            P.barrier()

    def p2_phase():
        with contextlib.ExitStack() as es:
            sb = lambda n, s, d=F32: es.enter_context(nc.sbuf_tensor("p2" + n, s, d))
            ps = lambda n, s, d=F32: es.enter_context(nc.psum_tensor("p2" + n, s, d))
            win = sb("win", [128, KC, 2760], BF16)
            for k in range(KC):
                P.dma("pool", win[:, k, :], win_d[tsl(k), 0:2760], writes=["win"])
            wp = sb("wp", [128, KC, 1280], BF16)
            wk2 = sb("wk2", [128, KC, 256], BF16)
            cf = sb("cf", [128, 130])
            P.dma("sp", cf[:], cf_d, writes=["cf"])
            posi = sb("posi", [128, NTOK], I32)
            P.dma("sp", posi[:], pos_d.broadcast_to([128, NTOK]), writes=["posi"])
            ang = sb("ang", [128, NTOK])
            kk = sb("kk", [128, NTOK])
            kki = sb("kki", [128, NTOK], I32)
            Ct = sb("Ct", [128, NTOK])
            St = sb("St", [128, NTOK])
            invf = cf[:, 128:129]
            sgn = cf[:, 129:130]
            P.op("dve", lambda e: e.tensor_copy(ang[:], posi[:]), reads=["posi"], writes=["ang"])
            P.op("dve", lambda e: e.tensor_scalar(ang[:], ang[:], invf, None, op0=ALU.mult), reads=["ang", "cf"], writes=["ang"])

            def reduce_to(dst, shift, key):
                P.op("dve", lambda e: e.tensor_scalar(kk[:], ang[:], shift, 1.0 / TWO_PI, op0=ALU.add, op1=ALU.mult),
                     reads=["ang"], writes=["kk"])
                P.op("dve", lambda e: e.tensor_copy(kki[:], kk[:]), reads=["kk"], writes=["kki"])
                P.op("dve", lambda e: e.tensor_copy(kk[:], kki[:]), reads=["kki"], writes=["kk"])
                P.op("dve", lambda e: e.scalar_tensor_tensor(dst[:], kk[:], -CW1, ang[:], op0=ALU.mult, op1=ALU.add),
                     reads=["kk", "ang"], writes=[key])
                P.op("dve", lambda e: e.scalar_tensor_tensor(dst[:], kk[:], -CW2, dst[:], op0=ALU.mult, op1=ALU.add),
                     reads=["kk"], writes=[key])
                P.op("dve", lambda e: e.tensor_scalar(dst[:], dst[:], shift, 3.1415925, op0=ALU.add, op1=ALU.min),
                     reads=[], writes=[key])
                P.op("dve", lambda e: e.tensor_scalar(dst[:], dst[:], -3.1415925, None, op0=ALU.max), reads=[], writes=[key])
                P.op("act", lambda e: e.activation(out=dst[:], in_=dst[:], func=AF.Sin), reads=[], writes=[key])

            reduce_to(St, 0.0, "St")
            P.op("dve", lambda e: e.tensor_scalar(St[:], St[:], sgn, None, op0=ALU.mult), reads=["cf"], writes=["St"])
            reduce_to(Ct, float(np.pi / 2), "Ct")
            P.op("pool", lambda e: e.memset(wp[:], 0.0), writes=["wp"])
            for hh in range(2):
                P.op("pool", lambda e, hh=hh: e.tensor_copy(wk2[:, :, hh * 64:(hh + 1) * 64], win[:, :, OFF_KD:OFF_KD + 64]),
                     reads=["win"], writes=["wk2"])
                P.op("pool", lambda e, hh=hh: e.tensor_copy(wk2[:, :, 128 + hh * 64:128 + (hh + 1) * 64],
                                                            win[:, :, OFF_KI:OFF_KI + 64]), reads=["win"], writes=["wk2"])
            pbase = [OFF_QD + 128 * j for j in range(4)] + [OFF_QI + 128 * j for j in range(4)]
            for pc in range(10):
                for hh in range(2):
                    if pc < 8:
                        b0 = pbase[pc] + 64 * hh
                    else:
                        b0 = OFF_KD if pc == 8 else OFF_KI
                    o = pc * 128 + 64 * hh
                    P.op("pool", lambda e, o=o, b0=b0: e.tensor_copy(wp[:, :, o:o + 8], win[:, :, b0 + 8:b0 + 16]),
                         reads=["win"], writes=["wp"])
                    P.op("pool", lambda e, o=o, b0=b0: e.tensor_copy(wp[:, :, o + 8:o + 16], win[:, :, b0:b0 + 8]),
                         reads=["win"], writes=["wp"])
            uT = sb("uT", [128, 2, KC, TB], BF16)
            fst = sb("fst", [128, 4, TB], BF16)
            t1 = sb("t1", [128, 2, TB])
            t2 = sb("t2", [128, 2, TB])
            vst = sb("vst", [128, 2, 576], BF16)
            wist = sb("wist", [128, 2, 8])
            pA = [ps("pA%d" % i, [128, TB]) for i in range(3)]
            pB = [ps("pB%d" % i, [128, TB]) for i in range(2)]
            pV = [ps("pV%d" % i, [128, 512]) for i in range(2)]
            pW = ps("pW", [128, 128])
            chunks = []
            for j in range(4):
                chunks.append((j, win, OFF_QSB + 128 * j, None))
            for j in range(4):
                chunks.append((4 + j, win, OFF_KSB + 128 * j, None))
            for j in range(4):
                chunks.append((8 + j, win, OFF_QD + 128 * j, j))
            for j in range(4):
                chunks.append((12 + j, win, OFF_QI + 128 * j, 4 + j))
            chunks.append((16, wk2, 0, 8))
            chunks.append((17, wk2, 128, 9))
            ca = cbn = fs = rc = vc = 0
            def load_uT(b):
                if b < NB:
                    P.dma("sp", uT[:, b % 2, :, :], uT_s[:, :, b * TB:(b + 1) * TB].rearrange("c p t -> p c t"),
                          reads=["uTd%d" % t for t in range(4 * b, 4 * b + 4)], writes=["uT%d" % (b % 2)])
            load_uT(0)
            for b in range(NB):
                u = b % 2
                load_uT(b + 1)
                tok = slice(b * TB, (b + 1) * TB)
                for (ci, wt, off, pidx) in chunks:
                    qa = ca % 3
                    ca += 1
                    for k in range(KC):
                        P.op("pe", lambda e, k=k, wt=wt, off=off, qa=qa: e.matmul(pA[qa][:], wt[:, k, off:off + 128], uT[:, u, k, :],
                                                                                 start=(k == 0), stop=(k == KC - 1)),
                             reads=["win", "wk2", "uT%d" % u], writes=["pA%d" % qa])
                    f = fs % 4
                    fs += 1
                    qscale = 0.125 if (ci < 4 or 8 <= ci < 12) else 1.0
                    if pidx is None:
                        P.op("act", lambda e, qa=qa, f=f: e.activation(out=fst[:, f, :], in_=pA[qa][:], func=AF.Identity, scale=qscale),
                             reads=[], writes=["pA%d" % qa, "fst%d" % f])
                    else:
                        qb = cbn % 2
                        cbn += 1
                        for k in range(KC):
                            P.op("pe", lambda e, k=k, pidx=pidx, qb=qb: e.matmul(pB[qb][:], wp[:, k, pidx * 128:(pidx + 1) * 128],
                                                                                uT[:, u, k, :], start=(k == 0), stop=(k == KC - 1)),
                                 reads=["wp", "uT%d" % u], writes=["pB%d" % qb])
                        r = rc % 2
                        rc += 1
                        P.op("dve", lambda e, qa=qa, r=r: e.scalar_tensor_tensor(t1[:, r, :], pA[qa][:], qscale, Ct[:, tok], op0=ALU.mult, op1=ALU.mult),
                             reads=["Ct"], writes=["pA%d" % qa, "t1%d" % r])
                        P.op("dve", lambda e, qb=qb, r=r: e.scalar_tensor_tensor(t2[:, r, :], pB[qb][:], qscale, St[:, tok], op0=ALU.mult, op1=ALU.mult),
                             reads=["St"], writes=["pB%d" % qb, "t2%d" % r])
                        P.op("pool", lambda e, r=r, f=f: e.tensor_tensor(fst[:, f, :], t1[:, r, :], t2[:, r, :], op=ALU.add),
                             reads=["t1%d" % r, "t2%d" % r], writes=["fst%d" % f])
                    P.dma("sp", qk_s[ci, :, tok], fst[:, f, :], reads=["fst%d" % f], writes=["qkd%d_%d" % (ci, b)])
                for i in range(4):
                    t = 4 * b + i
                    q = vc % 2
                    vc += 1
                    for k in range(KC):
                        P.op("pe", lambda e, k=k, i=i, q=q: e.matmul(pV[q][:], uT[:, u, k, tsl(i)], win[:, k, OFF_VSB:OFF_VSB + 512],
                                                                    start=(k == 0), stop=(k == KC - 1)),
                             reads=["win", "uT%d" % u], writes=["pV%d" % q])
                    for k in range(KC):
                        P.op("pe", lambda e, k=k, i=i: e.matmul(pW[:, 0:64], uT[:, u, k, tsl(i)], win[:, k, OFF_VD:OFF_VD + 64],
                                                               start=(k == 0), stop=(k == KC - 1), skip_group_check=True),
                             reads=["win", "uT%d" % u], writes=["pW"])
                    for k in range(KC):
                        P.op("pe", lambda e, k=k, i=i: e.matmul(pW[:, 64:72], uT[:, u, k, tsl(i)], win[:, k, OFF_WI:OFF_WI + 8],
                                                               start=False, stop=(k == KC - 1), skip_group_check=True),
                             reads=["win", "uT%d" % u], writes=["pW"])
                    P.op("act", lambda e, q=q: e.copy(vst[:, q, 0:512], pV[q][:]), reads=[], writes=["pV%d" % q, "vst%d" % q])
                    P.op("dve", lambda e, q=q: e.tensor_copy(vst[:, q, 512:576], pW[:, 0:64]), reads=[], writes=["pW", "vst%d" % q])
                    P.op("dve", lambda e, q=q: e.tensor_scalar(wist[:, q, :], pW[:, 64:72], float(8 ** -0.5 * 0.125), None, op0=ALU.mult),
                         reads=[], writes=["pW", "wist%d" % q])
                    P.dma("sp", v_s[tsl(t), :], vst[:, q, :], reads=["vst%d" % q], writes=["vd%d" % t])
                    P.dma("sp", wi_s[tsl(t), :], wist[:, q, :], reads=["wist%d" % q], writes=["wid%d" % t])
            P.barrier()

    def p3_phase():
        with contextlib.ExitStack() as es:
            sb = lambda n, s, d=F32: es.enter_context(nc.sbuf_tensor("p3" + n, s, d))
            cb = sb("cb", [128, 384 + 2048], BF16)
            P.dma("pool", cb[:], cb_d, writes=["cb"])
            cf = sb("cf", [128, 130])
            P.dma("sp", cf[:], cf_d, writes=["cf"])
            ident = cb[:, 0:128]
            nUincl = cb[:, 128:256]
            nLstr = cb[:, 256:384]
            sbmask = cb[:, 384:384 + 2048].rearrange("p (r t) -> p r t", r=4)
            dsaneg = cf[:, 0:128]
            qz = [sb("qz%d" % i, [128, 4, SEQ], BF16) for i in range(2)]
            P.op("pool", lambda e: e.memset(qz[0][64:128, :, :], 0.0), writes=["qz0z"])
            P.op("pool", lambda e: e.memset(qz[1][0:64, :, :], 0.0), writes=["qz1z"])
            ksb = sb("ksb", [128, 4, SEQ], BF16)
            nksb = sb("nksb", [128, 4, SEQ], BF16)
            qd = sb("qd", [128, 4, SEQ], BF16)
            qi = sb("qi", [128, 4, SEQ], BF16)
            kdz = sb("kdz", [128, 2, SEQ], BF16)
            kiz = sb("kiz", [128, 2, SEQ], BF16)
            for kz_ in (kdz, kiz):
                P.op("pool", lambda e: e.memset(kz_[64:128, 0, :], 0.0), writes=["kzz"])
                P.op("pool", lambda e: e.memset(kz_[0:64, 1, :], 0.0), writes=["kzz"])
            v = sb("v", [128, 16, 578], BF16)
            wi = sb("wi", [128, 16, 8])
            P.op("pool", lambda e: e.memset(v[:, :, 576:578], 0.0), writes=["vone"])
            P.op("pool", lambda e: e.memset(v[:, :, 576:577], 1.0), writes=["vone"])
            def load_sb(sq):
                tok = slice(sq * SEQ, (sq + 1) * SEQ)
                for j in range(4):
                    P.dma("sp", qz[0][0:64, j, :], qk_s[j, 0:64, tok], writes=["qsb"])
                    P.dma("sp", qz[1][64:128, j, :], qk_s[j, 64:128, tok], writes=["qsb"])
                for j in range(4):
                    P.dma("sp", ksb[:, j, :], qk_s[4 + j, :, tok], writes=["ksb"])
                for j in range(4):
                    P.op("dve", lambda e: e.tensor_scalar(nksb[:, j, :], ksb[:, j, :], -1.0, None, op0=ALU.mult), reads=["ksb"], writes=["nksb"])

            def load_rest(sq):
                tok = slice(sq * SEQ, (sq + 1) * SEQ)
                P.dma("act", v[:, :, 0:576], v_s[tok, :].rearrange("(n p) c -> p n c", p=128), writes=["v"])
                P.dma("act", wi[:], wi_s[tok, :].rearrange("(n p) c -> p n c", p=128), writes=["wi"])
                for (tile_, c0, key) in ((qd, 8, "qd"), (qi, 12, "qi")):
                    for j in range(4):
                        P.dma("sp", tile_[:, j, :], qk_s[c0 + j, :, tok], writes=[key])
                for (kz_, ci, key) in ((kdz, 16, "kd2"), (kiz, 17, "ki2")):
                    P.dma("sp", kz_[0:64, 0, :], qk_s[ci, 0:64, tok], writes=[key])
                    P.dma("sp", kz_[64:128, 1, :], qk_s[ci, 64:128, tok], writes=[key])

            load_sb(0)
            for sq in range(NSEQ):
                load_rest(sq)

                with contextlib.ExitStack() as es2:
                  if "nosb" not in phases:
                    sb2 = lambda n, s, d=F32: es2.enter_context(nc.sbuf_tensor("sb%d" % sq + n, s, d))
                    ps2 = lambda n, s, d=F32: es2.enter_context(nc.psum_tensor("sb%d" % sq + n, s, d))
                    NS = 4
                    E = sb2("E", [128, NS, 512])
                    SP = sb2("SP", [128, NS, 2, 512], BF16)
                    A = sb2("A", [128, NS, 2, 512], BF16)
                    yacc = sb2("yacc", [64, NS, 512])
                    yst = sb2("yst", [64, NS, 512], BF16)
                    pZ = [ps2("pZ%d" % i, [128, 512]) for i in range(2)]
                    pC = [ps2("pC%d" % i, [128, 512]) for i in range(NS)]
                    pY = [ps2("pY%d" % i, [64, 512]) for i in range(2)]
                    mask128 = sbmask[:, 0, 0:128]

                    def sb_stream(s_, h, qc):
                        j, half = h // 2, h % 2
                        po = slice(64 * half, 64 * half + 64)
                        zb = s_ % 2
                        S = "s%d" % s_
                        kmax = 4 * qc + 3
                        nstep = kmax + 1
                        P.op("dve", lambda e: e.memset(yacc[:, s_, :], 0.0), writes=["yacc" + S])
                        if s_ >= 2:
                            yield
                        for step in range(nstep):
                            kb = kmax - step
                            r = kb - 4 * qc
                            c0 = 128 * max(0, r)
                            cols = slice(c0, 512)
                            dcols = slice(c0, c0 + 128)
                            qcols = slice(qc * 512 + c0, (qc + 1) * 512)
                            kcols = tsl(kb)
                            par = step % 2
                            spk = "SP%s%d" % (S, par)
                            ak = "A%s%d" % (S, par)
                            P.op("pe", lambda e: e.matmul(pZ[zb][:, cols], ksb[:, j, kcols], qz[half][:, j, qcols], start=True, stop=True,
                                                          skip_group_check=True),
                                 reads=["ksb", "qsb", "qz0z", "qz1z"], writes=["pZ%d" % zb])
                            yield
                            P.op("act", lambda e: e.activation(out=E[:, s_, cols], in_=pZ[zb][:, cols], func=AF.Exp),
                                 reads=[], writes=["pZ%d" % zb, "E" + S])
                            P.op("act", lambda e: e.activation(out=SP[:, s_, par, cols], in_=E[:, s_, cols], func=AF.Ln, bias=1.0),
                                 reads=["E" + S], writes=[spk])
                            if r >= 0:
                                P.op("dve", lambda e: e.tensor_tensor(SP[:, s_, par, dcols], SP[:, s_, par, dcols], mask128, op=ALU.mult),
                                     reads=["cb", spk], writes=[spk])
                            yield
                            P.op("pe", lambda e: e.matmul(pC[s_][:, cols], ksb[:, j, kcols], qz[half][:, j, qcols], start=(step == 0), stop=False,
                                                          skip_group_check=True),
                                 reads=["ksb", "qsb"], writes=["pC" + S])
                            P.op("pe", lambda e: e.matmul(pC[s_][:, cols], nUincl, SP[:, s_, par, cols], start=False, stop=True,
                                                          skip_group_check=True),
                                 reads=["cb", spk], writes=["pC" + S])
                            yield
                            P.op("act", lambda e: e.activation(out=A[:, s_, par, cols], in_=pC[s_][:, cols], func=AF.Exp),
                                 reads=[], writes=["pC" + S, ak])
                            if r >= 0:
                                P.op("dve", lambda e: e.tensor_tensor(A[:, s_, par, dcols], A[:, s_, par, dcols], mask128, op=ALU.mult),
                                     reads=["cb", ak], writes=[ak])
                            yield
                            P.op("pe", lambda e: e.matmul(pY[zb][:, cols], v[:, kb, h * 64:(h + 1) * 64], A[:, s_, par, cols],
                                                          start=True, stop=True, skip_group_check=True),
                                 reads=["v", ak], writes=["pY%d" % zb])
                            if step < nstep - 1:
                                P.op("pe", lambda e: e.matmul(pC[s_][:, cols], nksb[:, j, kcols], qz[half][:, j, qcols], start=False, stop=False,
                                                              skip_group_check=True),
                                     reads=["nksb", "qsb"], writes=["pC" + S])
                                P.op("pe", lambda e: e.matmul(pC[s_][:, cols], nLstr, SP[:, s_, par, cols], start=False, stop=False,
                                                              skip_group_check=True),
                                     reads=["cb", spk], writes=["pC" + S])
                            P.op("dve", lambda e: e.tensor_tensor(yacc[:, s_, cols], yacc[:, s_, cols], pY[zb][:, cols], op=ALU.add),
                                 reads=["yacc" + S], writes=["pY%d" % zb, "yacc" + S])
                            yield
                        P.op("dve", lambda e: e.tensor_copy(yst[:, s_, :], yacc[:, s_, :]), reads=["yacc" + S], writes=["yst" + S])
                        P.dma("sp", ysbT_s[h // 2, 64 * (h % 2):64 * (h % 2) + 64, sq * SEQ + qc * 512: sq * SEQ + (qc + 1) * 512], yst[:, s_, :],
                              reads=["yst" + S], writes=["ysbd%d_%d" % (h, sq * 4 + qc)])

                    for qc in range(4):
                        for g in range(2):
                            P.run_threads([sb_stream(s_, 4 * g + s_, qc) for s_ in range(NS)])
                    P.barrier()
                    if sq + 1 < NSEQ:
                        load_sb(sq + 1)

                with contextlib.ExitStack() as es2:
                  if "nodsa" not in phases:
                    sb2 = lambda n, s, d=F32: es2.enter_context(nc.sbuf_tensor("ds%d" % sq + n, s, d))
                    ps2 = lambda n, s, d=F32: es2.enter_context(nc.psum_tensor("ds%d" % sq + n, s, d))
                    Sc = sb2("Sc", [128, 4, SEQ])
                    R = sb2("R", [128, 2, 512])
                    Mb = sb2("Mb", [128, 2, SEQ], BF16)
                    MT = sb2("MT", [128, 4, 16, 128], BF16)
                    PT = sb2("PT", [128, 3, 512], BF16)
                    yd = sb2("yd", [128, 512], BF16)
                    ydst = sb2("ydst", [128, 2, 4, 128], BF16)
                    sm = sb2("sm", [128, 16])
                    rec = sb2("rec", [128, 8, 1])
                    pD = [ps2("pD%d" % i, [128, 512]) for i in range(2)]
                    pM = ps2("pM", [128, 8, 128], BF16)
                    pL = [ps2("pL%d" % i, [128, 512]) for i in range(2)]
                    pYd = ps2("pYd", [128, 2, 512])
                    pM2 = ps2("pM2", [128, 8, 128], BF16)
                    cnts = dict(d=0, l=0)

                    def stage1(i):
                        nk = (i + 1) * 128
                        nch = (nk + 511) // 512
                        tcols = tsl(i)
                        z = i % 4
                        sck = "Sc%d" % z
                        for hh in range(8):
                            j, s_ = hh // 2, hh % 2
                            po = slice(64 * s_, 64 * s_ + 64)
                            for c in range(nch):
                                n = min(512, nk - c * 512)
                                q = cnts["d"] % 2
                                cnts["d"] += 1
                                cc = slice(c * 512, c * 512 + n)
                                P.op("pe", lambda e: e.matmul(pD[q][:, 0:n], qi[:, j, tcols], kiz[:, s_, cc], start=True, stop=True),
                                     reads=["qi", "ki2", "kzz"], writes=["pD%d" % q])
                                P.op("act", lambda e: e.activation(out=R[:, q, 0:n], in_=pD[q][:, 0:n], func=AF.Relu),
                                     reads=[], writes=["pD%d" % q, "R%d" % q])
                                if hh == 0:
                                    P.op("dve", lambda e: e.tensor_scalar(Sc[:, z, cc], R[:, q, 0:n], wi[:, i, 0:1], None, op0=ALU.mult),
                                         reads=["R%d" % q, "wi"], writes=[sck])
                                else:
                                    P.op("dve", lambda e: e.scalar_tensor_tensor(Sc[:, z, cc], R[:, q, 0:n], wi[:, i, hh:hh + 1], Sc[:, z, cc],
                                                                               op0=ALU.mult, op1=ALU.add),
                                         reads=["R%d" % q, "wi", sck], writes=[sck])
                                yield

                    def stage2(i):
                        nk = (i + 1) * 128
                        z = i % 4
                        w = i % 2
                        smk = "sm%d" % w
                        mbk = "Mb%d" % w
                        lo, hi, mid, cnt, dlt = (sm[:, 8 * w + c:8 * w + c + 1] for c in range(5))
                        sck = "Sc%d" % z
                        if i >= 2:
                            P.op("dve", lambda e: e.tensor_reduce(lo, Sc[:, z, 0:nk], axis=AX.X, op=ALU.min), reads=[sck], writes=[smk])
                        P.op("dve", lambda e: e.tensor_tensor(Sc[:, z, nk - 128:nk], Sc[:, z, nk - 128:nk], dsaneg, op=ALU.add),
                             reads=["cf", sck], writes=[sck])
                        yield
                        if i >= 2:
                            P.op("dve", lambda e: e.tensor_reduce(hi, Sc[:, z, 0:nk], axis=AX.X, op=ALU.max), reads=[sck], writes=[smk])
                            P.op("dve", lambda e: e.tensor_tensor(hi, hi, lo, op=ALU.subtract), reads=[smk], writes=[smk])
                            on_act = w in ACT_CHAINS
                            sg = -1.0 if on_act else 1.0
                            if on_act:
                                P.op("dve", lambda e: e.tensor_scalar(lo, lo, -1.0, None, op0=ALU.mult), reads=[smk], writes=[smk])
                            yield
                            thr = float(2 * TOPK - 1 - nk) if on_act else (float(TOPK) - 0.5)
                            for it in range(NBIS):
                                ck = float(0.5 ** (it + 1))
                                P.op("dve", lambda e: e.scalar_tensor_tensor(mid, hi, sg * ck, lo, op0=ALU.mult, op1=ALU.add),
                                     reads=[smk], writes=[smk + "m"])
                                yield
                                if on_act:
                                    P.op("act", lambda e: e.activation(out=Mb[:, w, 0:nk], in_=Sc[:, z, 0:nk], func=AF.Sign, bias=mid,
                                                                       accum_out=cnt),
                                         reads=[sck, smk + "m"], writes=[mbk, smk + "c"])
                                else:
                                    P.op("dve", lambda e: e.tensor_scalar(Mb[:, w, 0:nk], Sc[:, z, 0:nk], mid, 0.0, op0=ALU.is_gt, op1=ALU.add,
                                                                          accum_out=cnt), reads=[sck, smk + "m"], writes=[mbk, smk + "c"])
                                yield
                                P.op("dve", lambda e: e.tensor_scalar(dlt, cnt, thr, sg * ck, op0=ALU.is_gt, op1=ALU.mult),
                                     reads=[smk + "c"], writes=[smk + "d"])
                                P.op("dve", lambda e: e.scalar_tensor_tensor(lo, dlt, hi, lo, op0=ALU.mult, op1=ALU.add),
                                     reads=[smk + "d", smk], writes=[smk])
                                yield
                            if on_act:
                                P.op("dve", lambda e: e.tensor_scalar(lo, lo, -1.0, None, op0=ALU.mult), reads=[smk], writes=[smk])
                            P.op("dve", lambda e: e.tensor_scalar(Mb[:, w, 0:nk], Sc[:, z, 0:nk], lo, None, op0=ALU.is_gt),
                                 reads=[sck, smk], writes=[mbk])
                        else:
                            P.op("dve", lambda e: e.tensor_scalar(Mb[:, w, 0:nk], Sc[:, z, 0:nk], -1e29, None, op0=ALU.is_gt),
                                 reads=[sck], writes=[mbk])
                        yield
                        for g0 in range(0, i + 1, 8):
                            g1 = min(i + 1, g0 + 8)
                            for kb in range(g0, g1):
                                P.op("pe", lambda e: e.transpose(pM[:, kb - g0, :], Mb[:, w, tsl(kb)], ident),
                                     reads=[mbk, "cb"], writes=["pM"])
                            P.op("act", lambda e: e.activation(out=MT[:, z, g0:g1, :], in_=pM[:, 0:g1 - g0, :], func=AF.Identity,
                                                               scale=MASK_BIG, bias=-MASK_BIG),
                                 reads=[], writes=["pM", "MT%d" % z])
                            yield

                    def stage3(i):
                        tcols = tsl(i)
                        z = i % 4
                        def emit_pv(kb, s_, q3):
                            for j in range(4):
                                P.op("pe", lambda e: e.matmul(pYd[:, s_, j * 66:(j + 1) * 66], PT[:, q3, j * 128:(j + 1) * 128], v[:, kb, 512:578],
                                                              start=(kb == 0 and j == 0), stop=(kb == i), skip_group_check=True),
                                     reads=["PT%d" % q3, "v", "vone"], writes=["pYd%d" % s_])
                        pend = []
                        for kb in range(i + 1):
                            for s_ in range(2):
                                q = cnts["l"] % 2
                                q3 = cnts["l"] % 3
                                cnts["l"] += 1
                                po = slice(64 * s_, 64 * s_ + 64)
                                for j in range(4):
                                    P.op("pe", lambda e: e.matmul(pL[q][:, j * 128:(j + 1) * 128], kdz[:, s_, tsl(kb)], qd[:, j, tcols],
                                                                  start=True, stop=False, skip_group_check=True),
                                         reads=["kd2", "kzz", "qd"], writes=["pL%d" % q])
                                    P.op("pe", lambda e: e.matmul(pL[q][:, j * 128:(j + 1) * 128], ident, MT[:, z, kb, :],
                                                                  start=False, stop=True, skip_group_check=True),
                                         reads=["cb", "MT%d" % z], writes=["pL%d" % q])
                                P.op("act", lambda e: e.activation(out=PT[:, q3, :], in_=pL[q][:], func=AF.Exp),
                                     reads=[], writes=["pL%d" % q, "PT%d" % q3])
                                pend.append((kb, s_, q3))
                                if len(pend) > 2:
                                    emit_pv(*pend.pop(0))
                                yield
                        while pend:
                            emit_pv(*pend.pop(0))
                        for bank in range(2):
                            yv = pYd[:, bank, 0:264].rearrange("p (h c) -> p h c", h=4)
                            ydv = yd[:].rearrange("p (j s c) -> p j s c", j=4, s=2)[:, :, bank, :]
                            P.op("dve", lambda e: e.reciprocal(rec[:, bank * 4:(bank + 1) * 4, :], yv[:, :, 64:65]),
                                 reads=[], writes=["pYd%d" % bank, "rec%d" % bank])
                            P.op("dve", lambda e: e.tensor_tensor(ydv, yv[:, :, 0:64],
                                                                  rec[:, bank * 4:(bank + 1) * 4, :].to_broadcast([128, 4, 64]), op=ALU.mult),
                                 reads=["rec%d" % bank], writes=["pYd%d" % bank, "yd"])
                        yield
                        u = i % 2
                        for cchunk in range(4):
                            P.op("pe", lambda e: e.transpose(pM2[:, cchunk, :], yd[:, tsl(cchunk)], ident),
                                 reads=["yd", "cb"], writes=["pM2"])
                        P.op("act", lambda e: e.copy(ydst[:, u, :, :], pM2[:, 0:4, :]), reads=[], writes=["pM2", "ydst%d" % u])
                        P.dma("sp", ydT_s[:, :, sq * SEQ + i * 128: sq * SEQ + (i + 1) * 128].rearrange("c p t -> p c t"),
                              ydst[:, u, :, :], reads=["ydst%d" % u], writes=["ydd%d" % (sq * 16 + i)])

                    def seq(*gens):
                        for g in gens:
                            yield from g
                    pairs = [(2 * m + 1, 2 * m) for m in range(7, -1, -1)]
                    for tau in range(len(pairs) + 2):
                        th = []
                        if tau < len(pairs):
                            th.append(seq(stage1(pairs[tau][0]), stage1(pairs[tau][1])))
                        if 0 <= tau - 1 < len(pairs):
                            th.append(stage2(pairs[tau - 1][0]))
                            th.append(stage2(pairs[tau - 1][1]))
                        if 0 <= tau - 2 < len(pairs):
                            th.append(seq(stage3(pairs[tau - 2][0]), stage3(pairs[tau - 2][1])))
                        P.run_threads(th)
                    P.barrier()
            P.barrier()

    def p4a_phase():
        with contextlib.ExitStack() as es:
            sb = lambda n, s, d=F32: es.enter_context(nc.sbuf_tensor("p4a" + n, s, d))
            ps = lambda n, s, d=F32: es.enter_context(nc.psum_tensor("p4a" + n, s, d))
            wg = sb("wg", [128, KC, 2048], BF16)
            for k in range(KC):
                P.dma("pool", wg[:, k, :], win_d[tsl(k), OFF_GSB:OFF_GSB + 2048], writes=["wg"])
            wosb = sb("wosb", [128, 4, D], BF16)
            P.dma("pool", wosb[:], wosb_d.rearrange("(c p) n -> p c n", p=128), writes=["wosb"])
            wod = sb("wod", [128, 4, D], BF16)
            P.dma("pool", wod[:], wod_d.rearrange("(c p) n -> p c n", p=128), writes=["wod"])
            wo = sb("wo", [128, KC, D], BF16)
            P.dma("pool", wo[:], wo_d.rearrange("(c p) n -> p c n", p=128), writes=["wo"])
            uT = sb("uT", [128, 2, KC, TB], BF16)
            ysbT = sb("ysbT", [128, 2, 4, TB], BF16)
            ydT = sb("ydT", [128, 2, 4, TB], BF16)
            mT = sb("mT", [128, KC, TB], BF16)
            s1 = sb("s1", [128, 2, TB])
            s2 = sb("s2", [128, 2, TB])
            t1 = sb("t1", [128, 2, TB])
            t2 = sb("t2", [128, 2, TB])
            hs = sb("hs", [128, 4, D])
            pG = [ps("pG%d" % i, [128, TB]) for i in range(2)]
            pY = [ps("pY%d" % i, [128, TB]) for i in range(2)]
            pO = [ps("pO%d" % i, [128, 512]) for i in range(2)]
            oc = hc = 0
            def load_blk(b):
                if b >= NB:
                    return
                u = b % 2
                tok = slice(b * TB, (b + 1) * TB)
                P.dma("sp", uT[:, u, :, :], uT_s[:, :, tok].rearrange("c p t -> p c t"), writes=["uT%d" % u])
                P.dma("sp", ysbT[:, u, :, :], ysbT_s[:, :, tok].rearrange("c p t -> p c t"), writes=["ysbT%d" % u])
                P.dma("sp", ydT[:, u, :, :], ydT_s[:, :, tok].rearrange("c p t -> p c t"), writes=["ydT%d" % u])
            load_blk(0)
            for b in range(NB):
                u = b % 2
                tok = slice(b * TB, (b + 1) * TB)
                load_blk(b + 1)
                for i in range(4):
                    P.dma("sp", hs[:, i, :], h_s[tsl(4 * b + i), :], reads=["hd%d" % (4 * b + i)], writes=["hs%d" % i])
                for c in range(KC):
                    r = c % 2
                    for (g, pg, sdst, skey) in ((0, pG[0], s1, "s1"), (1, pG[1], s2, "s2")):
                        for k in range(KC):
                            P.op("pe", lambda e, k=k, g=g, pg=pg, c=c: e.matmul(
                                pg[:], wg[:, k, g * 1024 + c * 128: g * 1024 + (c + 1) * 128], uT[:, u, k, :],
                                start=(k == 0), stop=(k == KC - 1)),
                                reads=["wg", "uT%d" % u], writes=["pG%d" % g])
                        P.op("act", lambda e, pg=pg, sdst=sdst, r=r: e.activation(out=sdst[:, r, :], in_=pg[:], func=AF.Tanh, scale=0.5),
                             reads=[], writes=["pG%d" % g, "%s%d" % (skey, r)])
                    for hh in range(4):
                        P.op("pe", lambda e, hh=hh, c=c: e.matmul(pY[0][:], wosb[:, hh, tsl(c)], ysbT[:, u, hh, :],
                                                                   start=(hh == 0), stop=(hh == 3)),
                             reads=["wosb", "ysbT%d" % u], writes=["pY0"])
                    for k in range(4):
                        P.op("pe", lambda e, k=k, c=c: e.matmul(pY[1][:], wod[:, k, tsl(c)], ydT[:, u, k, :],
                                                                 start=(k == 0), stop=(k == 3)),
                             reads=["wod", "ydT%d" % u], writes=["pY1"])
                    P.op("dve", lambda e, r=r: e.scalar_tensor_tensor(t1[:, r, :], s1[:, r, :], 1.0, pY[0][:], op0=ALU.add, op1=ALU.mult),
                         reads=["s1%d" % r], writes=["pY0", "t1%d" % r])
                    P.op("dve", lambda e, r=r: e.scalar_tensor_tensor(t2[:, r, :], s2[:, r, :], 1.0, pY[1][:], op0=ALU.add, op1=ALU.mult),
                         reads=["s2%d" % r], writes=["pY1", "t2%d" % r])
                    P.op("pool", lambda e, r=r, c=c: e.tensor_tensor(mT[:, c, :], t1[:, r, :], t2[:, r, :], op=ALU.add),
                         reads=["t1%d" % r, "t2%d" % r], writes=["mT"])
                for i in range(4):
                    t = 4 * b + i
                    sl = i
                    for c2 in range(2):
                        q = oc % 2
                        oc += 1
                        for k in range(KC):
                            P.op("pe", lambda e, k=k, i=i, c2=c2, q=q: e.matmul(pO[q][:], mT[:, k, tsl(i)], wo[:, k, c2 * 512:(c2 + 1) * 512],
                                                                               start=(k == 0), stop=(k == KC - 1)),
                                 reads=["mT", "wo"], writes=["pO%d" % q])
                        P.op("dve", lambda e, q=q, sl=sl, c2=c2: e.scalar_tensor_tensor(
                            hs[:, sl, c2 * 512:(c2 + 1) * 512], pO[q][:], 0.5, hs[:, sl, c2 * 512:(c2 + 1) * 512],
                            op0=ALU.mult, op1=ALU.add), reads=[], writes=["pO%d" % q, "hs%d" % sl])
                    P.dma("sp", h_s[tsl(t), :], hs[:, sl, :], reads=["hs%d" % sl], writes=["hd%d" % t])
            P.barrier()

    def p4c_phase():
        with contextlib.ExitStack() as es:
            sb = lambda n, s, d=F32: es.enter_context(nc.sbuf_tensor("p4c" + n, s, d))
            ps = lambda n, s, d=F32: es.enter_context(nc.psum_tensor("p4c" + n, s, d))
            c = load_consts(es)
            ident = c["ident"]
            wpg = sb("wpg", [128, KC, D], BF16)
            P.dma("pool", wpg[:], wpg_d.rearrange("(c p) n -> p c n", p=128), writes=["wpg"])
            wpp = sb("wpp", [128, 2, D], BF16)
            P.dma("pool", wpp[:], wpp_d.rearrange("(c p) n -> p c n", p=128), writes=["wpp"])
            gB = make_gB(es, c, 3, "p4cgB")
            gfin = sb("gfin", [128, D])
            P.dma("sp", gfin[:], gfin_d.broadcast_to([128, D]), writes=["gfin"])
            hs = sb("hs", [128, 8, D])
            pt = sb("pt", [128, 8, 256], BF16)
            xn = sb("xn", [128, 2, D], BF16)
            hnT = sb("hnT", [128, 2, KC, 128], BF16)
            pTs = sb("pTs", [128, 2, 2, 128], BF16)
            gate = sb("gate", [128, 2, D])
            ot = sb("ot", [128, 2, D])
            ss = sb("ss", [128, 2, 4])
            rstd = sb("rstd", [128, 2, 4])
            ss2 = sb("ss2", [128, 2, 4])
            rstd2 = sb("rstd2", [128, 2, 4])
            pT = [ps("pT%d" % i, [128, KC, 128], BF16) for i in range(2)]
            pP = [ps("pP%d" % i, [128, KC, 128], BF16) for i in range(2)]
            pGt = [ps("pGt%d" % i, [128, 512]) for i in range(2)]
            pPp = [ps("pPp%d" % i, [128, 512]) for i in range(2)]
            NG = NT // 4

            def load_g(g):
                if g < NG:
                    for i in range(4):
                        t = 4 * g + i
                        sl = (g % 2) * 4 + i
                        P.dma("sp", hs[:, sl, :], h_s[tsl(t), :], reads=["hd%d" % t], writes=["hs%d" % sl])
                        P.dma("pool", pt[:, sl, :], p_d[tsl(t), :], writes=["pt%d" % sl])

            def tile_thread(g, i):
                sl = (g % 2) * 4 + i
                u = i % 2
                gp = g % 2
                P.op("dve", lambda e: e.tensor_scalar(xn[:, u, :], hs[:, sl, :], rstd[:, gp, i:i + 1], None, op0=ALU.mult),
                     reads=["hs%d" % sl, "rstd%d" % gp], writes=["xn%d" % u])
                yield
                for k in range(KC):
                    P.op("pe", lambda e: e.transpose(pT[u][:, k, :], xn[:, u, tsl(k)], ident[:]),
                         reads=["xn%d" % u, "c_ident"], writes=["pT%d" % u])
                for k in range(2):
                    P.op("pe", lambda e: e.transpose(pP[u][:, k, :], pt[:, sl, tsl(k)], ident[:]),
                         reads=["pt%d" % sl, "c_ident"], writes=["pP%d" % u])
                yield
                P.op("dve", lambda e: e.tensor_tensor(hnT[:, u, :, :], pT[u][:], gB[:], op=ALU.mult),
                     reads=["p4cgB"], writes=["pT%d" % u, "hnT%d" % u])
                P.op("act", lambda e: e.copy(pTs[:, u, :, :], pP[u][:, 0:2, :]), reads=[], writes=["pP%d" % u, "pTs%d" % u])
                yield
                for c2 in range(2):
                    cs = slice(c2 * 512, (c2 + 1) * 512)
                    for k in range(KC):
                        P.op("pe", lambda e: e.matmul(pGt[u][:], hnT[:, u, k, :], wpg[:, k, cs], start=(k == 0), stop=(k == KC - 1)),
                             reads=["hnT%d" % u, "wpg"], writes=["pGt%d" % u])
                    for k in range(2):
                        P.op("pe", lambda e: e.matmul(pPp[u][:], pTs[:, u, k, :], wpp[:, k, cs], start=(k == 0), stop=(k == 1)),
                             reads=["pTs%d" % u, "wpp"], writes=["pPp%d" % u])
                    yield
                    P.op("act", lambda e: e.activation(out=gate[:, u, cs], in_=pGt[u][:], func=AF.Tanh, scale=0.5),
                         reads=[], writes=["pGt%d" % u, "gate%d" % u])
                    yield
                    P.op("dve", lambda e: e.scalar_tensor_tensor(gate[:, u, cs], gate[:, u, cs], 1.0, pPp[u][:], op0=ALU.add, op1=ALU.mult),
                         reads=["gate%d" % u], writes=["pPp%d" % u, "gate%d" % u])
                    yield
                P.op("dve", lambda e: e.scalar_tensor_tensor(hs[:, sl, :], gate[:, u, :], 0.5, hs[:, sl, :], op0=ALU.mult, op1=ALU.add),
                     reads=["gate%d" % u, "hs%d" % sl], writes=["hs%d" % sl])
                yield

            def seq(*gens):
                for g_ in gens:
                    yield from g_

            load_g(0)
            for g in range(NG):
                gp = g % 2
                load_g(g + 1)
                for i in range(4):
                    sl = gp * 4 + i
                    P.op("act", lambda e: e.activation(out=xn[:, i % 2, :], in_=hs[:, sl, :], func=AF.Square, accum_out=ss[:, gp, i:i + 1]),
                         reads=["hs%d" % sl], writes=["xn%d" % (i % 2), "ss%d" % gp])
                rstd_from_ss(ss[:, gp, :], rstd[:, gp, :], 4, ["ss%d" % gp], "rstd%d" % gp)
                P.run_threads([seq(tile_thread(g, 0), tile_thread(g, 2)), seq(tile_thread(g, 1), tile_thread(g, 3))])
                for i in range(4):
                    sl = gp * 4 + i
                    P.op("act", lambda e: e.activation(out=xn[:, i % 2, :], in_=hs[:, sl, :], func=AF.Square, accum_out=ss2[:, gp, i:i + 1]),
                         reads=["hs%d" % sl], writes=["xn%d" % (i % 2), "ss2%d" % gp])
                rstd_from_ss(ss2[:, gp, :], rstd2[:, gp, :], 4, ["ss2%d" % gp], "rstd2%d" % gp)
                for i in range(4):
                    sl = gp * 4 + i
                    t = 4 * g + i
                    u = i % 2
                    P.op("dve", lambda e: e.scalar_tensor_tensor(ot[:, u, :], hs[:, sl, :], rstd2[:, gp, i:i + 1], gfin[:], op0=ALU.mult, op1=ALU.mult),
                         reads=["hs%d" % sl, "rstd2%d" % gp, "gfin"], writes=["ot%d" % u])
                    P.dma("sp", out_d[tsl(t), :], ot[:, u, :], reads=["ot%d" % u], writes=["outd%d" % t])
            P.barrier()

    if "p1" in phases:
        ffn_phase("f1", x_d, h_s, w1a_d, w2a_d, 0, 1)
    if "p2" in phases:
        p2_phase()
    if "p3" in phases:
        p3_phase()
    if "p4a" in phases:
        p4a_phase()
    if "p4b" in phases:
        ffn_phase("f2", h_s, h_s, w1b_d, w2b_d, 2, None)
    if "p4c" in phases:
        p4c_phase()
    P.emit()
    return nc


def host_consts():
    j = np.arange(128)
    ident = np.eye(128, dtype=np.float32)
    nUincl = -(j[:, None] >= j[None, :]).astype(np.float32)
    nLstr = -(j[:, None] < j[None, :]).astype(np.float32)
    t = np.arange(512)
    sbmask = np.stack([(j[:, None] + 128 * r < t[None, :]).astype(np.float32) for r in range(4)], 1)
    cb = np.concatenate([ident, nUincl, nLstr, sbmask.reshape(128, 2048)], 1).astype(np.float32)
    dsaneg = np.where(j[None, :] > j[:, None], -1e30, 0.0).astype(np.float32)
    inv_freq = (500000.0 ** (-np.arange(0, 16, 2, dtype=np.float32) / 16)).astype(np.float32)
    pm = j % 64
    invf = np.where(pm < 16, inv_freq[pm % 8], 0.0).astype(np.float32)
    sgn = np.where(pm < 8, -1.0, np.where(pm < 16, 1.0, 0.0)).astype(np.float32)
    cf = np.concatenate([dsaneg, invf[:, None], sgn[:, None]], 1).astype(np.float32)
    return cb, cf


def make_in_maps(inputs, ncores=8):
    f = lambda a: np.ascontiguousarray(np.asarray(a), dtype=np.float32)
    x = f(inputs["x"])
    p = f(inputs["p"])[0]
    pos = np.ascontiguousarray(np.asarray(inputs["positions"]), dtype=np.int32)
    cb, cf = host_consts()
    gl = lambda g: f(g).reshape(KC, 128).T
    gcols = np.ascontiguousarray(np.concatenate([gl(inputs["ffn1_norm"][0]), gl(inputs["mix_norm"][0]),
                                                 gl(inputs["ffn2_norm"][0]), gl(inputs["ple_norm"][0])], 1))
    shared = {
        "w1a": f(inputs["ffn1_w1"][0]), "w2a": f(inputs["ffn1_w2"][0]), "win": f(inputs["w_in"][0]),
        "wosb": f(inputs["w_out_sb"][0]), "wod": f(inputs["w_out_dsa"][0]), "wo": f(inputs["w_out"][0]),
        "w1b": f(inputs["ffn2_w1"][0]), "w2b": f(inputs["ffn2_w2"][0]), "wpg": f(inputs["ple_w_gate"][0]),
        "wpp": f(inputs["ple_w_proj"][0]), "gcols": gcols, "gfin": f(inputs["final_norm"]).reshape(1, D),
        "cb": cb, "cf": cf,
    }
    maps = []
    for c in range(ncores):
        m = dict(shared)
        m["x"] = np.ascontiguousarray(x[NSEQ * c:NSEQ * (c + 1)].reshape(NTOK, D))
        m["p"] = np.ascontiguousarray(p[NSEQ * c:NSEQ * (c + 1)].reshape(NTOK, 256))
        m["pos"] = np.ascontiguousarray(pos[NSEQ * c:NSEQ * (c + 1)].reshape(1, NTOK))
        maps.append(m)
    return maps


def kernel(**inputs):
    nc = bass.Bass("TRN2", target_bir_lowering=False)
    build(nc)
    maps = make_in_maps(inputs, 8)
    res = run_bass_kernel_spmd(nc, maps, core_ids=list(range(8)))
    out = np.stack([r["out"].reshape(NSEQ, SEQ, D) for r in res.results], 0).reshape(16, SEQ, D)
    return out.astype(np.float32)
```

```python
import contextlib
import numpy as np
import concourse.bass as bass
import concourse.mybir as mybir
from concourse.bass_utils import run_bass_kernel_spmd

F32 = mybir.dt.float32
BF16 = mybir.dt.bfloat16
I32 = mybir.dt.int32
AF = mybir.ActivationFunctionType
ALU = mybir.AluOpType
AX = mybir.AxisListType

D = 1024
KC = 8
SEQ = 2048
NSEQ = 2
NTOK = NSEQ * SEQ
NT = NTOK // 128
TB = 512
NB = NTOK // TB
FF = 2816
FC = FF // 128
DIN = 4808
EPS = 1e-6
TOPK = 256
NBIS = 14
MASK_BIG = 30000.0
ACT_CHAINS = (1,)
OFF_QSB, OFF_KSB, OFF_VSB, OFF_QD, OFF_KD, OFF_VD, OFF_QI, OFF_KI, OFF_WI, OFF_GSB, OFF_GD = (
    0, 512, 1024, 1536, 2048, 2112, 2176, 2688, 2752, 2760, 3784)
TWO_PI = 6.283185307179586
CW1 = 6.28125
CW2 = TWO_PI - CW1


class _Key:
    __slots__ = ("w", "rs")

    def __init__(self):
        self.w = None
        self.rs = []


class _Rec:
    def __init__(self):
        self.call = None

    def __getattr__(self, name):
        def f(*a, **k):
            self.call = (name, a, k)
            return self
        return f


def _freeze(fn):
    rec = _Rec()
    fn(rec)
    name, a, k = rec.call
    return lambda e: getattr(e, name)(*a, **k)


class Prog:
    ENG = ("pe", "act", "dve", "pool", "sp")

    def __init__(self, nc, same_engine_sync=True, ndma_sems=8):
        self.nc = nc
        self.es = contextlib.ExitStack()
        self.streams = {e: [] for e in self.ENG}
        self.cnt = {e: 0 for e in self.ENG}
        self.sems = {}
        for e in self.ENG:
            self.sems["p_" + e] = self.es.enter_context(nc.semaphore("prog_" + e))
        self.known = {e: {} for e in self.ENG}
        self.same = same_engine_sync
        self.ndma = ndma_sems
        self.dq = {}
        self.keys = {}

    def key(self, name):
        k = self.keys.get(name)
        if k is None:
            k = _Key()
            self.keys[name] = k
        return k

    def _deps(self, reads, writes):
        ev = []
        for r in reads:
            k = self.key(r)
            if k.w is not None:
                ev.append(k.w)
        for w in writes:
            k = self.key(w)
            if k.w is not None:
                ev.append(k.w)
            ev.extend(k.rs)
        return ev

    def _waits(self, eng, evs):
        need = {}
        kn = self.known[eng]
        for (sid, val) in evs:
            if sid == "p_" + eng and (eng == "pe" or not self.same):
                continue
            if kn.get(sid, 0) >= val:
                continue
            if need.get(sid, 0) < val:
                need[sid] = val
        for sid, val in need.items():
            kn[sid] = val
        return list(need.items())

    def _commit(self, reads, writes, event):
        for r in reads:
            self.key(r).rs.append(event)
        for w in writes:
            k = self.key(w)
            k.w = event
            k.rs = []

    def op(self, eng, fn, reads=(), writes=()):
        waits = self._waits(eng, self._deps(reads, writes))
        self.cnt[eng] += 1
        event = ("p_" + eng, self.cnt[eng])
        self.streams[eng].append((waits, _freeze(fn), ("p_" + eng, 1)))
        self._commit(reads, writes, event)
        return event

    def dma(self, q, out, in_, reads=(), writes=()):
        d = self.dq.get(q)
        if d is None:
            ids = [f"d_{q}_{j}" for j in range(self.ndma)]
            for s in ids:
                self.sems[s] = self.es.enter_context(self.nc.semaphore(s))
            d = dict(i=0, ids=ids, vals=[0] * self.ndma, last=[None] * self.ndma)
            self.dq[q] = d
        j = d["i"] % self.ndma
        d["i"] += 1
        evs = self._deps(reads, writes)
        if d["last"][j] is not None:
            evs.append(d["last"][j])
        waits = self._waits(q, evs)
        d["vals"][j] += 16
        event = (d["ids"][j], d["vals"][j])
        d["last"][j] = event
        fn = lambda e, out=out, in_=in_: e.dma_start(out=out, in_=in_)
        self.streams[q].append((waits, fn, (d["ids"][j], 16)))
        self._commit(reads, writes, event)
        return event

    def _all_events(self):
        ev = []
        for q, d in self.dq.items():
            for e in d["last"]:
                if e is not None:
                    ev.append(e)
        for e in self.ENG:
            if self.cnt[e] > 0:
                ev.append(("p_" + e, self.cnt[e]))
        return ev

    def run_threads(self, gens):
        live = list(gens)
        while live:
            nxt = []
            for g in live:
                try:
                    next(g)
                    nxt.append(g)
                except StopIteration:
                    pass
            live = nxt

    def barrier(self):
        ev = self._all_events()
        for e in self.ENG:
            w = self._waits(e, ev)
            if w:
                self.streams[e].append((w, None, None))
        self.keys = {}

    def emit(self):
        nc = self.nc
        fw = self._waits("sp", self._all_events())
        self.streams["sp"].append((fw, None, None))
        with nc.Block() as block:
            def run(engname, handle):
                for waits, fn, inc in self.streams[engname]:
                    for sid, val in waits:
                        handle.wait_ge(self.sems[sid], val)
                    if fn is not None:
                        fn(handle).then_inc(self.sems[inc[0]], inc[1])

            @block.tensor
            def _(e):
                run("pe", e)

            @block.scalar
            def _(e):
                run("act", e)

            @block.vector
            def _(e):
                run("dve", e)

            @block.gpsimd
            def _(e):
                run("pool", e)

            @block.sync
            def _(e):
                run("sp", e)
        self.es.close()


def build(nc, dbg=False, phases=("p1", "p2", "p3", "p4a", "p4b", "p4c")):
    P = Prog(nc)

    def din(name, shape, dt=F32):
        return nc.dram_tensor(name, shape, dt, kind="ExternalInput").ap()

    def dscr(name, shape, dt):
        return nc.dram_tensor(name, shape, dt, kind=("ExternalOutput" if dbg else "Internal")).ap()

    x_d = din("x", [NTOK, D])
    p_d = din("p", [NTOK, 256])
    pos_d = din("pos", [1, NTOK], I32)
    w1a_d = din("w1a", [D, 2 * FF])
    w2a_d = din("w2a", [FF, D])
    win_d = din("win", [D, DIN])
    wosb_d = din("wosb", [512, D])
    wod_d = din("wod", [512, D])
    wo_d = din("wo", [D, D])
    w1b_d = din("w1b", [D, 2 * FF])
    w2b_d = din("w2b", [FF, D])
    wpg_d = din("wpg", [D, D])
    wpp_d = din("wpp", [256, D])
    gcols_d = din("gcols", [128, 4 * KC])
    gfin_d = din("gfin", [1, D])
    cb_d = din("cb", [128, 384 + 2048])
    cf_d = din("cf", [128, 130])
    out_d = nc.dram_tensor("out", [NTOK, D], F32, kind="ExternalOutput").ap()

    h_s = dscr("h_s", [NTOK, D], F32)
    uT_s = dscr("uT_s", [KC, 128, NTOK], BF16)
    qk_s = dscr("qk_s", [18, 128, NTOK], BF16)
    v_s = dscr("v_s", [NTOK, 576], BF16)
    wi_s = dscr("wi_s", [NTOK, 8], F32)
    ysbT_s = dscr("ysbT_s", [4, 128, NTOK], BF16)
    ydT_s = dscr("ydT_s", [4, 128, NTOK], BF16)

    def tsl(t):
        return slice(t * 128, (t + 1) * 128)

    ccount = [0]

    def load_consts(es, need_cb=True):
        ccount[0] += 1
        sb = lambda n, s, d=F32: es.enter_context(nc.sbuf_tensor("%s_%d" % (n, ccount[0]), s, d))
        c = {}
        c["gcols"] = sb("c_gcols", [128, 4 * KC])
        P.dma("sp", c["gcols"][:], gcols_d, writes=["c_gcols"])
        c["ident"] = sb("c_ident", [128, 128], BF16)
        P.dma("pool", c["ident"][:], cb_d[:, 0:128], writes=["c_ident"])
        return c

    def make_gB(es, c, which, name):
        gB = es.enter_context(nc.sbuf_tensor(name, [128, KC, 128], F32))
        src = c["gcols"][:, which * KC:(which + 1) * KC]
        P.op("dve", lambda e: e.tensor_copy(gB[:], src.unsqueeze(2).to_broadcast([128, KC, 128])),
             reads=["c_gcols"], writes=[name])
        return gB

    def rstd_from_ss(ss, rstd, n, rkeys, wkey):
        P.op("dve", lambda e: e.tensor_scalar(ss[:, 0:n], ss[:, 0:n], 1.0 / D, EPS, op0=ALU.mult, op1=ALU.add),
             reads=rkeys, writes=rkeys)
        P.op("act", lambda e: e.activation(out=ss[:, 0:n], in_=ss[:, 0:n], func=AF.Sqrt), reads=rkeys, writes=rkeys)
        P.op("dve", lambda e: e.reciprocal(rstd[:, 0:n], ss[:, 0:n]), reads=rkeys, writes=[wkey])

    def ffn_phase(tag, src_d, dst_d, w1_d, w2_d, gsel, post_gsel):
        with contextlib.ExitStack() as es:
            sb = lambda n, s, d=F32: es.enter_context(nc.sbuf_tensor(tag + n, s, d))
            ps = lambda n, s, d=F32: es.enter_context(nc.psum_tensor(tag + n, s, d))
            c = load_consts(es)
            w1 = sb("w1", [128, KC, 2 * FF], BF16)
            w2 = sb("w2", [128, FC, D], BF16)
            for k in range(KC):
                P.dma("pool", w1[:, k, :], w1_d[tsl(k), :], writes=["w1g0", "w1g1", "w1g2", "w1g3"])
            for k in range(FC):
                P.dma("pool", w2[:, k, :], w2_d[tsl(k), :], writes=["w2"])
            gB = make_gB(es, c, gsel, tag + "gB")
            gB2 = make_gB(es, c, post_gsel, tag + "gB2") if post_gsel is not None else None
            NXS = 6 if post_gsel is not None else 8
            xs = sb("xs", [128, NXS, D])
            issued = set()

            def issue_load(t):
                if t in issued or t >= NT:
                    return
                issued.add(t)
                P.dma("sp", xs[:, t % NXS, :], src_d[tsl(t), :], reads=["hd%d" % t], writes=["xs%d" % (t % NXS)])
            xn = sb("xn", [128, 2, D], BF16)
            xnT = sb("xnT", [128, KC, TB], BF16)
            gT = sb("gT", [128, FC, TB], BF16)
            stmp = sb("stmp", [128, 2, TB])
            ss = sb("ss", [128, 2, 4])
            rstd = sb("rstd", [128, 2, 4])
            ss2 = sb("ss2", [128, 2, 4])
            rstd2 = sb("rstd2", [128, 2, 4])
            ust = sb("ust", [128, 2, KC, 128], BF16) if post_gsel is not None else None
            pT = [ps("pT%d" % i, [128, KC, 128], BF16) for i in range(2)]
            pA = [ps("pA%d" % i, [128, TB]) for i in range(2)]
            pB = [ps("pB%d" % i, [128, TB]) for i in range(2)]
            pO = [ps("pO%d" % i, [128, 512]) for i in range(2)]
            ident = c["ident"]
            xnc = [0]
            ptc = [0]

            def norm_T(tile_ap, xs_key, rstd_col, rstd_key, gBt, gB_key, out_ap, out_key, dve_out=True):
                s = xnc[0] % 2
                xnc[0] += 1
                q = ptc[0] % 2
                ptc[0] += 1
                P.op("dve", lambda e: e.tensor_scalar(xn[:, s, :], tile_ap, rstd_col, None, op0=ALU.mult),
                     reads=[xs_key, rstd_key], writes=["xn%d" % s])
                for k in range(KC):
                    P.op("pe", lambda e, k=k: e.transpose(pT[q][:, k, :], xn[:, s, tsl(k)], ident[:]),
                         reads=["xn%d" % s, "c_ident"], writes=["pT%d" % q])
                P.op("dve", lambda e: e.tensor_tensor(out_ap, pT[q][:], gBt[:], op=ALU.mult),
                     reads=[gB_key], writes=["pT%d" % q, out_key])

            for b in range(NB):
                sp_ = b % 2
                tiles = [4 * b + i for i in range(4)]
                for i, t in enumerate(tiles):
                    sl = t % NXS
                    issue_load(t)
                    s = xnc[0] % 2
                    P.op("act", lambda e, sl=sl, s=s, i=i: e.activation(out=xn[:, s, :], in_=xs[:, sl, :], func=AF.Square,
                                                                       accum_out=ss[:, sp_, i:i + 1]),
                         reads=["xs%d" % sl], writes=["xn%d" % s, "ss%d" % sp_])
                rstd_from_ss(ss[:, sp_, :], rstd[:, sp_, :], 4, ["ss%d" % sp_], "rstd%d" % sp_)
                for i, t in enumerate(tiles):
                    sl = t % NXS
                    norm_T(xs[:, sl, :], "xs%d" % sl, rstd[:, sp_, i:i + 1], "rstd%d" % sp_, gB, tag + "gB",
                           xnT[:, :, tsl(i)], "xnT")
                for j in range(FC):
                    q = j % 2
                    for k in range(KC):
                        P.op("pe", lambda e, k=k, j=j, q=q: e.matmul(pA[q][:], w1[:, k, tsl(j)], xnT[:, k, :],
                                                                    start=(k == 0), stop=(k == KC - 1)),
                             reads=["w1g%d" % (j // 6), "xnT"], writes=["pA%d" % q])
                    for k in range(KC):
                        P.op("pe", lambda e, k=k, j=j, q=q: e.matmul(pB[q][:], w1[:, k, FF + j * 128:FF + (j + 1) * 128],
                                                                    xnT[:, k, :], start=(k == 0), stop=(k == KC - 1)),
                             reads=["w1g%d" % (j // 6), "xnT"], writes=["pB%d" % q])
                    P.op("act", lambda e, q=q: e.activation(out=stmp[:, q, :], in_=pA[q][:], func=AF.Silu),
                         reads=[], writes=["pA%d" % q, "stmp%d" % q])
                    P.op("dve", lambda e, q=q, j=j: e.tensor_tensor(gT[:, j, :], stmp[:, q, :], pB[q][:], op=ALU.mult),
                         reads=["stmp%d" % q], writes=["pB%d" % q, "gT"])
                for t2_ in range(4 * b + 4, 4 * b + 4 + (NXS - 4)):
                    issue_load(t2_)
                oc = 0
                for i, t in enumerate(tiles):
                    sl = t % NXS
                    for c2 in range(2):
                        q = oc % 2
                        oc += 1
                        for j in range(FC):
                            P.op("pe", lambda e, j=j, i=i, c2=c2, q=q: e.matmul(
                                pO[q][:], gT[:, j, tsl(i)], w2[:, j, c2 * 512:(c2 + 1) * 512],
                                start=(j == 0), stop=(j == FC - 1)),
                                reads=["w2", "gT"], writes=["pO%d" % q])
                        P.op("dve", lambda e, q=q, sl=sl, c2=c2: e.scalar_tensor_tensor(
                            xs[:, sl, c2 * 512:(c2 + 1) * 512], pO[q][:], 0.5, xs[:, sl, c2 * 512:(c2 + 1) * 512],
                            op0=ALU.mult, op1=ALU.add),
                            reads=[], writes=["pO%d" % q, "xs%d" % sl])
                    P.dma("sp", dst_d[tsl(t), :], xs[:, sl, :], reads=["xs%d" % sl], writes=["hd%d" % t])
                    if post_gsel is not None:
                        s = xnc[0] % 2
                        P.op("act", lambda e, sl=sl, s=s, i=i: e.activation(out=xn[:, s, :], in_=xs[:, sl, :], func=AF.Square,
                                                                           accum_out=ss2[:, sp_, i:i + 1]),
                             reads=["xs%d" % sl], writes=["xn%d" % s, "ss2%d" % sp_])
                if post_gsel is not None:
                    rstd_from_ss(ss2[:, sp_, :], rstd2[:, sp_, :], 4, ["ss2%d" % sp_], "rstd2%d" % sp_)
                    for i, t in enumerate(tiles):
                        sl = t % NXS
                        u = t % 2
                        norm_T(xs[:, sl, :], "xs%d" % sl, rstd2[:, sp_, i:i + 1], "rstd2%d" % sp_, gB2, tag + "gB2",
                               ust[:, u, :, :], "ust%d" % u)
                        P.dma("sp", uT_s[:, :, tsl(t)].rearrange("c p t -> p c t"), ust[:, u, :, :],
                              reads=["ust%d" % u], writes=["uTd%d" % t])
            P.barrier()

    def p2_phase():
        with contextlib.ExitStack() as es:
            sb = lambda n, s, d=F32: es.enter_context(nc.sbuf_tensor("p2" + n, s, d))
            ps = lambda n, s, d=F32: es.enter_context(nc.psum_tensor("p2" + n, s, d))
            win = sb("win", [128, KC, 2760], BF16)
            for k in range(KC):
                P.dma("pool", win[:, k, :], win_d[tsl(k), 0:2760], writes=["win"])
            wp = sb("wp", [128, KC, 1280], BF16)
            wk2 = sb("wk2", [128, KC, 256], BF16)
            cf = sb("cf", [128, 130])
            P.dma("sp", cf[:], cf_d, writes=["cf"])
            posi = sb("posi", [128, NTOK], I32)
            P.dma("sp", posi[:], pos_d.broadcast_to([128, NTOK]), writes=["posi"])
            ang = sb("ang", [128, NTOK])
            kk = sb("kk", [128, NTOK])
            kki = sb("kki", [128, NTOK], I32)
            Ct = sb("Ct", [128, NTOK])
            St = sb("St", [128, NTOK])
            invf = cf[:, 128:129]
            sgn = cf[:, 129:130]
            for h2 in range(2):
                hc = slice(h2 * SEQ, (h2 + 1) * SEQ)
                kA, kK, kI = "ang%d" % h2, "kk%d" % h2, "kki%d" % h2
                P.op("dve", lambda e: e.tensor_copy(ang[:, hc], posi[:, hc]), reads=["posi"], writes=[kA])
                P.op("dve", lambda e: e.tensor_scalar(ang[:, hc], ang[:, hc], invf, None, op0=ALU.mult), reads=[kA, "cf"], writes=[kA])

                def reduce_to(dst, shift, key):
                    P.op("dve", lambda e: e.tensor_scalar(kk[:, hc], ang[:, hc], shift, 1.0 / TWO_PI, op0=ALU.add, op1=ALU.mult),
                         reads=[kA], writes=[kK])
                    P.op("dve", lambda e: e.tensor_copy(kki[:, hc], kk[:, hc]), reads=[kK], writes=[kI])
                    P.op("dve", lambda e: e.tensor_copy(kk[:, hc], kki[:, hc]), reads=[kI], writes=[kK])
                    P.op("dve", lambda e: e.scalar_tensor_tensor(dst[:, hc], kk[:, hc], -CW1, ang[:, hc], op0=ALU.mult, op1=ALU.add),
                         reads=[kK, kA], writes=[key])
                    P.op("dve", lambda e: e.scalar_tensor_tensor(dst[:, hc], kk[:, hc], -CW2, dst[:, hc], op0=ALU.mult, op1=ALU.add),
                         reads=[kK, key], writes=[key])
                    P.op("dve", lambda e: e.tensor_scalar(dst[:, hc], dst[:, hc], shift, 3.1415925, op0=ALU.add, op1=ALU.min),
                         reads=[key], writes=[key])
                    P.op("dve", lambda e: e.tensor_scalar(dst[:, hc], dst[:, hc], -3.1415925, None, op0=ALU.max), reads=[key], writes=[key])
                    P.op("act", lambda e: e.activation(out=dst[:, hc], in_=dst[:, hc], func=AF.Sin), reads=[key], writes=[key])

                reduce_to(St, 0.0, "St%d" % h2)
                P.op("dve", lambda e: e.tensor_scalar(St[:, hc], St[:, hc], sgn, None, op0=ALU.mult), reads=["cf", "St%d" % h2], writes=["St%d" % h2])
                reduce_to(Ct, float(np.pi / 2), "Ct%d" % h2)
            P.op("pool", lambda e: e.memset(wp[:], 0.0), writes=["wp"])
            for hh in range(2):
                P.op("pool", lambda e, hh=hh: e.tensor_copy(wk2[:, :, hh * 64:(hh + 1) * 64], win[:, :, OFF_KD:OFF_KD + 64]),
                     reads=["win"], writes=["wk2"])
                P.op("pool", lambda e, hh=hh: e.tensor_copy(wk2[:, :, 128 + hh * 64:128 + (hh + 1) * 64],
                                                            win[:, :, OFF_KI:OFF_KI + 64]), reads=["win"], writes=["wk2"])
            pbase = [OFF_QD + 128 * j for j in range(4)] + [OFF_QI + 128 * j for j in range(4)]
            for pc in range(10):
                for hh in range(2):
                    if pc < 8:
                        b0 = pbase[pc] + 64 * hh
                    else:
                        b0 = OFF_KD if pc == 8 else OFF_KI
                    o = pc * 128 + 64 * hh
                    P.op("pool", lambda e, o=o, b0=b0: e.tensor_copy(wp[:, :, o:o + 8], win[:, :, b0 + 8:b0 + 16]),
                         reads=["win"], writes=["wp"])
                    P.op("pool", lambda e, o=o, b0=b0: e.tensor_copy(wp[:, :, o + 8:o + 16], win[:, :, b0:b0 + 8]),
                         reads=["win"], writes=["wp"])
            uT = sb("uT", [128, 2, KC, TB], BF16)
            fst = sb("fst", [128, 4, TB], BF16)
            t1 = sb("t1", [128, 2, TB])
            t2 = sb("t2", [128, 2, TB])
            vst = sb("vst", [128, 2, 576], BF16)
            wist = sb("wist", [128, 2, 8])
            pA = [ps("pA%d" % i, [128, TB]) for i in range(3)]
            pB = [ps("pB%d" % i, [128, TB]) for i in range(2)]
            pV = [ps("pV%d" % i, [128, 512]) for i in range(2)]
            pW = ps("pW", [128, 128])
            chunks = []
            for j in range(4):
                chunks.append((j, win, OFF_QSB + 128 * j, None))
            for j in range(4):
                chunks.append((4 + j, win, OFF_KSB + 128 * j, None))
            for j in range(4):
                chunks.append((8 + j, win, OFF_QD + 128 * j, j))
            for j in range(4):
                chunks.append((12 + j, win, OFF_QI + 128 * j, 4 + j))
            chunks.append((16, wk2, 0, 8))
            chunks.append((17, wk2, 128, 9))
            ca = cbn = fs = rc = vc = 0
            def load_uT(b):
                if b < NB:
                    P.dma("sp", uT[:, b % 2, :, :], uT_s[:, :, b * TB:(b + 1) * TB].rearrange("c p t -> p c t"),
                          reads=["uTd%d" % t for t in range(4 * b, 4 * b + 4)], writes=["uT%d" % (b % 2)])
            load_uT(0)
            for b in range(NB):
                u = b % 2
                load_uT(b + 1)
                tok = slice(b * TB, (b + 1) * TB)
                for (ci, wt, off, pidx) in chunks:
                    qa = ca % 3
                    ca += 1
                    for k in range(KC):
                        P.op("pe", lambda e, k=k, wt=wt, off=off, qa=qa: e.matmul(pA[qa][:], wt[:, k, off:off + 128], uT[:, u, k, :],
                                                                                 start=(k == 0), stop=(k == KC - 1)),
                             reads=["win", "wk2", "uT%d" % u], writes=["pA%d" % qa])
                    f = fs % 4
                    fs += 1
                    qscale = 0.125 if (ci < 4 or 8 <= ci < 12) else 1.0
                    if pidx is None:
                        P.op("act", lambda e, qa=qa, f=f: e.activation(out=fst[:, f, :], in_=pA[qa][:], func=AF.Identity, scale=qscale),
                             reads=[], writes=["pA%d" % qa, "fst%d" % f])
                    else:
                        qb = cbn % 2
                        cbn += 1
                        for k in range(KC):
                            P.op("pe", lambda e, k=k, pidx=pidx, qb=qb: e.matmul(pB[qb][:], wp[:, k, pidx * 128:(pidx + 1) * 128],
                                                                                uT[:, u, k, :], start=(k == 0), stop=(k == KC - 1)),
                                 reads=["wp", "uT%d" % u], writes=["pB%d" % qb])
                        r = rc % 2
                        rc += 1
                        P.op("dve", lambda e, qa=qa, r=r: e.scalar_tensor_tensor(t1[:, r, :], pA[qa][:], qscale, Ct[:, tok], op0=ALU.mult, op1=ALU.mult),
                             reads=["Ct%d" % (b // 4)], writes=["pA%d" % qa, "t1%d" % r])
                        P.op("dve", lambda e, qb=qb, r=r: e.scalar_tensor_tensor(t2[:, r, :], pB[qb][:], qscale, St[:, tok], op0=ALU.mult, op1=ALU.mult),
                             reads=["St%d" % (b // 4)], writes=["pB%d" % qb, "t2%d" % r])
                        P.op("pool", lambda e, r=r, f=f: e.tensor_tensor(fst[:, f, :], t1[:, r, :], t2[:, r, :], op=ALU.add),
                             reads=["t1%d" % r, "t2%d" % r], writes=["fst%d" % f])
                    P.dma("sp", qk_s[ci, :, tok], fst[:, f, :], reads=["fst%d" % f], writes=["qkd%d_%d" % (ci, b)])
                for i in range(4):
                    t = 4 * b + i
                    q = vc % 2
                    vc += 1
                    for k in range(KC):
                        P.op("pe", lambda e, k=k, i=i, q=q: e.matmul(pV[q][:], uT[:, u, k, tsl(i)], win[:, k, OFF_VSB:OFF_VSB + 512],
                                                                    start=(k == 0), stop=(k == KC - 1)),
                             reads=["win", "uT%d" % u], writes=["pV%d" % q])
                    for k in range(KC):
                        P.op("pe", lambda e, k=k, i=i: e.matmul(pW[:, 0:64], uT[:, u, k, tsl(i)], win[:, k, OFF_VD:OFF_VD + 64],
                                                               start=(k == 0), stop=(k == KC - 1), skip_group_check=True),
                             reads=["win", "uT%d" % u], writes=["pW"])
                    for k in range(KC):
                        P.op("pe", lambda e, k=k, i=i: e.matmul(pW[:, 64:72], uT[:, u, k, tsl(i)], win[:, k, OFF_WI:OFF_WI + 8],
                                                               start=False, stop=(k == KC - 1), skip_group_check=True),
                             reads=["win", "uT%d" % u], writes=["pW"])
                    P.op("act", lambda e, q=q: e.copy(vst[:, q, 0:512], pV[q][:]), reads=[], writes=["pV%d" % q, "vst%d" % q])
                    P.op("dve", lambda e, q=q: e.tensor_copy(vst[:, q, 512:576], pW[:, 0:64]), reads=[], writes=["pW", "vst%d" % q])
                    P.op("dve", lambda e, q=q: e.tensor_scalar(wist[:, q, :], pW[:, 64:72], float(8 ** -0.5 * 0.125), None, op0=ALU.mult),
                         reads=[], writes=["pW", "wist%d" % q])
                    P.dma("sp", v_s[tsl(t), :], vst[:, q, :], reads=["vst%d" % q], writes=["vd%d" % t])
                    P.dma("sp", wi_s[tsl(t), :], wist[:, q, :], reads=["wist%d" % q], writes=["wid%d" % t])
            P.barrier()

    def p3_phase():
        with contextlib.ExitStack() as es:
            sb = lambda n, s, d=F32: es.enter_context(nc.sbuf_tensor("p3" + n, s, d))
            cb = sb("cb", [128, 384 + 2048], BF16)
            P.dma("pool", cb[:], cb_d, writes=["cb"])
            cf = sb("cf", [128, 130])
            P.dma("sp", cf[:], cf_d, writes=["cf"])
            ident = cb[:, 0:128]
            nUincl = cb[:, 128:256]
            nLstr = cb[:, 256:384]
            sbmask = cb[:, 384:384 + 2048].rearrange("p (r t) -> p r t", r=4)
            dsaneg = cf[:, 0:128]
            qz = [sb("qz%d" % i, [128, 4, SEQ], BF16) for i in range(2)]
            P.op("pool", lambda e: e.memset(qz[0][64:128, :, :], 0.0), writes=["qz0z"])
            P.op("pool", lambda e: e.memset(qz[1][0:64, :, :], 0.0), writes=["qz1z"])
            ksb = sb("ksb", [128, 4, SEQ], BF16)
            nksb = sb("nksb", [128, 4, SEQ], BF16)
            qd = sb("qd", [128, 4, SEQ], BF16)
            qi = sb("qi", [128, 4, SEQ], BF16)
            kdz = sb("kdz", [128, 2, SEQ], BF16)
            kiz = sb("kiz", [128, 2, SEQ], BF16)
            for kz_ in (kdz, kiz):
                P.op("pool", lambda e: e.memset(kz_[64:128, 0, :], 0.0), writes=["kzz"])
                P.op("pool", lambda e: e.memset(kz_[0:64, 1, :], 0.0), writes=["kzz"])
            v = sb("v", [128, 16, 578], BF16)
            wi = sb("wi", [128, 16, 8])
            P.op("pool", lambda e: e.memset(v[:, :, 576:578], 0.0), writes=["vone"])
            P.op("pool", lambda e: e.memset(v[:, :, 576:577], 1.0), writes=["vone"])
            def load_sb(sq):
                tok = slice(sq * SEQ, (sq + 1) * SEQ)
                for j in range(4):
                    P.dma("sp", qz[0][0:64, j, :], qk_s[j, 0:64, tok], writes=["qsb"])
                    P.dma("sp", qz[1][64:128, j, :], qk_s[j, 64:128, tok], writes=["qsb"])
                for j in range(4):
                    P.dma("sp", ksb[:, j, :], qk_s[4 + j, :, tok], writes=["ksb"])
                for j in range(4):
                    P.op("dve", lambda e: e.tensor_scalar(nksb[:, j, :], ksb[:, j, :], -1.0, None, op0=ALU.mult), reads=["ksb"], writes=["nksb"])

            def load_rest(sq):
                tok = slice(sq * SEQ, (sq + 1) * SEQ)
                P.dma("act", v[:, :, 0:576], v_s[tok, :].rearrange("(n p) c -> p n c", p=128), writes=["v"])
                P.dma("act", wi[:], wi_s[tok, :].rearrange("(n p) c -> p n c", p=128), writes=["wi"])
                for (tile_, c0, key) in ((qd, 8, "qd"), (qi, 12, "qi")):
                    for j in range(4):
                        P.dma("sp", tile_[:, j, :], qk_s[c0 + j, :, tok], writes=[key])
                for (kz_, ci, key) in ((kdz, 16, "kd2"), (kiz, 17, "ki2")):
                    P.dma("sp", kz_[0:64, 0, :], qk_s[ci, 0:64, tok], writes=[key])
                    P.dma("sp", kz_[64:128, 1, :], qk_s[ci, 64:128, tok], writes=[key])

            load_sb(0)
            for sq in range(NSEQ):
                load_rest(sq)

                with contextlib.ExitStack() as es2:
                  if "nosb" not in phases:
                    sb2 = lambda n, s, d=F32: es2.enter_context(nc.sbuf_tensor("sb%d" % sq + n, s, d))
                    ps2 = lambda n, s, d=F32: es2.enter_context(nc.psum_tensor("sb%d" % sq + n, s, d))
                    NS = 4
                    E = sb2("E", [128, NS, 512])
                    SP = sb2("SP", [128, NS, 2, 512], BF16)
                    A = sb2("A", [128, NS, 2, 512], BF16)
                    yacc = sb2("yacc", [64, NS, 512])
                    yst = sb2("yst", [64, NS, 512], BF16)
                    pZ = [ps2("pZ%d" % i, [128, 512]) for i in range(2)]
                    pC = [ps2("pC%d" % i, [128, 512]) for i in range(NS)]
                    pY = [ps2("pY%d" % i, [64, 512]) for i in range(2)]
                    mask128 = sbmask[:, 0, 0:128]

                    def sb_stream(s_, h, qc):
                        j, half = h // 2, h % 2
                        po = slice(64 * half, 64 * half + 64)
                        zb = s_ % 2
                        S = "s%d" % s_
                        kmax = 4 * qc + 3
                        nstep = kmax + 1
                        P.op("dve", lambda e: e.memset(yacc[:, s_, :], 0.0), writes=["yacc" + S])
                        if s_ >= 2:
                            yield
                        for step in range(nstep):
                            kb = kmax - step
                            r = kb - 4 * qc
                            c0 = 128 * max(0, r)
                            cols = slice(c0, 512)
                            dcols = slice(c0, c0 + 128)
                            qcols = slice(qc * 512 + c0, (qc + 1) * 512)
                            kcols = tsl(kb)
                            par = step % 2
                            spk = "SP%s%d" % (S, par)
                            ak = "A%s%d" % (S, par)
                            P.op("pe", lambda e: e.matmul(pZ[zb][:, cols], ksb[:, j, kcols], qz[half][:, j, qcols], start=True, stop=True,
                                                          skip_group_check=True),
                                 reads=["ksb", "qsb", "qz0z", "qz1z"], writes=["pZ%d" % zb])
                            yield
                            P.op("act", lambda e: e.activation(out=E[:, s_, cols], in_=pZ[zb][:, cols], func=AF.Exp),
                                 reads=[], writes=["pZ%d" % zb, "E" + S])
                            P.op("act", lambda e: e.activation(out=SP[:, s_, par, cols], in_=E[:, s_, cols], func=AF.Ln, bias=1.0),
                                 reads=["E" + S], writes=[spk])
                            if r >= 0:
                                P.op("dve", lambda e: e.tensor_tensor(SP[:, s_, par, dcols], SP[:, s_, par, dcols], mask128, op=ALU.mult),
                                     reads=["cb", spk], writes=[spk])
                            yield
                            P.op("pe", lambda e: e.matmul(pC[s_][:, cols], ksb[:, j, kcols], qz[half][:, j, qcols], start=(step == 0), stop=False,
                                                          skip_group_check=True),
                                 reads=["ksb", "qsb"], writes=["pC" + S])
                            P.op("pe", lambda e: e.matmul(pC[s_][:, cols], nUincl, SP[:, s_, par, cols], start=False, stop=True,
                                                          skip_group_check=True),
                                 reads=["cb", spk], writes=["pC" + S])
                            yield
                            P.op("act", lambda e: e.activation(out=A[:, s_, par, cols], in_=pC[s_][:, cols], func=AF.Exp),
                                 reads=[], writes=["pC" + S, ak])
                            if r >= 0:
                                P.op("dve", lambda e: e.tensor_tensor(A[:, s_, par, dcols], A[:, s_, par, dcols], mask128, op=ALU.mult),
                                     reads=["cb", ak], writes=[ak])
                            yield
                            P.op("pe", lambda e: e.matmul(pY[zb][:, cols], v[:, kb, h * 64:(h + 1) * 64], A[:, s_, par, cols],
                                                          start=True, stop=True, skip_group_check=True),
                                 reads=["v", ak], writes=["pY%d" % zb])
                            if step < nstep - 1:
                                P.op("pe", lambda e: e.matmul(pC[s_][:, cols], nksb[:, j, kcols], qz[half][:, j, qcols], start=False, stop=False,
                                                              skip_group_check=True),
                                     reads=["nksb", "qsb"], writes=["pC" + S])
                                P.op("pe", lambda e: e.matmul(pC[s_][:, cols], nLstr, SP[:, s_, par, cols], start=False, stop=False,
                                                              skip_group_check=True),
                                     reads=["cb", spk], writes=["pC" + S])
                            P.op("dve", lambda e: e.tensor_tensor(yacc[:, s_, cols], yacc[:, s_, cols], pY[zb][:, cols], op=ALU.add),
                                 reads=["yacc" + S], writes=["pY%d" % zb, "yacc" + S])
                            yield
                        P.op("dve", lambda e: e.tensor_copy(yst[:, s_, :], yacc[:, s_, :]), reads=["yacc" + S], writes=["yst" + S])
                        P.dma("sp", ysbT_s[h // 2, 64 * (h % 2):64 * (h % 2) + 64, sq * SEQ + qc * 512: sq * SEQ + (qc + 1) * 512], yst[:, s_, :],
                              reads=["yst" + S], writes=["ysbd%d_%d" % (h, sq * 4 + qc)])

                    for qc in range(4):
                        for g in range(2):
                            P.run_threads([sb_stream(s_, 4 * g + s_, qc) for s_ in range(NS)])
                    P.barrier()
                    if sq + 1 < NSEQ:
                        load_sb(sq + 1)

                with contextlib.ExitStack() as es2:
                  if "nodsa" not in phases:
                    sb2 = lambda n, s, d=F32: es2.enter_context(nc.sbuf_tensor("ds%d" % sq + n, s, d))
                    ps2 = lambda n, s, d=F32: es2.enter_context(nc.psum_tensor("ds%d" % sq + n, s, d))
                    Sc = sb2("Sc", [128, 4, SEQ])
                    R = sb2("R", [128, 2, 512])
                    Mb = sb2("Mb", [128, 2, SEQ], BF16)
                    MT = sb2("MT", [128, 4, 16, 128], BF16)
                    PT = sb2("PT", [128, 3, 512], BF16)
                    yd = sb2("yd", [128, 512], BF16)
                    ydst = sb2("ydst", [128, 2, 4, 128], BF16)
                    sm = sb2("sm", [128, 16])
                    rec = sb2("rec", [128, 8, 1])
                    pD = [ps2("pD%d" % i, [128, 512]) for i in range(2)]
                    pM = ps2("pM", [128, 8, 128], BF16)
                    pL = [ps2("pL%d" % i, [128, 512]) for i in range(2)]
                    pYd = ps2("pYd", [128, 2, 512])
                    pM2 = ps2("pM2", [128, 8, 128], BF16)
                    cnts = dict(d=0, l=0)

                    def stage1(i):
                        nk = (i + 1) * 128
                        nch = (nk + 511) // 512
                        tcols = tsl(i)
                        z = i % 4
                        sck = "Sc%d" % z
                        for hh in range(8):
                            j, s_ = hh // 2, hh % 2
                            po = slice(64 * s_, 64 * s_ + 64)
                            for c in range(nch):
                                n = min(512, nk - c * 512)
                                q = cnts["d"] % 2
                                cnts["d"] += 1
                                cc = slice(c * 512, c * 512 + n)
                                P.op("pe", lambda e: e.matmul(pD[q][:, 0:n], qi[:, j, tcols], kiz[:, s_, cc], start=True, stop=True),
                                     reads=["qi", "ki2", "kzz"], writes=["pD%d" % q])
                                P.op("act", lambda e: e.activation(out=R[:, q, 0:n], in_=pD[q][:, 0:n], func=AF.Relu),
                                     reads=[], writes=["pD%d" % q, "R%d" % q])
                                if hh == 0:
                                    P.op("dve", lambda e: e.tensor_scalar(Sc[:, z, cc], R[:, q, 0:n], wi[:, i, 0:1], None, op0=ALU.mult),
                                         reads=["R%d" % q, "wi"], writes=[sck])
                                else:
                                    P.op("dve", lambda e: e.scalar_tensor_tensor(Sc[:, z, cc], R[:, q, 0:n], wi[:, i, hh:hh + 1], Sc[:, z, cc],
                                                                               op0=ALU.mult, op1=ALU.add),
                                         reads=["R%d" % q, "wi", sck], writes=[sck])
                                yield

                    def stage2(i):
                        nk = (i + 1) * 128
                        z = i % 4
                        w = i % 2
                        smk = "sm%d" % w
                        mbk = "Mb%d" % w
                        lo, hi, mid, cnt, dlt = (sm[:, 8 * w + c:8 * w + c + 1] for c in range(5))
                        sck = "Sc%d" % z
                        if i >= 2:
                            P.op("dve", lambda e: e.tensor_reduce(lo, Sc[:, z, 0:nk], axis=AX.X, op=ALU.min), reads=[sck], writes=[smk])
                        P.op("dve", lambda e: e.tensor_tensor(Sc[:, z, nk - 128:nk], Sc[:, z, nk - 128:nk], dsaneg, op=ALU.add),
                             reads=["cf", sck], writes=[sck])
                        yield
                        if i >= 2:
                            P.op("dve", lambda e: e.tensor_reduce(hi, Sc[:, z, 0:nk], axis=AX.X, op=ALU.max), reads=[sck], writes=[smk])
                            P.op("dve", lambda e: e.tensor_tensor(hi, hi, lo, op=ALU.subtract), reads=[smk], writes=[smk])
                            on_act = w in ACT_CHAINS
                            sg = -1.0 if on_act else 1.0
                            if on_act:
                                P.op("dve", lambda e: e.tensor_scalar(lo, lo, -1.0, None, op0=ALU.mult), reads=[smk], writes=[smk])
                            yield
                            thr = float(2 * TOPK - 1 - nk) if on_act else (float(TOPK) - 0.5)
                            for it in range(NBIS):
                                ck = float(0.5 ** (it + 1))
                                P.op("dve", lambda e: e.scalar_tensor_tensor(mid, hi, sg * ck, lo, op0=ALU.mult, op1=ALU.add),
                                     reads=[smk], writes=[smk + "m"])
                                yield
                                if on_act:
                                    P.op("act", lambda e: e.activation(out=Mb[:, w, 0:nk], in_=Sc[:, z, 0:nk], func=AF.Sign, bias=mid,
                                                                       accum_out=cnt),
                                         reads=[sck, smk + "m"], writes=[mbk, smk + "c"])
                                else:
                                    P.op("dve", lambda e: e.tensor_scalar(Mb[:, w, 0:nk], Sc[:, z, 0:nk], mid, 0.0, op0=ALU.is_gt, op1=ALU.add,
                                                                          accum_out=cnt), reads=[sck, smk + "m"], writes=[mbk, smk + "c"])
                                yield
                                P.op("dve", lambda e: e.tensor_scalar(dlt, cnt, thr, sg * ck, op0=ALU.is_gt, op1=ALU.mult),
                                     reads=[smk + "c"], writes=[smk + "d"])
                                P.op("dve", lambda e: e.scalar_tensor_tensor(lo, dlt, hi, lo, op0=ALU.mult, op1=ALU.add),
                                     reads=[smk + "d", smk], writes=[smk])
                                yield
                            if on_act:
                                P.op("dve", lambda e: e.tensor_scalar(lo, lo, -1.0, None, op0=ALU.mult), reads=[smk], writes=[smk])
                            P.op("dve", lambda e: e.tensor_scalar(Mb[:, w, 0:nk], Sc[:, z, 0:nk], lo, None, op0=ALU.is_gt),
                                 reads=[sck, smk], writes=[mbk])
                        else:
                            P.op("dve", lambda e: e.tensor_scalar(Mb[:, w, 0:nk], Sc[:, z, 0:nk], -1e29, None, op0=ALU.is_gt),
                                 reads=[sck], writes=[mbk])
                        yield
                        for g0 in range(0, i + 1, 8):
                            g1 = min(i + 1, g0 + 8)
                            for kb in range(g0, g1):
                                P.op("pe", lambda e: e.transpose(pM[:, kb - g0, :], Mb[:, w, tsl(kb)], ident),
                                     reads=[mbk, "cb"], writes=["pM"])
                            P.op("act", lambda e: e.activation(out=MT[:, z, g0:g1, :], in_=pM[:, 0:g1 - g0, :], func=AF.Identity,
                                                               scale=MASK_BIG, bias=-MASK_BIG),
                                 reads=[], writes=["pM", "MT%d" % z])
                            yield

                    def stage3(i):
                        tcols = tsl(i)
                        z = i % 4
                        def emit_pv(kb, s_, q3):
                            for j in range(4):
                                P.op("pe", lambda e: e.matmul(pYd[:, s_, j * 66:(j + 1) * 66], PT[:, q3, j * 128:(j + 1) * 128], v[:, kb, 512:578],
                                                              start=(kb == 0 and j == 0), stop=(kb == i), skip_group_check=True),
                                     reads=["PT%d" % q3, "v", "vone"], writes=["pYd%d" % s_])
                        pend = []
                        for kb in range(i + 1):
                            for s_ in range(2):
                                q = cnts["l"] % 2
                                q3 = cnts["l"] % 3
                                cnts["l"] += 1
                                po = slice(64 * s_, 64 * s_ + 64)
                                for j in range(4):
                                    P.op("pe", lambda e: e.matmul(pL[q][:, j * 128:(j + 1) * 128], kdz[:, s_, tsl(kb)], qd[:, j, tcols],
                                                                  start=True, stop=False, skip_group_check=True),
                                         reads=["kd2", "kzz", "qd"], writes=["pL%d" % q])
                                    P.op("pe", lambda e: e.matmul(pL[q][:, j * 128:(j + 1) * 128], ident, MT[:, z, kb, :],
                                                                  start=False, stop=True, skip_group_check=True),
                                         reads=["cb", "MT%d" % z], writes=["pL%d" % q])
                                P.op("act", lambda e: e.activation(out=PT[:, q3, :], in_=pL[q][:], func=AF.Exp),
                                     reads=[], writes=["pL%d" % q, "PT%d" % q3])
                                pend.append((kb, s_, q3))
                                if len(pend) > 2:
                                    emit_pv(*pend.pop(0))
                                yield
                        while pend:
                            emit_pv(*pend.pop(0))
                        for bank in range(2):
                            yv = pYd[:, bank, 0:264].rearrange("p (h c) -> p h c", h=4)
                            ydv = yd[:].rearrange("p (j s c) -> p j s c", j=4, s=2)[:, :, bank, :]
                            P.op("dve", lambda e: e.reciprocal(rec[:, bank * 4:(bank + 1) * 4, :], yv[:, :, 64:65]),
                                 reads=[], writes=["pYd%d" % bank, "rec%d" % bank])
                            P.op("dve", lambda e: e.tensor_tensor(ydv, yv[:, :, 0:64],
                                                                  rec[:, bank * 4:(bank + 1) * 4, :].to_broadcast([128, 4, 64]), op=ALU.mult),
                                 reads=["rec%d" % bank], writes=["pYd%d" % bank, "yd"])
                        yield
                        u = i % 2
                        for cchunk in range(4):
                            P.op("pe", lambda e: e.transpose(pM2[:, cchunk, :], yd[:, tsl(cchunk)], ident),
                                 reads=["yd", "cb"], writes=["pM2"])
                        P.op("act", lambda e: e.copy(ydst[:, u, :, :], pM2[:, 0:4, :]), reads=[], writes=["pM2", "ydst%d" % u])
                        P.dma("sp", ydT_s[:, :, sq * SEQ + i * 128: sq * SEQ + (i + 1) * 128].rearrange("c p t -> p c t"),
                              ydst[:, u, :, :], reads=["ydst%d" % u], writes=["ydd%d" % (sq * 16 + i)])

                    def seq(*gens):
                        for g in gens:
                            yield from g
                    pairs = [(2 * m + 1, 2 * m) for m in (1, 6, 7, 4, 5, 2, 3, 0)]
                    for tau in range(len(pairs) + 2):
                        th = []
                        if tau < len(pairs):
                            th.append(seq(stage1(pairs[tau][0]), stage1(pairs[tau][1])))
                        if 0 <= tau - 1 < len(pairs):
                            th.append(stage2(pairs[tau - 1][0]))
                            th.append(stage2(pairs[tau - 1][1]))
                        if 0 <= tau - 2 < len(pairs):
                            th.append(seq(stage3(pairs[tau - 2][0]), stage3(pairs[tau - 2][1])))
                        P.run_threads(th)
                    P.barrier()
            P.barrier()

    def p4a_phase():
        with contextlib.ExitStack() as es:
            sb = lambda n, s, d=F32: es.enter_context(nc.sbuf_tensor("p4a" + n, s, d))
            ps = lambda n, s, d=F32: es.enter_context(nc.psum_tensor("p4a" + n, s, d))
            wg = sb("wg", [128, KC, 2048], BF16)
            for k in range(KC):
                P.dma("pool", wg[:, k, :], win_d[tsl(k), OFF_GSB:OFF_GSB + 2048], writes=["wg"])
            wosb = sb("wosb", [128, 4, D], BF16)
            P.dma("pool", wosb[:], wosb_d.rearrange("(c p) n -> p c n", p=128), writes=["wosb"])
            wod = sb("wod", [128, 4, D], BF16)
            P.dma("pool", wod[:], wod_d.rearrange("(c p) n -> p c n", p=128), writes=["wod"])
            wo = sb("wo", [128, KC, D], BF16)
            P.dma("pool", wo[:], wo_d.rearrange("(c p) n -> p c n", p=128), writes=["wo"])
            uT = sb("uT", [128, 2, KC, TB], BF16)
            ysbT = sb("ysbT", [128, 2, 4, TB], BF16)
            ydT = sb("ydT", [128, 2, 4, TB], BF16)
            mT = sb("mT", [128, KC, TB], BF16)
            s1 = sb("s1", [128, 2, TB])
            s2 = sb("s2", [128, 2, TB])
            t1 = sb("t1", [128, 2, TB])
            t2 = sb("t2", [128, 2, TB])
            hs = sb("hs", [128, 4, D])
            pG = [ps("pG%d" % i, [128, TB]) for i in range(2)]
            pY = [ps("pY%d" % i, [128, TB]) for i in range(2)]
            pO = [ps("pO%d" % i, [128, 512]) for i in range(2)]
            oc = hc = 0
            def load_blk(b):
                if b >= NB:
                    return
                u = b % 2
                tok = slice(b * TB, (b + 1) * TB)
                P.dma("sp", uT[:, u, :, :], uT_s[:, :, tok].rearrange("c p t -> p c t"), writes=["uT%d" % u])
                P.dma("sp", ysbT[:, u, :, :], ysbT_s[:, :, tok].rearrange("c p t -> p c t"), writes=["ysbT%d" % u])
                P.dma("sp", ydT[:, u, :, :], ydT_s[:, :, tok].rearrange("c p t -> p c t"), writes=["ydT%d" % u])
            load_blk(0)
            for b in range(NB):
                u = b % 2
                tok = slice(b * TB, (b + 1) * TB)
                load_blk(b + 1)
                for i in range(4):
                    P.dma("sp", hs[:, i, :], h_s[tsl(4 * b + i), :], reads=["hd%d" % (4 * b + i)], writes=["hs%d" % i])
                for c in range(KC):
                    r = c % 2
                    for (g, pg, sdst, skey) in ((0, pG[0], s1, "s1"), (1, pG[1], s2, "s2")):
                        for k in range(KC):
                            P.op("pe", lambda e, k=k, g=g, pg=pg, c=c: e.matmul(
                                pg[:], wg[:, k, g * 1024 + c * 128: g * 1024 + (c + 1) * 128], uT[:, u, k, :],
                                start=(k == 0), stop=(k == KC - 1)),
                                reads=["wg", "uT%d" % u], writes=["pG%d" % g])
                        P.op("act", lambda e, pg=pg, sdst=sdst, r=r: e.activation(out=sdst[:, r, :], in_=pg[:], func=AF.Tanh, scale=0.5),
                             reads=[], writes=["pG%d" % g, "%s%d" % (skey, r)])
                    for hh in range(4):
                        P.op("pe", lambda e, hh=hh, c=c: e.matmul(pY[0][:], wosb[:, hh, tsl(c)], ysbT[:, u, hh, :],
                                                                   start=(hh == 0), stop=(hh == 3)),
                             reads=["wosb", "ysbT%d" % u], writes=["pY0"])
                    for k in range(4):
                        P.op("pe", lambda e, k=k, c=c: e.matmul(pY[1][:], wod[:, k, tsl(c)], ydT[:, u, k, :],
                                                                 start=(k == 0), stop=(k == 3)),
                             reads=["wod", "ydT%d" % u], writes=["pY1"])
                    P.op("dve", lambda e, r=r: e.scalar_tensor_tensor(t1[:, r, :], s1[:, r, :], 1.0, pY[0][:], op0=ALU.add, op1=ALU.mult),
                         reads=["s1%d" % r], writes=["pY0", "t1%d" % r])
                    P.op("dve", lambda e, r=r: e.scalar_tensor_tensor(t2[:, r, :], s2[:, r, :], 1.0, pY[1][:], op0=ALU.add, op1=ALU.mult),
                         reads=["s2%d" % r], writes=["pY1", "t2%d" % r])
                    P.op("pool", lambda e, r=r, c=c: e.tensor_tensor(mT[:, c, :], t1[:, r, :], t2[:, r, :], op=ALU.add),
                         reads=["t1%d" % r, "t2%d" % r], writes=["mT"])
                for i in range(4):
                    t = 4 * b + i
                    sl = i
                    for c2 in range(2):
                        q = oc % 2
                        oc += 1
                        for k in range(KC):
                            P.op("pe", lambda e, k=k, i=i, c2=c2, q=q: e.matmul(pO[q][:], mT[:, k, tsl(i)], wo[:, k, c2 * 512:(c2 + 1) * 512],
                                                                               start=(k == 0), stop=(k == KC - 1)),
                                 reads=["mT", "wo"], writes=["pO%d" % q])
                        P.op("dve", lambda e, q=q, sl=sl, c2=c2: e.scalar_tensor_tensor(
                            hs[:, sl, c2 * 512:(c2 + 1) * 512], pO[q][:], 0.5, hs[:, sl, c2 * 512:(c2 + 1) * 512],
                            op0=ALU.mult, op1=ALU.add), reads=[], writes=["pO%d" % q, "hs%d" % sl])
                    P.dma("sp", h_s[tsl(t), :], hs[:, sl, :], reads=["hs%d" % sl], writes=["hd%d" % t])
            P.barrier()

    def p4c_phase():
        with contextlib.ExitStack() as es:
            sb = lambda n, s, d=F32: es.enter_context(nc.sbuf_tensor("p4c" + n, s, d))
            ps = lambda n, s, d=F32: es.enter_context(nc.psum_tensor("p4c" + n, s, d))
            c = load_consts(es)
            ident = c["ident"]
            wpg = sb("wpg", [128, KC, D], BF16)
            P.dma("pool", wpg[:], wpg_d.rearrange("(c p) n -> p c n", p=128), writes=["wpg"])
            wpp = sb("wpp", [128, 2, D], BF16)
            P.dma("pool", wpp[:], wpp_d.rearrange("(c p) n -> p c n", p=128), writes=["wpp"])
            gB = make_gB(es, c, 3, "p4cgB")
            gfin = sb("gfin", [128, D])
            P.dma("sp", gfin[:], gfin_d.broadcast_to([128, D]), writes=["gfin"])
            NSG = 4
            hs = sb("hs", [128, 4 * NSG, D])
            pt = sb("pt", [128, 4 * NSG, 256], BF16)
            xn = sb("xn", [128, 2, D], BF16)
            sqj = sb("sqj", [128, 2, D], BF16)
            hnT = sb("hnT", [128, 2, KC, 128], BF16)
            pTs = sb("pTs", [128, 2, 2, 128], BF16)
            gate = sb("gate", [128, 2, D])
            ot = sb("ot", [128, 2, D])
            ss = sb("ss", [128, NSG, 4])
            rstd = sb("rstd", [128, NSG, 4])
            ss2 = sb("ss2", [128, NSG, 4])
            rstd2 = sb("rstd2", [128, NSG, 4])
            pT = [ps("pT%d" % i, [128, KC, 128], BF16) for i in range(2)]
            pP = [ps("pP%d" % i, [128, KC, 128], BF16) for i in range(2)]
            pGt = [ps("pGt%d" % i, [128, 512]) for i in range(2)]
            pPp = [ps("pPp%d" % i, [128, 512]) for i in range(2)]
            NG = NT // 4

            def load_g(g):
                if 0 <= g < NG:
                    for i in range(4):
                        t = 4 * g + i
                        sl = (g % NSG) * 4 + i
                        P.dma("sp", hs[:, sl, :], h_s[tsl(t), :], reads=["hd%d" % t], writes=["hs%d" % sl])
                        P.dma("pool", pt[:, sl, :], p_d[tsl(t), :], writes=["pt%d" % sl])

            def stage_a(g):
                gp = g % NSG
                for i in range(4):
                    sl = gp * 4 + i
                    P.op("act", lambda e: e.activation(out=sqj[:, 0, :], in_=hs[:, sl, :], func=AF.Square, accum_out=ss[:, gp, i:i + 1]),
                         reads=["hs%d" % sl], writes=["sqj0", "ss%d" % gp])
                    yield
                P.op("dve", lambda e: e.tensor_scalar(ss[:, gp, :], ss[:, gp, :], 1.0 / D, EPS, op0=ALU.mult, op1=ALU.add),
                     reads=["ss%d" % gp], writes=["ss%d" % gp])
                yield
                P.op("act", lambda e: e.activation(out=ss[:, gp, :], in_=ss[:, gp, :], func=AF.Sqrt), reads=["ss%d" % gp], writes=["ss%d" % gp])
                yield
                P.op("dve", lambda e: e.reciprocal(rstd[:, gp, :], ss[:, gp, :]), reads=["ss%d" % gp], writes=["rstd%d" % gp])
                yield

            def tile_thread(g, i):
                gp = g % NSG
                sl = gp * 4 + i
                u = i % 2
                P.op("dve", lambda e: e.tensor_scalar(xn[:, u, :], hs[:, sl, :], rstd[:, gp, i:i + 1], None, op0=ALU.mult),
                     reads=["hs%d" % sl, "rstd%d" % gp], writes=["xn%d" % u])
                yield
                for k in range(KC):
                    P.op("pe", lambda e: e.transpose(pT[u][:, k, :], xn[:, u, tsl(k)], ident[:]),
                         reads=["xn%d" % u, "c_ident"], writes=["pT%d" % u])
                for k in range(2):
                    P.op("pe", lambda e: e.transpose(pP[u][:, k, :], pt[:, sl, tsl(k)], ident[:]),
                         reads=["pt%d" % sl, "c_ident"], writes=["pP%d" % u])
                yield
                P.op("dve", lambda e: e.tensor_tensor(hnT[:, u, :, :], pT[u][:], gB[:], op=ALU.mult),
                     reads=["p4cgB"], writes=["pT%d" % u, "hnT%d" % u])
                P.op("act", lambda e: e.copy(pTs[:, u, :, :], pP[u][:, 0:2, :]), reads=[], writes=["pP%d" % u, "pTs%d" % u])
                yield
                for c2 in range(2):
                    cs = slice(c2 * 512, (c2 + 1) * 512)
                    for k in range(KC):
                        P.op("pe", lambda e: e.matmul(pGt[u][:], hnT[:, u, k, :], wpg[:, k, cs], start=(k == 0), stop=(k == KC - 1)),
                             reads=["hnT%d" % u, "wpg"], writes=["pGt%d" % u])
                    for k in range(2):
                        P.op("pe", lambda e: e.matmul(pPp[u][:], pTs[:, u, k, :], wpp[:, k, cs], start=(k == 0), stop=(k == 1)),
                             reads=["pTs%d" % u, "wpp"], writes=["pPp%d" % u])
                    yield
                    P.op("act", lambda e: e.activation(out=gate[:, u, cs], in_=pGt[u][:], func=AF.Tanh, scale=0.5),
                         reads=[], writes=["pGt%d" % u, "gate%d" % u])
                    yield
                    P.op("dve", lambda e: e.scalar_tensor_tensor(gate[:, u, cs], gate[:, u, cs], 1.0, pPp[u][:], op0=ALU.add, op1=ALU.mult),
                         reads=["gate%d" % u], writes=["pPp%d" % u, "gate%d" % u])
                    yield
                P.op("dve", lambda e: e.scalar_tensor_tensor(hs[:, sl, :], gate[:, u, :], 0.5, hs[:, sl, :], op0=ALU.mult, op1=ALU.add),
                     reads=["gate%d" % u, "hs%d" % sl], writes=["hs%d" % sl])
                yield

            def stage_c(g):
                gp = g % NSG
                for i in range(4):
                    sl = gp * 4 + i
                    P.op("act", lambda e: e.activation(out=sqj[:, 1, :], in_=hs[:, sl, :], func=AF.Square, accum_out=ss2[:, gp, i:i + 1]),
                         reads=["hs%d" % sl], writes=["sqj1", "ss2%d" % gp])
                    yield
                P.op("dve", lambda e: e.tensor_scalar(ss2[:, gp, :], ss2[:, gp, :], 1.0 / D, EPS, op0=ALU.mult, op1=ALU.add),
                     reads=["ss2%d" % gp], writes=["ss2%d" % gp])
                yield
                P.op("act", lambda e: e.activation(out=ss2[:, gp, :], in_=ss2[:, gp, :], func=AF.Sqrt), reads=["ss2%d" % gp], writes=["ss2%d" % gp])
                yield
                P.op("dve", lambda e: e.reciprocal(rstd2[:, gp, :], ss2[:, gp, :]), reads=["ss2%d" % gp], writes=["rstd2%d" % gp])
                yield
                for i in range(4):
                    sl = gp * 4 + i
                    t = 4 * g + i
                    u = i % 2
                    P.op("dve", lambda e: e.scalar_tensor_tensor(ot[:, u, :], hs[:, sl, :], rstd2[:, gp, i:i + 1], gfin[:], op0=ALU.mult, op1=ALU.mult),
                         reads=["hs%d" % sl, "rstd2%d" % gp, "gfin"], writes=["ot%d" % u])
                    P.dma("sp", out_d[tsl(t), :], ot[:, u, :], reads=["ot%d" % u], writes=["outd%d" % t])
                    yield

            def seq(*gens):
                for g_ in gens:
                    yield from g_

            load_g(0)
            for tau in range(NG + 2):
                load_g(tau + 1)
                th = []
                if tau < NG:
                    th.append(stage_a(tau))
                if 0 <= tau - 1 < NG:
                    th.append(seq(tile_thread(tau - 1, 0), tile_thread(tau - 1, 2)))
                    th.append(seq(tile_thread(tau - 1, 1), tile_thread(tau - 1, 3)))
                if 0 <= tau - 2 < NG:
                    th.append(stage_c(tau - 2))
                P.run_threads(th)
            P.barrier()

    if "p1" in phases:
        ffn_phase("f1", x_d, h_s, w1a_d, w2a_d, 0, 1)
    if "p2" in phases:
        p2_phase()
    if "p3" in phases:
        p3_phase()
    if "p4a" in phases:
        p4a_phase()
    if "p4b" in phases:
        ffn_phase("f2", h_s, h_s, w1b_d, w2b_d, 2, None)
    if "p4c" in phases:
        p4c_phase()
    P.emit()
    return nc


def host_consts():
    j = np.arange(128)
    ident = np.eye(128, dtype=np.float32)
    nUincl = -(j[:, None] >= j[None, :]).astype(np.float32)
    nLstr = -(j[:, None] < j[None, :]).astype(np.float32)
    t = np.arange(512)
    sbmask = np.stack([(j[:, None] + 128 * r < t[None, :]).astype(np.float32) for r in range(4)], 1)
    cb = np.concatenate([ident, nUincl, nLstr, sbmask.reshape(128, 2048)], 1).astype(np.float32)
    dsaneg = np.where(j[None, :] > j[:, None], -1e30, 0.0).astype(np.float32)
    inv_freq = (500000.0 ** (-np.arange(0, 16, 2, dtype=np.float32) / 16)).astype(np.float32)
    pm = j % 64
    invf = np.where(pm < 16, inv_freq[pm % 8], 0.0).astype(np.float32)
    sgn = np.where(pm < 8, -1.0, np.where(pm < 16, 1.0, 0.0)).astype(np.float32)
    cf = np.concatenate([dsaneg, invf[:, None], sgn[:, None]], 1).astype(np.float32)
    return cb, cf


def make_in_maps(inputs, ncores=8):
    f = lambda a: np.ascontiguousarray(np.asarray(a), dtype=np.float32)
    x = f(inputs["x"])
    p = f(inputs["p"])[0]
    pos = np.ascontiguousarray(np.asarray(inputs["positions"]), dtype=np.int32)
    cb, cf = host_consts()
    gl = lambda g: f(g).reshape(KC, 128).T
    gcols = np.ascontiguousarray(np.concatenate([gl(inputs["ffn1_norm"][0]), gl(inputs["mix_norm"][0]),
                                                 gl(inputs["ffn2_norm"][0]), gl(inputs["ple_norm"][0])], 1))
    shared = {
        "w1a": f(inputs["ffn1_w1"][0]), "w2a": f(inputs["ffn1_w2"][0]), "win": f(inputs["w_in"][0]),
        "wosb": f(inputs["w_out_sb"][0]), "wod": f(inputs["w_out_dsa"][0]), "wo": f(inputs["w_out"][0]),
        "w1b": f(inputs["ffn2_w1"][0]), "w2b": f(inputs["ffn2_w2"][0]), "wpg": f(inputs["ple_w_gate"][0]),
        "wpp": f(inputs["ple_w_proj"][0]), "gcols": gcols, "gfin": f(inputs["final_norm"]).reshape(1, D),
        "cb": cb, "cf": cf,
    }
    maps = []
    for c in range(ncores):
        m = dict(shared)
        m["x"] = np.ascontiguousarray(x[NSEQ * c:NSEQ * (c + 1)].reshape(NTOK, D))
        m["p"] = np.ascontiguousarray(p[NSEQ * c:NSEQ * (c + 1)].reshape(NTOK, 256))
        m["pos"] = np.ascontiguousarray(pos[NSEQ * c:NSEQ * (c + 1)].reshape(1, NTOK))
        maps.append(m)
    return maps


def kernel(**inputs):
    nc = bass.Bass("TRN2", target_bir_lowering=False)
    build(nc)
    maps = make_in_maps(inputs, 8)
    res = run_bass_kernel_spmd(nc, maps, core_ids=list(range(8)))
    out = np.stack([r["out"].reshape(NSEQ, SEQ, D) for r in res.results], 0).reshape(16, SEQ, D)
    return out.astype(np.float32)
```

```python
import contextlib
import numpy as np
import concourse.bass as bass
import concourse.mybir as mybir
from concourse.bass_utils import run_bass_kernel_spmd

F32 = mybir.dt.float32
BF16 = mybir.dt.bfloat16
I32 = mybir.dt.int32
AF = mybir.ActivationFunctionType
ALU = mybir.AluOpType
AX = mybir.AxisListType

D = 1024
KC = 8
SEQ = 2048
NSEQ = 2
NTOK = NSEQ * SEQ
NT = NTOK // 128
TB = 512
NB = NTOK // TB
FF = 2816
FC = FF // 128
DIN = 4808
EPS = 1e-6
TOPK = 256
NBIS = 14
MASK_BIG = 30000.0
ACT_CHAINS = (1,)
OFF_QSB, OFF_KSB, OFF_VSB, OFF_QD, OFF_KD, OFF_VD, OFF_QI, OFF_KI, OFF_WI, OFF_GSB, OFF_GD = (
    0, 512, 1024, 1536, 2048, 2112, 2176, 2688, 2752, 2760, 3784)
TWO_PI = 6.283185307179586
CW1 = 6.28125
CW2 = TWO_PI - CW1


class _Key:
    __slots__ = ("w", "rs")

    def __init__(self):
        self.w = None
        self.rs = []


class _Rec:
    def __init__(self):
        self.call = None

    def __getattr__(self, name):
        def f(*a, **k):
            self.call = (name, a, k)
            return self
        return f


def _freeze(fn):
    rec = _Rec()
    fn(rec)
    name, a, k = rec.call
    return lambda e: getattr(e, name)(*a, **k)


class Prog:
    ENG = ("pe", "act", "dve", "pool", "sp")

    def __init__(self, nc, same_engine_sync=True, ndma_sems=8):
        self.nc = nc
        self.es = contextlib.ExitStack()
        self.streams = {e: [] for e in self.ENG}
        self.cnt = {e: 0 for e in self.ENG}
        self.sems = {}
        for e in self.ENG:
            self.sems["p_" + e] = self.es.enter_context(nc.semaphore("prog_" + e))
        self.known = {e: {} for e in self.ENG}
        self.same = same_engine_sync
        self.ndma = ndma_sems
        self.dq = {}
        self.keys = {}

    def key(self, name):
        k = self.keys.get(name)
        if k is None:
            k = _Key()
            self.keys[name] = k
        return k

    def _deps(self, reads, writes):
        ev = []
        for r in reads:
            k = self.key(r)
            if k.w is not None:
                ev.append(k.w)
        for w in writes:
            k = self.key(w)
            if k.w is not None:
                ev.append(k.w)
            ev.extend(k.rs)
        return ev

    def _waits(self, eng, evs):
        need = {}
        kn = self.known[eng]
        for (sid, val) in evs:
            if sid == "p_" + eng and (eng == "pe" or not self.same):
                continue
            if kn.get(sid, 0) >= val:
                continue
            if need.get(sid, 0) < val:
                need[sid] = val
        for sid, val in need.items():
            kn[sid] = val
        return list(need.items())

    def _commit(self, reads, writes, event):
        for r in reads:
            self.key(r).rs.append(event)
        for w in writes:
            k = self.key(w)
            k.w = event
            k.rs = []

    def op(self, eng, fn, reads=(), writes=()):
        waits = self._waits(eng, self._deps(reads, writes))
        self.cnt[eng] += 1
        event = ("p_" + eng, self.cnt[eng])
        self.streams[eng].append((waits, _freeze(fn), ("p_" + eng, 1)))
        self._commit(reads, writes, event)
        return event

    def dma(self, q, out, in_, reads=(), writes=()):
        d = self.dq.get(q)
        if d is None:
            ids = [f"d_{q}_{j}" for j in range(self.ndma)]
            for s in ids:
                self.sems[s] = self.es.enter_context(self.nc.semaphore(s))
            d = dict(i=0, ids=ids, vals=[0] * self.ndma, last=[None] * self.ndma)
            self.dq[q] = d
        j = d["i"] % self.ndma
        d["i"] += 1
        evs = self._deps(reads, writes)
        if d["last"][j] is not None:
            evs.append(d["last"][j])
        waits = self._waits(q, evs)
        d["vals"][j] += 16
        event = (d["ids"][j], d["vals"][j])
        d["last"][j] = event
        fn = lambda e, out=out, in_=in_: e.dma_start(out=out, in_=in_)
        self.streams[q].append((waits, fn, (d["ids"][j], 16)))
        self._commit(reads, writes, event)
        return event

    def _all_events(self):
        ev = []
        for q, d in self.dq.items():
            for e in d["last"]:
                if e is not None:
                    ev.append(e)
        for e in self.ENG:
            if self.cnt[e] > 0:
                ev.append(("p_" + e, self.cnt[e]))
        return ev

    def run_threads(self, gens):
        live = list(gens)
        while live:
            nxt = []
            for g in live:
                try:
                    next(g)
                    nxt.append(g)
                except StopIteration:
                    pass
            live = nxt

    def barrier(self):
        ev = self._all_events()
        for e in self.ENG:
            w = self._waits(e, ev)
            if w:
                self.streams[e].append((w, None, None))
        self.keys = {}

    def emit(self):
        nc = self.nc
        fw = self._waits("sp", self._all_events())
        self.streams["sp"].append((fw, None, None))
        with nc.Block() as block:
            def run(engname, handle):
                for waits, fn, inc in self.streams[engname]:
                    for sid, val in waits:
                        handle.wait_ge(self.sems[sid], val)
                    if fn is not None:
                        fn(handle).then_inc(self.sems[inc[0]], inc[1])

            @block.tensor
            def _(e):
                run("pe", e)

            @block.scalar
            def _(e):
                run("act", e)

            @block.vector
            def _(e):
                run("dve", e)

            @block.gpsimd
            def _(e):
                run("pool", e)

            @block.sync
            def _(e):
                run("sp", e)
        self.es.close()


def build(nc, dbg=False, phases=("p1", "p2", "p3", "p4a", "p4b", "p4c")):
    P = Prog(nc)

    def din(name, shape, dt=F32):
        return nc.dram_tensor(name, shape, dt, kind="ExternalInput").ap()

    def dscr(name, shape, dt):
        return nc.dram_tensor(name, shape, dt, kind=("ExternalOutput" if dbg else "Internal")).ap()

    x_d = din("x", [NTOK, D])
    p_d = din("p", [NTOK, 256])
    pos_d = din("pos", [1, NTOK], I32)
    w1a_d = din("w1a", [D, 2 * FF])
    w2a_d = din("w2a", [FF, D])
    win_d = din("win", [D, DIN])
    wosb_d = din("wosb", [512, D])
    wod_d = din("wod", [512, D])
    wo_d = din("wo", [D, D])
    w1b_d = din("w1b", [D, 2 * FF])
    w2b_d = din("w2b", [FF, D])
    wpg_d = din("wpg", [D, D])
    wpp_d = din("wpp", [256, D])
    gcols_d = din("gcols", [128, 4 * KC])
    gfin_d = din("gfin", [1, D])
    cb_d = din("cb", [128, 384 + 2048])
    cf_d = din("cf", [128, 130])
    out_d = nc.dram_tensor("out", [NTOK, D], F32, kind="ExternalOutput").ap()

    h_s = dscr("h_s", [NTOK, D], F32)
    uT_s = dscr("uT_s", [KC, 128, NTOK], BF16)
    qk_s = dscr("qk_s", [18, 128, NTOK], BF16)
    v_s = dscr("v_s", [NTOK, 576], BF16)
    wi_s = dscr("wi_s", [NTOK, 8], F32)
    ysbT_s = dscr("ysbT_s", [4, 128, NTOK], BF16)
    ydT_s = dscr("ydT_s", [4, 128, NTOK], BF16)

    def tsl(t):
        return slice(t * 128, (t + 1) * 128)

    ccount = [0]

    def load_consts(es, need_cb=True):
        ccount[0] += 1
        sb = lambda n, s, d=F32: es.enter_context(nc.sbuf_tensor("%s_%d" % (n, ccount[0]), s, d))
        c = {}
        c["gcols"] = sb("c_gcols", [128, 4 * KC])
        P.dma("sp", c["gcols"][:], gcols_d, writes=["c_gcols"])
        c["ident"] = sb("c_ident", [128, 128], BF16)
        P.dma("pool", c["ident"][:], cb_d[:, 0:128], writes=["c_ident"])
        return c

    def make_gB(es, c, which, name):
        gB = es.enter_context(nc.sbuf_tensor(name, [128, KC, 128], F32))
        src = c["gcols"][:, which * KC:(which + 1) * KC]
        P.op("dve", lambda e: e.tensor_copy(gB[:], src.unsqueeze(2).to_broadcast([128, KC, 128])),
             reads=["c_gcols"], writes=[name])
        return gB

    def rstd_from_ss(ss, rstd, n, rkeys, wkey):
        P.op("dve", lambda e: e.tensor_scalar(ss[:, 0:n], ss[:, 0:n], 1.0 / D, EPS, op0=ALU.mult, op1=ALU.add),
             reads=rkeys, writes=rkeys)
        P.op("act", lambda e: e.activation(out=ss[:, 0:n], in_=ss[:, 0:n], func=AF.Sqrt), reads=rkeys, writes=rkeys)
        P.op("dve", lambda e: e.reciprocal(rstd[:, 0:n], ss[:, 0:n]), reads=rkeys, writes=[wkey])

    def ffn_phase(tag, src_d, dst_d, w1_d, w2_d, gsel, post_gsel):
        with contextlib.ExitStack() as es:
            sb = lambda n, s, d=F32: es.enter_context(nc.sbuf_tensor(tag + n, s, d))
            ps = lambda n, s, d=F32: es.enter_context(nc.psum_tensor(tag + n, s, d))
            c = load_consts(es)
            w1 = sb("w1", [128, KC, 2 * FF], BF16)
            w2 = sb("w2", [128, FC, D], BF16)
            for k in range(KC):
                P.dma("pool", w1[:, k, :], w1_d[tsl(k), :], writes=["w1g0", "w1g1", "w1g2", "w1g3"])
            for k in range(FC):
                P.dma("pool", w2[:, k, :], w2_d[tsl(k), :], writes=["w2"])
            gB = make_gB(es, c, gsel, tag + "gB")
            gB2 = make_gB(es, c, post_gsel, tag + "gB2") if post_gsel is not None else None
            NXS = 6 if post_gsel is not None else 8
            xs = sb("xs", [128, NXS, D])
            issued = set()

            def issue_load(t):
                if t in issued or t >= NT:
                    return
                issued.add(t)
                P.dma("sp", xs[:, t % NXS, :], src_d[tsl(t), :], reads=["hd%d" % t], writes=["xs%d" % (t % NXS)])
            xn = sb("xn", [128, 2, D], BF16)
            xnT = sb("xnT", [128, KC, TB], BF16)
            gT = sb("gT", [128, FC, TB], BF16)
            stmp = sb("stmp", [128, 2, TB])
            ss = sb("ss", [128, 2, 4])
            rstd = sb("rstd", [128, 2, 4])
            ss2 = sb("ss2", [128, 2, 4])
            rstd2 = sb("rstd2", [128, 2, 4])
            ust = sb("ust", [128, 2, KC, 128], BF16) if post_gsel is not None else None
            pT = [ps("pT%d" % i, [128, KC, 128], BF16) for i in range(2)]
            pA = [ps("pA%d" % i, [128, TB]) for i in range(2)]
            pB = [ps("pB%d" % i, [128, TB]) for i in range(2)]
            pO = [ps("pO%d" % i, [128, 512]) for i in range(2)]
            ident = c["ident"]
            xnc = [0]
            ptc = [0]

            sqjunk = stmp[:].rearrange("p a b -> p (a b)")

            def sq_stat(t, dst_col, dst_key):
                sl = t % NXS
                P.op("act", lambda e: e.activation(out=sqjunk, in_=xs[:, sl, :], func=AF.Square, accum_out=dst_col),
                     reads=["xs%d" % sl], writes=["stmp0", "stmp1", dst_key])

            def norm_front(tile_ap, xs_key, rstd_col, rstd_key):
                s_ = xnc[0] % 2
                xnc[0] += 1
                P.op("dve", lambda e: e.tensor_scalar(xn[:, s_, :], tile_ap, rstd_col, None, op0=ALU.mult),
                     reads=[xs_key, rstd_key], writes=["xn%d" % s_])
                return s_

            def norm_pe(s_, gBt, gB_key, out_ap, out_key):
                q = ptc[0] % 2
                ptc[0] += 1
                for k in range(KC):
                    P.op("pe", lambda e: e.transpose(pT[q][:, k, :], xn[:, s_, tsl(k)], ident[:]),
                         reads=["xn%d" % s_, "c_ident"], writes=["pT%d" % q])
                P.op("dve", lambda e: e.tensor_tensor(out_ap, pT[q][:], gBt[:], op=ALU.mult),
                     reads=[gB_key], writes=["pT%d" % q, out_key])

            def prep_stats(tl, b_):
                sp2 = b_ % 2
                i0, i1 = tl[0] - 4 * b_, tl[-1] - 4 * b_ + 1
                key = "ss%d_%d" % (sp2, i0)
                for t in tl:
                    issue_load(t)
                    i = t - 4 * b_
                    sq_stat(t, ss[:, sp2, i:i + 1], key)
                rstd_from_ss(ss[:, sp2, i0:i1], rstd[:, sp2, i0:i1], i1 - i0, [key], "rstd%d_%d" % (sp2, i0))
                return "rstd%d_%d" % (sp2, i0)

            class Job:
                pass

            def prep_job(t, b_, rkey):
                i = t - 4 * b_
                sl = t % NXS
                j_ = Job()
                j_.front = lambda: norm_front(xs[:, sl, :], "xs%d" % sl, rstd[:, b_ % 2, i:i + 1], rkey)
                j_.back = lambda s_: norm_pe(s_, gB, tag + "gB", xnT[:, :, tsl(i)], "xnT")
                return j_

            def post_job(t, b_):
                i = t - 4 * b_
                sl = t % NXS
                u = t % 2
                sp2 = b_ % 2
                key = "ss2_%d_%d" % (sp2, i)
                sq_stat(t, ss2[:, sp2, i:i + 1], key)
                rstd_from_ss(ss2[:, sp2, i:i + 1], rstd2[:, sp2, i:i + 1], 1, [key], "rstd2_%d_%d" % (sp2, i))
                j_ = Job()
                j_.front = lambda: norm_front(xs[:, sl, :], "xs%d" % sl, rstd2[:, sp2, i:i + 1], "rstd2_%d_%d" % (sp2, i))

                def back(s_):
                    norm_pe(s_, gB2, tag + "gB2", ust[:, u, :, :], "ust%d" % u)
                    P.dma("sp", uT_s[:, :, tsl(t)].rearrange("c p t -> p c t"), ust[:, u, :, :],
                          reads=["ust%d" % u], writes=["uTd%d" % t])
                j_.back = back
                return j_

            def run_jobs(jobs):
                ss_ = [j_.front() for j_ in jobs]
                for j_, s_ in zip(jobs, ss_):
                    j_.back(s_)

            n_early = NXS - 4
            rk0 = prep_stats([0, 1, 2, 3], 0)
            for t in range(4):
                run_jobs([prep_job(t, 0, rk0)])
            for b in range(NB):
                tiles = [4 * b + i for i in range(4)]
                for j in range(FC):
                    q = j % 2
                    for k in range(KC):
                        P.op("pe", lambda e, k=k, j=j, q=q: e.matmul(pA[q][:], w1[:, k, tsl(j)], xnT[:, k, :],
                                                                    start=(k == 0), stop=(k == KC - 1)),
                             reads=["w1g%d" % (j // 6), "xnT"], writes=["pA%d" % q])
                    for k in range(KC):
                        P.op("pe", lambda e, k=k, j=j, q=q: e.matmul(pB[q][:], w1[:, k, FF + j * 128:FF + (j + 1) * 128],
                                                                    xnT[:, k, :], start=(k == 0), stop=(k == KC - 1)),
                             reads=["w1g%d" % (j // 6), "xnT"], writes=["pB%d" % q])
                    P.op("act", lambda e, q=q: e.activation(out=stmp[:, q, :], in_=pA[q][:], func=AF.Silu),
                         reads=[], writes=["pA%d" % q, "stmp%d" % q])
                    P.op("dve", lambda e, q=q, j=j: e.tensor_tensor(gT[:, j, :], stmp[:, q, :], pB[q][:], op=ALU.mult),
                         reads=["stmp%d" % q], writes=["pB%d" % q, "gT"])
                nxt = [t for t in range(4 * b + 4, 4 * b + 4 + n_early) if t < NT]
                ejobs = []
                if nxt:
                    rk = prep_stats(nxt, b + 1)
                    ejobs = [prep_job(t, b + 1, rk) for t in nxt]
                pjobs = []
                oc = 0
                for i, t in enumerate(tiles):
                    sl = t % NXS
                    slot_jobs = []
                    if i < len(ejobs):
                        slot_jobs.append(ejobs[i])
                    if pjobs:
                        slot_jobs.append(pjobs.pop(0))
                    fronts = [j_.front() for j_ in slot_jobs]
                    for c2 in range(2):
                        q = oc % 2
                        oc += 1
                        for j in range(FC):
                            P.op("pe", lambda e, j=j, i=i, c2=c2, q=q: e.matmul(
                                pO[q][:], gT[:, j, tsl(i)], w2[:, j, c2 * 512:(c2 + 1) * 512],
                                start=(j == 0), stop=(j == FC - 1)),
                                reads=["w2", "gT"], writes=["pO%d" % q])
                        P.op("dve", lambda e, q=q, sl=sl, c2=c2: e.scalar_tensor_tensor(
                            xs[:, sl, c2 * 512:(c2 + 1) * 512], pO[q][:], 0.5, xs[:, sl, c2 * 512:(c2 + 1) * 512],
                            op0=ALU.mult, op1=ALU.add),
                            reads=["xs%d" % sl], writes=["pO%d" % q, "xs%d" % sl])
                    P.dma("sp", dst_d[tsl(t), :], xs[:, sl, :], reads=["xs%d" % sl], writes=["hd%d" % t])
                    for j_, s_ in zip(slot_jobs, fronts):
                        j_.back(s_)
                    if post_gsel is not None:
                        pjobs.append(post_job(t, b))
                    if post_gsel is not None and n_early == 2 and i in (1, 2):
                        issue_load(4 * b + 5 + i)
                while pjobs:
                    run_jobs([pjobs.pop(0)])
                late = [t for t in range(4 * b + 4 + n_early, 4 * b + 8) if t < NT]
                if late:
                    rk = prep_stats(late, b + 1)
                    for t in late:
                        run_jobs([prep_job(t, b + 1, rk)])
            P.barrier()

    def p2_phase():
        with contextlib.ExitStack() as es:
            sb = lambda n, s, d=F32: es.enter_context(nc.sbuf_tensor("p2" + n, s, d))
            ps = lambda n, s, d=F32: es.enter_context(nc.psum_tensor("p2" + n, s, d))
            win = sb("win", [128, KC, 2760], BF16)
            for k in range(KC):
                P.dma("pool", win[:, k, :], win_d[tsl(k), 0:2760], writes=["win"])
            wp = sb("wp", [128, KC, 1280], BF16)
            wk2 = sb("wk2", [128, KC, 256], BF16)
            cf = sb("cf", [128, 130])
            P.dma("sp", cf[:], cf_d, writes=["cf"])
            posi = sb("posi", [128, NTOK], I32)
            P.dma("sp", posi[:], pos_d.broadcast_to([128, NTOK]), writes=["posi"])
            ang = sb("ang", [128, NTOK])
            kk = sb("kk", [128, NTOK])
            kki = sb("kki", [128, NTOK], I32)
            Ct = sb("Ct", [128, NTOK])
            St = sb("St", [128, NTOK])
            invf = cf[:, 128:129]
            sgn = cf[:, 129:130]
            for h2 in range(2):
                hc = slice(h2 * SEQ, (h2 + 1) * SEQ)
                kA, kK, kI = "ang%d" % h2, "kk%d" % h2, "kki%d" % h2
                P.op("dve", lambda e: e.tensor_copy(ang[:, hc], posi[:, hc]), reads=["posi"], writes=[kA])
                P.op("dve", lambda e: e.tensor_scalar(ang[:, hc], ang[:, hc], invf, None, op0=ALU.mult), reads=[kA, "cf"], writes=[kA])

                def reduce_to(dst, shift, key):
                    P.op("dve", lambda e: e.tensor_scalar(kk[:, hc], ang[:, hc], shift, 1.0 / TWO_PI, op0=ALU.add, op1=ALU.mult),
                         reads=[kA], writes=[kK])
                    P.op("dve", lambda e: e.tensor_copy(kki[:, hc], kk[:, hc]), reads=[kK], writes=[kI])
                    P.op("dve", lambda e: e.tensor_copy(kk[:, hc], kki[:, hc]), reads=[kI], writes=[kK])
                    P.op("dve", lambda e: e.scalar_tensor_tensor(dst[:, hc], kk[:, hc], -CW1, ang[:, hc], op0=ALU.mult, op1=ALU.add),
                         reads=[kK, kA], writes=[key])
                    P.op("dve", lambda e: e.scalar_tensor_tensor(dst[:, hc], kk[:, hc], -CW2, dst[:, hc], op0=ALU.mult, op1=ALU.add),
                         reads=[kK, key], writes=[key])
                    P.op("dve", lambda e: e.tensor_scalar(dst[:, hc], dst[:, hc], shift, 3.1415925, op0=ALU.add, op1=ALU.min),
                         reads=[key], writes=[key])
                    P.op("dve", lambda e: e.tensor_scalar(dst[:, hc], dst[:, hc], -3.1415925, None, op0=ALU.max), reads=[key], writes=[key])
                    P.op("act", lambda e: e.activation(out=dst[:, hc], in_=dst[:, hc], func=AF.Sin), reads=[key], writes=[key])

                reduce_to(St, 0.0, "St%d" % h2)
                P.op("dve", lambda e: e.tensor_scalar(St[:, hc], St[:, hc], sgn, None, op0=ALU.mult), reads=["cf", "St%d" % h2], writes=["St%d" % h2])
                reduce_to(Ct, float(np.pi / 2), "Ct%d" % h2)
            P.op("pool", lambda e: e.memset(wp[:], 0.0), writes=["wp"])
            for hh in range(2):
                P.op("pool", lambda e, hh=hh: e.tensor_copy(wk2[:, :, hh * 64:(hh + 1) * 64], win[:, :, OFF_KD:OFF_KD + 64]),
                     reads=["win"], writes=["wk2"])
                P.op("pool", lambda e, hh=hh: e.tensor_copy(wk2[:, :, 128 + hh * 64:128 + (hh + 1) * 64],
                                                            win[:, :, OFF_KI:OFF_KI + 64]), reads=["win"], writes=["wk2"])
            pbase = [OFF_QD + 128 * j for j in range(4)] + [OFF_QI + 128 * j for j in range(4)]
            for pc in range(10):
                for hh in range(2):
                    if pc < 8:
                        b0 = pbase[pc] + 64 * hh
                    else:
                        b0 = OFF_KD if pc == 8 else OFF_KI
                    o = pc * 128 + 64 * hh
                    P.op("pool", lambda e, o=o, b0=b0: e.tensor_copy(wp[:, :, o:o + 8], win[:, :, b0 + 8:b0 + 16]),
                         reads=["win"], writes=["wp"])
                    P.op("pool", lambda e, o=o, b0=b0: e.tensor_copy(wp[:, :, o + 8:o + 16], win[:, :, b0:b0 + 8]),
                         reads=["win"], writes=["wp"])
            uT = sb("uT", [128, 2, KC, TB], BF16)
            fst = sb("fst", [128, 4, TB], BF16)
            t1 = sb("t1", [128, 2, TB])
            t2 = sb("t2", [128, 2, TB])
            vst = sb("vst", [128, 2, 576], BF16)
            wist = sb("wist", [128, 2, 8])
            pA = [ps("pA%d" % i, [128, TB]) for i in range(3)]
            pB = [ps("pB%d" % i, [128, TB]) for i in range(2)]
            pV = [ps("pV%d" % i, [128, 512]) for i in range(2)]
            pW = ps("pW", [128, 128])
            chunks = []
            for j in range(4):
                chunks.append((j, win, OFF_QSB + 128 * j, None))
            for j in range(4):
                chunks.append((4 + j, win, OFF_KSB + 128 * j, None))
            for j in range(4):
                chunks.append((8 + j, win, OFF_QD + 128 * j, j))
            for j in range(4):
                chunks.append((12 + j, win, OFF_QI + 128 * j, 4 + j))
            chunks.append((16, wk2, 0, 8))
            chunks.append((17, wk2, 128, 9))
            ca = cbn = fs = rc = vc = 0
            def load_uT(b):
                if b < NB:
                    P.dma("sp", uT[:, b % 2, :, :], uT_s[:, :, b * TB:(b + 1) * TB].rearrange("c p t -> p c t"),
                          reads=["uTd%d" % t for t in range(4 * b, 4 * b + 4)], writes=["uT%d" % (b % 2)])
            load_uT(0)
            for b in range(NB):
                u = b % 2
                load_uT(b + 1)
                tok = slice(b * TB, (b + 1) * TB)
                for (ci, wt, off, pidx) in chunks:
                    qa = ca % 3
                    ca += 1
                    for k in range(KC):
                        P.op("pe", lambda e, k=k, wt=wt, off=off, qa=qa: e.matmul(pA[qa][:], wt[:, k, off:off + 128], uT[:, u, k, :],
                                                                                 start=(k == 0), stop=(k == KC - 1)),
                             reads=["win", "wk2", "uT%d" % u], writes=["pA%d" % qa])
                    f = fs % 4
                    fs += 1
                    qscale = 0.125 if (ci < 4 or 8 <= ci < 12) else 1.0
                    if pidx is None:
                        P.op("act", lambda e, qa=qa, f=f: e.activation(out=fst[:, f, :], in_=pA[qa][:], func=AF.Identity, scale=qscale),
                             reads=[], writes=["pA%d" % qa, "fst%d" % f])
                    else:
                        qb = cbn % 2
                        cbn += 1
                        for k in range(KC):
                            P.op("pe", lambda e, k=k, pidx=pidx, qb=qb: e.matmul(pB[qb][:], wp[:, k, pidx * 128:(pidx + 1) * 128],
                                                                                uT[:, u, k, :], start=(k == 0), stop=(k == KC - 1)),
                                 reads=["wp", "uT%d" % u], writes=["pB%d" % qb])
                        r = rc % 2
                        rc += 1
                        P.op("dve", lambda e, qa=qa, r=r: e.scalar_tensor_tensor(t1[:, r, :], pA[qa][:], qscale, Ct[:, tok], op0=ALU.mult, op1=ALU.mult),
                             reads=["Ct%d" % (b // 4)], writes=["pA%d" % qa, "t1%d" % r])
                        P.op("dve", lambda e, qb=qb, r=r: e.scalar_tensor_tensor(t2[:, r, :], pB[qb][:], qscale, St[:, tok], op0=ALU.mult, op1=ALU.mult),
                             reads=["St%d" % (b // 4)], writes=["pB%d" % qb, "t2%d" % r])
                        P.op("pool", lambda e, r=r, f=f: e.tensor_tensor(fst[:, f, :], t1[:, r, :], t2[:, r, :], op=ALU.add),
                             reads=["t1%d" % r, "t2%d" % r], writes=["fst%d" % f])
                    P.dma("sp", qk_s[ci, :, tok], fst[:, f, :], reads=["fst%d" % f], writes=["qkd%d_%d" % (ci, b)])
                for i in range(4):
                    t = 4 * b + i
                    q = vc % 2
                    vc += 1
                    for k in range(KC):
                        P.op("pe", lambda e, k=k, i=i, q=q: e.matmul(pV[q][:], uT[:, u, k, tsl(i)], win[:, k, OFF_VSB:OFF_VSB + 512],
                                                                    start=(k == 0), stop=(k == KC - 1)),
                             reads=["win", "uT%d" % u], writes=["pV%d" % q])
                    for k in range(KC):
                        P.op("pe", lambda e, k=k, i=i: e.matmul(pW[:, 0:64], uT[:, u, k, tsl(i)], win[:, k, OFF_VD:OFF_VD + 64],
                                                               start=(k == 0), stop=(k == KC - 1), skip_group_check=True),
                             reads=["win", "uT%d" % u], writes=["pW"])
                    for k in range(KC):
                        P.op("pe", lambda e, k=k, i=i: e.matmul(pW[:, 64:72], uT[:, u, k, tsl(i)], win[:, k, OFF_WI:OFF_WI + 8],
                                                               start=False, stop=(k == KC - 1), skip_group_check=True),
                             reads=["win", "uT%d" % u], writes=["pW"])
                    P.op("act", lambda e, q=q: e.copy(vst[:, q, 0:512], pV[q][:]), reads=[], writes=["pV%d" % q, "vst%d" % q])
                    P.op("dve", lambda e, q=q: e.tensor_copy(vst[:, q, 512:576], pW[:, 0:64]), reads=[], writes=["pW", "vst%d" % q])
                    P.op("dve", lambda e, q=q: e.tensor_scalar(wist[:, q, :], pW[:, 64:72], float(8 ** -0.5 * 0.125), None, op0=ALU.mult),
                         reads=[], writes=["pW", "wist%d" % q])
                    P.dma("sp", v_s[tsl(t), :], vst[:, q, :], reads=["vst%d" % q], writes=["vd%d" % t])
                    P.dma("sp", wi_s[tsl(t), :], wist[:, q, :], reads=["wist%d" % q], writes=["wid%d" % t])
            P.barrier()

    def p3_phase():
        with contextlib.ExitStack() as es:
            sb = lambda n, s, d=F32: es.enter_context(nc.sbuf_tensor("p3" + n, s, d))
            cb = sb("cb", [128, 384 + 2048], BF16)
            P.dma("pool", cb[:], cb_d, writes=["cb"])
            cf = sb("cf", [128, 130])
            P.dma("sp", cf[:], cf_d, writes=["cf"])
            ident = cb[:, 0:128]
            nUincl = cb[:, 128:256]
            nLstr = cb[:, 256:384]
            sbmask = cb[:, 384:384 + 2048].rearrange("p (r t) -> p r t", r=4)
            dsaneg = cf[:, 0:128]
            qz = [sb("qz%d" % i, [128, 4, SEQ], BF16) for i in range(2)]
            P.op("pool", lambda e: e.memset(qz[0][64:128, :, :], 0.0), writes=["qz0z"])
            P.op("pool", lambda e: e.memset(qz[1][0:64, :, :], 0.0), writes=["qz1z"])
            ksb = sb("ksb", [128, 4, SEQ], BF16)
            nksb = sb("nksb", [128, 4, SEQ], BF16)
            qd = sb("qd", [128, 4, SEQ], BF16)
            qi = sb("qi", [128, 4, SEQ], BF16)
            kdz = sb("kdz", [128, 2, SEQ], BF16)
            kiz = sb("kiz", [128, 2, SEQ], BF16)
            for kz_ in (kdz, kiz):
                P.op("pool", lambda e: e.memset(kz_[64:128, 0, :], 0.0), writes=["kzz"])
                P.op("pool", lambda e: e.memset(kz_[0:64, 1, :], 0.0), writes=["kzz"])
            v = sb("v", [128, 16, 578], BF16)
            wi = sb("wi", [128, 16, 8])
            P.op("pool", lambda e: e.memset(v[:, :, 576:578], 0.0), writes=["vone"])
            P.op("pool", lambda e: e.memset(v[:, :, 576:577], 1.0), writes=["vone"])
            def load_sb(sq):
                tok = slice(sq * SEQ, (sq + 1) * SEQ)
                for j in range(4):
                    P.dma("sp", qz[0][0:64, j, :], qk_s[j, 0:64, tok], writes=["qsb"])
                    P.dma("sp", qz[1][64:128, j, :], qk_s[j, 64:128, tok], writes=["qsb"])
                for j in range(4):
                    P.dma("sp", ksb[:, j, :], qk_s[4 + j, :, tok], writes=["ksb"])
                for j in range(4):
                    P.op("dve", lambda e: e.tensor_scalar(nksb[:, j, :], ksb[:, j, :], -1.0, None, op0=ALU.mult), reads=["ksb"], writes=["nksb"])

            def load_rest(sq):
                tok = slice(sq * SEQ, (sq + 1) * SEQ)
                P.dma("act", v[:, :, 0:576], v_s[tok, :].rearrange("(n p) c -> p n c", p=128), writes=["v"])
                P.dma("act", wi[:], wi_s[tok, :].rearrange("(n p) c -> p n c", p=128), writes=["wi"])
                for (tile_, c0, key) in ((qd, 8, "qd"), (qi, 12, "qi")):
                    for j in range(4):
                        P.dma("sp", tile_[:, j, :], qk_s[c0 + j, :, tok], writes=[key])
                for (kz_, ci, key) in ((kdz, 16, "kd2"), (kiz, 17, "ki2")):
                    P.dma("sp", kz_[0:64, 0, :], qk_s[ci, 0:64, tok], writes=[key])
                    P.dma("sp", kz_[64:128, 1, :], qk_s[ci, 64:128, tok], writes=[key])

            load_sb(0)
            for sq in range(NSEQ):
                load_rest(sq)

                with contextlib.ExitStack() as es2:
                  if "nosb" not in phases:
                    sb2 = lambda n, s, d=F32: es2.enter_context(nc.sbuf_tensor("sb%d" % sq + n, s, d))
                    ps2 = lambda n, s, d=F32: es2.enter_context(nc.psum_tensor("sb%d" % sq + n, s, d))
                    NS = 4
                    E = sb2("E", [128, NS, 512])
                    SP = sb2("SP", [128, NS, 2, 512], BF16)
                    A = sb2("A", [128, NS, 2, 512], BF16)
                    yacc = sb2("yacc", [64, NS, 512])
                    yst = sb2("yst", [64, NS, 512], BF16)
                    pZ = [ps2("pZ%d" % i, [128, 512]) for i in range(2)]
                    pC = [ps2("pC%d" % i, [128, 512]) for i in range(NS)]
                    pY = [ps2("pY%d" % i, [64, 512]) for i in range(2)]
                    mask128 = sbmask[:, 0, 0:128]

                    def sb_stream(s_, h, qc):
                        j, half = h // 2, h % 2
                        po = slice(64 * half, 64 * half + 64)
                        zb = s_ % 2
                        S = "s%d" % s_
                        kmax = 4 * qc + 3
                        nstep = kmax + 1
                        P.op("dve", lambda e: e.memset(yacc[:, s_, :], 0.0), writes=["yacc" + S])
                        if s_ >= 2:
                            yield
                        for step in range(nstep):
                            kb = kmax - step
                            r = kb - 4 * qc
                            c0 = 128 * max(0, r)
                            cols = slice(c0, 512)
                            dcols = slice(c0, c0 + 128)
                            qcols = slice(qc * 512 + c0, (qc + 1) * 512)
                            kcols = tsl(kb)
                            par = step % 2
                            spk = "SP%s%d" % (S, par)
                            ak = "A%s%d" % (S, par)
                            P.op("pe", lambda e: e.matmul(pZ[zb][:, cols], ksb[:, j, kcols], qz[half][:, j, qcols], start=True, stop=True,
                                                          skip_group_check=True),
                                 reads=["ksb", "qsb", "qz0z", "qz1z"], writes=["pZ%d" % zb])
                            yield
                            P.op("act", lambda e: e.activation(out=E[:, s_, cols], in_=pZ[zb][:, cols], func=AF.Exp),
                                 reads=[], writes=["pZ%d" % zb, "E" + S])
                            P.op("act", lambda e: e.activation(out=SP[:, s_, par, cols], in_=E[:, s_, cols], func=AF.Ln, bias=1.0),
                                 reads=["E" + S], writes=[spk])
                            if r >= 0:
                                P.op("dve", lambda e: e.tensor_tensor(SP[:, s_, par, dcols], SP[:, s_, par, dcols], mask128, op=ALU.mult),
                                     reads=["cb", spk], writes=[spk])
                            yield
                            P.op("pe", lambda e: e.matmul(pC[s_][:, cols], ksb[:, j, kcols], qz[half][:, j, qcols], start=(step == 0), stop=False,
                                                          skip_group_check=True),
                                 reads=["ksb", "qsb"], writes=["pC" + S])
                            P.op("pe", lambda e: e.matmul(pC[s_][:, cols], nUincl, SP[:, s_, par, cols], start=False, stop=True,
                                                          skip_group_check=True),
                                 reads=["cb", spk], writes=["pC" + S])
                            yield
                            P.op("act", lambda e: e.activation(out=A[:, s_, par, cols], in_=pC[s_][:, cols], func=AF.Exp),
                                 reads=[], writes=["pC" + S, ak])
                            if r >= 0:
                                P.op("dve", lambda e: e.tensor_tensor(A[:, s_, par, dcols], A[:, s_, par, dcols], mask128, op=ALU.mult),
                                     reads=["cb", ak], writes=[ak])
                            yield
                            P.op("pe", lambda e: e.matmul(pY[zb][:, cols], v[:, kb, h * 64:(h + 1) * 64], A[:, s_, par, cols],
                                                          start=True, stop=True, skip_group_check=True),
                                 reads=["v", ak], writes=["pY%d" % zb])
                            if step < nstep - 1:
                                P.op("pe", lambda e: e.matmul(pC[s_][:, cols], nksb[:, j, kcols], qz[half][:, j, qcols], start=False, stop=False,
                                                              skip_group_check=True),
                                     reads=["nksb", "qsb"], writes=["pC" + S])
                                P.op("pe", lambda e: e.matmul(pC[s_][:, cols], nLstr, SP[:, s_, par, cols], start=False, stop=False,
                                                              skip_group_check=True),
                                     reads=["cb", spk], writes=["pC" + S])
                            P.op("dve", lambda e: e.tensor_tensor(yacc[:, s_, cols], yacc[:, s_, cols], pY[zb][:, cols], op=ALU.add),
                                 reads=["yacc" + S], writes=["pY%d" % zb, "yacc" + S])
                            yield
                        P.op("dve", lambda e: e.tensor_copy(yst[:, s_, :], yacc[:, s_, :]), reads=["yacc" + S], writes=["yst" + S])
                        P.dma("sp", ysbT_s[h // 2, 64 * (h % 2):64 * (h % 2) + 64, sq * SEQ + qc * 512: sq * SEQ + (qc + 1) * 512], yst[:, s_, :],
                              reads=["yst" + S], writes=["ysbd%d_%d" % (h, sq * 4 + qc)])

                    for qc in range(4):
                        for g in range(2):
                            P.run_threads([sb_stream(s_, 4 * g + s_, qc) for s_ in range(NS)])
                    P.barrier()
                    if sq + 1 < NSEQ:
                        load_sb(sq + 1)

                with contextlib.ExitStack() as es2:
                  if "nodsa" not in phases:
                    sb2 = lambda n, s, d=F32: es2.enter_context(nc.sbuf_tensor("ds%d" % sq + n, s, d))
                    ps2 = lambda n, s, d=F32: es2.enter_context(nc.psum_tensor("ds%d" % sq + n, s, d))
                    Sc = sb2("Sc", [128, 4, SEQ])
                    R = sb2("R", [128, 2, 512])
                    Mb = sb2("Mb", [128, 2, SEQ], BF16)
                    MT = sb2("MT", [128, 4, 16, 128], BF16)
                    PT = sb2("PT", [128, 3, 512], BF16)
                    yd = sb2("yd", [128, 512], BF16)
                    ydst = sb2("ydst", [128, 2, 4, 128], BF16)
                    sm = sb2("sm", [128, 16])
                    rec = sb2("rec", [128, 8, 1])
                    pD = [ps2("pD%d" % i, [128, 512]) for i in range(2)]
                    pM = ps2("pM", [128, 8, 128], BF16)
                    pL = [ps2("pL%d" % i, [128, 512]) for i in range(2)]
                    pYd = ps2("pYd", [128, 2, 512])
                    pM2 = ps2("pM2", [128, 8, 128], BF16)
                    cnts = dict(d=0, l=0)

                    def stage1(i):
                        nk = (i + 1) * 128
                        nch = (nk + 511) // 512
                        tcols = tsl(i)
                        z = i % 4
                        sck = "Sc%d" % z
                        for hh in range(8):
                            j, s_ = hh // 2, hh % 2
                            po = slice(64 * s_, 64 * s_ + 64)
                            for c in range(nch):
                                n = min(512, nk - c * 512)
                                q = cnts["d"] % 2
                                cnts["d"] += 1
                                cc = slice(c * 512, c * 512 + n)
                                P.op("pe", lambda e: e.matmul(pD[q][:, 0:n], qi[:, j, tcols], kiz[:, s_, cc], start=True, stop=True),
                                     reads=["qi", "ki2", "kzz"], writes=["pD%d" % q])
                                P.op("act", lambda e: e.activation(out=R[:, q, 0:n], in_=pD[q][:, 0:n], func=AF.Relu),
                                     reads=[], writes=["pD%d" % q, "R%d" % q])
                                if hh == 0:
                                    P.op("dve", lambda e: e.tensor_scalar(Sc[:, z, cc], R[:, q, 0:n], wi[:, i, 0:1], None, op0=ALU.mult),
                                         reads=["R%d" % q, "wi"], writes=[sck])
                                else:
                                    P.op("dve", lambda e: e.scalar_tensor_tensor(Sc[:, z, cc], R[:, q, 0:n], wi[:, i, hh:hh + 1], Sc[:, z, cc],
                                                                               op0=ALU.mult, op1=ALU.add),
                                         reads=["R%d" % q, "wi", sck], writes=[sck])
                                yield

                    def stage2(i):
                        nk = (i + 1) * 128
                        z = i % 4
                        w = i % 2
                        smk = "sm%d" % w
                        mbk = "Mb%d" % w
                        lo, hi, mid, cnt, dlt = (sm[:, 8 * w + c:8 * w + c + 1] for c in range(5))
                        sck = "Sc%d" % z
                        if i >= 2:
                            P.op("dve", lambda e: e.tensor_reduce(lo, Sc[:, z, 0:nk], axis=AX.X, op=ALU.min), reads=[sck], writes=[smk])
                        P.op("dve", lambda e: e.tensor_tensor(Sc[:, z, nk - 128:nk], Sc[:, z, nk - 128:nk], dsaneg, op=ALU.add),
                             reads=["cf", sck], writes=[sck])
                        yield
                        if i >= 2:
                            P.op("dve", lambda e: e.tensor_reduce(hi, Sc[:, z, 0:nk], axis=AX.X, op=ALU.max), reads=[sck], writes=[smk])
                            P.op("dve", lambda e: e.tensor_tensor(hi, hi, lo, op=ALU.subtract), reads=[smk], writes=[smk])
                            on_act = w in ACT_CHAINS
                            sg = -1.0 if on_act else 1.0
                            if on_act:
                                P.op("dve", lambda e: e.tensor_scalar(lo, lo, -1.0, None, op0=ALU.mult), reads=[smk], writes=[smk])
                            yield
                            thr = float(2 * TOPK - 1 - nk) if on_act else (float(TOPK) - 0.5)
                            for it in range(NBIS):
                                ck = float(0.5 ** (it + 1))
                                P.op("dve", lambda e: e.scalar_tensor_tensor(mid, hi, sg * ck, lo, op0=ALU.mult, op1=ALU.add),
                                     reads=[smk], writes=[smk + "m"])
                                yield
                                if on_act:
                                    P.op("act", lambda e: e.activation(out=Mb[:, w, 0:nk], in_=Sc[:, z, 0:nk], func=AF.Sign, bias=mid,
                                                                       accum_out=cnt),
                                         reads=[sck, smk + "m"], writes=[mbk, smk + "c"])
                                else:
                                    P.op("dve", lambda e: e.tensor_scalar(Mb[:, w, 0:nk], Sc[:, z, 0:nk], mid, 0.0, op0=ALU.is_gt, op1=ALU.add,
                                                                          accum_out=cnt), reads=[sck, smk + "m"], writes=[mbk, smk + "c"])
                                yield
                                P.op("dve", lambda e: e.tensor_scalar(dlt, cnt, thr, sg * ck, op0=ALU.is_gt, op1=ALU.mult),
                                     reads=[smk + "c"], writes=[smk + "d"])
                                P.op("dve", lambda e: e.scalar_tensor_tensor(lo, dlt, hi, lo, op0=ALU.mult, op1=ALU.add),
                                     reads=[smk + "d", smk], writes=[smk])
                                yield
                            if on_act:
                                P.op("dve", lambda e: e.tensor_scalar(lo, lo, -1.0, None, op0=ALU.mult), reads=[smk], writes=[smk])
                            P.op("dve", lambda e: e.tensor_scalar(Mb[:, w, 0:nk], Sc[:, z, 0:nk], lo, None, op0=ALU.is_gt),
                                 reads=[sck, smk], writes=[mbk])
                        else:
                            P.op("dve", lambda e: e.tensor_scalar(Mb[:, w, 0:nk], Sc[:, z, 0:nk], -1e29, None, op0=ALU.is_gt),
                                 reads=[sck], writes=[mbk])
                        yield
                        for g0 in range(0, i + 1, 8):
                            g1 = min(i + 1, g0 + 8)
                            for kb in range(g0, g1):
                                P.op("pe", lambda e: e.transpose(pM[:, kb - g0, :], Mb[:, w, tsl(kb)], ident),
                                     reads=[mbk, "cb"], writes=["pM"])
                            P.op("act", lambda e: e.activation(out=MT[:, z, g0:g1, :], in_=pM[:, 0:g1 - g0, :], func=AF.Identity,
                                                               scale=MASK_BIG, bias=-MASK_BIG),
                                 reads=[], writes=["pM", "MT%d" % z])
                            yield

                    def stage3(i):
                        tcols = tsl(i)
                        z = i % 4
                        def emit_pv(kb, s_, q3):
                            for j in range(4):
                                P.op("pe", lambda e: e.matmul(pYd[:, s_, j * 66:(j + 1) * 66], PT[:, q3, j * 128:(j + 1) * 128], v[:, kb, 512:578],
                                                              start=(kb == 0 and j == 0), stop=(kb == i), skip_group_check=True),
                                     reads=["PT%d" % q3, "v", "vone"], writes=["pYd%d" % s_])
                        pend = []
                        for kb in range(i + 1):
                            for s_ in range(2):
                                q = cnts["l"] % 2
                                q3 = cnts["l"] % 3
                                cnts["l"] += 1
                                po = slice(64 * s_, 64 * s_ + 64)
                                for j in range(4):
                                    P.op("pe", lambda e: e.matmul(pL[q][:, j * 128:(j + 1) * 128], kdz[:, s_, tsl(kb)], qd[:, j, tcols],
                                                                  start=True, stop=False, skip_group_check=True),
                                         reads=["kd2", "kzz", "qd"], writes=["pL%d" % q])
                                    P.op("pe", lambda e: e.matmul(pL[q][:, j * 128:(j + 1) * 128], ident, MT[:, z, kb, :],
                                                                  start=False, stop=True, skip_group_check=True),
                                         reads=["cb", "MT%d" % z], writes=["pL%d" % q])
                                P.op("act", lambda e: e.activation(out=PT[:, q3, :], in_=pL[q][:], func=AF.Exp),
                                     reads=[], writes=["pL%d" % q, "PT%d" % q3])
                                pend.append((kb, s_, q3))
                                if len(pend) > 2:
                                    emit_pv(*pend.pop(0))
                                yield
                        while pend:
                            emit_pv(*pend.pop(0))
                        for bank in range(2):
                            yv = pYd[:, bank, 0:264].rearrange("p (h c) -> p h c", h=4)
                            ydv = yd[:].rearrange("p (j s c) -> p j s c", j=4, s=2)[:, :, bank, :]
                            P.op("dve", lambda e: e.reciprocal(rec[:, bank * 4:(bank + 1) * 4, :], yv[:, :, 64:65]),
                                 reads=[], writes=["pYd%d" % bank, "rec%d" % bank])
                            P.op("dve", lambda e: e.tensor_tensor(ydv, yv[:, :, 0:64],
                                                                  rec[:, bank * 4:(bank + 1) * 4, :].to_broadcast([128, 4, 64]), op=ALU.mult),
                                 reads=["rec%d" % bank], writes=["pYd%d" % bank, "yd"])
                        yield
                        u = i % 2
                        for cchunk in range(4):
                            P.op("pe", lambda e: e.transpose(pM2[:, cchunk, :], yd[:, tsl(cchunk)], ident),
                                 reads=["yd", "cb"], writes=["pM2"])
                        P.op("act", lambda e: e.copy(ydst[:, u, :, :], pM2[:, 0:4, :]), reads=[], writes=["pM2", "ydst%d" % u])
                        P.dma("sp", ydT_s[:, :, sq * SEQ + i * 128: sq * SEQ + (i + 1) * 128].rearrange("c p t -> p c t"),
                              ydst[:, u, :, :], reads=["ydst%d" % u], writes=["ydd%d" % (sq * 16 + i)])

                    def seq(*gens):
                        for g in gens:
                            yield from g
                    pairs = [(2 * m + 1, 2 * m) for m in (1, 6, 7, 4, 5, 2, 3, 0)]
                    for tau in range(len(pairs) + 2):
                        th = []
                        if tau < len(pairs):
                            th.append(seq(stage1(pairs[tau][0]), stage1(pairs[tau][1])))
                        if 0 <= tau - 1 < len(pairs):
                            th.append(stage2(pairs[tau - 1][0]))
                            th.append(stage2(pairs[tau - 1][1]))
                        if 0 <= tau - 2 < len(pairs):
                            th.append(seq(stage3(pairs[tau - 2][0]), stage3(pairs[tau - 2][1])))
                        P.run_threads(th)
                    P.barrier()
            P.barrier()

    def p4a_phase():
        with contextlib.ExitStack() as es:
            sb = lambda n, s, d=F32: es.enter_context(nc.sbuf_tensor("p4a" + n, s, d))
            ps = lambda n, s, d=F32: es.enter_context(nc.psum_tensor("p4a" + n, s, d))
            wg = sb("wg", [128, KC, 2048], BF16)
            for k in range(KC):
                P.dma("pool", wg[:, k, :], win_d[tsl(k), OFF_GSB:OFF_GSB + 2048], writes=["wg"])
            wosb = sb("wosb", [128, 4, D], BF16)
            P.dma("pool", wosb[:], wosb_d.rearrange("(c p) n -> p c n", p=128), writes=["wosb"])
            wod = sb("wod", [128, 4, D], BF16)
            P.dma("pool", wod[:], wod_d.rearrange("(c p) n -> p c n", p=128), writes=["wod"])
            wo = sb("wo", [128, KC, D], BF16)
            P.dma("pool", wo[:], wo_d.rearrange("(c p) n -> p c n", p=128), writes=["wo"])
            uT = sb("uT", [128, 2, KC, TB], BF16)
            ysbT = sb("ysbT", [128, 2, 4, TB], BF16)
            ydT = sb("ydT", [128, 2, 4, TB], BF16)
            mT = sb("mT", [128, KC, TB], BF16)
            s1 = sb("s1", [128, 2, TB])
            s2 = sb("s2", [128, 2, TB])
            t1 = sb("t1", [128, 2, TB])
            t2 = sb("t2", [128, 2, TB])
            hs = sb("hs", [128, 4, D])
            pG = [ps("pG%d" % i, [128, TB]) for i in range(2)]
            pY = [ps("pY%d" % i, [128, TB]) for i in range(2)]
            pO = [ps("pO%d" % i, [128, 512]) for i in range(2)]
            oc = hc = 0
            def load_blk(b):
                if b >= NB:
                    return
                u = b % 2
                tok = slice(b * TB, (b + 1) * TB)
                P.dma("sp", uT[:, u, :, :], uT_s[:, :, tok].rearrange("c p t -> p c t"), writes=["uT%d" % u])
                P.dma("sp", ysbT[:, u, :, :], ysbT_s[:, :, tok].rearrange("c p t -> p c t"), writes=["ysbT%d" % u])
                P.dma("sp", ydT[:, u, :, :], ydT_s[:, :, tok].rearrange("c p t -> p c t"), writes=["ydT%d" % u])
            load_blk(0)
            for b in range(NB):
                u = b % 2
                tok = slice(b * TB, (b + 1) * TB)
                load_blk(b + 1)
                for i in range(4):
                    P.dma("sp", hs[:, i, :], h_s[tsl(4 * b + i), :], reads=["hd%d" % (4 * b + i)], writes=["hs%d" % i])
                for c in range(KC):
                    r = c % 2
                    for (g, pg, sdst, skey) in ((0, pG[0], s1, "s1"), (1, pG[1], s2, "s2")):
                        for k in range(KC):
                            P.op("pe", lambda e, k=k, g=g, pg=pg, c=c: e.matmul(
                                pg[:], wg[:, k, g * 1024 + c * 128: g * 1024 + (c + 1) * 128], uT[:, u, k, :],
                                start=(k == 0), stop=(k == KC - 1)),
                                reads=["wg", "uT%d" % u], writes=["pG%d" % g])
                        P.op("act", lambda e, pg=pg, sdst=sdst, r=r: e.activation(out=sdst[:, r, :], in_=pg[:], func=AF.Tanh, scale=0.5),
                             reads=[], writes=["pG%d" % g, "%s%d" % (skey, r)])
                    for hh in range(4):
                        P.op("pe", lambda e, hh=hh, c=c: e.matmul(pY[0][:], wosb[:, hh, tsl(c)], ysbT[:, u, hh, :],
                                                                   start=(hh == 0), stop=(hh == 3)),
                             reads=["wosb", "ysbT%d" % u], writes=["pY0"])
                    for k in range(4):
                        P.op("pe", lambda e, k=k, c=c: e.matmul(pY[1][:], wod[:, k, tsl(c)], ydT[:, u, k, :],
                                                                 start=(k == 0), stop=(k == 3)),
                             reads=["wod", "ydT%d" % u], writes=["pY1"])
                    P.op("dve", lambda e, r=r: e.scalar_tensor_tensor(t1[:, r, :], s1[:, r, :], 1.0, pY[0][:], op0=ALU.add, op1=ALU.mult),
                         reads=["s1%d" % r], writes=["pY0", "t1%d" % r])
                    P.op("dve", lambda e, r=r: e.scalar_tensor_tensor(t2[:, r, :], s2[:, r, :], 1.0, pY[1][:], op0=ALU.add, op1=ALU.mult),
                         reads=["s2%d" % r], writes=["pY1", "t2%d" % r])
                    P.op("pool", lambda e, r=r, c=c: e.tensor_tensor(mT[:, c, :], t1[:, r, :], t2[:, r, :], op=ALU.add),
                         reads=["t1%d" % r, "t2%d" % r], writes=["mT"])
                for i in range(4):
                    t = 4 * b + i
                    sl = i
                    for c2 in range(2):
                        q = oc % 2
                        oc += 1
                        for k in range(KC):
                            P.op("pe", lambda e, k=k, i=i, c2=c2, q=q: e.matmul(pO[q][:], mT[:, k, tsl(i)], wo[:, k, c2 * 512:(c2 + 1) * 512],
                                                                               start=(k == 0), stop=(k == KC - 1)),
                                 reads=["mT", "wo"], writes=["pO%d" % q])
                        P.op("dve", lambda e, q=q, sl=sl, c2=c2: e.scalar_tensor_tensor(
                            hs[:, sl, c2 * 512:(c2 + 1) * 512], pO[q][:], 0.5, hs[:, sl, c2 * 512:(c2 + 1) * 512],
                            op0=ALU.mult, op1=ALU.add), reads=[], writes=["pO%d" % q, "hs%d" % sl])
                    P.dma("sp", h_s[tsl(t), :], hs[:, sl, :], reads=["hs%d" % sl], writes=["hd%d" % t])
            P.barrier()

    def p4c_phase():
        with contextlib.ExitStack() as es:
            sb = lambda n, s, d=F32: es.enter_context(nc.sbuf_tensor("p4c" + n, s, d))
            ps = lambda n, s, d=F32: es.enter_context(nc.psum_tensor("p4c" + n, s, d))
            c = load_consts(es)
            ident = c["ident"]
            wpg = sb("wpg", [128, KC, D], BF16)
            P.dma("pool", wpg[:], wpg_d.rearrange("(c p) n -> p c n", p=128), writes=["wpg"])
            wpp = sb("wpp", [128, 2, D], BF16)
            P.dma("pool", wpp[:], wpp_d.rearrange("(c p) n -> p c n", p=128), writes=["wpp"])
            gB = make_gB(es, c, 3, "p4cgB")
            gfin = sb("gfin", [128, D])
            P.dma("sp", gfin[:], gfin_d.broadcast_to([128, D]), writes=["gfin"])
            NSG = 4
            hs = sb("hs", [128, 4 * NSG, D])
            pt = sb("pt", [128, 4 * NSG, 256], BF16)
            xn = sb("xn", [128, 2, D], BF16)
            sqj = sb("sqj", [128, 2, D], BF16)
            hnT = sb("hnT", [128, 2, KC, 128], BF16)
            pTs = sb("pTs", [128, 2, 2, 128], BF16)
            gate = sb("gate", [128, 2, D])
            ot = sb("ot", [128, 2, D])
            ss = sb("ss", [128, NSG, 4])
            rstd = sb("rstd", [128, NSG, 4])
            ss2 = sb("ss2", [128, NSG, 4])
            rstd2 = sb("rstd2", [128, NSG, 4])
            pT = [ps("pT%d" % i, [128, KC, 128], BF16) for i in range(2)]
            pP = [ps("pP%d" % i, [128, KC, 128], BF16) for i in range(2)]
            pGt = [ps("pGt%d" % i, [128, 512]) for i in range(2)]
            pPp = [ps("pPp%d" % i, [128, 512]) for i in range(2)]
            NG = NT // 4

            def load_g(g):
                if 0 <= g < NG:
                    for i in range(4):
                        t = 4 * g + i
                        sl = (g % NSG) * 4 + i
                        P.dma("sp", hs[:, sl, :], h_s[tsl(t), :], reads=["hd%d" % t], writes=["hs%d" % sl])
                        P.dma("pool", pt[:, sl, :], p_d[tsl(t), :], writes=["pt%d" % sl])

            def stage_a(g):
                gp = g % NSG
                for i in range(4):
                    sl = gp * 4 + i
                    P.op("act", lambda e: e.activation(out=sqj[:, 0, :], in_=hs[:, sl, :], func=AF.Square, accum_out=ss[:, gp, i:i + 1]),
                         reads=["hs%d" % sl], writes=["sqj0", "ss%d" % gp])
                    yield
                P.op("dve", lambda e: e.tensor_scalar(ss[:, gp, :], ss[:, gp, :], 1.0 / D, EPS, op0=ALU.mult, op1=ALU.add),
                     reads=["ss%d" % gp], writes=["ss%d" % gp])
                yield
                P.op("act", lambda e: e.activation(out=ss[:, gp, :], in_=ss[:, gp, :], func=AF.Sqrt), reads=["ss%d" % gp], writes=["ss%d" % gp])
                yield
                P.op("dve", lambda e: e.reciprocal(rstd[:, gp, :], ss[:, gp, :]), reads=["ss%d" % gp], writes=["rstd%d" % gp])
                yield

            def tile_thread(g, i):
                gp = g % NSG
                sl = gp * 4 + i
                u = i % 2
                P.op("dve", lambda e: e.tensor_scalar(xn[:, u, :], hs[:, sl, :], rstd[:, gp, i:i + 1], None, op0=ALU.mult),
                     reads=["hs%d" % sl, "rstd%d" % gp], writes=["xn%d" % u])
                yield
                for k in range(KC):
                    P.op("pe", lambda e: e.transpose(pT[u][:, k, :], xn[:, u, tsl(k)], ident[:]),
                         reads=["xn%d" % u, "c_ident"], writes=["pT%d" % u])
                for k in range(2):
                    P.op("pe", lambda e: e.transpose(pP[u][:, k, :], pt[:, sl, tsl(k)], ident[:]),
                         reads=["pt%d" % sl, "c_ident"], writes=["pP%d" % u])
                yield
                P.op("dve", lambda e: e.tensor_tensor(hnT[:, u, :, :], pT[u][:], gB[:], op=ALU.mult),
                     reads=["p4cgB"], writes=["pT%d" % u, "hnT%d" % u])
                P.op("act", lambda e: e.copy(pTs[:, u, :, :], pP[u][:, 0:2, :]), reads=[], writes=["pP%d" % u, "pTs%d" % u])
                yield
                for c2 in range(2):
                    cs = slice(c2 * 512, (c2 + 1) * 512)
                    for k in range(KC):
                        P.op("pe", lambda e: e.matmul(pGt[u][:], hnT[:, u, k, :], wpg[:, k, cs], start=(k == 0), stop=(k == KC - 1)),
                             reads=["hnT%d" % u, "wpg"], writes=["pGt%d" % u])
                    for k in range(2):
                        P.op("pe", lambda e: e.matmul(pPp[u][:], pTs[:, u, k, :], wpp[:, k, cs], start=(k == 0), stop=(k == 1)),
                             reads=["pTs%d" % u, "wpp"], writes=["pPp%d" % u])
                    yield
                    P.op("act", lambda e: e.activation(out=gate[:, u, cs], in_=pGt[u][:], func=AF.Tanh, scale=0.5),
                         reads=[], writes=["pGt%d" % u, "gate%d" % u])
                    yield
                    P.op("dve", lambda e: e.scalar_tensor_tensor(gate[:, u, cs], gate[:, u, cs], 1.0, pPp[u][:], op0=ALU.add, op1=ALU.mult),
                         reads=["gate%d" % u], writes=["pPp%d" % u, "gate%d" % u])
                    yield
                P.op("dve", lambda e: e.scalar_tensor_tensor(hs[:, sl, :], gate[:, u, :], 0.5, hs[:, sl, :], op0=ALU.mult, op1=ALU.add),
                     reads=["gate%d" % u, "hs%d" % sl], writes=["hs%d" % sl])
                yield

            def stage_c(g):
                gp = g % NSG
                for i in range(4):
                    sl = gp * 4 + i
                    P.op("act", lambda e: e.activation(out=sqj[:, 1, :], in_=hs[:, sl, :], func=AF.Square, accum_out=ss2[:, gp, i:i + 1]),
                         reads=["hs%d" % sl], writes=["sqj1", "ss2%d" % gp])
                    yield
                P.op("dve", lambda e: e.tensor_scalar(ss2[:, gp, :], ss2[:, gp, :], 1.0 / D, EPS, op0=ALU.mult, op1=ALU.add),
                     reads=["ss2%d" % gp], writes=["ss2%d" % gp])
                yield
                P.op("act", lambda e: e.activation(out=ss2[:, gp, :], in_=ss2[:, gp, :], func=AF.Sqrt), reads=["ss2%d" % gp], writes=["ss2%d" % gp])
                yield
                P.op("dve", lambda e: e.reciprocal(rstd2[:, gp, :], ss2[:, gp, :]), reads=["ss2%d" % gp], writes=["rstd2%d" % gp])
                yield
                for i in range(4):
                    sl = gp * 4 + i
                    t = 4 * g + i
                    u = i % 2
                    P.op("dve", lambda e: e.scalar_tensor_tensor(ot[:, u, :], hs[:, sl, :], rstd2[:, gp, i:i + 1], gfin[:], op0=ALU.mult, op1=ALU.mult),
                         reads=["hs%d" % sl, "rstd2%d" % gp, "gfin"], writes=["ot%d" % u])
                    P.dma("sp", out_d[tsl(t), :], ot[:, u, :], reads=["ot%d" % u], writes=["outd%d" % t])
                    yield

            def seq(*gens):
                for g_ in gens:
                    yield from g_

            load_g(0)
            for tau in range(NG + 2):
                load_g(tau + 1)
                th = []
                if tau < NG:
                    th.append(stage_a(tau))
                if 0 <= tau - 1 < NG:
                    th.append(seq(tile_thread(tau - 1, 0), tile_thread(tau - 1, 2)))
                    th.append(seq(tile_thread(tau - 1, 1), tile_thread(tau - 1, 3)))
                if 0 <= tau - 2 < NG:
                    th.append(stage_c(tau - 2))
                P.run_threads(th)
            P.barrier()

    if "p1" in phases:
        ffn_phase("f1", x_d, h_s, w1a_d, w2a_d, 0, 1)
    if "p2" in phases:
        p2_phase()
    if "p3" in phases:
        p3_phase()
    if "p4a" in phases:
        p4a_phase()
    if "p4b" in phases:
        ffn_phase("f2", h_s, h_s, w1b_d, w2b_d, 2, None)
    if "p4c" in phases:
        p4c_phase()
    P.emit()
    return nc


def host_consts():
    j = np.arange(128)
    ident = np.eye(128, dtype=np.float32)
    nUincl = -(j[:, None] >= j[None, :]).astype(np.float32)
    nLstr = -(j[:, None] < j[None, :]).astype(np.float32)
    t = np.arange(512)
    sbmask = np.stack([(j[:, None] + 128 * r < t[None, :]).astype(np.float32) for r in range(4)], 1)
    cb = np.concatenate([ident, nUincl, nLstr, sbmask.reshape(128, 2048)], 1).astype(np.float32)
    dsaneg = np.where(j[None, :] > j[:, None], -1e30, 0.0).astype(np.float32)
    inv_freq = (500000.0 ** (-np.arange(0, 16, 2, dtype=np.float32) / 16)).astype(np.float32)
    pm = j % 64
    invf = np.where(pm < 16, inv_freq[pm % 8], 0.0).astype(np.float32)
    sgn = np.where(pm < 8, -1.0, np.where(pm < 16, 1.0, 0.0)).astype(np.float32)
    cf = np.concatenate([dsaneg, invf[:, None], sgn[:, None]], 1).astype(np.float32)
    return cb, cf


def make_in_maps(inputs, ncores=8):
    f = lambda a: np.ascontiguousarray(np.asarray(a), dtype=np.float32)
    x = f(inputs["x"])
    p = f(inputs["p"])[0]
    pos = np.ascontiguousarray(np.asarray(inputs["positions"]), dtype=np.int32)
    cb, cf = host_consts()
    gl = lambda g: f(g).reshape(KC, 128).T
    gcols = np.ascontiguousarray(np.concatenate([gl(inputs["ffn1_norm"][0]), gl(inputs["mix_norm"][0]),
                                                 gl(inputs["ffn2_norm"][0]), gl(inputs["ple_norm"][0])], 1))
    shared = {
        "w1a": f(inputs["ffn1_w1"][0]), "w2a": f(inputs["ffn1_w2"][0]), "win": f(inputs["w_in"][0]),
        "wosb": f(inputs["w_out_sb"][0]), "wod": f(inputs["w_out_dsa"][0]), "wo": f(inputs["w_out"][0]),
        "w1b": f(inputs["ffn2_w1"][0]), "w2b": f(inputs["ffn2_w2"][0]), "wpg": f(inputs["ple_w_gate"][0]),
        "wpp": f(inputs["ple_w_proj"][0]), "gcols": gcols, "gfin": f(inputs["final_norm"]).reshape(1, D),
        "cb": cb, "cf": cf,
    }
    maps = []
    for c in range(ncores):
        m = dict(shared)
        m["x"] = np.ascontiguousarray(x[NSEQ * c:NSEQ * (c + 1)].reshape(NTOK, D))
        m["p"] = np.ascontiguousarray(p[NSEQ * c:NSEQ * (c + 1)].reshape(NTOK, 256))
        m["pos"] = np.ascontiguousarray(pos[NSEQ * c:NSEQ * (c + 1)].reshape(1, NTOK))
        maps.append(m)
    return maps


def kernel(**inputs):
    nc = bass.Bass("TRN2", target_bir_lowering=False)
    build(nc)
    maps = make_in_maps(inputs, 8)
    res = run_bass_kernel_spmd(nc, maps, core_ids=list(range(8)))
    out = np.stack([r["out"].reshape(NSEQ, SEQ, D) for r in res.results], 0).reshape(16, SEQ, D)
    return out.astype(np.float32)
```

```python
import contextlib
import numpy as np
import concourse.bass as bass
import concourse.mybir as mybir
from concourse.bass_utils import run_bass_kernel_spmd

F32 = mybir.dt.float32
BF16 = mybir.dt.bfloat16
I32 = mybir.dt.int32
AF = mybir.ActivationFunctionType
ALU = mybir.AluOpType
AX = mybir.AxisListType

D = 1024
KC = 8
SEQ = 2048
NSEQ = 2
NTOK = NSEQ * SEQ
NT = NTOK // 128
TB = 512
NB = NTOK // TB
FF = 2816
FC = FF // 128
DIN = 4808
EPS = 1e-6
TOPK = 256
NBIS = 14
MASK_BIG = 30000.0
ACT_CHAINS = (1,)
OFF_QSB, OFF_KSB, OFF_VSB, OFF_QD, OFF_KD, OFF_VD, OFF_QI, OFF_KI, OFF_WI, OFF_GSB, OFF_GD = (
    0, 512, 1024, 1536, 2048, 2112, 2176, 2688, 2752, 2760, 3784)
TWO_PI = 6.283185307179586
CW1 = 6.28125
CW2 = TWO_PI - CW1


class _Key:
    __slots__ = ("w", "rs")

    def __init__(self):
        self.w = None
        self.rs = []


class _Rec:
    def __init__(self):
        self.call = None

    def __getattr__(self, name):
        def f(*a, **k):
            self.call = (name, a, k)
            return self
        return f


def _freeze(fn):
    rec = _Rec()
    fn(rec)
    name, a, k = rec.call
    return lambda e: getattr(e, name)(*a, **k)


class Prog:
    ENG = ("pe", "act", "dve", "pool", "sp")

    def __init__(self, nc, same_engine_sync=True, ndma_sems=8):
        self.nc = nc
        self.es = contextlib.ExitStack()
        self.streams = {e: [] for e in self.ENG}
        self.cnt = {e: 0 for e in self.ENG}
        self.sems = {}
        for e in self.ENG:
            self.sems["p_" + e] = self.es.enter_context(nc.semaphore("prog_" + e))
        self.known = {e: {} for e in self.ENG}
        self.same = same_engine_sync
        self.ndma = ndma_sems
        self.dq = {}
        self.keys = {}

    def key(self, name):
        k = self.keys.get(name)
        if k is None:
            k = _Key()
            self.keys[name] = k
        return k

    def _deps(self, reads, writes):
        ev = []
        for r in reads:
            k = self.key(r)
            if k.w is not None:
                ev.append(k.w)
        for w in writes:
            k = self.key(w)
            if k.w is not None:
                ev.append(k.w)
            ev.extend(k.rs)
        return ev

    def _waits(self, eng, evs):
        need = {}
        kn = self.known[eng]
        for (sid, val) in evs:
            if sid == "p_" + eng and (eng == "pe" or not self.same):
                continue
            if kn.get(sid, 0) >= val:
                continue
            if need.get(sid, 0) < val:
                need[sid] = val
        for sid, val in need.items():
            kn[sid] = val
        return list(need.items())

    def _commit(self, reads, writes, event):
        for r in reads:
            self.key(r).rs.append(event)
        for w in writes:
            k = self.key(w)
            k.w = event
            k.rs = []

    def op(self, eng, fn, reads=(), writes=()):
        waits = self._waits(eng, self._deps(reads, writes))
        self.cnt[eng] += 1
        event = ("p_" + eng, self.cnt[eng])
        self.streams[eng].append((waits, _freeze(fn), ("p_" + eng, 1)))
        self._commit(reads, writes, event)
        return event

    def dma(self, q, out, in_, reads=(), writes=()):
        d = self.dq.get(q)
        if d is None:
            ids = [f"d_{q}_{j}" for j in range(self.ndma)]
            for s in ids:
                self.sems[s] = self.es.enter_context(self.nc.semaphore(s))
            d = dict(i=0, ids=ids, vals=[0] * self.ndma, last=[None] * self.ndma)
            self.dq[q] = d
        j = d["i"] % self.ndma
        d["i"] += 1
        evs = self._deps(reads, writes)
        if d["last"][j] is not None:
            evs.append(d["last"][j])
        waits = self._waits(q, evs)
        d["vals"][j] += 16
        event = (d["ids"][j], d["vals"][j])
        d["last"][j] = event
        fn = lambda e, out=out, in_=in_: e.dma_start(out=out, in_=in_)
        self.streams[q].append((waits, fn, (d["ids"][j], 16)))
        self._commit(reads, writes, event)
        return event

    def _all_events(self):
        ev = []
        for q, d in self.dq.items():
            for e in d["last"]:
                if e is not None:
                    ev.append(e)
        for e in self.ENG:
            if self.cnt[e] > 0:
                ev.append(("p_" + e, self.cnt[e]))
        return ev

    def run_threads(self, gens):
        live = list(gens)
        while live:
            nxt = []
            for g in live:
                try:
                    next(g)
                    nxt.append(g)
                except StopIteration:
                    pass
            live = nxt

    def barrier(self):
        ev = self._all_events()
        for e in self.ENG:
            w = self._waits(e, ev)
            if w:
                self.streams[e].append((w, None, None))
        self.keys = {}

    def emit(self):
        nc = self.nc
        fw = self._waits("sp", self._all_events())
        self.streams["sp"].append((fw, None, None))
        with nc.Block() as block:
            def run(engname, handle):
                for waits, fn, inc in self.streams[engname]:
                    for sid, val in waits:
                        handle.wait_ge(self.sems[sid], val)
                    if fn is not None:
                        fn(handle).then_inc(self.sems[inc[0]], inc[1])

            @block.tensor
            def _(e):
                run("pe", e)

            @block.scalar
            def _(e):
                run("act", e)

            @block.vector
            def _(e):
                run("dve", e)

            @block.gpsimd
            def _(e):
                run("pool", e)

            @block.sync
            def _(e):
                run("sp", e)
        self.es.close()


def build(nc, dbg=False, phases=("p1", "p2", "p3", "p4a", "p4b", "p4c")):
    P = Prog(nc)

    def din(name, shape, dt=F32):
        return nc.dram_tensor(name, shape, dt, kind="ExternalInput").ap()

    def dscr(name, shape, dt):
        return nc.dram_tensor(name, shape, dt, kind=("ExternalOutput" if dbg else "Internal")).ap()

    x_d = din("x", [NTOK, D])
    p_d = din("p", [NTOK, 256])
    pos_d = din("pos", [1, NTOK], I32)
    w1a_d = din("w1a", [D, 2 * FF])
    w2a_d = din("w2a", [FF, D])
    win_d = din("win", [D, DIN])
    wosb_d = din("wosb", [512, D])
    wod_d = din("wod", [512, D])
    wo_d = din("wo", [D, D])
    w1b_d = din("w1b", [D, 2 * FF])
    w2b_d = din("w2b", [FF, D])
    wpg_d = din("wpg", [D, D])
    wpp_d = din("wpp", [256, D])
    gcols_d = din("gcols", [128, 4 * KC])
    gfin_d = din("gfin", [1, D])
    cb_d = din("cb", [128, 384 + 2048])
    cf_d = din("cf", [128, 130])
    out_d = nc.dram_tensor("out", [NTOK, D], F32, kind="ExternalOutput").ap()

    h_s = dscr("h_s", [NTOK, D], F32)
    uT_s = dscr("uT_s", [KC, 128, NTOK], BF16)
    qk_s = dscr("qk_s", [18, 128, NTOK], BF16)
    v_s = dscr("v_s", [NTOK, 576], BF16)
    wi_s = dscr("wi_s", [NTOK, 8], F32)
    ysbT_s = dscr("ysbT_s", [4, 128, NTOK], BF16)
    ydT_s = dscr("ydT_s", [4, 128, NTOK], BF16)

    def tsl(t):
        return slice(t * 128, (t + 1) * 128)

    ccount = [0]

    def load_consts(es, need_cb=True):
        ccount[0] += 1
        sb = lambda n, s, d=F32: es.enter_context(nc.sbuf_tensor("%s_%d" % (n, ccount[0]), s, d))
        c = {}
        c["gcols"] = sb("c_gcols", [128, 4 * KC])
        P.dma("sp", c["gcols"][:], gcols_d, writes=["c_gcols"])
        c["ident"] = sb("c_ident", [128, 128], BF16)
        P.dma("pool", c["ident"][:], cb_d[:, 0:128], writes=["c_ident"])
        return c

    def make_gB(es, c, which, name):
        gB = es.enter_context(nc.sbuf_tensor(name, [128, KC, 128], F32))
        src = c["gcols"][:, which * KC:(which + 1) * KC]
        P.op("dve", lambda e: e.tensor_copy(gB[:], src.unsqueeze(2).to_broadcast([128, KC, 128])),
             reads=["c_gcols"], writes=[name])
        return gB

    def rstd_from_ss(ss, rstd, n, rkeys, wkey):
        P.op("dve", lambda e: e.tensor_scalar(ss[:, 0:n], ss[:, 0:n], 1.0 / D, EPS, op0=ALU.mult, op1=ALU.add),
             reads=rkeys, writes=rkeys)
        P.op("act", lambda e: e.activation(out=ss[:, 0:n], in_=ss[:, 0:n], func=AF.Sqrt), reads=rkeys, writes=rkeys)
        P.op("dve", lambda e: e.reciprocal(rstd[:, 0:n], ss[:, 0:n]), reads=rkeys, writes=[wkey])

    def ffn_phase(tag, src_d, dst_d, w1_d, w2_d, gsel, post_gsel):
        with contextlib.ExitStack() as es:
            sb = lambda n, s, d=F32: es.enter_context(nc.sbuf_tensor(tag + n, s, d))
            ps = lambda n, s, d=F32: es.enter_context(nc.psum_tensor(tag + n, s, d))
            c = load_consts(es)
            w1 = sb("w1", [128, KC, 2 * FF], BF16)
            w2 = sb("w2", [128, FC, D], BF16)
            for k in range(KC):
                P.dma("pool", w1[:, k, :], w1_d[tsl(k), :], writes=["w1g0", "w1g1", "w1g2", "w1g3"])
            for k in range(FC):
                P.dma("pool", w2[:, k, :], w2_d[tsl(k), :], writes=["w2"])
            gB = make_gB(es, c, gsel, tag + "gB")
            gB2 = make_gB(es, c, post_gsel, tag + "gB2") if post_gsel is not None else None
            NXS = 6 if post_gsel is not None else 8
            xs = sb("xs", [128, NXS, D])
            issued = set()

            def issue_load(t):
                if t in issued or t >= NT:
                    return
                issued.add(t)
                P.dma("sp", xs[:, t % NXS, :], src_d[tsl(t), :], reads=["hd%d" % t], writes=["xs%d" % (t % NXS)])
            xn = sb("xn", [128, 2, D], BF16)
            xnT = sb("xnT", [128, KC, TB], BF16)
            gT = sb("gT", [128, FC, TB], BF16)
            stmp = sb("stmp", [128, 2, TB])
            ss = sb("ss", [128, 2, 4])
            rstd = sb("rstd", [128, 2, 4])
            ss2 = sb("ss2", [128, 2, 4])
            rstd2 = sb("rstd2", [128, 2, 4])
            ust = sb("ust", [128, 2, KC, 128], BF16) if post_gsel is not None else None
            pT = [ps("pT%d" % i, [128, KC, 128], BF16) for i in range(2)]
            pA = [ps("pA%d" % i, [128, TB]) for i in range(2)]
            pB = [ps("pB%d" % i, [128, TB]) for i in range(2)]
            pO = [ps("pO%d" % i, [128, 512]) for i in range(2)]
            ident = c["ident"]
            xnc = [0]
            ptc = [0]

            sqjunk = stmp[:].rearrange("p a b -> p (a b)")

            def sq_stat(t, dst_col, dst_key):
                sl = t % NXS
                P.op("act", lambda e: e.activation(out=sqjunk, in_=xs[:, sl, :], func=AF.Square, accum_out=dst_col),
                     reads=["xs%d" % sl], writes=["stmp0", "stmp1", dst_key])

            def norm_front(tile_ap, xs_key, rstd_col, rstd_key):
                s_ = xnc[0] % 2
                xnc[0] += 1
                P.op("dve", lambda e: e.tensor_scalar(xn[:, s_, :], tile_ap, rstd_col, None, op0=ALU.mult),
                     reads=[xs_key, rstd_key], writes=["xn%d" % s_])
                return s_

            def norm_pe(s_, gBt, gB_key, out_ap, out_key):
                q = ptc[0] % 2
                ptc[0] += 1
                for k in range(KC):
                    P.op("pe", lambda e: e.transpose(pT[q][:, k, :], xn[:, s_, tsl(k)], ident[:]),
                         reads=["xn%d" % s_, "c_ident"], writes=["pT%d" % q])
                P.op("dve", lambda e: e.tensor_tensor(out_ap, pT[q][:], gBt[:], op=ALU.mult),
                     reads=[gB_key], writes=["pT%d" % q, out_key])

            def prep_stats(tl, b_):
                sp2 = b_ % 2
                i0, i1 = tl[0] - 4 * b_, tl[-1] - 4 * b_ + 1
                key = "ss%d_%d" % (sp2, i0)
                for t in tl:
                    issue_load(t)
                    i = t - 4 * b_
                    sq_stat(t, ss[:, sp2, i:i + 1], key)
                rstd_from_ss(ss[:, sp2, i0:i1], rstd[:, sp2, i0:i1], i1 - i0, [key], "rstd%d_%d" % (sp2, i0))
                return "rstd%d_%d" % (sp2, i0)

            class Job:
                pass

            def prep_job(t, b_, rkey):
                i = t - 4 * b_
                sl = t % NXS
                j_ = Job()
                j_.front = lambda: norm_front(xs[:, sl, :], "xs%d" % sl, rstd[:, b_ % 2, i:i + 1], rkey)
                j_.back = lambda s_: norm_pe(s_, gB, tag + "gB", xnT[:, :, tsl(i)], "xnT")
                return j_

            def post_job(t, b_):
                i = t - 4 * b_
                sl = t % NXS
                u = t % 2
                sp2 = b_ % 2
                key = "ss2_%d_%d" % (sp2, i)
                sq_stat(t, ss2[:, sp2, i:i + 1], key)
                rstd_from_ss(ss2[:, sp2, i:i + 1], rstd2[:, sp2, i:i + 1], 1, [key], "rstd2_%d_%d" % (sp2, i))
                j_ = Job()
                j_.front = lambda: norm_front(xs[:, sl, :], "xs%d" % sl, rstd2[:, sp2, i:i + 1], "rstd2_%d_%d" % (sp2, i))

                def back(s_):
                    norm_pe(s_, gB2, tag + "gB2", ust[:, u, :, :], "ust%d" % u)
                    P.dma("sp", uT_s[:, :, tsl(t)].rearrange("c p t -> p c t"), ust[:, u, :, :],
                          reads=["ust%d" % u], writes=["uTd%d" % t])
                j_.back = back
                return j_

            def run_jobs(jobs):
                ss_ = [j_.front() for j_ in jobs]
                for j_, s_ in zip(jobs, ss_):
                    j_.back(s_)

            n_early = NXS - 4
            rk0 = prep_stats([0, 1, 2, 3], 0)
            for t in range(4):
                run_jobs([prep_job(t, 0, rk0)])
            for b in range(NB):
                tiles = [4 * b + i for i in range(4)]
                for j in range(FC):
                    q = j % 2
                    for k in range(KC):
                        P.op("pe", lambda e, k=k, j=j, q=q: e.matmul(pA[q][:], w1[:, k, tsl(j)], xnT[:, k, :],
                                                                    start=(k == 0), stop=(k == KC - 1)),
                             reads=["w1g%d" % (j // 6), "xnT"], writes=["pA%d" % q])
                    for k in range(KC):
                        P.op("pe", lambda e, k=k, j=j, q=q: e.matmul(pB[q][:], w1[:, k, FF + j * 128:FF + (j + 1) * 128],
                                                                    xnT[:, k, :], start=(k == 0), stop=(k == KC - 1)),
                             reads=["w1g%d" % (j // 6), "xnT"], writes=["pB%d" % q])
                    P.op("act", lambda e, q=q: e.activation(out=stmp[:, q, :], in_=pA[q][:], func=AF.Silu),
                         reads=[], writes=["pA%d" % q, "stmp%d" % q])
                    P.op("dve", lambda e, q=q, j=j: e.tensor_tensor(gT[:, j, :], stmp[:, q, :], pB[q][:], op=ALU.mult),
                         reads=["stmp%d" % q], writes=["pB%d" % q, "gT"])
                nxt = [t for t in range(4 * b + 4, 4 * b + 4 + n_early) if t < NT]
                ejobs = []
                if nxt:
                    rk = prep_stats(nxt, b + 1)
                    ejobs = [prep_job(t, b + 1, rk) for t in nxt]
                pjobs = []
                oc = 0
                for i, t in enumerate(tiles):
                    sl = t % NXS
                    slot_jobs = []
                    if i < len(ejobs):
                        slot_jobs.append(ejobs[i])
                    if pjobs:
                        slot_jobs.append(pjobs.pop(0))
                    fronts = [j_.front() for j_ in slot_jobs]
                    for c2 in range(2):
                        q = oc % 2
                        oc += 1
                        for j in range(FC):
                            P.op("pe", lambda e, j=j, i=i, c2=c2, q=q: e.matmul(
                                pO[q][:], gT[:, j, tsl(i)], w2[:, j, c2 * 512:(c2 + 1) * 512],
                                start=(j == 0), stop=(j == FC - 1)),
                                reads=["w2", "gT"], writes=["pO%d" % q])
                        P.op("dve", lambda e, q=q, sl=sl, c2=c2: e.scalar_tensor_tensor(
                            xs[:, sl, c2 * 512:(c2 + 1) * 512], pO[q][:], 0.5, xs[:, sl, c2 * 512:(c2 + 1) * 512],
                            op0=ALU.mult, op1=ALU.add),
                            reads=["xs%d" % sl], writes=["pO%d" % q, "xs%d" % sl])
                    P.dma("sp", dst_d[tsl(t), :], xs[:, sl, :], reads=["xs%d" % sl], writes=["hd%d" % t])
                    for j_, s_ in zip(slot_jobs, fronts):
                        j_.back(s_)
                    if post_gsel is not None:
                        pjobs.append(post_job(t, b))
                    if post_gsel is not None and n_early == 2 and i in (1, 2):
                        issue_load(4 * b + 5 + i)
                while pjobs:
                    run_jobs([pjobs.pop(0)])
                late = [t for t in range(4 * b + 4 + n_early, 4 * b + 8) if t < NT]
                if late:
                    rk = prep_stats(late, b + 1)
                    for t in late:
                        run_jobs([prep_job(t, b + 1, rk)])
            P.barrier()

    def p2_phase():
        with contextlib.ExitStack() as es:
            sb = lambda n, s, d=F32: es.enter_context(nc.sbuf_tensor("p2" + n, s, d))
            ps = lambda n, s, d=F32: es.enter_context(nc.psum_tensor("p2" + n, s, d))
            win = sb("win", [128, KC, 2760], BF16)
            for k in range(KC):
                P.dma("pool", win[:, k, :], win_d[tsl(k), 0:2760], writes=["win"])
            wp = sb("wp", [128, KC, 1280], BF16)
            wk2 = sb("wk2", [128, KC, 256], BF16)
            cf = sb("cf", [128, 130])
            P.dma("sp", cf[:], cf_d, writes=["cf"])
            posi = sb("posi", [128, NTOK], I32)
            P.dma("sp", posi[:], pos_d.broadcast_to([128, NTOK]), writes=["posi"])
            ang = sb("ang", [128, NTOK])
            kk = sb("kk", [128, NTOK])
            kki = sb("kki", [128, NTOK], I32)
            Ct = sb("Ct", [128, NTOK])
            St = sb("St", [128, NTOK])
            invf = cf[:, 128:129]
            sgn = cf[:, 129:130]
            for h2 in range(2):
                hc = slice(h2 * SEQ, (h2 + 1) * SEQ)
                kA, kK, kI = "ang%d" % h2, "kk%d" % h2, "kki%d" % h2
                P.op("dve", lambda e: e.tensor_copy(ang[:, hc], posi[:, hc]), reads=["posi"], writes=[kA])
                P.op("dve", lambda e: e.tensor_scalar(ang[:, hc], ang[:, hc], invf, None, op0=ALU.mult), reads=[kA, "cf"], writes=[kA])

                def reduce_to(dst, shift, key):
                    P.op("dve", lambda e: e.tensor_scalar(kk[:, hc], ang[:, hc], shift, 1.0 / TWO_PI, op0=ALU.add, op1=ALU.mult),
                         reads=[kA], writes=[kK])
                    P.op("dve", lambda e: e.tensor_copy(kki[:, hc], kk[:, hc]), reads=[kK], writes=[kI])
                    P.op("dve", lambda e: e.tensor_copy(kk[:, hc], kki[:, hc]), reads=[kI], writes=[kK])
                    P.op("dve", lambda e: e.scalar_tensor_tensor(dst[:, hc], kk[:, hc], -CW1, ang[:, hc], op0=ALU.mult, op1=ALU.add),
                         reads=[kK, kA], writes=[key])
                    P.op("dve", lambda e: e.scalar_tensor_tensor(dst[:, hc], kk[:, hc], -CW2, dst[:, hc], op0=ALU.mult, op1=ALU.add),
                         reads=[kK, key], writes=[key])
                    P.op("dve", lambda e: e.tensor_scalar(dst[:, hc], dst[:, hc], shift, 3.1415925, op0=ALU.add, op1=ALU.min),
                         reads=[key], writes=[key])
                    P.op("dve", lambda e: e.tensor_scalar(dst[:, hc], dst[:, hc], -3.1415925, None, op0=ALU.max), reads=[key], writes=[key])
                    P.op("act", lambda e: e.activation(out=dst[:, hc], in_=dst[:, hc], func=AF.Sin), reads=[key], writes=[key])

                reduce_to(St, 0.0, "St%d" % h2)
                P.op("dve", lambda e: e.tensor_scalar(St[:, hc], St[:, hc], sgn, None, op0=ALU.mult), reads=["cf", "St%d" % h2], writes=["St%d" % h2])
                reduce_to(Ct, float(np.pi / 2), "Ct%d" % h2)
            P.op("pool", lambda e: e.memset(wp[:], 0.0), writes=["wp"])
            for hh in range(2):
                P.op("pool", lambda e, hh=hh: e.tensor_copy(wk2[:, :, hh * 64:(hh + 1) * 64], win[:, :, OFF_KD:OFF_KD + 64]),
                     reads=["win"], writes=["wk2"])
                P.op("pool", lambda e, hh=hh: e.tensor_copy(wk2[:, :, 128 + hh * 64:128 + (hh + 1) * 64],
                                                            win[:, :, OFF_KI:OFF_KI + 64]), reads=["win"], writes=["wk2"])
            pbase = [OFF_QD + 128 * j for j in range(4)] + [OFF_QI + 128 * j for j in range(4)]
            for pc in range(10):
                for hh in range(2):
                    if pc < 8:
                        b0 = pbase[pc] + 64 * hh
                    else:
                        b0 = OFF_KD if pc == 8 else OFF_KI
                    o = pc * 128 + 64 * hh
                    P.op("pool", lambda e, o=o, b0=b0: e.tensor_copy(wp[:, :, o:o + 8], win[:, :, b0 + 8:b0 + 16]),
                         reads=["win"], writes=["wp"])
                    P.op("pool", lambda e, o=o, b0=b0: e.tensor_copy(wp[:, :, o + 8:o + 16], win[:, :, b0:b0 + 8]),
                         reads=["win"], writes=["wp"])
            uT = sb("uT", [128, 2, KC, TB], BF16)
            fst = sb("fst", [128, 4, TB], BF16)
            t1 = sb("t1", [128, 2, TB])
            t2 = sb("t2", [128, 2, TB])
            vst = sb("vst", [128, 2, 576], BF16)
            wist = sb("wist", [128, 2, 8])
            pA = [ps("pA%d" % i, [128, TB]) for i in range(3)]
            pB = [ps("pB%d" % i, [128, TB]) for i in range(2)]
            pV = [ps("pV%d" % i, [128, 512]) for i in range(2)]
            pW = ps("pW", [128, 128])
            chunks = []
            for j in range(4):
                chunks.append((j, win, OFF_QSB + 128 * j, None))
            for j in range(4):
                chunks.append((4 + j, win, OFF_KSB + 128 * j, None))
            for j in range(4):
                chunks.append((8 + j, win, OFF_QD + 128 * j, j))
            for j in range(4):
                chunks.append((12 + j, win, OFF_QI + 128 * j, 4 + j))
            chunks.append((16, wk2, 0, 8))
            chunks.append((17, wk2, 128, 9))
            ca = cbn = fs = rc = vc = 0
            def load_uT(b):
                if b < NB:
                    P.dma("sp", uT[:, b % 2, :, :], uT_s[:, :, b * TB:(b + 1) * TB].rearrange("c p t -> p c t"),
                          reads=["uTd%d" % t for t in range(4 * b, 4 * b + 4)], writes=["uT%d" % (b % 2)])
            load_uT(0)
            for b in range(NB):
                u = b % 2
                load_uT(b + 1)
                tok = slice(b * TB, (b + 1) * TB)
                for (ci, wt, off, pidx) in chunks:
                    qa = ca % 3
                    ca += 1
                    for k in range(KC):
                        P.op("pe", lambda e, k=k, wt=wt, off=off, qa=qa: e.matmul(pA[qa][:], wt[:, k, off:off + 128], uT[:, u, k, :],
                                                                                 start=(k == 0), stop=(k == KC - 1)),
                             reads=["win", "wk2", "uT%d" % u], writes=["pA%d" % qa])
                    f = fs % 4
                    fs += 1
                    qscale = 0.125 if (ci < 4 or 8 <= ci < 12) else 1.0
                    if pidx is None:
                        P.op("act", lambda e, qa=qa, f=f: e.activation(out=fst[:, f, :], in_=pA[qa][:], func=AF.Identity, scale=qscale),
                             reads=[], writes=["pA%d" % qa, "fst%d" % f])
                    else:
                        qb = cbn % 2
                        cbn += 1
                        for k in range(KC):
                            P.op("pe", lambda e, k=k, pidx=pidx, qb=qb: e.matmul(pB[qb][:], wp[:, k, pidx * 128:(pidx + 1) * 128],
                                                                                uT[:, u, k, :], start=(k == 0), stop=(k == KC - 1)),
                                 reads=["wp", "uT%d" % u], writes=["pB%d" % qb])
                        r = rc % 2
                        rc += 1
                        P.op("dve", lambda e, qa=qa, r=r: e.scalar_tensor_tensor(t1[:, r, :], pA[qa][:], qscale, Ct[:, tok], op0=ALU.mult, op1=ALU.mult),
                             reads=["Ct%d" % (b // 4)], writes=["pA%d" % qa, "t1%d" % r])
                        P.op("dve", lambda e, qb=qb, r=r: e.scalar_tensor_tensor(t2[:, r, :], pB[qb][:], qscale, St[:, tok], op0=ALU.mult, op1=ALU.mult),
                             reads=["St%d" % (b // 4)], writes=["pB%d" % qb, "t2%d" % r])
                        P.op("pool", lambda e, r=r, f=f: e.tensor_tensor(fst[:, f, :], t1[:, r, :], t2[:, r, :], op=ALU.add),
                             reads=["t1%d" % r, "t2%d" % r], writes=["fst%d" % f])
                    P.dma("sp", qk_s[ci, :, tok], fst[:, f, :], reads=["fst%d" % f], writes=["qkd%d_%d" % (ci, b)])
                for i in range(4):
                    t = 4 * b + i
                    q = vc % 2
                    vc += 1
                    for k in range(KC):
                        P.op("pe", lambda e, k=k, i=i, q=q: e.matmul(pV[q][:], uT[:, u, k, tsl(i)], win[:, k, OFF_VSB:OFF_VSB + 512],
                                                                    start=(k == 0), stop=(k == KC - 1)),
                             reads=["win", "uT%d" % u], writes=["pV%d" % q])
                    for k in range(KC):
                        P.op("pe", lambda e, k=k, i=i: e.matmul(pW[:, 0:64], uT[:, u, k, tsl(i)], win[:, k, OFF_VD:OFF_VD + 64],
                                                               start=(k == 0), stop=(k == KC - 1), skip_group_check=True),
                             reads=["win", "uT%d" % u], writes=["pW"])
                    for k in range(KC):
                        P.op("pe", lambda e, k=k, i=i: e.matmul(pW[:, 64:72], uT[:, u, k, tsl(i)], win[:, k, OFF_WI:OFF_WI + 8],
                                                               start=False, stop=(k == KC - 1), skip_group_check=True),
                             reads=["win", "uT%d" % u], writes=["pW"])
                    P.op("act", lambda e, q=q: e.copy(vst[:, q, 0:512], pV[q][:]), reads=[], writes=["pV%d" % q, "vst%d" % q])
                    P.op("dve", lambda e, q=q: e.tensor_copy(vst[:, q, 512:576], pW[:, 0:64]), reads=[], writes=["pW", "vst%d" % q])
                    P.op("dve", lambda e, q=q: e.tensor_scalar(wist[:, q, :], pW[:, 64:72], float(8 ** -0.5 * 0.125), None, op0=ALU.mult),
                         reads=[], writes=["pW", "wist%d" % q])
                    P.dma("sp", v_s[tsl(t), :], vst[:, q, :], reads=["vst%d" % q], writes=["vd%d" % t])
                    P.dma("sp", wi_s[tsl(t), :], wist[:, q, :], reads=["wist%d" % q], writes=["wid%d" % t])
            P.barrier()

    def p3_phase():
        with contextlib.ExitStack() as es:
            sb = lambda n, s, d=F32: es.enter_context(nc.sbuf_tensor("p3" + n, s, d))
            cb = sb("cb", [128, 384 + 2048], BF16)
            P.dma("pool", cb[:], cb_d, writes=["cb"])
            cf = sb("cf", [128, 130])
            P.dma("sp", cf[:], cf_d, writes=["cf"])
            ident = cb[:, 0:128]
            nUincl = cb[:, 128:256]
            nLstr = cb[:, 256:384]
            sbmask = cb[:, 384:384 + 2048].rearrange("p (r t) -> p r t", r=4)
            dsaneg = cf[:, 0:128]
            qz = [sb("qz%d" % i, [128, 4, SEQ], BF16) for i in range(2)]
            P.op("pool", lambda e: e.memset(qz[0][64:128, :, :], 0.0), writes=["qz0z"])
            P.op("pool", lambda e: e.memset(qz[1][0:64, :, :], 0.0), writes=["qz1z"])
            ksb = sb("ksb", [128, 4, SEQ], BF16)
            nksb = sb("nksb", [128, 4, SEQ], BF16)
            qd = sb("qd", [128, 4, SEQ], BF16)
            qi = sb("qi", [128, 4, SEQ], BF16)
            kdz = sb("kdz", [128, 2, SEQ], BF16)
            kiz = sb("kiz", [128, 2, SEQ], BF16)
            for kz_ in (kdz, kiz):
                P.op("pool", lambda e: e.memset(kz_[64:128, 0, :], 0.0), writes=["kzz"])
                P.op("pool", lambda e: e.memset(kz_[0:64, 1, :], 0.0), writes=["kzz"])
            v = sb("v", [128, 16, 578], BF16)
            wi = sb("wi", [128, 16, 8])
            P.op("pool", lambda e: e.memset(v[:, :, 576:578], 0.0), writes=["vone"])
            P.op("pool", lambda e: e.memset(v[:, :, 576:577], 1.0), writes=["vone"])
            def load_sb(sq):
                tok = slice(sq * SEQ, (sq + 1) * SEQ)
                for j in range(4):
                    P.dma("sp", qz[0][0:64, j, :], qk_s[j, 0:64, tok], writes=["qsb"])
                    P.dma("sp", qz[1][64:128, j, :], qk_s[j, 64:128, tok], writes=["qsb"])
                for j in range(4):
                    P.dma("sp", ksb[:, j, :], qk_s[4 + j, :, tok], writes=["ksb"])
                for j in range(4):
                    P.op("dve", lambda e: e.tensor_scalar(nksb[:, j, :], ksb[:, j, :], -1.0, None, op0=ALU.mult), reads=["ksb"], writes=["nksb"])

            def load_rest(sq):
                tok = slice(sq * SEQ, (sq + 1) * SEQ)
                P.dma("act", v[:, :, 0:576], v_s[tok, :].rearrange("(n p) c -> p n c", p=128), writes=["v"])
                P.dma("act", wi[:], wi_s[tok, :].rearrange("(n p) c -> p n c", p=128), writes=["wi"])
                for (tile_, c0, key) in ((qd, 8, "qd"), (qi, 12, "qi")):
                    for j in range(4):
                        P.dma("sp", tile_[:, j, :], qk_s[c0 + j, :, tok], writes=[key])
                for (kz_, ci, key) in ((kdz, 16, "kd2"), (kiz, 17, "ki2")):
                    P.dma("sp", kz_[0:64, 0, :], qk_s[ci, 0:64, tok], writes=[key])
                    P.dma("sp", kz_[64:128, 1, :], qk_s[ci, 64:128, tok], writes=[key])

            load_sb(0)
            for sq in range(NSEQ):
                load_rest(sq)

                with contextlib.ExitStack() as es2:
                  if "nosb" not in phases:
                    sb2 = lambda n, s, d=F32: es2.enter_context(nc.sbuf_tensor("sb%d" % sq + n, s, d))
                    ps2 = lambda n, s, d=F32: es2.enter_context(nc.psum_tensor("sb%d" % sq + n, s, d))
                    NS = 4
                    E = sb2("E", [128, NS, 512])
                    SP = sb2("SP", [128, NS, 2, 512], BF16)
                    A = sb2("A", [128, NS, 2, 512], BF16)
                    yacc = sb2("yacc", [64, NS, 512])
                    yst = sb2("yst", [64, NS, 512], BF16)
                    pZ = [ps2("pZ%d" % i, [128, 512]) for i in range(2)]
                    pC = [ps2("pC%d" % i, [128, 512]) for i in range(NS)]
                    pY = [ps2("pY%d" % i, [64, 512]) for i in range(2)]
                    mask128 = sbmask[:, 0, 0:128]

                    def sb_stream(s_, h, qc):
                        j, half = h // 2, h % 2
                        po = slice(64 * half, 64 * half + 64)
                        zb = s_ % 2
                        S = "s%d" % s_
                        kmax = 4 * qc + 3
                        nstep = kmax + 1
                        P.op("dve", lambda e: e.memset(yacc[:, s_, :], 0.0), writes=["yacc" + S])
                        if s_ >= 2:
                            yield
                        for step in range(nstep):
                            kb = kmax - step
                            r = kb - 4 * qc
                            c0 = 128 * max(0, r)
                            cols = slice(c0, 512)
                            dcols = slice(c0, c0 + 128)
                            qcols = slice(qc * 512 + c0, (qc + 1) * 512)
                            kcols = tsl(kb)
                            par = step % 2
                            spk = "SP%s%d" % (S, par)
                            ak = "A%s%d" % (S, par)
                            P.op("pe", lambda e: e.matmul(pZ[zb][:, cols], ksb[:, j, kcols], qz[half][:, j, qcols], start=True, stop=True,
                                                          skip_group_check=True),
                                 reads=["ksb", "qsb", "qz0z", "qz1z"], writes=["pZ%d" % zb])
                            yield
                            P.op("act", lambda e: e.activation(out=E[:, s_, cols], in_=pZ[zb][:, cols], func=AF.Exp),
                                 reads=[], writes=["pZ%d" % zb, "E" + S])
                            yield
                            P.op("act", lambda e: e.activation(out=SP[:, s_, par, cols], in_=E[:, s_, cols], func=AF.Ln, bias=1.0),
                                 reads=["E" + S], writes=[spk])
                            if r >= 0:
                                P.op("dve", lambda e: e.tensor_tensor(SP[:, s_, par, dcols], SP[:, s_, par, dcols], mask128, op=ALU.mult),
                                     reads=["cb", spk], writes=[spk])
                            yield
                            P.op("pe", lambda e: e.matmul(pC[s_][:, cols], ksb[:, j, kcols], qz[half][:, j, qcols], start=(step == 0), stop=False,
                                                          skip_group_check=True),
                                 reads=["ksb", "qsb"], writes=["pC" + S])
                            P.op("pe", lambda e: e.matmul(pC[s_][:, cols], nUincl, SP[:, s_, par, cols], start=False, stop=True,
                                                          skip_group_check=True),
                                 reads=["cb", spk], writes=["pC" + S])
                            yield
                            P.op("act", lambda e: e.activation(out=A[:, s_, par, cols], in_=pC[s_][:, cols], func=AF.Exp),
                                 reads=[], writes=["pC" + S, ak])
                            if r >= 0:
                                P.op("dve", lambda e: e.tensor_tensor(A[:, s_, par, dcols], A[:, s_, par, dcols], mask128, op=ALU.mult),
                                     reads=["cb", ak], writes=[ak])
                            yield
                            P.op("pe", lambda e: e.matmul(pY[zb][:, cols], v[:, kb, h * 64:(h + 1) * 64], A[:, s_, par, cols],
                                                          start=True, stop=True, skip_group_check=True),
                                 reads=["v", ak], writes=["pY%d" % zb])
                            if step < nstep - 1:
                                P.op("pe", lambda e: e.matmul(pC[s_][:, cols], nksb[:, j, kcols], qz[half][:, j, qcols], start=False, stop=False,
                                                              skip_group_check=True),
                                     reads=["nksb", "qsb"], writes=["pC" + S])
                                P.op("pe", lambda e: e.matmul(pC[s_][:, cols], nLstr, SP[:, s_, par, cols], start=False, stop=False,
                                                              skip_group_check=True),
                                     reads=["cb", spk], writes=["pC" + S])
                            P.op("dve", lambda e: e.tensor_tensor(yacc[:, s_, cols], yacc[:, s_, cols], pY[zb][:, cols], op=ALU.add),
                                 reads=["yacc" + S], writes=["pY%d" % zb, "yacc" + S])
                            yield
                        P.op("dve", lambda e: e.tensor_copy(yst[:, s_, :], yacc[:, s_, :]), reads=["yacc" + S], writes=["yst" + S])
                        P.dma("sp", ysbT_s[h // 2, 64 * (h % 2):64 * (h % 2) + 64, sq * SEQ + qc * 512: sq * SEQ + (qc + 1) * 512], yst[:, s_, :],
                              reads=["yst" + S], writes=["ysbd%d_%d" % (h, sq * 4 + qc)])

                    for qc in range(4):
                        for g in range(2):
                            P.run_threads([sb_stream(s_, 4 * g + s_, qc) for s_ in range(NS)])
                    P.barrier()
                    if sq + 1 < NSEQ:
                        load_sb(sq + 1)

                with contextlib.ExitStack() as es2:
                  if "nodsa" not in phases:
                    sb2 = lambda n, s, d=F32: es2.enter_context(nc.sbuf_tensor("ds%d" % sq + n, s, d))
                    ps2 = lambda n, s, d=F32: es2.enter_context(nc.psum_tensor("ds%d" % sq + n, s, d))
                    Sc = sb2("Sc", [128, 4, SEQ])
                    R = sb2("R", [128, 2, 512])
                    Mb = sb2("Mb", [128, 2, SEQ], BF16)
                    MT = sb2("MT", [128, 4, 16, 128], BF16)
                    PT = sb2("PT", [128, 3, 512], BF16)
                    yd = sb2("yd", [128, 512], BF16)
                    ydst = sb2("ydst", [128, 2, 4, 128], BF16)
                    sm = sb2("sm", [128, 16])
                    rec = sb2("rec", [128, 8, 1])
                    pD = [ps2("pD%d" % i, [128, 512]) for i in range(2)]
                    pM = ps2("pM", [128, 8, 128], BF16)
                    pL = [ps2("pL%d" % i, [128, 512]) for i in range(2)]
                    pYd = ps2("pYd", [128, 2, 512])
                    pM2 = ps2("pM2", [128, 8, 128], BF16)
                    cnts = dict(d=0, l=0)

                    def stage1(i):
                        nk = (i + 1) * 128
                        nch = (nk + 511) // 512
                        tcols = tsl(i)
                        z = i % 4
                        sck = "Sc%d" % z
                        for hh in range(8):
                            j, s_ = hh // 2, hh % 2
                            po = slice(64 * s_, 64 * s_ + 64)
                            for c in range(nch):
                                n = min(512, nk - c * 512)
                                q = cnts["d"] % 2
                                cnts["d"] += 1
                                cc = slice(c * 512, c * 512 + n)
                                P.op("pe", lambda e: e.matmul(pD[q][:, 0:n], qi[:, j, tcols], kiz[:, s_, cc], start=True, stop=True),
                                     reads=["qi", "ki2", "kzz"], writes=["pD%d" % q])
                                P.op("act", lambda e: e.activation(out=R[:, q, 0:n], in_=pD[q][:, 0:n], func=AF.Relu),
                                     reads=[], writes=["pD%d" % q, "R%d" % q])
                                if hh == 0:
                                    P.op("dve", lambda e: e.tensor_scalar(Sc[:, z, cc], R[:, q, 0:n], wi[:, i, 0:1], None, op0=ALU.mult),
                                         reads=["R%d" % q, "wi"], writes=[sck])
                                else:
                                    P.op("dve", lambda e: e.scalar_tensor_tensor(Sc[:, z, cc], R[:, q, 0:n], wi[:, i, hh:hh + 1], Sc[:, z, cc],
                                                                               op0=ALU.mult, op1=ALU.add),
                                         reads=["R%d" % q, "wi", sck], writes=[sck])
                                yield

                    def stage2(i):
                        nk = (i + 1) * 128
                        z = i % 4
                        w = i % 2
                        smk = "sm%d" % w
                        mbk = "Mb%d" % w
                        lo, hi, mid, cnt, dlt = (sm[:, 8 * w + c:8 * w + c + 1] for c in range(5))
                        sck = "Sc%d" % z
                        if i >= 2:
                            P.op("dve", lambda e: e.tensor_reduce(lo, Sc[:, z, 0:nk], axis=AX.X, op=ALU.min), reads=[sck], writes=[smk])
                        P.op("dve", lambda e: e.tensor_tensor(Sc[:, z, nk - 128:nk], Sc[:, z, nk - 128:nk], dsaneg, op=ALU.add),
                             reads=["cf", sck], writes=[sck])
                        yield
                        if i >= 2:
                            P.op("dve", lambda e: e.tensor_reduce(hi, Sc[:, z, 0:nk], axis=AX.X, op=ALU.max), reads=[sck], writes=[smk])
                            P.op("dve", lambda e: e.tensor_tensor(hi, hi, lo, op=ALU.subtract), reads=[smk], writes=[smk])
                            on_act = w in ACT_CHAINS
                            sg = -1.0 if on_act else 1.0
                            if on_act:
                                P.op("dve", lambda e: e.tensor_scalar(lo, lo, -1.0, None, op0=ALU.mult), reads=[smk], writes=[smk])
                            yield
                            thr = float(2 * TOPK - 1 - nk) if on_act else (float(TOPK) - 0.5)
                            for it in range(NBIS):
                                ck = float(0.5 ** (it + 1))
                                P.op("dve", lambda e: e.scalar_tensor_tensor(mid, hi, sg * ck, lo, op0=ALU.mult, op1=ALU.add),
                                     reads=[smk], writes=[smk + "m"])
                                yield
                                if on_act:
                                    P.op("act", lambda e: e.activation(out=Mb[:, w, 0:nk], in_=Sc[:, z, 0:nk], func=AF.Sign, bias=mid,
                                                                       accum_out=cnt),
                                         reads=[sck, smk + "m"], writes=[mbk, smk + "c"])
                                else:
                                    P.op("dve", lambda e: e.tensor_scalar(Mb[:, w, 0:nk], Sc[:, z, 0:nk], mid, 0.0, op0=ALU.is_gt, op1=ALU.add,
                                                                          accum_out=cnt), reads=[sck, smk + "m"], writes=[mbk, smk + "c"])
                                yield
                                P.op("dve", lambda e: e.tensor_scalar(dlt, cnt, thr, sg * ck, op0=ALU.is_gt, op1=ALU.mult),
                                     reads=[smk + "c"], writes=[smk + "d"])
                                P.op("dve", lambda e: e.scalar_tensor_tensor(lo, dlt, hi, lo, op0=ALU.mult, op1=ALU.add),
                                     reads=[smk + "d", smk], writes=[smk])
                                yield
                            if on_act:
                                P.op("dve", lambda e: e.tensor_scalar(lo, lo, -1.0, None, op0=ALU.mult), reads=[smk], writes=[smk])
                            P.op("dve", lambda e: e.tensor_scalar(Mb[:, w, 0:nk], Sc[:, z, 0:nk], lo, None, op0=ALU.is_gt),
                                 reads=[sck, smk], writes=[mbk])
                        else:
                            P.op("dve", lambda e: e.tensor_scalar(Mb[:, w, 0:nk], Sc[:, z, 0:nk], -1e29, None, op0=ALU.is_gt),
                                 reads=[sck], writes=[mbk])
                        yield
                        for g0 in range(0, i + 1, 8):
                            g1 = min(i + 1, g0 + 8)
                            for kb in range(g0, g1):
                                P.op("pe", lambda e: e.transpose(pM[:, kb - g0, :], Mb[:, w, tsl(kb)], ident),
                                     reads=[mbk, "cb"], writes=["pM"])
                            P.op("act", lambda e: e.activation(out=MT[:, z, g0:g1, :], in_=pM[:, 0:g1 - g0, :], func=AF.Identity,
                                                               scale=MASK_BIG, bias=-MASK_BIG),
                                 reads=[], writes=["pM", "MT%d" % z])
                            yield

                    def stage3(i):
                        tcols = tsl(i)
                        z = i % 4
                        def emit_pv(kb, s_, q3):
                            for j in range(4):
                                P.op("pe", lambda e: e.matmul(pYd[:, s_, j * 66:(j + 1) * 66], PT[:, q3, j * 128:(j + 1) * 128], v[:, kb, 512:578],
                                                              start=(kb == 0 and j == 0), stop=(kb == i), skip_group_check=True),
                                     reads=["PT%d" % q3, "v", "vone"], writes=["pYd%d" % s_])
                        pend = []
                        for kb in range(i + 1):
                            for s_ in range(2):
                                q = cnts["l"] % 2
                                q3 = cnts["l"] % 3
                                cnts["l"] += 1
                                po = slice(64 * s_, 64 * s_ + 64)
                                for j in range(4):
                                    P.op("pe", lambda e: e.matmul(pL[q][:, j * 128:(j + 1) * 128], kdz[:, s_, tsl(kb)], qd[:, j, tcols],
                                                                  start=True, stop=False, skip_group_check=True),
                                         reads=["kd2", "kzz", "qd"], writes=["pL%d" % q])
                                    P.op("pe", lambda e: e.matmul(pL[q][:, j * 128:(j + 1) * 128], ident, MT[:, z, kb, :],
                                                                  start=False, stop=True, skip_group_check=True),
                                         reads=["cb", "MT%d" % z], writes=["pL%d" % q])
                                P.op("act", lambda e: e.activation(out=PT[:, q3, :], in_=pL[q][:], func=AF.Exp),
                                     reads=[], writes=["pL%d" % q, "PT%d" % q3])
                                pend.append((kb, s_, q3))
                                if len(pend) > 2:
                                    emit_pv(*pend.pop(0))
                                yield
                        while pend:
                            emit_pv(*pend.pop(0))
                        for bank in range(2):
                            yv = pYd[:, bank, 0:264].rearrange("p (h c) -> p h c", h=4)
                            ydv = yd[:].rearrange("p (j s c) -> p j s c", j=4, s=2)[:, :, bank, :]
                            P.op("dve", lambda e: e.reciprocal(rec[:, bank * 4:(bank + 1) * 4, :], yv[:, :, 64:65]),
                                 reads=[], writes=["pYd%d" % bank, "rec%d" % bank])
                            P.op("dve", lambda e: e.tensor_tensor(ydv, yv[:, :, 0:64],
                                                                  rec[:, bank * 4:(bank + 1) * 4, :].to_broadcast([128, 4, 64]), op=ALU.mult),
                                 reads=["rec%d" % bank], writes=["pYd%d" % bank, "yd"])
                        yield
                        u = i % 2
                        for cchunk in range(4):
                            P.op("pe", lambda e: e.transpose(pM2[:, cchunk, :], yd[:, tsl(cchunk)], ident),
                                 reads=["yd", "cb"], writes=["pM2"])
                        P.op("act", lambda e: e.copy(ydst[:, u, :, :], pM2[:, 0:4, :]), reads=[], writes=["pM2", "ydst%d" % u])
                        P.dma("sp", ydT_s[:, :, sq * SEQ + i * 128: sq * SEQ + (i + 1) * 128].rearrange("c p t -> p c t"),
                              ydst[:, u, :, :], reads=["ydst%d" % u], writes=["ydd%d" % (sq * 16 + i)])

                    def seq(*gens):
                        for g in gens:
                            yield from g
                    pairs = [(2 * m + 1, 2 * m) for m in (1, 6, 7, 4, 5, 2, 3, 0)]
                    for tau in range(len(pairs) + 2):
                        th = []
                        if tau < len(pairs):
                            th.append(seq(stage1(pairs[tau][0]), stage1(pairs[tau][1])))
                        if 0 <= tau - 1 < len(pairs):
                            th.append(stage2(pairs[tau - 1][0]))
                            th.append(stage2(pairs[tau - 1][1]))
                        if 0 <= tau - 2 < len(pairs):
                            th.append(seq(stage3(pairs[tau - 2][0]), stage3(pairs[tau - 2][1])))
                        P.run_threads(th)
                    P.barrier()
            P.barrier()

    def p4a_phase():
        with contextlib.ExitStack() as es:
            sb = lambda n, s, d=F32: es.enter_context(nc.sbuf_tensor("p4a" + n, s, d))
            ps = lambda n, s, d=F32: es.enter_context(nc.psum_tensor("p4a" + n, s, d))
            wg = sb("wg", [128, KC, 2048], BF16)
            for k in range(KC):
                P.dma("pool", wg[:, k, :], win_d[tsl(k), OFF_GSB:OFF_GSB + 2048], writes=["wg"])
            wosb = sb("wosb", [128, 4, D], BF16)
            P.dma("pool", wosb[:], wosb_d.rearrange("(c p) n -> p c n", p=128), writes=["wosb"])
            wod = sb("wod", [128, 4, D], BF16)
            P.dma("pool", wod[:], wod_d.rearrange("(c p) n -> p c n", p=128), writes=["wod"])
            wo = sb("wo", [128, KC, D], BF16)
            P.dma("pool", wo[:], wo_d.rearrange("(c p) n -> p c n", p=128), writes=["wo"])
            uT = sb("uT", [128, 2, KC, TB], BF16)
            ysbT = sb("ysbT", [128, 2, 4, TB], BF16)
            ydT = sb("ydT", [128, 2, 4, TB], BF16)
            mT = sb("mT", [128, KC, TB], BF16)
            s1 = sb("s1", [128, 2, TB])
            s2 = sb("s2", [128, 2, TB])
            t1 = sb("t1", [128, 2, TB])
            t2 = sb("t2", [128, 2, TB])
            hs = sb("hs", [128, 4, D])
            pG = [ps("pG%d" % i, [128, TB]) for i in range(2)]
            pY = [ps("pY%d" % i, [128, TB]) for i in range(2)]
            pO = [ps("pO%d" % i, [128, 512]) for i in range(2)]
            oc = hc = 0
            def load_blk(b):
                if b >= NB:
                    return
                u = b % 2
                tok = slice(b * TB, (b + 1) * TB)
                P.dma("sp", uT[:, u, :, :], uT_s[:, :, tok].rearrange("c p t -> p c t"), writes=["uT%d" % u])
                P.dma("sp", ysbT[:, u, :, :], ysbT_s[:, :, tok].rearrange("c p t -> p c t"), writes=["ysbT%d" % u])
                P.dma("sp", ydT[:, u, :, :], ydT_s[:, :, tok].rearrange("c p t -> p c t"), writes=["ydT%d" % u])
            load_blk(0)
            for b in range(NB):
                u = b % 2
                tok = slice(b * TB, (b + 1) * TB)
                load_blk(b + 1)
                for i in range(4):
                    P.dma("sp", hs[:, i, :], h_s[tsl(4 * b + i), :], reads=["hd%d" % (4 * b + i)], writes=["hs%d" % i])
                for c in range(KC):
                    r = c % 2
                    for (g, pg, sdst, skey) in ((0, pG[0], s1, "s1"), (1, pG[1], s2, "s2")):
                        for k in range(KC):
                            P.op("pe", lambda e, k=k, g=g, pg=pg, c=c: e.matmul(
                                pg[:], wg[:, k, g * 1024 + c * 128: g * 1024 + (c + 1) * 128], uT[:, u, k, :],
                                start=(k == 0), stop=(k == KC - 1)),
                                reads=["wg", "uT%d" % u], writes=["pG%d" % g])
                        P.op("act", lambda e, pg=pg, sdst=sdst, r=r: e.activation(out=sdst[:, r, :], in_=pg[:], func=AF.Tanh, scale=0.5),
                             reads=[], writes=["pG%d" % g, "%s%d" % (skey, r)])
                    for hh in range(4):
                        P.op("pe", lambda e, hh=hh, c=c: e.matmul(pY[0][:], wosb[:, hh, tsl(c)], ysbT[:, u, hh, :],
                                                                   start=(hh == 0), stop=(hh == 3)),
                             reads=["wosb", "ysbT%d" % u], writes=["pY0"])
                    for k in range(4):
                        P.op("pe", lambda e, k=k, c=c: e.matmul(pY[1][:], wod[:, k, tsl(c)], ydT[:, u, k, :],
                                                                 start=(k == 0), stop=(k == 3)),
                             reads=["wod", "ydT%d" % u], writes=["pY1"])
                    P.op("dve", lambda e, r=r: e.scalar_tensor_tensor(t1[:, r, :], s1[:, r, :], 1.0, pY[0][:], op0=ALU.add, op1=ALU.mult),
                         reads=["s1%d" % r], writes=["pY0", "t1%d" % r])
                    P.op("dve", lambda e, r=r: e.scalar_tensor_tensor(t2[:, r, :], s2[:, r, :], 1.0, pY[1][:], op0=ALU.add, op1=ALU.mult),
                         reads=["s2%d" % r], writes=["pY1", "t2%d" % r])
                    P.op("pool", lambda e, r=r, c=c: e.tensor_tensor(mT[:, c, :], t1[:, r, :], t2[:, r, :], op=ALU.add),
                         reads=["t1%d" % r, "t2%d" % r], writes=["mT"])
                for i in range(4):
                    t = 4 * b + i
                    sl = i
                    for c2 in range(2):
                        q = oc % 2
                        oc += 1
                        for k in range(KC):
                            P.op("pe", lambda e, k=k, i=i, c2=c2, q=q: e.matmul(pO[q][:], mT[:, k, tsl(i)], wo[:, k, c2 * 512:(c2 + 1) * 512],
                                                                               start=(k == 0), stop=(k == KC - 1)),
                                 reads=["mT", "wo"], writes=["pO%d" % q])
                        P.op("dve", lambda e, q=q, sl=sl, c2=c2: e.scalar_tensor_tensor(
                            hs[:, sl, c2 * 512:(c2 + 1) * 512], pO[q][:], 0.5, hs[:, sl, c2 * 512:(c2 + 1) * 512],
                            op0=ALU.mult, op1=ALU.add), reads=[], writes=["pO%d" % q, "hs%d" % sl])
                    P.dma("sp", h_s[tsl(t), :], hs[:, sl, :], reads=["hs%d" % sl], writes=["hd%d" % t])
            P.barrier()

    def p4c_phase():
        with contextlib.ExitStack() as es:
            sb = lambda n, s, d=F32: es.enter_context(nc.sbuf_tensor("p4c" + n, s, d))
            ps = lambda n, s, d=F32: es.enter_context(nc.psum_tensor("p4c" + n, s, d))
            c = load_consts(es)
            ident = c["ident"]
            wpg = sb("wpg", [128, KC, D], BF16)
            P.dma("pool", wpg[:], wpg_d.rearrange("(c p) n -> p c n", p=128), writes=["wpg"])
            wpp = sb("wpp", [128, 2, D], BF16)
            P.dma("pool", wpp[:], wpp_d.rearrange("(c p) n -> p c n", p=128), writes=["wpp"])
            gB = make_gB(es, c, 3, "p4cgB")
            gfin = sb("gfin", [128, D])
            P.dma("sp", gfin[:], gfin_d.broadcast_to([128, D]), writes=["gfin"])
            NSG = 4
            hs = sb("hs", [128, 4 * NSG, D])
            pt = sb("pt", [128, 4 * NSG, 256], BF16)
            xn = sb("xn", [128, 2, D], BF16)
            sqj = sb("sqj", [128, 2, D], BF16)
            hnT = sb("hnT", [128, 2, KC, 128], BF16)
            pTs = sb("pTs", [128, 2, 2, 128], BF16)
            gate = sb("gate", [128, 2, D])
            ot = sb("ot", [128, 2, D])
            ss = sb("ss", [128, NSG, 4])
            rstd = sb("rstd", [128, NSG, 4])
            ss2 = sb("ss2", [128, NSG, 4])
            rstd2 = sb("rstd2", [128, NSG, 4])
            pT = [ps("pT%d" % i, [128, KC, 128], BF16) for i in range(2)]
            pP = [ps("pP%d" % i, [128, KC, 128], BF16) for i in range(2)]
            pGt = [ps("pGt%d" % i, [128, 512]) for i in range(2)]
            pPp = [ps("pPp%d" % i, [128, 512]) for i in range(2)]
            NG = NT // 4

            def load_g(g):
                if 0 <= g < NG:
                    for i in range(4):
                        t = 4 * g + i
                        sl = (g % NSG) * 4 + i
                        P.dma("sp", hs[:, sl, :], h_s[tsl(t), :], reads=["hd%d" % t], writes=["hs%d" % sl])
                        P.dma("pool", pt[:, sl, :], p_d[tsl(t), :], writes=["pt%d" % sl])

            def stage_a(g):
                gp = g % NSG
                for i in range(4):
                    sl = gp * 4 + i
                    P.op("act", lambda e: e.activation(out=sqj[:, 0, :], in_=hs[:, sl, :], func=AF.Square, accum_out=ss[:, gp, i:i + 1]),
                         reads=["hs%d" % sl], writes=["sqj0", "ss%d" % gp])
                    yield
                P.op("dve", lambda e: e.tensor_scalar(ss[:, gp, :], ss[:, gp, :], 1.0 / D, EPS, op0=ALU.mult, op1=ALU.add),
                     reads=["ss%d" % gp], writes=["ss%d" % gp])
                yield
                P.op("act", lambda e: e.activation(out=ss[:, gp, :], in_=ss[:, gp, :], func=AF.Sqrt), reads=["ss%d" % gp], writes=["ss%d" % gp])
                yield
                P.op("dve", lambda e: e.reciprocal(rstd[:, gp, :], ss[:, gp, :]), reads=["ss%d" % gp], writes=["rstd%d" % gp])
                yield

            def tile_thread(g, i):
                gp = g % NSG
                sl = gp * 4 + i
                u = i % 2
                P.op("dve", lambda e: e.tensor_scalar(xn[:, u, :], hs[:, sl, :], rstd[:, gp, i:i + 1], None, op0=ALU.mult),
                     reads=["hs%d" % sl, "rstd%d" % gp], writes=["xn%d" % u])
                yield
                for k in range(KC):
                    P.op("pe", lambda e: e.transpose(pT[u][:, k, :], xn[:, u, tsl(k)], ident[:]),
                         reads=["xn%d" % u, "c_ident"], writes=["pT%d" % u])
                for k in range(2):
                    P.op("pe", lambda e: e.transpose(pP[u][:, k, :], pt[:, sl, tsl(k)], ident[:]),
                         reads=["pt%d" % sl, "c_ident"], writes=["pP%d" % u])
                yield
                P.op("dve", lambda e: e.tensor_tensor(hnT[:, u, :, :], pT[u][:], gB[:], op=ALU.mult),
                     reads=["p4cgB"], writes=["pT%d" % u, "hnT%d" % u])
                P.op("act", lambda e: e.copy(pTs[:, u, :, :], pP[u][:, 0:2, :]), reads=[], writes=["pP%d" % u, "pTs%d" % u])
                yield
                for c2 in range(2):
                    cs = slice(c2 * 512, (c2 + 1) * 512)
                    for k in range(KC):
                        P.op("pe", lambda e: e.matmul(pGt[u][:], hnT[:, u, k, :], wpg[:, k, cs], start=(k == 0), stop=(k == KC - 1)),
                             reads=["hnT%d" % u, "wpg"], writes=["pGt%d" % u])
                    for k in range(2):
                        P.op("pe", lambda e: e.matmul(pPp[u][:], pTs[:, u, k, :], wpp[:, k, cs], start=(k == 0), stop=(k == 1)),
                             reads=["pTs%d" % u, "wpp"], writes=["pPp%d" % u])
                    yield
                    P.op("act", lambda e: e.activation(out=gate[:, u, cs], in_=pGt[u][:], func=AF.Tanh, scale=0.5),
                         reads=[], writes=["pGt%d" % u, "gate%d" % u])
                    yield
                    P.op("dve", lambda e: e.scalar_tensor_tensor(gate[:, u, cs], gate[:, u, cs], 1.0, pPp[u][:], op0=ALU.add, op1=ALU.mult),
                         reads=["gate%d" % u], writes=["pPp%d" % u, "gate%d" % u])
                    yield
                P.op("dve", lambda e: e.scalar_tensor_tensor(hs[:, sl, :], gate[:, u, :], 0.5, hs[:, sl, :], op0=ALU.mult, op1=ALU.add),
                     reads=["gate%d" % u, "hs%d" % sl], writes=["hs%d" % sl])
                yield

            def stage_c(g):
                gp = g % NSG
                for i in range(4):
                    sl = gp * 4 + i
                    P.op("act", lambda e: e.activation(out=sqj[:, 1, :], in_=hs[:, sl, :], func=AF.Square, accum_out=ss2[:, gp, i:i + 1]),
                         reads=["hs%d" % sl], writes=["sqj1", "ss2%d" % gp])
                    yield
                P.op("dve", lambda e: e.tensor_scalar(ss2[:, gp, :], ss2[:, gp, :], 1.0 / D, EPS, op0=ALU.mult, op1=ALU.add),
                     reads=["ss2%d" % gp], writes=["ss2%d" % gp])
                yield
                P.op("act", lambda e: e.activation(out=ss2[:, gp, :], in_=ss2[:, gp, :], func=AF.Sqrt), reads=["ss2%d" % gp], writes=["ss2%d" % gp])
                yield
                P.op("dve", lambda e: e.reciprocal(rstd2[:, gp, :], ss2[:, gp, :]), reads=["ss2%d" % gp], writes=["rstd2%d" % gp])
                yield
                for i in range(4):
                    sl = gp * 4 + i
                    t = 4 * g + i
                    u = i % 2
                    P.op("dve", lambda e: e.scalar_tensor_tensor(ot[:, u, :], hs[:, sl, :], rstd2[:, gp, i:i + 1], gfin[:], op0=ALU.mult, op1=ALU.mult),
                         reads=["hs%d" % sl, "rstd2%d" % gp, "gfin"], writes=["ot%d" % u])
                    P.dma("sp", out_d[tsl(t), :], ot[:, u, :], reads=["ot%d" % u], writes=["outd%d" % t])
                    yield

            def seq(*gens):
                for g_ in gens:
                    yield from g_

            load_g(0)
            for tau in range(NG + 2):
                load_g(tau + 1)
                th = []
                if tau < NG:
                    th.append(stage_a(tau))
                if 0 <= tau - 1 < NG:
                    th.append(seq(tile_thread(tau - 1, 0), tile_thread(tau - 1, 2)))
                    th.append(seq(tile_thread(tau - 1, 1), tile_thread(tau - 1, 3)))
                if 0 <= tau - 2 < NG:
                    th.append(stage_c(tau - 2))
                P.run_threads(th)
            P.barrier()

    if "p1" in phases:
        ffn_phase("f1", x_d, h_s, w1a_d, w2a_d, 0, 1)
    if "p2" in phases:
        p2_phase()
    if "p3" in phases:
        p3_phase()
    if "p4a" in phases:
        p4a_phase()
    if "p4b" in phases:
        ffn_phase("f2", h_s, h_s, w1b_d, w2b_d, 2, None)
    if "p4c" in phases:
        p4c_phase()
    P.emit()
    return nc


def host_consts():
    j = np.arange(128)
    ident = np.eye(128, dtype=np.float32)
    nUincl = -(j[:, None] >= j[None, :]).astype(np.float32)
    nLstr = -(j[:, None] < j[None, :]).astype(np.float32)
    t = np.arange(512)
    sbmask = np.stack([(j[:, None] + 128 * r < t[None, :]).astype(np.float32) for r in range(4)], 1)
    cb = np.concatenate([ident, nUincl, nLstr, sbmask.reshape(128, 2048)], 1).astype(np.float32)
    dsaneg = np.where(j[None, :] > j[:, None], -1e30, 0.0).astype(np.float32)
    inv_freq = (500000.0 ** (-np.arange(0, 16, 2, dtype=np.float32) / 16)).astype(np.float32)
    pm = j % 64
    invf = np.where(pm < 16, inv_freq[pm % 8], 0.0).astype(np.float32)
    sgn = np.where(pm < 8, -1.0, np.where(pm < 16, 1.0, 0.0)).astype(np.float32)
    cf = np.concatenate([dsaneg, invf[:, None], sgn[:, None]], 1).astype(np.float32)
    return cb, cf


def make_in_maps(inputs, ncores=8):
    f = lambda a: np.ascontiguousarray(np.asarray(a), dtype=np.float32)
    x = f(inputs["x"])
    p = f(inputs["p"])[0]
    pos = np.ascontiguousarray(np.asarray(inputs["positions"]), dtype=np.int32)
    cb, cf = host_consts()
    gl = lambda g: f(g).reshape(KC, 128).T
    gcols = np.ascontiguousarray(np.concatenate([gl(inputs["ffn1_norm"][0]), gl(inputs["mix_norm"][0]),
                                                 gl(inputs["ffn2_norm"][0]), gl(inputs["ple_norm"][0])], 1))
    shared = {
        "w1a": f(inputs["ffn1_w1"][0]), "w2a": f(inputs["ffn1_w2"][0]), "win": f(inputs["w_in"][0]),
        "wosb": f(inputs["w_out_sb"][0]), "wod": f(inputs["w_out_dsa"][0]), "wo": f(inputs["w_out"][0]),
        "w1b": f(inputs["ffn2_w1"][0]), "w2b": f(inputs["ffn2_w2"][0]), "wpg": f(inputs["ple_w_gate"][0]),
        "wpp": f(inputs["ple_w_proj"][0]), "gcols": gcols, "gfin": f(inputs["final_norm"]).reshape(1, D),
        "cb": cb, "cf": cf,
    }
    maps = []
    for c in range(ncores):
        m = dict(shared)
        m["x"] = np.ascontiguousarray(x[NSEQ * c:NSEQ * (c + 1)].reshape(NTOK, D))
        m["p"] = np.ascontiguousarray(p[NSEQ * c:NSEQ * (c + 1)].reshape(NTOK, 256))
        m["pos"] = np.ascontiguousarray(pos[NSEQ * c:NSEQ * (c + 1)].reshape(1, NTOK))
        maps.append(m)
    return maps


def kernel(**inputs):
    nc = bass.Bass("TRN2", target_bir_lowering=False)
    build(nc)
    maps = make_in_maps(inputs, 8)
    res = run_bass_kernel_spmd(nc, maps, core_ids=list(range(8)))
    out = np.stack([r["out"].reshape(NSEQ, SEQ, D) for r in res.results], 0).reshape(16, SEQ, D)
    return out.astype(np.float32)
```

```python
import contextlib
import numpy as np
import concourse.bass as bass
import concourse.mybir as mybir
from concourse.bass_utils import run_bass_kernel_spmd

F32 = mybir.dt.float32
BF16 = mybir.dt.bfloat16
I32 = mybir.dt.int32
AF = mybir.ActivationFunctionType
ALU = mybir.AluOpType
AX = mybir.AxisListType

D = 1024
KC = 8
SEQ = 2048
NSEQ = 2
NTOK = NSEQ * SEQ
NT = NTOK // 128
TB = 512
NB = NTOK // TB
FF = 2816
FC = FF // 128
DIN = 4808
EPS = 1e-6
TOPK = 256
NBIS = 14
MASK_BIG = 30000.0
ACT_CHAINS = (1,)
OFF_QSB, OFF_KSB, OFF_VSB, OFF_QD, OFF_KD, OFF_VD, OFF_QI, OFF_KI, OFF_WI, OFF_GSB, OFF_GD = (
    0, 512, 1024, 1536, 2048, 2112, 2176, 2688, 2752, 2760, 3784)
TWO_PI = 6.283185307179586
CW1 = 6.28125
CW2 = TWO_PI - CW1


class _Key:
    __slots__ = ("w", "rs")

    def __init__(self):
        self.w = None
        self.rs = []


class _Rec:
    def __init__(self):
        self.call = None

    def __getattr__(self, name):
        def f(*a, **k):
            self.call = (name, a, k)
            return self
        return f


def _freeze(fn):
    rec = _Rec()
    fn(rec)
    name, a, k = rec.call
    return lambda e: getattr(e, name)(*a, **k)


class Prog:
    ENG = ("pe", "act", "dve", "pool", "sp")

    def __init__(self, nc, same_engine_sync=True, ndma_sems=8):
        self.nc = nc
        self.es = contextlib.ExitStack()
        self.streams = {e: [] for e in self.ENG}
        self.cnt = {e: 0 for e in self.ENG}
        self.sems = {}
        for e in self.ENG:
            self.sems["p_" + e] = self.es.enter_context(nc.semaphore("prog_" + e))
        self.known = {e: {} for e in self.ENG}
        self.same = same_engine_sync
        self.ndma = ndma_sems
        self.dq = {}
        self.keys = {}

    def key(self, name):
        k = self.keys.get(name)
        if k is None:
            k = _Key()
            self.keys[name] = k
        return k

    def _deps(self, reads, writes):
        ev = []
        for r in reads:
            k = self.key(r)
            if k.w is not None:
                ev.append(k.w)
        for w in writes:
            k = self.key(w)
            if k.w is not None:
                ev.append(k.w)
            ev.extend(k.rs)
        return ev

    def _waits(self, eng, evs):
        need = {}
        kn = self.known[eng]
        for (sid, val) in evs:
            if sid == "p_" + eng and (eng == "pe" or not self.same):
                continue
            if kn.get(sid, 0) >= val:
                continue
            if need.get(sid, 0) < val:
                need[sid] = val
        for sid, val in need.items():
            kn[sid] = val
        return list(need.items())

    def _commit(self, reads, writes, event):
        for r in reads:
            self.key(r).rs.append(event)
        for w in writes:
            k = self.key(w)
            k.w = event
            k.rs = []

    def op(self, eng, fn, reads=(), writes=()):
        waits = self._waits(eng, self._deps(reads, writes))
        self.cnt[eng] += 1
        event = ("p_" + eng, self.cnt[eng])
        self.streams[eng].append((waits, _freeze(fn), ("p_" + eng, 1)))
        self._commit(reads, writes, event)
        return event

    def dma(self, q, out, in_, reads=(), writes=()):
        d = self.dq.get(q)
        if d is None:
            ids = [f"d_{q}_{j}" for j in range(self.ndma)]
            for s in ids:
                self.sems[s] = self.es.enter_context(self.nc.semaphore(s))
            d = dict(i=0, ids=ids, vals=[0] * self.ndma, last=[None] * self.ndma)
            self.dq[q] = d
        j = d["i"] % self.ndma
        d["i"] += 1
        evs = self._deps(reads, writes)
        if d["last"][j] is not None:
            evs.append(d["last"][j])
        waits = self._waits(q, evs)
        d["vals"][j] += 16
        event = (d["ids"][j], d["vals"][j])
        d["last"][j] = event
        fn = lambda e, out=out, in_=in_: e.dma_start(out=out, in_=in_)
        self.streams[q].append((waits, fn, (d["ids"][j], 16)))
        self._commit(reads, writes, event)
        return event

    def _all_events(self):
        ev = []
        for q, d in self.dq.items():
            for e in d["last"]:
                if e is not None:
                    ev.append(e)
        for e in self.ENG:
            if self.cnt[e] > 0:
                ev.append(("p_" + e, self.cnt[e]))
        return ev

    def run_threads(self, gens):
        live = list(gens)
        while live:
            nxt = []
            for g in live:
                try:
                    next(g)
                    nxt.append(g)
                except StopIteration:
                    pass
            live = nxt

    def barrier(self):
        ev = self._all_events()
        for e in self.ENG:
            w = self._waits(e, ev)
            if w:
                self.streams[e].append((w, None, None))
        self.keys = {}

    def emit(self):
        nc = self.nc
        fw = self._waits("sp", self._all_events())
        self.streams["sp"].append((fw, None, None))
        with nc.Block() as block:
            def run(engname, handle):
                for waits, fn, inc in self.streams[engname]:
                    for sid, val in waits:
                        handle.wait_ge(self.sems[sid], val)
                    if fn is not None:
                        fn(handle).then_inc(self.sems[inc[0]], inc[1])

            @block.tensor
            def _(e):
                run("pe", e)

            @block.scalar
            def _(e):
                run("act", e)

            @block.vector
            def _(e):
                run("dve", e)

            @block.gpsimd
            def _(e):
                run("pool", e)

            @block.sync
            def _(e):
                run("sp", e)
        self.es.close()


def build(nc, dbg=False, phases=("p1", "p2", "p3", "p4a", "p4b", "p4c")):
    P = Prog(nc)

    def din(name, shape, dt=F32):
        return nc.dram_tensor(name, shape, dt, kind="ExternalInput").ap()

    def dscr(name, shape, dt):
        return nc.dram_tensor(name, shape, dt, kind=("ExternalOutput" if dbg else "Internal")).ap()

    x_d = din("x", [NTOK, D])
    p_d = din("p", [NTOK, 256])
    pos_d = din("pos", [1, NTOK], I32)
    w1a_d = din("w1a", [D, 2 * FF])
    w2a_d = din("w2a", [FF, D])
    win_d = din("win", [D, DIN])
    wosb_d = din("wosb", [512, D])
    wod_d = din("wod", [512, D])
    wo_d = din("wo", [D, D])
    w1b_d = din("w1b", [D, 2 * FF])
    w2b_d = din("w2b", [FF, D])
    wpg_d = din("wpg", [D, D])
    wpp_d = din("wpp", [256, D])
    gcols_d = din("gcols", [128, 4 * KC])
    gfin_d = din("gfin", [1, D])
    cb_d = din("cb", [128, 384 + 2048])
    cf_d = din("cf", [128, 130])
    out_d = nc.dram_tensor("out", [NTOK, D], F32, kind="ExternalOutput").ap()

    h_s = dscr("h_s", [NTOK, D], F32)
    uT_s = dscr("uT_s", [KC, 128, NTOK], BF16)
    qk_s = dscr("qk_s", [18, 128, NTOK], BF16)
    v_s = dscr("v_s", [NTOK, 576], BF16)
    wi_s = dscr("wi_s", [NTOK, 8], F32)
    ysbT_s = dscr("ysbT_s", [4, 128, NTOK], BF16)
    ydT_s = dscr("ydT_s", [4, 128, NTOK], BF16)

    def tsl(t):
        return slice(t * 128, (t + 1) * 128)

    ccount = [0]

    def load_consts(es, need_cb=True):
        ccount[0] += 1
        sb = lambda n, s, d=F32: es.enter_context(nc.sbuf_tensor("%s_%d" % (n, ccount[0]), s, d))
        c = {}
        c["gcols"] = sb("c_gcols", [128, 4 * KC])
        P.dma("sp", c["gcols"][:], gcols_d, writes=["c_gcols"])
        c["ident"] = sb("c_ident", [128, 128], BF16)
        P.dma("pool", c["ident"][:], cb_d[:, 0:128], writes=["c_ident"])
        return c

    def make_gB(es, c, which, name):
        gB = es.enter_context(nc.sbuf_tensor(name, [128, KC, 128], F32))
        src = c["gcols"][:, which * KC:(which + 1) * KC]
        P.op("dve", lambda e: e.tensor_copy(gB[:], src.unsqueeze(2).to_broadcast([128, KC, 128])),
             reads=["c_gcols"], writes=[name])
        return gB

    def rstd_from_ss(ss, rstd, n, rkeys, wkey):
        P.op("dve", lambda e: e.tensor_scalar(ss[:, 0:n], ss[:, 0:n], 1.0 / D, EPS, op0=ALU.mult, op1=ALU.add),
             reads=rkeys, writes=rkeys)
        P.op("act", lambda e: e.activation(out=ss[:, 0:n], in_=ss[:, 0:n], func=AF.Sqrt), reads=rkeys, writes=rkeys)
        P.op("dve", lambda e: e.reciprocal(rstd[:, 0:n], ss[:, 0:n]), reads=rkeys, writes=[wkey])

    def ffn_phase(tag, src_d, dst_d, w1_d, w2_d, gsel, post_gsel):
        with contextlib.ExitStack() as es:
            sb = lambda n, s, d=F32: es.enter_context(nc.sbuf_tensor(tag + n, s, d))
            ps = lambda n, s, d=F32: es.enter_context(nc.psum_tensor(tag + n, s, d))
            c = load_consts(es)
            w1 = sb("w1", [128, KC, 2 * FF], BF16)
            w2 = sb("w2", [128, FC, D], BF16)
            for k in range(KC):
                P.dma("pool", w1[:, k, :], w1_d[tsl(k), :], writes=["w1g0", "w1g1", "w1g2", "w1g3"])
            for k in range(FC):
                P.dma("pool", w2[:, k, :], w2_d[tsl(k), :], writes=["w2"])
            gB = make_gB(es, c, gsel, tag + "gB")
            gB2 = make_gB(es, c, post_gsel, tag + "gB2") if post_gsel is not None else None
            NXS = 6 if post_gsel is not None else 8
            xs = sb("xs", [128, NXS, D])
            issued = set()

            def issue_load(t):
                if t in issued or t >= NT:
                    return
                issued.add(t)
                P.dma("sp", xs[:, t % NXS, :], src_d[tsl(t), :], reads=["hd%d" % t], writes=["xs%d" % (t % NXS)])
            xn = sb("xn", [128, 2, D], BF16)
            xnT = sb("xnT", [128, KC, TB], BF16)
            gT = sb("gT", [128, FC, TB], BF16)
            stmp = sb("stmp", [128, 2, TB])
            ss = sb("ss", [128, 2, 4])
            rstd = sb("rstd", [128, 2, 4])
            ss2 = sb("ss2", [128, 2, 4])
            rstd2 = sb("rstd2", [128, 2, 4])
            ust = sb("ust", [128, 2, KC, 128], BF16) if post_gsel is not None else None
            pT = [ps("pT%d" % i, [128, KC, 128], BF16) for i in range(2)]
            pA = [ps("pA%d" % i, [128, TB]) for i in range(2)]
            pB = [ps("pB%d" % i, [128, TB]) for i in range(2)]
            pO = [ps("pO%d" % i, [128, 512]) for i in range(2)]
            ident = c["ident"]
            xnc = [0]
            ptc = [0]

            sqjunk = stmp[:].rearrange("p a b -> p (a b)")

            def sq_stat(t, dst_col, dst_key):
                sl = t % NXS
                P.op("act", lambda e: e.activation(out=sqjunk, in_=xs[:, sl, :], func=AF.Square, accum_out=dst_col),
                     reads=["xs%d" % sl], writes=["stmp0", "stmp1", dst_key])

            def norm_front(tile_ap, xs_key, rstd_col, rstd_key):
                s_ = xnc[0] % 2
                xnc[0] += 1
                P.op("dve", lambda e: e.tensor_scalar(xn[:, s_, :], tile_ap, rstd_col, None, op0=ALU.mult),
                     reads=[xs_key, rstd_key], writes=["xn%d" % s_])
                return s_

            def norm_pe(s_, gBt, gB_key, out_ap, out_key):
                q = ptc[0] % 2
                ptc[0] += 1
                for k in range(KC):
                    P.op("pe", lambda e: e.transpose(pT[q][:, k, :], xn[:, s_, tsl(k)], ident[:]),
                         reads=["xn%d" % s_, "c_ident"], writes=["pT%d" % q])
                P.op("dve", lambda e: e.tensor_tensor(out_ap, pT[q][:], gBt[:], op=ALU.mult),
                     reads=[gB_key], writes=["pT%d" % q, out_key])

            def prep_stats(tl, b_):
                sp2 = b_ % 2
                i0, i1 = tl[0] - 4 * b_, tl[-1] - 4 * b_ + 1
                key = "ss%d_%d" % (sp2, i0)
                for t in tl:
                    issue_load(t)
                    i = t - 4 * b_
                    sq_stat(t, ss[:, sp2, i:i + 1], key)
                rstd_from_ss(ss[:, sp2, i0:i1], rstd[:, sp2, i0:i1], i1 - i0, [key], "rstd%d_%d" % (sp2, i0))
                return "rstd%d_%d" % (sp2, i0)

            class Job:
                pass

            def prep_job(t, b_, rkey):
                i = t - 4 * b_
                sl = t % NXS
                j_ = Job()
                j_.front = lambda: norm_front(xs[:, sl, :], "xs%d" % sl, rstd[:, b_ % 2, i:i + 1], rkey)
                j_.back = lambda s_: norm_pe(s_, gB, tag + "gB", xnT[:, :, tsl(i)], "xnT")
                return j_

            def post_job(t, b_):
                i = t - 4 * b_
                sl = t % NXS
                u = t % 2
                sp2 = b_ % 2
                key = "ss2_%d_%d" % (sp2, i)
                sq_stat(t, ss2[:, sp2, i:i + 1], key)
                rstd_from_ss(ss2[:, sp2, i:i + 1], rstd2[:, sp2, i:i + 1], 1, [key], "rstd2_%d_%d" % (sp2, i))
                j_ = Job()
                j_.front = lambda: norm_front(xs[:, sl, :], "xs%d" % sl, rstd2[:, sp2, i:i + 1], "rstd2_%d_%d" % (sp2, i))

                def back(s_):
                    norm_pe(s_, gB2, tag + "gB2", ust[:, u, :, :], "ust%d" % u)
                    P.dma("sp", uT_s[:, :, tsl(t)].rearrange("c p t -> p c t"), ust[:, u, :, :],
                          reads=["ust%d" % u], writes=["uTd%d" % t])
                j_.back = back
                return j_

            def run_jobs(jobs):
                ss_ = [j_.front() for j_ in jobs]
                for j_, s_ in zip(jobs, ss_):
                    j_.back(s_)

            n_early = NXS - 4
            rk0 = prep_stats([0, 1, 2, 3], 0)
            for t in range(4):
                run_jobs([prep_job(t, 0, rk0)])
            for b in range(NB):
                tiles = [4 * b + i for i in range(4)]
                for j in range(FC):
                    q = j % 2
                    for k in range(KC):
                        P.op("pe", lambda e, k=k, j=j, q=q: e.matmul(pA[q][:], w1[:, k, tsl(j)], xnT[:, k, :],
                                                                    start=(k == 0), stop=(k == KC - 1)),
                             reads=["w1g%d" % (j // 6), "xnT"], writes=["pA%d" % q])
                    for k in range(KC):
                        P.op("pe", lambda e, k=k, j=j, q=q: e.matmul(pB[q][:], w1[:, k, FF + j * 128:FF + (j + 1) * 128],
                                                                    xnT[:, k, :], start=(k == 0), stop=(k == KC - 1)),
                             reads=["w1g%d" % (j // 6), "xnT"], writes=["pB%d" % q])
                    P.op("act", lambda e, q=q: e.activation(out=stmp[:, q, :], in_=pA[q][:], func=AF.Silu),
                         reads=[], writes=["pA%d" % q, "stmp%d" % q])
                    P.op("dve", lambda e, q=q, j=j: e.tensor_tensor(gT[:, j, :], stmp[:, q, :], pB[q][:], op=ALU.mult),
                         reads=["stmp%d" % q], writes=["pB%d" % q, "gT"])
                nxt = [t for t in range(4 * b + 4, 4 * b + 4 + n_early) if t < NT]
                ejobs = []
                if nxt:
                    rk = prep_stats(nxt, b + 1)
                    ejobs = [prep_job(t, b + 1, rk) for t in nxt]
                pjobs = []
                oc = 0
                for i, t in enumerate(tiles):
                    sl = t % NXS
                    slot_jobs = []
                    if i < len(ejobs):
                        slot_jobs.append(ejobs[i])
                    if pjobs:
                        slot_jobs.append(pjobs.pop(0))
                    fronts = [j_.front() for j_ in slot_jobs]
                    for c2 in range(2):
                        q = oc % 2
                        oc += 1
                        for j in range(FC):
                            P.op("pe", lambda e, j=j, i=i, c2=c2, q=q: e.matmul(
                                pO[q][:], gT[:, j, tsl(i)], w2[:, j, c2 * 512:(c2 + 1) * 512],
                                start=(j == 0), stop=(j == FC - 1)),
                                reads=["w2", "gT"], writes=["pO%d" % q])
                        P.op("dve", lambda e, q=q, sl=sl, c2=c2: e.scalar_tensor_tensor(
                            xs[:, sl, c2 * 512:(c2 + 1) * 512], pO[q][:], 0.5, xs[:, sl, c2 * 512:(c2 + 1) * 512],
                            op0=ALU.mult, op1=ALU.add),
                            reads=["xs%d" % sl], writes=["pO%d" % q, "xs%d" % sl])
                    P.dma("sp", dst_d[tsl(t), :], xs[:, sl, :], reads=["xs%d" % sl], writes=["hd%d" % t])
                    for j_, s_ in zip(slot_jobs, fronts):
                        j_.back(s_)
                    if post_gsel is not None:
                        pjobs.append(post_job(t, b))
                    if post_gsel is not None and n_early == 2 and i in (1, 2):
                        issue_load(4 * b + 5 + i)
                while pjobs:
                    run_jobs([pjobs.pop(0)])
                late = [t for t in range(4 * b + 4 + n_early, 4 * b + 8) if t < NT]
                if late:
                    rk = prep_stats(late, b + 1)
                    for t in late:
                        run_jobs([prep_job(t, b + 1, rk)])
            P.barrier()

    def p2_phase():
        with contextlib.ExitStack() as es:
            sb = lambda n, s, d=F32: es.enter_context(nc.sbuf_tensor("p2" + n, s, d))
            ps = lambda n, s, d=F32: es.enter_context(nc.psum_tensor("p2" + n, s, d))
            win = sb("win", [128, KC, 2760], BF16)
            for k in range(KC):
                P.dma("pool", win[:, k, :], win_d[tsl(k), 0:2760], writes=["win"])
            wp = sb("wp", [128, KC, 1280], BF16)
            wk2 = sb("wk2", [128, KC, 256], BF16)
            cf = sb("cf", [128, 130])
            P.dma("sp", cf[:], cf_d, writes=["cf"])
            posi = sb("posi", [128, NTOK], I32)
            P.dma("sp", posi[:], pos_d.broadcast_to([128, NTOK]), writes=["posi"])
            ang = sb("ang", [128, NTOK])
            kk = sb("kk", [128, NTOK])
            kki = sb("kki", [128, NTOK], I32)
            Ct = sb("Ct", [128, NTOK])
            St = sb("St", [128, NTOK])
            invf = cf[:, 128:129]
            sgn = cf[:, 129:130]
            for h2 in range(2):
                hc = slice(h2 * SEQ, (h2 + 1) * SEQ)
                kA, kK, kI = "ang%d" % h2, "kk%d" % h2, "kki%d" % h2
                P.op("dve", lambda e: e.tensor_copy(ang[:, hc], posi[:, hc]), reads=["posi"], writes=[kA])
                P.op("dve", lambda e: e.tensor_scalar(ang[:, hc], ang[:, hc], invf, None, op0=ALU.mult), reads=[kA, "cf"], writes=[kA])

                def reduce_to(dst, shift, key):
                    P.op("dve", lambda e: e.tensor_scalar(kk[:, hc], ang[:, hc], shift, 1.0 / TWO_PI, op0=ALU.add, op1=ALU.mult),
                         reads=[kA], writes=[kK])
                    P.op("dve", lambda e: e.tensor_copy(kki[:, hc], kk[:, hc]), reads=[kK], writes=[kI])
                    P.op("dve", lambda e: e.tensor_copy(kk[:, hc], kki[:, hc]), reads=[kI], writes=[kK])
                    P.op("dve", lambda e: e.scalar_tensor_tensor(dst[:, hc], kk[:, hc], -CW1, ang[:, hc], op0=ALU.mult, op1=ALU.add),
                         reads=[kK, kA], writes=[key])
                    P.op("dve", lambda e: e.scalar_tensor_tensor(dst[:, hc], kk[:, hc], -CW2, dst[:, hc], op0=ALU.mult, op1=ALU.add),
                         reads=[kK, key], writes=[key])
                    P.op("dve", lambda e: e.tensor_scalar(dst[:, hc], dst[:, hc], shift, 3.1415925, op0=ALU.add, op1=ALU.min),
                         reads=[key], writes=[key])
                    P.op("dve", lambda e: e.tensor_scalar(dst[:, hc], dst[:, hc], -3.1415925, None, op0=ALU.max), reads=[key], writes=[key])
                    P.op("act", lambda e: e.activation(out=dst[:, hc], in_=dst[:, hc], func=AF.Sin), reads=[key], writes=[key])

                reduce_to(St, 0.0, "St%d" % h2)
                P.op("dve", lambda e: e.tensor_scalar(St[:, hc], St[:, hc], sgn, None, op0=ALU.mult), reads=["cf", "St%d" % h2], writes=["St%d" % h2])
                reduce_to(Ct, float(np.pi / 2), "Ct%d" % h2)
            P.op("pool", lambda e: e.memset(wp[:], 0.0), writes=["wp"])
            for hh in range(2):
                P.op("pool", lambda e, hh=hh: e.tensor_copy(wk2[:, :, hh * 64:(hh + 1) * 64], win[:, :, OFF_KD:OFF_KD + 64]),
                     reads=["win"], writes=["wk2"])
                P.op("pool", lambda e, hh=hh: e.tensor_copy(wk2[:, :, 128 + hh * 64:128 + (hh + 1) * 64],
                                                            win[:, :, OFF_KI:OFF_KI + 64]), reads=["win"], writes=["wk2"])
            pbase = [OFF_QD + 128 * j for j in range(4)] + [OFF_QI + 128 * j for j in range(4)]
            for pc in range(10):
                for hh in range(2):
                    if pc < 8:
                        b0 = pbase[pc] + 64 * hh
                    else:
                        b0 = OFF_KD if pc == 8 else OFF_KI
                    o = pc * 128 + 64 * hh
                    P.op("pool", lambda e, o=o, b0=b0: e.tensor_copy(wp[:, :, o:o + 8], win[:, :, b0 + 8:b0 + 16]),
                         reads=["win"], writes=["wp"])
                    P.op("pool", lambda e, o=o, b0=b0: e.tensor_copy(wp[:, :, o + 8:o + 16], win[:, :, b0:b0 + 8]),
                         reads=["win"], writes=["wp"])
            uT = sb("uT", [128, 2, KC, TB], BF16)
            fst = sb("fst", [128, 4, TB], BF16)
            t1 = sb("t1", [128, 2, TB])
            t2 = sb("t2", [128, 2, TB])
            vst = sb("vst", [128, 2, 576], BF16)
            wist = sb("wist", [128, 2, 8])
            pA = [ps("pA%d" % i, [128, TB]) for i in range(3)]
            pB = [ps("pB%d" % i, [128, TB]) for i in range(2)]
            pV = [ps("pV%d" % i, [128, 512]) for i in range(2)]
            pW = ps("pW", [128, 128])
            chunks = []
            for j in range(4):
                chunks.append((j, win, OFF_QSB + 128 * j, None))
            for j in range(4):
                chunks.append((4 + j, win, OFF_KSB + 128 * j, None))
            for j in range(4):
                chunks.append((8 + j, win, OFF_QD + 128 * j, j))
            for j in range(4):
                chunks.append((12 + j, win, OFF_QI + 128 * j, 4 + j))
            chunks.append((16, wk2, 0, 8))
            chunks.append((17, wk2, 128, 9))
            ca = cbn = fs = rc = vc = 0
            def load_uT(b):
                if b < NB:
                    P.dma("sp", uT[:, b % 2, :, :], uT_s[:, :, b * TB:(b + 1) * TB].rearrange("c p t -> p c t"),
                          reads=["uTd%d" % t for t in range(4 * b, 4 * b + 4)], writes=["uT%d" % (b % 2)])
            load_uT(0)
            for b in range(NB):
                u = b % 2
                load_uT(b + 1)
                tok = slice(b * TB, (b + 1) * TB)
                for (ci, wt, off, pidx) in chunks:
                    qa = ca % 3
                    ca += 1
                    for k in range(KC):
                        P.op("pe", lambda e, k=k, wt=wt, off=off, qa=qa: e.matmul(pA[qa][:], wt[:, k, off:off + 128], uT[:, u, k, :],
                                                                                 start=(k == 0), stop=(k == KC - 1)),
                             reads=["win", "wk2", "uT%d" % u], writes=["pA%d" % qa])
                    f = fs % 4
                    fs += 1
                    qscale = 0.125 if (ci < 4 or 8 <= ci < 12) else 1.0
                    if pidx is None:
                        P.op("act", lambda e, qa=qa, f=f: e.activation(out=fst[:, f, :], in_=pA[qa][:], func=AF.Identity, scale=qscale),
                             reads=[], writes=["pA%d" % qa, "fst%d" % f])
                    else:
                        qb = cbn % 2
                        cbn += 1
                        for k in range(KC):
                            P.op("pe", lambda e, k=k, pidx=pidx, qb=qb: e.matmul(pB[qb][:], wp[:, k, pidx * 128:(pidx + 1) * 128],
                                                                                uT[:, u, k, :], start=(k == 0), stop=(k == KC - 1)),
                                 reads=["wp", "uT%d" % u], writes=["pB%d" % qb])
                        r = rc % 2
                        rc += 1
                        P.op("dve", lambda e, qa=qa, r=r: e.scalar_tensor_tensor(t1[:, r, :], pA[qa][:], qscale, Ct[:, tok], op0=ALU.mult, op1=ALU.mult),
                             reads=["Ct%d" % (b // 4)], writes=["pA%d" % qa, "t1%d" % r])
                        P.op("dve", lambda e, qb=qb, r=r: e.scalar_tensor_tensor(t2[:, r, :], pB[qb][:], qscale, St[:, tok], op0=ALU.mult, op1=ALU.mult),
                             reads=["St%d" % (b // 4)], writes=["pB%d" % qb, "t2%d" % r])
                        P.op("pool", lambda e, r=r, f=f: e.tensor_tensor(fst[:, f, :], t1[:, r, :], t2[:, r, :], op=ALU.add),
                             reads=["t1%d" % r, "t2%d" % r], writes=["fst%d" % f])
                    P.dma("sp", qk_s[ci, :, tok], fst[:, f, :], reads=["fst%d" % f], writes=["qkd%d_%d" % (ci, b)])
                for i in range(4):
                    t = 4 * b + i
                    q = vc % 2
                    vc += 1
                    for k in range(KC):
                        P.op("pe", lambda e, k=k, i=i, q=q: e.matmul(pV[q][:], uT[:, u, k, tsl(i)], win[:, k, OFF_VSB:OFF_VSB + 512],
                                                                    start=(k == 0), stop=(k == KC - 1)),
                             reads=["win", "uT%d" % u], writes=["pV%d" % q])
                    for k in range(KC):
                        P.op("pe", lambda e, k=k, i=i: e.matmul(pW[:, 0:64], uT[:, u, k, tsl(i)], win[:, k, OFF_VD:OFF_VD + 64],
                                                               start=(k == 0), stop=(k == KC - 1), skip_group_check=True),
                             reads=["win", "uT%d" % u], writes=["pW"])
                    for k in range(KC):
                        P.op("pe", lambda e, k=k, i=i: e.matmul(pW[:, 64:72], uT[:, u, k, tsl(i)], win[:, k, OFF_WI:OFF_WI + 8],
                                                               start=False, stop=(k == KC - 1), skip_group_check=True),
                             reads=["win", "uT%d" % u], writes=["pW"])
                    P.op("act", lambda e, q=q: e.copy(vst[:, q, 0:512], pV[q][:]), reads=[], writes=["pV%d" % q, "vst%d" % q])
                    P.op("dve", lambda e, q=q: e.tensor_copy(vst[:, q, 512:576], pW[:, 0:64]), reads=[], writes=["pW", "vst%d" % q])
                    P.op("dve", lambda e, q=q: e.tensor_scalar(wist[:, q, :], pW[:, 64:72], float(8 ** -0.5 * 0.125), None, op0=ALU.mult),
                         reads=[], writes=["pW", "wist%d" % q])
                    P.dma("sp", v_s[tsl(t), :], vst[:, q, :], reads=["vst%d" % q], writes=["vd%d" % t])
                    P.dma("sp", wi_s[tsl(t), :], wist[:, q, :], reads=["wist%d" % q], writes=["wid%d" % t])
            P.barrier()

    def p3_phase():
        with contextlib.ExitStack() as es:
            sb = lambda n, s, d=F32: es.enter_context(nc.sbuf_tensor("p3" + n, s, d))
            cb = sb("cb", [128, 384 + 2048], BF16)
            P.dma("pool", cb[:], cb_d, writes=["cb"])
            cf = sb("cf", [128, 130])
            P.dma("sp", cf[:], cf_d, writes=["cf"])
            ident = cb[:, 0:128]
            nUincl = cb[:, 128:256]
            nLstr = cb[:, 256:384]
            sbmask = cb[:, 384:384 + 2048].rearrange("p (r t) -> p r t", r=4)
            dsaneg = cf[:, 0:128]
            qz = [sb("qz%d" % i, [128, 4, SEQ], BF16) for i in range(2)]
            P.op("pool", lambda e: e.memset(qz[0][64:128, :, :], 0.0), writes=["qz0z"])
            P.op("pool", lambda e: e.memset(qz[1][0:64, :, :], 0.0), writes=["qz1z"])
            ksb = sb("ksb", [128, 4, SEQ], BF16)
            nksb = sb("nksb", [128, 4, SEQ], BF16)
            qd = sb("qd", [128, 4, SEQ], BF16)
            qi = sb("qi", [128, 4, SEQ], BF16)
            kdz = sb("kdz", [128, 2, SEQ], BF16)
            kiz = sb("kiz", [128, 2, SEQ], BF16)
            for kz_ in (kdz, kiz):
                P.op("pool", lambda e: e.memset(kz_[64:128, 0, :], 0.0), writes=["kzz"])
                P.op("pool", lambda e: e.memset(kz_[0:64, 1, :], 0.0), writes=["kzz"])
            v = sb("v", [128, 16, 578], BF16)
            wi = sb("wi", [128, 16, 8])
            P.op("pool", lambda e: e.memset(v[:, :, 576:578], 0.0), writes=["vone"])
            P.op("pool", lambda e: e.memset(v[:, :, 576:577], 1.0), writes=["vone"])
            def load_sb(sq):
                tok = slice(sq * SEQ, (sq + 1) * SEQ)
                for j in range(4):
                    P.dma("sp", qz[0][0:64, j, :], qk_s[j, 0:64, tok], writes=["qsb"])
                    P.dma("sp", qz[1][64:128, j, :], qk_s[j, 64:128, tok], writes=["qsb"])
                for j in range(4):
                    P.dma("sp", ksb[:, j, :], qk_s[4 + j, :, tok], writes=["ksb"])
                for j in range(4):
                    P.op("dve", lambda e: e.tensor_scalar(nksb[:, j, :], ksb[:, j, :], -1.0, None, op0=ALU.mult), reads=["ksb"], writes=["nksb"])

            def load_rest(sq):
                tok = slice(sq * SEQ, (sq + 1) * SEQ)
                P.dma("act", v[:, :, 0:576], v_s[tok, :].rearrange("(n p) c -> p n c", p=128), writes=["v"])
                P.dma("act", wi[:], wi_s[tok, :].rearrange("(n p) c -> p n c", p=128), writes=["wi"])
                for (tile_, c0, key) in ((qd, 8, "qd"), (qi, 12, "qi")):
                    for j in range(4):
                        P.dma("sp", tile_[:, j, :], qk_s[c0 + j, :, tok], writes=[key])
                for (kz_, ci, key) in ((kdz, 16, "kd2"), (kiz, 17, "ki2")):
                    P.dma("sp", kz_[0:64, 0, :], qk_s[ci, 0:64, tok], writes=[key])
                    P.dma("sp", kz_[64:128, 1, :], qk_s[ci, 64:128, tok], writes=[key])

            load_sb(0)
            for sq in range(NSEQ):
                load_rest(sq)

                with contextlib.ExitStack() as es2:
                  if "nosb" not in phases:
                    sb2 = lambda n, s, d=F32: es2.enter_context(nc.sbuf_tensor("sb%d" % sq + n, s, d))
                    ps2 = lambda n, s, d=F32: es2.enter_context(nc.psum_tensor("sb%d" % sq + n, s, d))
                    NS = 4
                    E = sb2("E", [128, NS, 512])
                    SP = sb2("SP", [128, NS, 2, 512], BF16)
                    A = sb2("A", [128, NS, 2, 512], BF16)
                    yacc = sb2("yacc", [64, NS, 512])
                    yst = sb2("yst", [64, NS, 512], BF16)
                    pZ = [ps2("pZ%d" % i, [128, 512]) for i in range(2)]
                    pC = [ps2("pC%d" % i, [128, 512]) for i in range(NS)]
                    pY = [ps2("pY%d" % i, [64, 512]) for i in range(2)]
                    mask128 = sbmask[:, 0, 0:128]

                    def sb_stream(s_, h, qc):
                        j, half = h // 2, h % 2
                        po = slice(64 * half, 64 * half + 64)
                        zb = s_ % 2
                        S = "s%d" % s_
                        kmax = 4 * qc + 3
                        nstep = kmax + 1
                        P.op("dve", lambda e: e.memset(yacc[:, s_, :], 0.0), writes=["yacc" + S])
                        if s_ >= 2:
                            yield
                        for step in range(nstep):
                            kb = kmax - step
                            r = kb - 4 * qc
                            c0 = 128 * max(0, r)
                            cols = slice(c0, 512)
                            dcols = slice(c0, c0 + 128)
                            qcols = slice(qc * 512 + c0, (qc + 1) * 512)
                            kcols = tsl(kb)
                            par = step % 2
                            spk = "SP%s%d" % (S, par)
                            ak = "A%s%d" % (S, par)
                            P.op("pe", lambda e: e.matmul(pZ[zb][:, cols], ksb[:, j, kcols], qz[half][:, j, qcols], start=True, stop=True,
                                                          skip_group_check=True),
                                 reads=["ksb", "qsb", "qz0z", "qz1z"], writes=["pZ%d" % zb])
                            yield
                            P.op("act", lambda e: e.activation(out=E[:, s_, cols], in_=pZ[zb][:, cols], func=AF.Exp),
                                 reads=[], writes=["pZ%d" % zb, "E" + S])
                            yield
                            P.op("act", lambda e: e.activation(out=SP[:, s_, par, cols], in_=E[:, s_, cols], func=AF.Ln, bias=1.0),
                                 reads=["E" + S], writes=[spk])
                            if r >= 0:
                                P.op("dve", lambda e: e.tensor_tensor(SP[:, s_, par, dcols], SP[:, s_, par, dcols], mask128, op=ALU.mult),
                                     reads=["cb", spk], writes=[spk])
                            yield
                            P.op("pe", lambda e: e.matmul(pC[s_][:, cols], ksb[:, j, kcols], qz[half][:, j, qcols], start=(step == 0), stop=False,
                                                          skip_group_check=True),
                                 reads=["ksb", "qsb"], writes=["pC" + S])
                            P.op("pe", lambda e: e.matmul(pC[s_][:, cols], nUincl, SP[:, s_, par, cols], start=False, stop=True,
                                                          skip_group_check=True),
                                 reads=["cb", spk], writes=["pC" + S])
                            yield
                            P.op("act", lambda e: e.activation(out=A[:, s_, par, cols], in_=pC[s_][:, cols], func=AF.Exp),
                                 reads=[], writes=["pC" + S, ak])
                            if r >= 0:
                                P.op("dve", lambda e: e.tensor_tensor(A[:, s_, par, dcols], A[:, s_, par, dcols], mask128, op=ALU.mult),
                                     reads=["cb", ak], writes=[ak])
                            yield
                            P.op("pe", lambda e: e.matmul(pY[zb][:, cols], v[:, kb, h * 64:(h + 1) * 64], A[:, s_, par, cols],
                                                          start=True, stop=True, skip_group_check=True),
                                 reads=["v", ak], writes=["pY%d" % zb])
                            if step < nstep - 1:
                                P.op("pe", lambda e: e.matmul(pC[s_][:, cols], nksb[:, j, kcols], qz[half][:, j, qcols], start=False, stop=False,
                                                              skip_group_check=True),
                                     reads=["nksb", "qsb"], writes=["pC" + S])
                                P.op("pe", lambda e: e.matmul(pC[s_][:, cols], nLstr, SP[:, s_, par, cols], start=False, stop=False,
                                                              skip_group_check=True),
                                     reads=["cb", spk], writes=["pC" + S])
                            P.op("dve", lambda e: e.tensor_tensor(yacc[:, s_, cols], yacc[:, s_, cols], pY[zb][:, cols], op=ALU.add),
                                 reads=["yacc" + S], writes=["pY%d" % zb, "yacc" + S])
                            yield
                        P.op("dve", lambda e: e.tensor_copy(yst[:, s_, :], yacc[:, s_, :]), reads=["yacc" + S], writes=["yst" + S])
                        P.dma("sp", ysbT_s[h // 2, 64 * (h % 2):64 * (h % 2) + 64, sq * SEQ + qc * 512: sq * SEQ + (qc + 1) * 512], yst[:, s_, :],
                              reads=["yst" + S], writes=["ysbd%d_%d" % (h, sq * 4 + qc)])

                    for qc in range(4):
                        for g in range(2):
                            P.run_threads([sb_stream(s_, 4 * g + s_, qc) for s_ in range(NS)])
                    P.barrier()
                    if sq + 1 < NSEQ:
                        load_sb(sq + 1)

                with contextlib.ExitStack() as es2:
                  if "nodsa" not in phases:
                    sb2 = lambda n, s, d=F32: es2.enter_context(nc.sbuf_tensor("ds%d" % sq + n, s, d))
                    ps2 = lambda n, s, d=F32: es2.enter_context(nc.psum_tensor("ds%d" % sq + n, s, d))
                    Sc = sb2("Sc", [128, 4, SEQ])
                    R = sb2("R", [128, 2, 512])
                    Mb = sb2("Mb", [128, 2, SEQ], BF16)
                    MT = sb2("MT", [128, 4, 16, 128], BF16)
                    PT = sb2("PT", [128, 3, 512], BF16)
                    yd = sb2("yd", [128, 512], BF16)
                    ydst = sb2("ydst", [128, 2, 4, 128], BF16)
                    sm = sb2("sm", [128, 16])
                    rec = sb2("rec", [128, 8, 1])
                    pD = [ps2("pD%d" % i, [128, 512]) for i in range(2)]
                    pM = ps2("pM", [128, 8, 128], BF16)
                    pL = [ps2("pL%d" % i, [128, 512]) for i in range(2)]
                    pYd = ps2("pYd", [128, 2, 512])
                    pM2 = ps2("pM2", [128, 8, 128], BF16)
                    cnts = dict(d=0, l=0)

                    def stage1(i):
                        nk = (i + 1) * 128
                        nch = (nk + 511) // 512
                        tcols = tsl(i)
                        z = i % 4
                        sck = "Sc%d" % z
                        def acc_op(hh, q, n, cc):
                            if hh == 0:
                                P.op("dve", lambda e: e.tensor_scalar(Sc[:, z, cc], R[:, q, 0:n], wi[:, i, 0:1], None, op0=ALU.mult),
                                     reads=["R%d" % q, "wi"], writes=[sck])
                            else:
                                P.op("dve", lambda e: e.scalar_tensor_tensor(Sc[:, z, cc], R[:, q, 0:n], wi[:, i, hh:hh + 1], Sc[:, z, cc],
                                                                           op0=ALU.mult, op1=ALU.add),
                                     reads=["R%d" % q, "wi", sck], writes=[sck])
                        pend = None
                        for hh in range(8):
                            j, s_ = hh // 2, hh % 2
                            for c in range(nch):
                                n = min(512, nk - c * 512)
                                q = cnts["d"] % 2
                                cnts["d"] += 1
                                cc = slice(c * 512, c * 512 + n)
                                P.op("pe", lambda e: e.matmul(pD[q][:, 0:n], qi[:, j, tcols], kiz[:, s_, cc], start=True, stop=True),
                                     reads=["qi", "ki2", "kzz"], writes=["pD%d" % q])
                                P.op("act", lambda e: e.activation(out=R[:, q, 0:n], in_=pD[q][:, 0:n], func=AF.Relu),
                                     reads=[], writes=["pD%d" % q, "R%d" % q])
                                if pend is not None:
                                    acc_op(*pend)
                                pend = (hh, q, n, cc)
                                yield
                        if pend is not None:
                            acc_op(*pend)

                    def stage2(i):
                        nk = (i + 1) * 128
                        z = i % 4
                        w = i % 2
                        smk = "sm%d" % w
                        mbk = "Mb%d" % w
                        lo, hi, mid, cnt, dlt = (sm[:, 8 * w + c:8 * w + c + 1] for c in range(5))
                        sck = "Sc%d" % z
                        if i >= 2:
                            P.op("dve", lambda e: e.tensor_reduce(lo, Sc[:, z, 0:nk], axis=AX.X, op=ALU.min), reads=[sck], writes=[smk])
                        P.op("dve", lambda e: e.tensor_tensor(Sc[:, z, nk - 128:nk], Sc[:, z, nk - 128:nk], dsaneg, op=ALU.add),
                             reads=["cf", sck], writes=[sck])
                        yield
                        if i >= 2:
                            P.op("dve", lambda e: e.tensor_reduce(hi, Sc[:, z, 0:nk], axis=AX.X, op=ALU.max), reads=[sck], writes=[smk])
                            P.op("dve", lambda e: e.tensor_tensor(hi, hi, lo, op=ALU.subtract), reads=[smk], writes=[smk])
                            on_act = w in ACT_CHAINS
                            sg = -1.0 if on_act else 1.0
                            if on_act:
                                P.op("dve", lambda e: e.tensor_scalar(lo, lo, -1.0, None, op0=ALU.mult), reads=[smk], writes=[smk])
                            yield
                            thr = float(2 * TOPK - 1 - nk) if on_act else (float(TOPK) - 0.5)
                            for it in range(NBIS):
                                ck = float(0.5 ** (it + 1))
                                P.op("dve", lambda e: e.scalar_tensor_tensor(mid, hi, sg * ck, lo, op0=ALU.mult, op1=ALU.add),
                                     reads=[smk], writes=[smk + "m"])
                                yield
                                if on_act:
                                    P.op("act", lambda e: e.activation(out=Mb[:, w, 0:nk], in_=Sc[:, z, 0:nk], func=AF.Sign, bias=mid,
                                                                       accum_out=cnt),
                                         reads=[sck, smk + "m"], writes=[mbk, smk + "c"])
                                else:
                                    P.op("dve", lambda e: e.tensor_scalar(Mb[:, w, 0:nk], Sc[:, z, 0:nk], mid, 0.0, op0=ALU.is_gt, op1=ALU.add,
                                                                          accum_out=cnt), reads=[sck, smk + "m"], writes=[mbk, smk + "c"])
                                yield
                                P.op("dve", lambda e: e.tensor_scalar(dlt, cnt, thr, sg * ck, op0=ALU.is_gt, op1=ALU.mult),
                                     reads=[smk + "c"], writes=[smk + "d"])
                                yield
                                P.op("dve", lambda e: e.scalar_tensor_tensor(lo, dlt, hi, lo, op0=ALU.mult, op1=ALU.add),
                                     reads=[smk + "d", smk], writes=[smk])
                                yield
                            if on_act:
                                P.op("dve", lambda e: e.tensor_scalar(lo, lo, -1.0, None, op0=ALU.mult), reads=[smk], writes=[smk])
                            P.op("dve", lambda e: e.tensor_scalar(Mb[:, w, 0:nk], Sc[:, z, 0:nk], lo, None, op0=ALU.is_gt),
                                 reads=[sck, smk], writes=[mbk])
                        else:
                            P.op("dve", lambda e: e.tensor_scalar(Mb[:, w, 0:nk], Sc[:, z, 0:nk], -1e29, None, op0=ALU.is_gt),
                                 reads=[sck], writes=[mbk])
                        yield
                        for g0 in range(0, i + 1, 8):
                            g1 = min(i + 1, g0 + 8)
                            for kb in range(g0, g1):
                                P.op("pe", lambda e: e.transpose(pM[:, kb - g0, :], Mb[:, w, tsl(kb)], ident),
                                     reads=[mbk, "cb"], writes=["pM"])
                            P.op("act", lambda e: e.activation(out=MT[:, z, g0:g1, :], in_=pM[:, 0:g1 - g0, :], func=AF.Identity,
                                                               scale=MASK_BIG, bias=-MASK_BIG),
                                 reads=[], writes=["pM", "MT%d" % z])
                            yield

                    def stage3(i):
                        tcols = tsl(i)
                        z = i % 4
                        def emit_pv(kb, s_, q3):
                            for j in range(4):
                                P.op("pe", lambda e: e.matmul(pYd[:, s_, j * 66:(j + 1) * 66], PT[:, q3, j * 128:(j + 1) * 128], v[:, kb, 512:578],
                                                              start=(kb == 0 and j == 0), stop=(kb == i), skip_group_check=True),
                                     reads=["PT%d" % q3, "v", "vone"], writes=["pYd%d" % s_])
                        def emit_exp(q, q3):
                            P.op("act", lambda e: e.activation(out=PT[:, q3, :], in_=pL[q][:], func=AF.Exp),
                                 reads=[], writes=["pL%d" % q, "PT%d" % q3])
                        pend_exp = None
                        pend = []
                        for kb in range(i + 1):
                            for s_ in range(2):
                                q = cnts["l"] % 2
                                q3 = cnts["l"] % 3
                                cnts["l"] += 1
                                for j in range(4):
                                    P.op("pe", lambda e: e.matmul(pL[q][:, j * 128:(j + 1) * 128], kdz[:, s_, tsl(kb)], qd[:, j, tcols],
                                                                  start=True, stop=False, skip_group_check=True),
                                         reads=["kd2", "kzz", "qd"], writes=["pL%d" % q])
                                    P.op("pe", lambda e: e.matmul(pL[q][:, j * 128:(j + 1) * 128], ident, MT[:, z, kb, :],
                                                                  start=False, stop=True, skip_group_check=True),
                                         reads=["cb", "MT%d" % z], writes=["pL%d" % q])
                                if pend_exp is not None:
                                    emit_exp(pend_exp[2], pend_exp[3])
                                    pend.append((pend_exp[0], pend_exp[1], pend_exp[3]))
                                pend_exp = (kb, s_, q, q3)
                                if len(pend) > 2:
                                    emit_pv(*pend.pop(0))
                                yield
                        emit_exp(pend_exp[2], pend_exp[3])
                        pend.append((pend_exp[0], pend_exp[1], pend_exp[3]))
                        while pend:
                            emit_pv(*pend.pop(0))
                        for bank in range(2):
                            yv = pYd[:, bank, 0:264].rearrange("p (h c) -> p h c", h=4)
                            ydv = yd[:].rearrange("p (j s c) -> p j s c", j=4, s=2)[:, :, bank, :]
                            P.op("dve", lambda e: e.reciprocal(rec[:, bank * 4:(bank + 1) * 4, :], yv[:, :, 64:65]),
                                 reads=[], writes=["pYd%d" % bank, "rec%d" % bank])
                            P.op("dve", lambda e: e.tensor_tensor(ydv, yv[:, :, 0:64],
                                                                  rec[:, bank * 4:(bank + 1) * 4, :].to_broadcast([128, 4, 64]), op=ALU.mult),
                                 reads=["rec%d" % bank], writes=["pYd%d" % bank, "yd"])
                        yield
                        u = i % 2
                        for cchunk in range(4):
                            P.op("pe", lambda e: e.transpose(pM2[:, cchunk, :], yd[:, tsl(cchunk)], ident),
                                 reads=["yd", "cb"], writes=["pM2"])
                        P.op("act", lambda e: e.copy(ydst[:, u, :, :], pM2[:, 0:4, :]), reads=[], writes=["pM2", "ydst%d" % u])
                        P.dma("sp", ydT_s[:, :, sq * SEQ + i * 128: sq * SEQ + (i + 1) * 128].rearrange("c p t -> p c t"),
                              ydst[:, u, :, :], reads=["ydst%d" % u], writes=["ydd%d" % (sq * 16 + i)])

                    def seq(*gens):
                        for g in gens:
                            yield from g
                    pairs = [(2 * m + 1, 2 * m) for m in (1, 6, 7, 4, 5, 2, 3, 0)]
                    for tau in range(len(pairs) + 2):
                        th = []
                        if tau < len(pairs):
                            th.append(seq(stage1(pairs[tau][0]), stage1(pairs[tau][1])))
                        if 0 <= tau - 1 < len(pairs):
                            th.append(stage2(pairs[tau - 1][0]))
                            th.append(stage2(pairs[tau - 1][1]))
                        if 0 <= tau - 2 < len(pairs):
                            th.append(seq(stage3(pairs[tau - 2][0]), stage3(pairs[tau - 2][1])))
                        P.run_threads(th)
                    P.barrier()
            P.barrier()

    def p4a_phase():
        with contextlib.ExitStack() as es:
            sb = lambda n, s, d=F32: es.enter_context(nc.sbuf_tensor("p4a" + n, s, d))
            ps = lambda n, s, d=F32: es.enter_context(nc.psum_tensor("p4a" + n, s, d))
            wg = sb("wg", [128, KC, 2048], BF16)
            for k in range(KC):
                P.dma("pool", wg[:, k, :], win_d[tsl(k), OFF_GSB:OFF_GSB + 2048], writes=["wg"])
            wosb = sb("wosb", [128, 4, D], BF16)
            P.dma("pool", wosb[:], wosb_d.rearrange("(c p) n -> p c n", p=128), writes=["wosb"])
            wod = sb("wod", [128, 4, D], BF16)
            P.dma("pool", wod[:], wod_d.rearrange("(c p) n -> p c n", p=128), writes=["wod"])
            wo = sb("wo", [128, KC, D], BF16)
            P.dma("pool", wo[:], wo_d.rearrange("(c p) n -> p c n", p=128), writes=["wo"])
            uT = sb("uT", [128, 2, KC, TB], BF16)
            ysbT = sb("ysbT", [128, 2, 4, TB], BF16)
            ydT = sb("ydT", [128, 2, 4, TB], BF16)
            mT = sb("mT", [128, KC, TB], BF16)
            s1 = sb("s1", [128, 2, TB])
            s2 = sb("s2", [128, 2, TB])
            t1 = sb("t1", [128, 2, TB])
            t2 = sb("t2", [128, 2, TB])
            hs = sb("hs", [128, 4, D])
            pG = [ps("pG%d" % i, [128, TB]) for i in range(2)]
            pY = [ps("pY%d" % i, [128, TB]) for i in range(2)]
            pO = [ps("pO%d" % i, [128, 512]) for i in range(2)]
            oc = hc = 0
            def load_blk(b):
                if b >= NB:
                    return
                u = b % 2
                tok = slice(b * TB, (b + 1) * TB)
                P.dma("sp", uT[:, u, :, :], uT_s[:, :, tok].rearrange("c p t -> p c t"), writes=["uT%d" % u])
                P.dma("sp", ysbT[:, u, :, :], ysbT_s[:, :, tok].rearrange("c p t -> p c t"), writes=["ysbT%d" % u])
                P.dma("sp", ydT[:, u, :, :], ydT_s[:, :, tok].rearrange("c p t -> p c t"), writes=["ydT%d" % u])
            load_blk(0)
            for b in range(NB):
                u = b % 2
                tok = slice(b * TB, (b + 1) * TB)
                load_blk(b + 1)
                for i in range(4):
                    P.dma("sp", hs[:, i, :], h_s[tsl(4 * b + i), :], reads=["hd%d" % (4 * b + i)], writes=["hs%d" % i])
                for c in range(KC):
                    r = c % 2
                    for (g, pg, sdst, skey) in ((0, pG[0], s1, "s1"), (1, pG[1], s2, "s2")):
                        for k in range(KC):
                            P.op("pe", lambda e, k=k, g=g, pg=pg, c=c: e.matmul(
                                pg[:], wg[:, k, g * 1024 + c * 128: g * 1024 + (c + 1) * 128], uT[:, u, k, :],
                                start=(k == 0), stop=(k == KC - 1)),
                                reads=["wg", "uT%d" % u], writes=["pG%d" % g])
                        P.op("act", lambda e, pg=pg, sdst=sdst, r=r: e.activation(out=sdst[:, r, :], in_=pg[:], func=AF.Tanh, scale=0.5),
                             reads=[], writes=["pG%d" % g, "%s%d" % (skey, r)])
                    for hh in range(4):
                        P.op("pe", lambda e, hh=hh, c=c: e.matmul(pY[0][:], wosb[:, hh, tsl(c)], ysbT[:, u, hh, :],
                                                                   start=(hh == 0), stop=(hh == 3)),
                             reads=["wosb", "ysbT%d" % u], writes=["pY0"])
                    for k in range(4):
                        P.op("pe", lambda e, k=k, c=c: e.matmul(pY[1][:], wod[:, k, tsl(c)], ydT[:, u, k, :],
                                                                 start=(k == 0), stop=(k == 3)),
                             reads=["wod", "ydT%d" % u], writes=["pY1"])
                    P.op("dve", lambda e, r=r: e.scalar_tensor_tensor(t1[:, r, :], s1[:, r, :], 1.0, pY[0][:], op0=ALU.add, op1=ALU.mult),
                         reads=["s1%d" % r], writes=["pY0", "t1%d" % r])
                    P.op("dve", lambda e, r=r: e.scalar_tensor_tensor(t2[:, r, :], s2[:, r, :], 1.0, pY[1][:], op0=ALU.add, op1=ALU.mult),
                         reads=["s2%d" % r], writes=["pY1", "t2%d" % r])
                    P.op("pool", lambda e, r=r, c=c: e.tensor_tensor(mT[:, c, :], t1[:, r, :], t2[:, r, :], op=ALU.add),
                         reads=["t1%d" % r, "t2%d" % r], writes=["mT"])
                for i in range(4):
                    t = 4 * b + i
                    sl = i
                    for c2 in range(2):
                        q = oc % 2
                        oc += 1
                        for k in range(KC):
                            P.op("pe", lambda e, k=k, i=i, c2=c2, q=q: e.matmul(pO[q][:], mT[:, k, tsl(i)], wo[:, k, c2 * 512:(c2 + 1) * 512],
                                                                               start=(k == 0), stop=(k == KC - 1)),
                                 reads=["mT", "wo"], writes=["pO%d" % q])
                        P.op("dve", lambda e, q=q, sl=sl, c2=c2: e.scalar_tensor_tensor(
                            hs[:, sl, c2 * 512:(c2 + 1) * 512], pO[q][:], 0.5, hs[:, sl, c2 * 512:(c2 + 1) * 512],
                            op0=ALU.mult, op1=ALU.add), reads=[], writes=["pO%d" % q, "hs%d" % sl])
                    P.dma("sp", h_s[tsl(t), :], hs[:, sl, :], reads=["hs%d" % sl], writes=["hd%d" % t])
            P.barrier()

    def p4c_phase():
        with contextlib.ExitStack() as es:
            sb = lambda n, s, d=F32: es.enter_context(nc.sbuf_tensor("p4c" + n, s, d))
            ps = lambda n, s, d=F32: es.enter_context(nc.psum_tensor("p4c" + n, s, d))
            c = load_consts(es)
            ident = c["ident"]
            wpg = sb("wpg", [128, KC, D], BF16)
            P.dma("pool", wpg[:], wpg_d.rearrange("(c p) n -> p c n", p=128), writes=["wpg"])
            wpp = sb("wpp", [128, 2, D], BF16)
            P.dma("pool", wpp[:], wpp_d.rearrange("(c p) n -> p c n", p=128), writes=["wpp"])
            gB = make_gB(es, c, 3, "p4cgB")
            gfin = sb("gfin", [128, D])
            P.dma("sp", gfin[:], gfin_d.broadcast_to([128, D]), writes=["gfin"])
            NSG = 4
            hs = sb("hs", [128, 4 * NSG, D])
            pt = sb("pt", [128, 4 * NSG, 256], BF16)
            xn = sb("xn", [128, 2, D], BF16)
            sqj = sb("sqj", [128, 2, D], BF16)
            hnT = sb("hnT", [128, 2, KC, 128], BF16)
            pTs = sb("pTs", [128, 2, 2, 128], BF16)
            gate = sb("gate", [128, 2, D])
            ot = sb("ot", [128, 2, D])
            ss = sb("ss", [128, NSG, 4])
            rstd = sb("rstd", [128, NSG, 4])
            ss2 = sb("ss2", [128, NSG, 4])
            rstd2 = sb("rstd2", [128, NSG, 4])
            pT = [ps("pT%d" % i, [128, KC, 128], BF16) for i in range(2)]
            pP = [ps("pP%d" % i, [128, KC, 128], BF16) for i in range(2)]
            pGt = [ps("pGt%d" % i, [128, 512]) for i in range(2)]
            pPp = [ps("pPp%d" % i, [128, 512]) for i in range(2)]
            NG = NT // 4

            def load_g(g):
                if 0 <= g < NG:
                    for i in range(4):
                        t = 4 * g + i
                        sl = (g % NSG) * 4 + i
                        P.dma("sp", hs[:, sl, :], h_s[tsl(t), :], reads=["hd%d" % t], writes=["hs%d" % sl])
                        P.dma("pool", pt[:, sl, :], p_d[tsl(t), :], writes=["pt%d" % sl])

            def stage_a(g):
                gp = g % NSG
                for i in range(4):
                    sl = gp * 4 + i
                    P.op("act", lambda e: e.activation(out=sqj[:, 0, :], in_=hs[:, sl, :], func=AF.Square, accum_out=ss[:, gp, i:i + 1]),
                         reads=["hs%d" % sl], writes=["sqj0", "ss%d" % gp])
                    yield
                P.op("dve", lambda e: e.tensor_scalar(ss[:, gp, :], ss[:, gp, :], 1.0 / D, EPS, op0=ALU.mult, op1=ALU.add),
                     reads=["ss%d" % gp], writes=["ss%d" % gp])
                yield
                P.op("act", lambda e: e.activation(out=ss[:, gp, :], in_=ss[:, gp, :], func=AF.Sqrt), reads=["ss%d" % gp], writes=["ss%d" % gp])
                yield
                P.op("dve", lambda e: e.reciprocal(rstd[:, gp, :], ss[:, gp, :]), reads=["ss%d" % gp], writes=["rstd%d" % gp])
                yield

            def tile_thread(g, i):
                gp = g % NSG
                sl = gp * 4 + i
                u = i % 2
                P.op("dve", lambda e: e.tensor_scalar(xn[:, u, :], hs[:, sl, :], rstd[:, gp, i:i + 1], None, op0=ALU.mult),
                     reads=["hs%d" % sl, "rstd%d" % gp], writes=["xn%d" % u])
                yield
                for k in range(KC):
                    P.op("pe", lambda e: e.transpose(pT[u][:, k, :], xn[:, u, tsl(k)], ident[:]),
                         reads=["xn%d" % u, "c_ident"], writes=["pT%d" % u])
                for k in range(2):
                    P.op("pe", lambda e: e.transpose(pP[u][:, k, :], pt[:, sl, tsl(k)], ident[:]),
                         reads=["pt%d" % sl, "c_ident"], writes=["pP%d" % u])
                yield
                P.op("dve", lambda e: e.tensor_tensor(hnT[:, u, :, :], pT[u][:], gB[:], op=ALU.mult),
                     reads=["p4cgB"], writes=["pT%d" % u, "hnT%d" % u])
                P.op("act", lambda e: e.copy(pTs[:, u, :, :], pP[u][:, 0:2, :]), reads=[], writes=["pP%d" % u, "pTs%d" % u])
                yield
                for c2 in range(2):
                    cs = slice(c2 * 512, (c2 + 1) * 512)
                    for k in range(KC):
                        P.op("pe", lambda e: e.matmul(pGt[u][:], hnT[:, u, k, :], wpg[:, k, cs], start=(k == 0), stop=(k == KC - 1)),
                             reads=["hnT%d" % u, "wpg"], writes=["pGt%d" % u])
                    for k in range(2):
                        P.op("pe", lambda e: e.matmul(pPp[u][:], pTs[:, u, k, :], wpp[:, k, cs], start=(k == 0), stop=(k == 1)),
                             reads=["pTs%d" % u, "wpp"], writes=["pPp%d" % u])
                    yield
                    P.op("act", lambda e: e.activation(out=gate[:, u, cs], in_=pGt[u][:], func=AF.Tanh, scale=0.5),
                         reads=[], writes=["pGt%d" % u, "gate%d" % u])
                    yield
                    P.op("dve", lambda e: e.scalar_tensor_tensor(gate[:, u, cs], gate[:, u, cs], 1.0, pPp[u][:], op0=ALU.add, op1=ALU.mult),
                         reads=["gate%d" % u], writes=["pPp%d" % u, "gate%d" % u])
                    yield
                P.op("dve", lambda e: e.scalar_tensor_tensor(hs[:, sl, :], gate[:, u, :], 0.5, hs[:, sl, :], op0=ALU.mult, op1=ALU.add),
                     reads=["gate%d" % u, "hs%d" % sl], writes=["hs%d" % sl])
                yield

            def stage_c(g):
                gp = g % NSG
                for i in range(4):
                    sl = gp * 4 + i
                    P.op("act", lambda e: e.activation(out=sqj[:, 1, :], in_=hs[:, sl, :], func=AF.Square, accum_out=ss2[:, gp, i:i + 1]),
                         reads=["hs%d" % sl], writes=["sqj1", "ss2%d" % gp])
                    yield
                P.op("dve", lambda e: e.tensor_scalar(ss2[:, gp, :], ss2[:, gp, :], 1.0 / D, EPS, op0=ALU.mult, op1=ALU.add),
                     reads=["ss2%d" % gp], writes=["ss2%d" % gp])
                yield
                P.op("act", lambda e: e.activation(out=ss2[:, gp, :], in_=ss2[:, gp, :], func=AF.Sqrt), reads=["ss2%d" % gp], writes=["ss2%d" % gp])
                yield
                P.op("dve", lambda e: e.reciprocal(rstd2[:, gp, :], ss2[:, gp, :]), reads=["ss2%d" % gp], writes=["rstd2%d" % gp])
                yield
                for i in range(4):
                    sl = gp * 4 + i
                    t = 4 * g + i
                    u = i % 2
                    P.op("dve", lambda e: e.scalar_tensor_tensor(ot[:, u, :], hs[:, sl, :], rstd2[:, gp, i:i + 1], gfin[:], op0=ALU.mult, op1=ALU.mult),
                         reads=["hs%d" % sl, "rstd2%d" % gp, "gfin"], writes=["ot%d" % u])
                    P.dma("sp", out_d[tsl(t), :], ot[:, u, :], reads=["ot%d" % u], writes=["outd%d" % t])
                    yield

            def seq(*gens):
                for g_ in gens:
                    yield from g_

            load_g(0)
            for tau in range(NG + 2):
                load_g(tau + 1)
                th = []
                if tau < NG:
                    th.append(stage_a(tau))
                if 0 <= tau - 1 < NG:
                    th.append(seq(tile_thread(tau - 1, 0), tile_thread(tau - 1, 2)))
                    th.append(seq(tile_thread(tau - 1, 1), tile_thread(tau - 1, 3)))
                if 0 <= tau - 2 < NG:
                    th.append(stage_c(tau - 2))
                P.run_threads(th)
            P.barrier()

    if "p1" in phases:
        ffn_phase("f1", x_d, h_s, w1a_d, w2a_d, 0, 1)
    if "p2" in phases:
        p2_phase()
    if "p3" in phases:
        p3_phase()
    if "p4a" in phases:
        p4a_phase()
    if "p4b" in phases:
        ffn_phase("f2", h_s, h_s, w1b_d, w2b_d, 2, None)
    if "p4c" in phases:
        p4c_phase()
    P.emit()
    return nc


def host_consts():
    j = np.arange(128)
    ident = np.eye(128, dtype=np.float32)
    nUincl = -(j[:, None] >= j[None, :]).astype(np.float32)
    nLstr = -(j[:, None] < j[None, :]).astype(np.float32)
    t = np.arange(512)
    sbmask = np.stack([(j[:, None] + 128 * r < t[None, :]).astype(np.float32) for r in range(4)], 1)
    cb = np.concatenate([ident, nUincl, nLstr, sbmask.reshape(128, 2048)], 1).astype(np.float32)
    dsaneg = np.where(j[None, :] > j[:, None], -1e30, 0.0).astype(np.float32)
    inv_freq = (500000.0 ** (-np.arange(0, 16, 2, dtype=np.float32) / 16)).astype(np.float32)
    pm = j % 64
    invf = np.where(pm < 16, inv_freq[pm % 8], 0.0).astype(np.float32)
    sgn = np.where(pm < 8, -1.0, np.where(pm < 16, 1.0, 0.0)).astype(np.float32)
    cf = np.concatenate([dsaneg, invf[:, None], sgn[:, None]], 1).astype(np.float32)
    return cb, cf


def make_in_maps(inputs, ncores=8):
    f = lambda a: np.ascontiguousarray(np.asarray(a), dtype=np.float32)
    x = f(inputs["x"])
    p = f(inputs["p"])[0]
    pos = np.ascontiguousarray(np.asarray(inputs["positions"]), dtype=np.int32)
    cb, cf = host_consts()
    gl = lambda g: f(g).reshape(KC, 128).T
    gcols = np.ascontiguousarray(np.concatenate([gl(inputs["ffn1_norm"][0]), gl(inputs["mix_norm"][0]),
                                                 gl(inputs["ffn2_norm"][0]), gl(inputs["ple_norm"][0])], 1))
    shared = {
        "w1a": f(inputs["ffn1_w1"][0]), "w2a": f(inputs["ffn1_w2"][0]), "win": f(inputs["w_in"][0]),
        "wosb": f(inputs["w_out_sb"][0]), "wod": f(inputs["w_out_dsa"][0]), "wo": f(inputs["w_out"][0]),
        "w1b": f(inputs["ffn2_w1"][0]), "w2b": f(inputs["ffn2_w2"][0]), "wpg": f(inputs["ple_w_gate"][0]),
        "wpp": f(inputs["ple_w_proj"][0]), "gcols": gcols, "gfin": f(inputs["final_norm"]).reshape(1, D),
        "cb": cb, "cf": cf,
    }
    maps = []
    for c in range(ncores):
        m = dict(shared)
        m["x"] = np.ascontiguousarray(x[NSEQ * c:NSEQ * (c + 1)].reshape(NTOK, D))
        m["p"] = np.ascontiguousarray(p[NSEQ * c:NSEQ * (c + 1)].reshape(NTOK, 256))
        m["pos"] = np.ascontiguousarray(pos[NSEQ * c:NSEQ * (c + 1)].reshape(1, NTOK))
        maps.append(m)
    return maps


def kernel(**inputs):
    nc = bass.Bass("TRN2", target_bir_lowering=False)
    build(nc)
    maps = make_in_maps(inputs, 8)
    res = run_bass_kernel_spmd(nc, maps, core_ids=list(range(8)))
    out = np.stack([r["out"].reshape(NSEQ, SEQ, D) for r in res.results], 0).reshape(16, SEQ, D)
    return out.astype(np.float32)
```
